# Optimizing a Trainium2 kernel written in Bass

```python
import math
import jax, jax.numpy as jnp
from jax import lax
import numpy as np

D_MODEL = 1024
BATCH = 8
SEQ = 4096
DEPTH = 2

GRID_W = 64
CTX_LEN = 256
NORM_EPS = 1e-6
FFN_DIM = 4 * D_MODEL

HEAD_DIM = 64
RWKV_DIM = D_MODEL // 2
RWKV_HEADS = RWKV_DIM // HEAD_DIM
DECAY_LORA = 64
AAA_LORA = 64
GATE_LORA = 128
RWKV_COLS = 3 * RWKV_DIM + DECAY_LORA + AAA_LORA + GATE_LORA
RWKV_GN_EPS = 64e-5
DIFF_V = 2 * HEAD_DIM
DIFF_HEADS = (D_MODEL // 2) // DIFF_V
DIFF_QK = DIFF_HEADS * 2 * HEAD_DIM
DIFF_COLS = 2 * DIFF_QK + DIFF_HEADS * DIFF_V
HYB_IN = RWKV_COLS + DIFF_COLS
MIX_WIDTH = RWKV_DIM + DIFF_HEADS * DIFF_V
Q_BLOCK = 128
ROPE_BASE = 10000.0
ROPE_AXIS = HEAD_DIM // 2

D_INNER = 2 * D_MODEL
SSD_HEAD_DIM = 64
SSD_HEADS = D_INNER // SSD_HEAD_DIM
SSD_GROUPS = 8
SSD_HPG = SSD_HEADS // SSD_GROUPS
D_STATE = 128
CONV_W = 5
CONV_DIM = D_INNER + 2 * SSD_GROUPS * D_STATE
SSD_IN = D_INNER + CONV_DIM + SSD_HEADS
CHUNK = 128

N_EVEN = (DEPTH + 1) // 2
N_ODD = DEPTH // 2

kernel_name = "hybrid_rwkv7_diffattn_ssd_dit"


def rmsnorm(x, g, eps=NORM_EPS):
    xf = x.astype(jnp.float32)
    y = xf * lax.rsqrt(jnp.mean(xf * xf, axis=-1, keepdims=True) + eps)
    return (y * g.astype(jnp.float32)).astype(x.dtype)


def centred_shift(p):
    prev = jnp.pad(p[:, :-1], ((0, 0), (1, 0), (0, 0)))
    nxt = jnp.pad(p[:, 1:], ((0, 0), (0, 1), (0, 0)))
    return 0.5 * (prev + nxt)


def dwconv_centred(x, w, b):
    k = w.shape[0]
    out = lax.conv_general_dilated(x, w[:, None, :].astype(x.dtype), window_strides=(1,),
                                   padding=[(k // 2, k // 2)],
                                   dimension_numbers=('NWC', 'WIO', 'NWC'),
                                   feature_group_count=x.shape[-1])
    return out + b


def axial_angles(t_len):
    n_rows = t_len // GRID_W
    rows = jnp.broadcast_to(jnp.arange(n_rows)[:, None], (n_rows, GRID_W)).reshape(-1)
    cols = jnp.broadcast_to(jnp.arange(GRID_W)[None, :], (n_rows, GRID_W)).reshape(-1)
    inv = ROPE_BASE ** (-jnp.arange(0, ROPE_AXIS, 2, dtype=jnp.float32) / ROPE_AXIS)
    return rows.astype(jnp.float32)[:, None] * inv, cols.astype(jnp.float32)[:, None] * inv


def rope_half(x, ang):
    h = ROPE_AXIS // 2
    x1, x2 = x[..., :h], x[..., h:]
    cos, sin = jnp.cos(ang).astype(x.dtype), jnp.sin(ang).astype(x.dtype)
    return jnp.concatenate([x1 * cos - x2 * sin, x2 * cos + x1 * sin], axis=-1)


def rope_2d(x, ang_r, ang_c):
    ar = ang_r[:, None, None, :]
    ac = ang_c[:, None, None, :]
    return jnp.concatenate([rope_half(x[..., :ROPE_AXIS], ar), rope_half(x[..., ROPE_AXIS:], ac)], axis=-1)


def rwkv_prepare(p, mu, w0, w_up, a0, a_up, g_up, k_k, k_a):
    bsz, t_len = p.shape[:2]
    p = p + (centred_shift(p) - p) * mu
    r, k, v, wd, ad, gd = jnp.split(p, [RWKV_DIM, 2 * RWKV_DIM, 3 * RWKV_DIM,
                                        3 * RWKV_DIM + DECAY_LORA,
                                        3 * RWKV_DIM + DECAY_LORA + AAA_LORA], axis=-1)
    hd = lambda t: t.reshape(bsz, t_len, RWKV_HEADS, HEAD_DIM)
    kkf = hd(k * k_k).astype(jnp.float32)
    kk = kkf * lax.rsqrt(jnp.sum(kkf * kkf, axis=-1, keepdims=True) + 1e-12)
    g = jax.nn.sigmoid(gd) @ g_up
    dirs = []
    for d in range(2):
        wlog = -jax.nn.softplus(-(w0[d] + jnp.tanh(wd) @ w_up[d])) - 0.5
        decay = jnp.exp(-jnp.exp(wlog.astype(jnp.float32)))
        a = jax.nn.sigmoid(a0[d] + ad @ a_up[d])
        kmod = k * (1.0 + (a - 1.0) * k_a)
        dirs.append((hd(decay), hd(kmod), kk * hd(a).astype(jnp.float32)))
    return hd(r), hd(v), kk, g, dirs


def wkv_scan(r, decay, k, v, kk, b, s0, reverse):
    xs = tuple(jnp.moveaxis(t.astype(jnp.float32), 1, 0) for t in (r, decay, k, v, kk, b))

    def step(s, inp):
        r_t, w_t, k_t, v_t, kk_t, b_t = inp
        sa = jnp.einsum('bhvk,bhk->bhv', s, -kk_t)
        s = s * w_t[:, :, None, :] + sa[..., None] * b_t[:, :, None, :] + v_t[..., None] * k_t[:, :, None, :]
        return s, jnp.einsum('bhvk,bhk->bhv', s, r_t)

    s, ys = lax.scan(step, s0, xs, reverse=reverse)
    return jnp.moveaxis(ys, 0, 1), s


def wkv_bidir(prep, s0s):
    r, v, kk, _, dirs = prep
    (df, kf, bf), (db, kb, bb) = dirs
    yf, sf = wkv_scan(r, df, kf, v, kk, bf, s0s[0], False)
    yb, sb = wkv_scan(r, db, kb, v, kk, bb, s0s[1], True)
    return yf + yb, sf, sb


def rwkv_output(y, prep, r_k, ln_w, ln_b):
    r, v, _, g, dirs = prep
    bsz, t_len = r.shape[:2]
    mean = jnp.mean(y, axis=-1, keepdims=True)
    var = jnp.mean(jnp.square(y - mean), axis=-1, keepdims=True)
    yn = ((y - mean) * lax.rsqrt(var + RWKV_GN_EPS)).reshape(bsz, t_len, RWKV_DIM)
    yn = (yn * ln_w + ln_b).astype(r.dtype)
    k_bonus = 0.5 * (dirs[0][1] + dirs[1][1])
    bonus = (jnp.sum(r * k_bonus * r_k, axis=-1, keepdims=True) * v).reshape(bsz, t_len, RWKV_DIM)
    return (yn + bonus) * g


def diff_combine(s, lam, v):
    p = jax.nn.softmax(s.astype(jnp.float32), axis=-1)
    w = p[:, :, 0] - lam * p[:, :, 1]
    return jnp.einsum('bhqk,bhkd->bhqd', w.astype(v.dtype), v)


def diff_attn_latent(q_rot, q_plain, k_rot, k_ctx, v_all, lam):
    bsz, t_len = q_rot.shape[:2]
    nblk = t_len // Q_BLOCK
    to_blocks = lambda q: q.reshape(bsz, nblk, Q_BLOCK, DIFF_HEADS, 2, HEAD_DIM).transpose(1, 0, 3, 4, 2, 5)
    kr = k_rot.transpose(0, 2, 3, 1, 4)
    kc = k_ctx.transpose(0, 2, 3, 1, 4)
    vt = v_all.transpose(0, 2, 1, 3)
    scale = HEAD_DIM ** -0.5

    def block(qs):
        qr, qp = qs
        s = jnp.concatenate([jnp.einsum('bhmqd,bhmkd->bhmqk', qr, kr),
                             jnp.einsum('bhmqd,bhmkd->bhmqk', qp, kc)], axis=-1) * scale
        return diff_combine(s, lam, vt)

    o = lax.map(block, (to_blocks(q_rot), to_blocks(q_plain)))
    return o.transpose(1, 0, 3, 2, 4).reshape(bsz, t_len, DIFF_HEADS, DIFF_V)


def hybrid_mixer(h_lat, h_ctx, ang_r, ang_c, lam_init, ctx_out, w_in, w_out, mu, w0, w_up, a0, a_up,
                 g_up, k_k, k_a, r_k, ln_w, ln_b, lq1, lk1, lq2, lk2, subln_g):
    bsz, t_len = h_lat.shape[:2]
    l_ctx = h_ctx.shape[1]
    p_lat = h_lat @ w_in
    p_ctx = h_ctx @ w_in

    rc = rwkv_prepare(p_ctx[..., :RWKV_COLS], mu, w0, w_up, a0, a_up, g_up, k_k, k_a)
    rl = rwkv_prepare(p_lat[..., :RWKV_COLS], mu, w0, w_up, a0, a_up, g_up, k_k, k_a)
    s_zero = jnp.zeros((bsz, RWKV_HEADS, HEAD_DIM, HEAD_DIM), jnp.float32)
    yc, sf, sb = wkv_bidir(rc, (s_zero, s_zero))
    yl, _, _ = wkv_bidir(rl, (sf, sb))
    a_lat = rwkv_output(yl, rl, r_k, ln_w, ln_b)

    lam = (jnp.exp(jnp.sum(lq1 * lk1).astype(jnp.float32)) - jnp.exp(jnp.sum(lq2 * lk2).astype(jnp.float32))
           + lam_init)
    def qkv(p, n):
        d = p[..., RWKV_COLS:]
        q = d[..., :DIFF_QK].reshape(bsz, n, DIFF_HEADS, 2, HEAD_DIM)
        k = d[..., DIFF_QK:2 * DIFF_QK].reshape(bsz, n, DIFF_HEADS, 2, HEAD_DIM)
        v = d[..., 2 * DIFF_QK:].reshape(bsz, n, DIFF_HEADS, DIFF_V)
        return q, k, v
    ql, kl, vl = qkv(p_lat, t_len)
    qc, kc, vc = qkv(p_ctx, l_ctx)
    o_l = diff_attn_latent(rope_2d(ql, ang_r, ang_c), ql, rope_2d(kl, ang_r, ang_c), kc,
                           jnp.concatenate([vl, vc], axis=1), lam)
    subln = lambda o, n: (rmsnorm(o, subln_g) * (1.0 - lam_init)).reshape(bsz, n, DIFF_HEADS * DIFF_V)
    out_lat = jnp.concatenate([a_lat, subln(o_l, t_len)], axis=-1) @ w_out
    if not ctx_out:
        return out_lat, None
    a_ctx = rwkv_output(yc, rc, r_k, ln_w, ln_b)
    s_c = jnp.einsum('bqhmd,bkhmd->bhmqk', qc, kc) * (HEAD_DIM ** -0.5)
    o_c = diff_combine(s_c, lam, vc.transpose(0, 2, 1, 3)).transpose(0, 2, 1, 3)
    out_ctx = jnp.concatenate([a_ctx, subln(o_c, l_ctx)], axis=-1) @ w_out
    return out_lat, out_ctx


def ssd_scan(x, dt, a, bm, cm, h0):
    bsz, t_len = x.shape[:2]
    nc = t_len // CHUNK
    chunks = lambda t: jnp.moveaxis(t.reshape((bsz, nc, CHUNK) + t.shape[2:]), 1, 0)
    idx = jnp.arange(CHUNK)
    mask = (idx[:, None] >= idx[None, :])[:, :, None, None]

    def step(h, inp):
        xc, dtc, bc, cc = inp
        cs = jnp.cumsum(dtc * a, axis=1)
        seg = cs[:, :, None] - cs[:, None, :]
        lmat = jnp.exp(jnp.where(mask, seg, -jnp.inf))
        xdt = xc * dtc[..., None]
        cb = jnp.einsum('blgn,bsgn->blsg', cc, bc)
        y_diag = jnp.einsum('blsg,blsgh,bsghp->blghp', cb, lmat, xdt)
        y_off = jnp.einsum('blgn,bghpn->blghp', cc, h) * jnp.exp(cs)[..., None]
        h_new = (h * jnp.exp(cs[:, -1])[..., None, None]
                 + jnp.einsum('bsgn,bsgh,bsghp->bghpn', bc, jnp.exp(cs[:, -1:] - cs), xdt))
        return h_new, y_diag + y_off

    h, ys = lax.scan(step, h0, (chunks(x), chunks(dt), chunks(bm), chunks(cm)))
    return jnp.moveaxis(ys, 0, 1).reshape(x.shape), h


def ssd_mixer(h_lat, h_ctx, ctx_out, w_in, conv_w, conv_b, dt_bias, a_log, d_skip, norm_g, w_out):
    bsz = h_lat.shape[0]

    def prep(h):
        n = h.shape[1]
        p = h @ w_in
        z, xbc, dt = jnp.split(p, [D_INNER, D_INNER + CONV_DIM], axis=-1)
        xbc = jax.nn.silu(dwconv_centred(xbc, conv_w, conv_b))
        xs, bm, cm = jnp.split(xbc, [D_INNER, D_INNER + SSD_GROUPS * D_STATE], axis=-1)
        f = jnp.float32
        return (z, xs.reshape(bsz, n, SSD_GROUPS, SSD_HPG, SSD_HEAD_DIM).astype(f),
                bm.reshape(bsz, n, SSD_GROUPS, D_STATE).astype(f),
                cm.reshape(bsz, n, SSD_GROUPS, D_STATE).astype(f),
                dt.reshape(bsz, n, SSD_GROUPS, SSD_HPG).astype(f))

    def run(pr, h0s):
        _, xs, bm, cm, dt = pr
        y = d_skip.reshape(SSD_GROUPS, SSD_HPG).astype(jnp.float32)[..., None] * xs
        flip = lambda t: jnp.flip(t, axis=1)
        finals = []
        for d in range(2):
            dtd = jax.nn.softplus(dt + dt_bias[d].reshape(SSD_GROUPS, SSD_HPG).astype(jnp.float32))
            a = -jnp.exp(a_log[d].reshape(SSD_GROUPS, SSD_HPG).astype(jnp.float32))
            if d == 0:
                yd, hd = ssd_scan(xs, dtd, a, bm, cm, h0s[0])
            else:
                yd, hd = ssd_scan(flip(xs), flip(dtd), a, flip(bm), flip(cm), h0s[1])
                yd = flip(yd)
            y = y + yd
            finals.append(hd)
        return y, finals

    def out(y, z):
        n = z.shape[1]
        yg = y.reshape(bsz, n, D_INNER).astype(z.dtype) * jax.nn.silu(z)
        yn = rmsnorm(yg.reshape(bsz, n, SSD_GROUPS, D_INNER // SSD_GROUPS),
                     norm_g.reshape(SSD_GROUPS, D_INNER // SSD_GROUPS))
        return yn.reshape(bsz, n, D_INNER) @ w_out

    pc = prep(h_ctx)
    h_zero = jnp.zeros((bsz, SSD_GROUPS, SSD_HPG, SSD_HEAD_DIM, D_STATE), jnp.float32)
    yc, finals = run(pc, (h_zero, h_zero))
    pl = prep(h_lat)
    yl, _ = run(pl, finals)
    out_lat = out(yl, pl[0])
    out_ctx = out(yc, pc[0]) if ctx_out else None
    return out_lat, out_ctx


def sq_relu_mlp(h, w1, w2):
    return jnp.square(jax.nn.relu(h @ w1)) @ w2


def setup_inputs(seed: int = 0) -> dict:
    key = jax.random.key(seed)
    ks = iter(jax.random.split(key, 48))
    f32 = jnp.float32
    nrm = lambda shape, s: jax.random.normal(next(ks), shape, f32) * s
    D = D_MODEL
    dt0 = jnp.exp(jax.random.uniform(next(ks), (N_ODD, 2, SSD_HEADS), f32)
                  * (math.log(0.1) - math.log(0.001)) + math.log(0.001))
    return {
        "x": nrm((BATCH, SEQ, D), 1.0),
        "c": nrm((BATCH, D), 1.0),
        "ctx": nrm((BATCH, CTX_LEN, D), 1.0),
        "c_ctx": nrm((D,), 1.0),
        "ada_w": nrm((DEPTH, D, 6 * D), 0.5 * D ** -0.5),
        "ada_b": nrm((DEPTH, 6 * D), 0.02),
        "norm1_g": 1.0 + nrm((DEPTH, D), 0.02),
        "norm2_g": 1.0 + nrm((DEPTH, D), 0.02),
        "mlp_w1": nrm((DEPTH, D, FFN_DIM), D ** -0.5),
        "mlp_w2": nrm((DEPTH, FFN_DIM, D), FFN_DIM ** -0.5),
        "hy_w_in": nrm((N_EVEN, D, HYB_IN), D ** -0.5),
        "hy_w_out": nrm((N_EVEN, MIX_WIDTH, D), MIX_WIDTH ** -0.5),
        "rwkv_mu": jax.random.uniform(next(ks), (N_EVEN, RWKV_COLS), f32),
        "rwkv_w0": nrm((N_EVEN, 2, RWKV_DIM), 0.3),
        "rwkv_w_up": nrm((N_EVEN, 2, DECAY_LORA, RWKV_DIM), 0.5 * DECAY_LORA ** -0.5),
        "rwkv_a0": nrm((N_EVEN, 2, RWKV_DIM), 0.3),
        "rwkv_a_up": nrm((N_EVEN, 2, AAA_LORA, RWKV_DIM), 0.5 * AAA_LORA ** -0.5),
        "rwkv_g_up": nrm((N_EVEN, GATE_LORA, RWKV_DIM), GATE_LORA ** -0.5),
        "rwkv_k_k": 0.85 + nrm((N_EVEN, RWKV_DIM), 0.02),
        "rwkv_k_a": 1.0 + nrm((N_EVEN, RWKV_DIM), 0.02),
        "rwkv_r_k": nrm((N_EVEN, RWKV_HEADS, HEAD_DIM), 0.1),
        "rwkv_ln_w": 1.0 + nrm((N_EVEN, RWKV_DIM), 0.02),
        "rwkv_ln_b": nrm((N_EVEN, RWKV_DIM), 0.02),
        "diff_lq1": nrm((N_EVEN, HEAD_DIM), 0.1),
        "diff_lk1": nrm((N_EVEN, HEAD_DIM), 0.1),
        "diff_lq2": nrm((N_EVEN, HEAD_DIM), 0.1),
        "diff_lk2": nrm((N_EVEN, HEAD_DIM), 0.1),
        "diff_subln_g": 1.0 + nrm((N_EVEN, DIFF_V), 0.02),
        "ssd_w_in": nrm((N_ODD, D, SSD_IN), D ** -0.5),
        "ssd_conv_w": nrm((N_ODD, CONV_W, CONV_DIM), CONV_W ** -0.5),
        "ssd_conv_b": nrm((N_ODD, CONV_DIM), 0.02),
        "ssd_dt_bias": dt0 + jnp.log(-jnp.expm1(-dt0)),
        "ssd_a_log": jnp.log(jax.random.uniform(next(ks), (N_ODD, 2, SSD_HEADS), f32, 1.0, 16.0)),
        "ssd_d": 1.0 + nrm((N_ODD, SSD_HEADS), 0.1),
        "ssd_norm_g": 1.0 + nrm((N_ODD, D_INNER), 0.02),
        "ssd_w_out": nrm((N_ODD, D_INNER, D), D_INNER ** -0.5),
        "norm_f_g": 1.0 + nrm((D,), 0.02),
    }


def reference(x, c, ctx, c_ctx, ada_w, ada_b, norm1_g, norm2_g, mlp_w1, mlp_w2, hy_w_in, hy_w_out,
              rwkv_mu, rwkv_w0, rwkv_w_up, rwkv_a0, rwkv_a_up, rwkv_g_up, rwkv_k_k, rwkv_k_a, rwkv_r_k,
              rwkv_ln_w, rwkv_ln_b, diff_lq1, diff_lk1, diff_lq2, diff_lk2, diff_subln_g,
              ssd_w_in, ssd_conv_w, ssd_conv_b, ssd_dt_bias, ssd_a_log, ssd_d, ssd_norm_g, ssd_w_out,
              norm_f_g):
    t_len = x.shape[1]
    ang_r, ang_c = axial_angles(t_len)
    xl, xc = x, ctx
    for li in range(DEPTH):
        last = li == DEPTH - 1
        mod_l = jnp.split((jax.nn.silu(c) @ ada_w[li] + ada_b[li])[:, None, :], 6, axis=-1)
        mod_c = jnp.split(jax.nn.silu(c_ctx) @ ada_w[li] + ada_b[li], 6, axis=-1)
        hl = rmsnorm(xl, norm1_g[li]) * (1.0 + mod_l[1]) + mod_l[0]
        hc = rmsnorm(xc, norm1_g[li]) * (1.0 + mod_c[1]) + mod_c[0]
        if li % 2 == 0:
            e = li // 2
            lam_init = 0.8 - 0.6 * math.exp(-0.3 * li)
            ol, oc = hybrid_mixer(hl, hc, ang_r, ang_c, lam_init, not last, hy_w_in[e], hy_w_out[e],
                                  rwkv_mu[e], rwkv_w0[e], rwkv_w_up[e], rwkv_a0[e], rwkv_a_up[e],
                                  rwkv_g_up[e], rwkv_k_k[e], rwkv_k_a[e], rwkv_r_k[e], rwkv_ln_w[e],
                                  rwkv_ln_b[e], diff_lq1[e], diff_lk1[e], diff_lq2[e], diff_lk2[e],
                                  diff_subln_g[e])
        else:
            o = li // 2
            ol, oc = ssd_mixer(hl, hc, not last, ssd_w_in[o], ssd_conv_w[o], ssd_conv_b[o],
                               ssd_dt_bias[o], ssd_a_log[o], ssd_d[o], ssd_norm_g[o], ssd_w_out[o])
        xl = xl + mod_l[2] * ol
        hl = rmsnorm(xl, norm2_g[li]) * (1.0 + mod_l[4]) + mod_l[3]
        xl = xl + mod_l[5] * sq_relu_mlp(hl, mlp_w1[li], mlp_w2[li])
        if not last:
            xc = xc + mod_c[2] * oc
            hc = rmsnorm(xc, norm2_g[li]) * (1.0 + mod_c[4]) + mod_c[3]
            xc = xc + mod_c[5] * sq_relu_mlp(hc, mlp_w1[li], mlp_w2[li])
    return rmsnorm(xl, norm_f_g)
```

```python
import concourse.bass as bass
import concourse.mybir as mybir

F32 = mybir.dt.float32
BF16 = mybir.dt.bfloat16
AF = mybir.ActivationFunctionType
ALU = mybir.AluOpType
AX = mybir.AxisListType


class Src:
    def __init__(s, kb, name, inc, limit):
        s.kb, s.name, s.inc, s.limit = kb, name, inc, limit
        s.sems = []
        s.n = 0

    def sem_for(s, n):
        e = (n - 1) // s.limit
        while len(s.sems) <= e:
            s.sems.append(s.kb.es.enter_context(s.kb.nc.semaphore(f"{s.name}_{len(s.sems)}")))
        return s.sems[e], ((n - 1) % s.limit + 1) * s.inc


class Tk:
    __slots__ = ("w", "r")

    def __init__(s):
        s.w = {}
        s.r = {}


class TT:
    def __init__(s, t, k=None, ps=False):
        s.t = t
        s.k = k if k is not None else Tk()
        s.ps = ps

    def __getitem__(s, idx):
        return s.t[idx]


class KB:
    NSLOT = 20

    def __init__(s, nc, es):
        s.nc, s.es = nc, es
        s.eng = {"pe": nc.tensor, "dve": nc.vector, "act": nc.scalar, "pool": nc.gpsimd, "sp": nc.sync}
        s.src = {k: Src(s, "c" + k, 1, 30000) for k in s.eng}
        s.waited = {k: {} for k in s.eng}
        s.slots = {q: [Src(s, f"d{q}{i}", 16, 1800) for i in range(s.NSLOT)] for q in ("sp", "pool", "act")}
        s.rr = {q: 0 for q in s.slots}
        s.nins = 0
        s.same_engine_sync = True

    def sb(s, name, shape, dt=F32):
        return TT(s.es.enter_context(s.nc.sbuf_tensor("g_" + name, list(shape), dt)))

    def ps(s, name, shape, dt=F32):
        return TT(s.es.enter_context(s.nc.psum_tensor("gp_" + name, list(shape), dt)))

    def _deps(s, R, W, me=None):
        d = {}
        for r in R:
            k = r.k if isinstance(r, TT) else r
            for src, n in k.w.items():
                if d.get(src, 0) < n:
                    d[src] = n
            if isinstance(r, TT) and r.ps:
                for src, n in k.r.items():
                    if src is not me and d.get(src, 0) < n:
                        d[src] = n
        for w in W:
            k = w.k if isinstance(w, TT) else w
            for dd in (k.w, k.r):
                for src, n in dd.items():
                    if d.get(src, 0) < n:
                        d[src] = n
        return d

    def _wait(s, eng, d):
        wd = s.waited[eng]
        for src, n in d.items():
            if src is s.src[eng] and (eng == "pe" or not s.same_engine_sync):
                continue
            if wd.get(src, 0) >= n:
                continue
            sem, val = src.sem_for(n)
            s.eng[eng].wait_ge(sem, val)
            wd[src] = n

    def _mark(s, src, n, R, W):
        for w in W:
            k = w.k if isinstance(w, TT) else w
            k.w = {src: n}
            k.r = {}
        for r in R:
            k = r.k if isinstance(r, TT) else r
            if k.r.get(src, 0) < n:
                k.r[src] = n

    def op(s, eng, fn, R=(), W=()):
        d = s._deps(R, W, s.src[eng])
        s._wait(eng, d)
        src = s.src[eng]
        src.n += 1
        sem, _ = src.sem_for(src.n)
        ins = fn(s.eng[eng])
        ins.then_inc(sem, 1)
        s._mark(src, src.n, R, W)
        s.nins += 1

    def dma(s, q, out, in_, R=(), W=(), **kw):
        i = s.rr[q]
        s.rr[q] = (i + 1) % s.NSLOT
        slot = s.slots[q][i]
        d = s._deps(R, W)
        if slot.n > 0 and d.get(slot, 0) < slot.n:
            d[slot] = slot.n
        s._wait(q, d)
        slot.n += 1
        sem, _ = slot.sem_for(slot.n)
        s.eng[q].dma_start(out=out, in_=in_, **kw).then_inc(sem, 16)
        s._mark(slot, slot.n, R, W)
        s.nins += 1

    def load(s, out, in_, R=(), W=(), **kw):
        s.dma("sp", out, in_, R, W, **kw)

    def store(s, out, in_, R=(), W=(), **kw):
        s.dma("pool", out, in_, R, W, **kw)

    def finish(s):
        d = {}
        for q in s.slots:
            for sl in s.slots[q]:
                if sl.n:
                    d[sl] = sl.n
        for k, src in s.src.items():
            if src.n and k != "sp":
                d[src] = src.n
        s._wait("sp", d)

    def mm(s, out, lhsT, rhs, start=True, stop=True, R=(), W=()):
        s.op("pe", lambda e: e.matmul(out, lhsT=lhsT, rhs=rhs, start=start, stop=stop), R, W)

    def tr(s, out, in_, ident, R=(), W=()):
        s.op("pe", lambda e: e.transpose(out, in_, ident), R, W)

    def act(s, out, in_, func, bias=None, scale=None, R=(), W=(), accum_out=None):
        kw = {}
        if bias is not None:
            kw["bias"] = bias
        if scale is not None:
            kw["scale"] = scale
        if accum_out is not None:
            kw["accum_out"] = accum_out
        s.op("act", lambda e: e.activation(out=out, in_=in_, func=func, **kw), R, W)

    def tt(s, out, in0, in1, op, R=(), W=(), eng="dve"):
        s.op(eng, lambda e: e.tensor_tensor(out=out, in0=in0, in1=in1, op=op), R, W)

    def ts(s, out, in0, s1, s2, op0, op1=None, R=(), W=(), eng="dve"):
        if op1 is None:
            s.op(eng, lambda e: e.tensor_scalar(out=out, in0=in0, scalar1=s1, scalar2=None, op0=op0), R, W)
        else:
            s.op(eng, lambda e: e.tensor_scalar(out=out, in0=in0, scalar1=s1, scalar2=s2, op0=op0, op1=op1), R, W)

    def stt(s, out, in0, scalar, in1, op0, op1, R=(), W=()):
        s.op("dve", lambda e: e.scalar_tensor_tensor(out=out, in0=in0, scalar=scalar, in1=in1, op0=op0, op1=op1), R, W)

    def copy(s, out, in_, R=(), W=(), eng="dve"):
        if eng == "act":
            s.op("act", lambda e: e.copy(out=out, in_=in_), R, W)
        else:
            s.op(eng, lambda e: e.tensor_copy(out=out, in_=in_), R, W)

    def memset(s, ap, val, W=(), eng="dve"):
        s.op(eng, lambda e: e.memset(ap, val), (), W)
import math
import numpy as np
from contextlib import ExitStack, contextmanager
from concourse.bass_utils import run_bass_kernel_spmd

T = 4352
NCTX = 256
TBS = [(0, 256)] + [(256 + 512 * i, 512) for i in range(8)]
KAPPA = math.exp(-0.5)
EPS = 1e-6


class G:
    pass


@contextmanager
def phase(kb):
    old = kb.es
    barrier(kb)
    with ExitStack() as es:
        kb.es_t = es
        yield
        barrier(kb)
    kb.es_t = None


def barrier(kb):
    d = {}
    for q in kb.slots:
        for sl in kb.slots[q]:
            if sl.n:
                d[sl] = sl.n
    for k, src in kb.src.items():
        if src.n:
            d[src] = src.n
    for e in ("pe", "dve", "act", "pool", "sp"):
        dd = {s_: n for s_, n in d.items() if s_ is not kb.src[e]}
        kb._wait(e, dd)


_uid = [0]


def sbt(kb, name, shape, dt=F32):
    _uid[0] += 1
    return TT(kb.es_t.enter_context(kb.nc.sbuf_tensor(f"s{_uid[0]}_{name}", list(shape), dt)))


def pst(kb, name, shape=None, dt=F32):
    _uid[0] += 1
    full = [128, 512] if dt == F32 else [128, 1024]
    return TT(kb.es_t.enter_context(kb.nc.psum_tensor(f"p{_uid[0]}_{name}", full, dt)), ps=True)


def run_interleaved(gens):
    gens = list(gens)
    while gens:
        for g_ in list(gens):
            try:
                next(g_)
            except StopIteration:
                gens.remove(g_)


class PPool:
    def __init__(s, banks):
        s.b = banks
        s.live = [False] * len(banks)
        s.i = 0

    def get(s):
        n = len(s.b)
        for k in range(n):
            j = (s.i + k) % n
            if not s.live[j]:
                s.live[j] = True
                s.i = (j + 1) % n
                return s.b[j]
        raise RuntimeError("PSUM pool exhausted")

    def put(s, bank):
        s.live[s.b.index(bank)] = False


class Rot:
    def __init__(s, items):
        s.items = items
        s.i = 0

    def next(s):
        x = s.items[s.i]
        s.i = (s.i + 1) % len(s.items)
        return x


_AS_INPUT = set()


def dram(nc, name, shape, dt, debug):
    kind = "ExternalInput" if name in _AS_INPUT else ("ExternalOutput" if debug else "Internal")
    return nc.dram_tensor(name, list(shape), dt, kind=kind).ap()


def phase_mods(kb, g):
    nc = kb.nc
    with phase(kb):
        cT = sbt(kb, "cT", [128, 8, 2])
        scT = sbt(kb, "scT", [128, 8, 2])
        sg_ = sbt(kb, "sgc", [128, 8, 2])
        kb.load(cT[:], g.cT[:, :, :], W=[cT])
        kb.act(sg_[:], cT[:], AF.Sigmoid, R=[cT], W=[sg_])
        kb.tt(scT[:], cT[:], sg_[:], ALU.mult, R=[cT, sg_], W=[scT])
        wb = Rot([sbt(kb, f"adaw{i}", [128, 8, 1024]) for i in range(2)])
        pmb = pst(kb, "pmod")
        pm = TT(pmb.t[:, 0:96].rearrange("p (a b) -> p a b", b=2), pmb.k, ps=True)
        adab = sbt(kb, "adab", [128, 48])
        g1 = sbt(kb, "g1", [128, 8])
        g2 = sbt(kb, "g2", [128, 8])
        for li in range(2):
            kb.load(adab[:], g.ada_bT[li], W=[adab])
            kb.load(g1[:], g.g1T[li], W=[g1])
            kb.load(g2[:], g.g2T[li], W=[g2])
            src = g.ada_w[li].rearrange("(kc p) n -> p kc n", p=128)
            for pc in range(6):
                w = wb.next()
                kb.load(w[:], src[:, :, pc * 1024:(pc + 1) * 1024], W=[w])
                for cc in range(8):
                    col = pc * 8 + cc
                    for kc in range(8):
                        kb.mm(pm[:, col, :], w[:, kc, cc * 128:(cc + 1) * 128], scT[:, kc, :],
                              start=(kc == 0), stop=(kc == 7), R=[w, scT], W=[pm])
            mod = g.mod[li]
            kb.tt(mod[:], pm[:], adab[:, :, None].to_broadcast([128, 48, 2]), ALU.add, R=[pm, adab], W=[mod])
            for (sc, gi, m) in ((g.sc1[li], g1, 1), (g.sc2[li], g2, 4)):
                kb.ts(sc[:], mod[:, m * 8:(m + 1) * 8, :], 1.0, None, ALU.add, R=[mod], W=[sc])
                kb.tt(sc[:], sc[:], gi[:, :, None].to_broadcast([128, 8, 2]), ALU.mult, R=[sc, gi], W=[sc])


def phase_xT(kb, g):
    with phase(kb):
        xin = Rot([sbt(kb, f"xin{i}", [128, 1024]) for i in range(2)])
        xo = Rot([sbt(kb, f"xo{i}", [128, 8, 128]) for i in range(2)])
        pt = Rot([pst(kb, f"pT{i}") for i in range(4)])
        dst = g.xT.rearrange("(kc p) t -> p kc t", p=128)
        for i in range(34):
            xi = xin.next()
            src = g.ctx[i * 128:(i + 1) * 128, :] if i < 2 else g.x[(i - 2) * 128:(i - 1) * 128, :]
            kb.load(xi[:], src, W=[xi])
            o = xo.next()
            for hf in range(2):
                p = pt.next()
                for j in range(4):
                    kc = hf * 4 + j
                    kb.tr(p[:, j * 128:(j + 1) * 128], xi[:, kc * 128:(kc + 1) * 128], g.ident[:], R=[xi, g.ident], W=[p])
                kb.copy(o[:, hf * 4:(hf + 1) * 4, :], p[:, :].rearrange("p (a b) -> p a b", b=128), R=[p], W=[o], eng=("act" if hf else "dve"))
            kb.store(dst[:, :, i * 128:(i + 1) * 128], o[:], R=[o])


class NormBufs:
    def __init__(s, kb, tag):
        s.sq = sbt(kb, f"nsq{tag}", [128, 8, 512], BF16)
        s.tmp = sbt(kb, f"ntmp{tag}", [128, 8, 512])
        s.rt = sbt(kb, f"nrt{tag}", [128, 512])
        s.rstd = sbt(kb, f"nrstd{tag}", [128, 512])
        s.ss = pst(kb, f"nss{tag}", [128, 512])


def norm_mod(kb, g, nb, xTb, n, sc, sh, j, hT, sh_off=0):
    kb.act(nb.sq[:, :, :n], xTb[:, :, :n], AF.Square, R=[xTb], W=[nb.sq])
    for kc in range(8):
        kb.mm(nb.ss[:, :n], g.ones_bf[:], nb.sq[:, kc, :n], start=(kc == 0), stop=(kc == 7), R=[nb.sq, g.ones_bf], W=[nb.ss])
    kb.act(nb.rt[:, :n], nb.ss[:, :n], AF.Sqrt, bias=g.eps_t[:, 0:1], scale=1.0 / 1024.0, R=[nb.ss, g.eps_t], W=[nb.rt])
    kb.op("dve", lambda e: e.reciprocal(out=nb.rstd[:, :n], in_=nb.rt[:, :n]), R=[nb.rt], W=[nb.rstd])
    kb.tt(nb.tmp[:, :, :n], xTb[:, :, :n], nb.rstd[:, None, :n].to_broadcast([128, 8, n]), ALU.mult, R=[xTb, nb.rstd], W=[nb.tmp])
    for kc in range(8):
        kb.act(hT[:, kc, :n], nb.tmp[:, kc, :n], AF.Identity, bias=sh[:, sh_off + kc, j:j + 1], scale=sc[:, kc, j:j + 1],
               R=[nb.tmp, sc, sh], W=[hT])


def load_weight_bf16(kb, W, src_ap, ncols, stg, piece=256):
    src = src_ap.rearrange("(kc p) n -> p kc n", p=128)
    kcn = src.shape[1]
    i = 0
    for c0 in range(0, ncols, piece):
        c1 = min(ncols, c0 + piece)
        w = c1 - c0
        st = stg.next()
        sv = st.t[:, 0:kcn * w].rearrange("p (k n) -> p k n", n=w)
        kb.load(sv, src[:, :, c0:c1], W=[st])
        kb.copy(W[:, :kcn, c0:c1], sv, R=[st], W=[W], eng=("act" if i % 2 else "dve"))
        i += 1


def phase_A0(kb, g):
    with phase(kb):
        W = sbt(kb, "wA", [128, 8, 4352], BF16)
        stg = Rot([sbt(kb, f"wstg{i}", [128, 2048]) for i in range(2)])
        import os
        STG = int(os.environ.get("A0_STAGE", "9"))
        load_weight_bf16(kb, W, g.w_in0, 4352, stg)
        nb = NormBufs(kb, "A")
        xb = Rot([sbt(kb, f"xTb{i}", [128, 8, 512]) for i in range(2)])
        hTs = Rot([sbt(kb, f"hT{i}", [128, 8, 512], BF16) for i in range(2)])
        cosb = Rot([sbt(kb, f"cos{i}", [128, 512]) for i in range(2)])
        sinb = Rot([sbt(kb, f"sin{i}", [128, 512]) for i in range(2)])
        pmm = Rot([pst(kb, f"pmm{i}", [128, 512]) for i in range(6)])
        st32 = Rot([sbt(kb, f"st32_{i}", [128, 512]) for i in range(4)])
        st16 = Rot([sbt(kb, f"st16_{i}", [128, 512], BF16) for i in range(6)])
        t1s = Rot([sbt(kb, f"t1_{i}", [128, 512]) for i in range(2)])
        t2s = Rot([sbt(kb, f"t2_{i}", [128, 512]) for i in range(2)])
        xsrc = g.xT.rearrange("(kc p) t -> p kc t", p=128)
        for bi, (s0, n) in enumerate(TBS):
            if STG < 2 or (STG < 9 and bi > 0):
                break
            j = 1 if bi == 0 else 0
            x = xb.next()
            kb.load(x[:, :, :n], xsrc[:, :, s0:s0 + n], W=[x])
            cs, sn = cosb.next(), sinb.next()
            kb.load(cs[:, :n], g.cosT[:, s0:s0 + n], W=[cs])
            kb.load(sn[:, :n], g.sinT[:, s0:s0 + n], W=[sn])
            hT = hTs.next()
            norm_mod(kb, g, nb, x, n, g.sc1[0], g.mod[0], j, hT, sh_off=0)

            def proj(ct):
                p = pmm.next()
                for kc in range(8):
                    kb.mm(p[:, :n], W[:, kc, ct * 128:(ct + 1) * 128], hT[:, kc, :n], start=(kc == 0), stop=(kc == 7), R=[W, hT], W=[p])
                return p
            if STG < 3:
                continue
            for ct in range(14):
                p = proj(ct)
                st = st32.next()
                kb.copy(st[:, :n], p[:, :n], R=[p], W=[st], eng=("act" if ct % 2 else "dve"))
                kb.store(g.PrT[ct * 128:(ct + 1) * 128, s0:s0 + n], st[:, :n], R=[st])
            if STG < 4:
                continue
            for h in range(4):
                for (base, dst_rot, dst_pl) in ((14, g.QrotT, g.QplT), (22, g.KrotT, None)):
                    pq = proj(base + h)
                    pw = proj(base + 4 + h)
                    t1, t2 = t1s.next(), t2s.next()
                    kb.tt(t1[:, :n], pq[:, :n], cs[:, :n], ALU.mult, R=[pq, cs], W=[t1])
                    kb.tt(t2[:, :n], pw[:, :n], sn[:, :n], ALU.mult, R=[pw, sn], W=[t2])
                    so = st16.next()
                    kb.tt(so[:, :n], t1[:, :n], t2[:, :n], ALU.add, R=[t1, t2], W=[so])
                    if not os.environ.get("NOSTORE4"):
                        kb.store(dst_rot[h * 128:(h + 1) * 128, s0:s0 + n], so[:, :n], R=[so])
                    if dst_pl is not None:
                        sp_ = st16.next()
                        kb.copy(sp_[:, :n], pq[:, :n], R=[pq], W=[sp_], eng="act")
                        if not os.environ.get("NOSTORE4"):
                            kb.store(dst_pl[h * 128:(h + 1) * 128, s0:s0 + n], sp_[:, :n], R=[sp_])
            if STG < 5:
                continue
            for tt_ in range(n // 128):
                p = pmm.next()
                for kc in range(8):
                    kb.mm(p[:, :], hT[:, kc, tt_ * 128:(tt_ + 1) * 128], W[:, kc, 3840:4352], start=(kc == 0), stop=(kc == 7), R=[W, hT], W=[p])
                so = st16.next()
                kb.copy(so[:], p[:], R=[p], W=[so], eng=("act" if tt_ % 2 else "dve"))
                kb.store(g.Vd[s0 + tt_ * 128:s0 + (tt_ + 1) * 128, :], so[:], R=[so])

def phase_B0(kb, g):
    with phase(kb):
        rwc = sbt(kb, "rwc", [128, 58])
        hmu = sbt(kb, "hmu", [128, 14])
        omm = sbt(kb, "omm", [128, 14])
        omka = sbt(kb, "omka", [128, 4])
        hrk = sbt(kb, "hrk", [128, 4])
        lora = sbt(kb, "lora", [128, 2, 512])
        gup = sbt(kb, "gup", [128, 512])
        blk = sbt(kb, "blk", [128, 128])
        cmask = sbt(kb, "cmask", [128, 512])
        kb.load(rwc[:], g.rw_cols[:, :], W=[rwc])
        kb.load(lora[:], g.rw_lora[:, :, :], W=[lora])
        kb.load(gup[:], g.rw_gup[:, :], W=[gup])
        kb.load(blk[:], g.blk_h[:, :], W=[blk])
        kb.load(cmask[:], g.cmask_h[:, :], W=[cmask])
        kb.ts(hmu[:], rwc[:, 0:14], 0.5, None, ALU.mult, R=[rwc], W=[hmu])
        kb.ts(omm[:], rwc[:, 0:14], -1.0, 1.0, ALU.mult, ALU.add, R=[rwc], W=[omm])
        kb.ts(omka[:], rwc[:, 18:22], -1.0, 1.0, ALU.mult, ALU.add, R=[rwc], W=[omka])
        kb.ts(hrk[:], rwc[:, 22:26], 0.5, None, ALU.mult, R=[rwc], W=[hrk])

        pin = sbt(kb, "pin", [128, 14, 514])
        psx = sbt(kb, "psx", [128, 14, 512])
        lwin = sbt(kb, "lwin", [128, 512])
        sgd = sbt(kb, "sgd", [128, 512])
        F = lambda nm, dt=F32: sbt(kb, nm, [128, 512], dt)
        R2 = lambda nm: Rot([F(f"{nm}{i}") for i in range(2)])
        hp_rots = [R2(nm) for nm in ("kku", "sq", "rt", "rs", "kk", "rk", "bon", "kbs")]
        d_rots = [R2(nm) for nm in ("sg", "a_", "tmp", "kmod", "b_", "ci", "cr", "ce", "e1", "e2", "e3")]
        fm16 = [Rot([F(f"fm{k}_{i}", BF16) for i in range(2)]) for k in range(4)]
        vb = F("vb", BF16)
        st32 = Rot([F(f"bst{i}") for i in range(2)])
        gst = Rot([sbt(kb, f"gst{i}", [128, 8]) for i in range(2)])
        tms = Rot([sbt(kb, f"tms{i}", [128, 1024], BF16) for i in range(3)])
        pmm = Rot([pst(kb, f"bp{i}") for i in range(4)])
        ptr = Rot([pst(kb, f"bt{i}", dt=BF16) for i in range(3)])
        psrc = g.PrT.rearrange("(ti p) t -> p ti t", p=128)
        for bi, (s0, n) in enumerate(TBS):
            nt = n // 128
            nch = n // 64
            c0 = s0 // 64
            lo = s0 if s0 in (0, NCTX) else s0 - 1
            hi = s0 + n if (s0 + n) in (NCTX, T) else s0 + n + 1
            kb.memset(pin[:, :, 0:1], 0.0, W=[pin])
            kb.memset(pin[:, :, n + 1:n + 2], 0.0, W=[pin])
            kb.load(pin[:, :, lo - s0 + 1:hi - s0 + 1], psrc[:, :, lo:hi], W=[pin])
            kb.tt(psx[:, :, :n], pin[:, :, 0:n], pin[:, :, 2:n + 2], ALU.add, R=[pin], W=[psx])
            for ti in range(14):
                kb.act(psx[:, ti, :n], psx[:, ti, :n], AF.Identity, scale=hmu[:, ti:ti + 1], R=[psx, hmu], W=[psx])
            for ti in range(14):
                kb.stt(psx[:, ti, :n], pin[:, ti, 1:n + 1], omm[:, ti:ti + 1], psx[:, ti, :n], ALU.mult, ALU.add, R=[pin, psx, omm], W=[psx])
            kb.act(lwin[0:64, :n], psx[0:64, 12, :n], AF.Tanh, R=[psx], W=[lwin])
            kb.copy(lwin[64:128, :n], psx[64:128, 12, :n], R=[psx], W=[lwin])
            kb.act(sgd[:, :n], psx[:, 13, :n], AF.Sigmoid, R=[psx], W=[sgd])
            for hp in range(4):
                kku, sq, rt, rs, kk, rk, bon, kbs = [R_.next() for R_ in hp_rots]
                r = psx[:, hp, :n]
                k = psx[:, 4 + hp, :n]
                v = psx[:, 8 + hp, :n]
                hs = slice(hp * 128, (hp + 1) * 128)
                kb.ts(kku[:, :n], k, rwc[:, 14 + hp:15 + hp], None, ALU.mult, R=[psx, rwc], W=[kku])
                kb.tt(sq[:, :n], kku[:, :n], kku[:, :n], ALU.mult, R=[kku], W=[sq])
                p = pmm.next()
                kb.mm(p[:, :n], blk[:], sq[:, :n], R=[blk, sq], W=[p])
                kb.act(rt[:, :n], p[:, :n], AF.Sqrt, bias=g.eps_t[:, 1:2], scale=1.0, R=[p, g.eps_t], W=[rt])
                kb.op("dve", lambda e: e.reciprocal(out=rs[:, :n], in_=rt[:, :n]), R=[rt], W=[rs])
                kb.tt(kk[:, :n], kku[:, :n], rs[:, :n], ALU.mult, R=[kku, rs], W=[kk])
                p = pmm.next()
                kb.mm(p[:, :n], gup[:, hs], sgd[:, :n], R=[gup, sgd], W=[p])
                st = st32.next()
                kb.copy(st[:, :n], p[:, :n], R=[p], W=[st], eng="act")
                kb.store(g.gT[hs, s0:s0 + n], st[:, :n], R=[st])
                kb.copy(vb[:, :n], v, R=[psx], W=[vb], eng="act")
                pt_ = ptr.next()
                for tt_ in range(nt):
                    kb.tr(pt_[:, tt_ * 128:(tt_ + 1) * 128], vb[:, tt_ * 128:(tt_ + 1) * 128], g.ident_bf[:], R=[vb, g.ident_bf], W=[pt_])
                tm = tms.next()
                kb.copy(tm[:, :nt * 128], pt_[:, :nt * 128], R=[pt_], W=[tm], eng="act")
                for tt_ in range(nt):
                    kb.store(g.Vtm[s0 + tt_ * 128:s0 + (tt_ + 1) * 128, hs], tm[:, tt_ * 128:(tt_ + 1) * 128], R=[tm])
                for d in range(2):
                    sg, a_, tmp, kmod, b_, ci, cr, ce, e1, e2, e3 = [R_.next() for R_ in d_rots]
                    p = pmm.next()
                    kb.mm(p[:, :n], lora[0:64, d, hs], lwin[0:64, :n], R=[lora, lwin], W=[p])
                    kb.act(sg[:, :n], p[:, :n], AF.Sigmoid, bias=rwc[:, 34 + 4 * d + hp:35 + 4 * d + hp], scale=1.0, R=[p, rwc], W=[sg])
                    p = pmm.next()
                    kb.mm(p[:, :n], lora[64:128, d, hs], lwin[64:128, :n], R=[lora, lwin], W=[p])
                    kb.act(a_[:, :n], p[:, :n], AF.Sigmoid, bias=rwc[:, 42 + 4 * d + hp:43 + 4 * d + hp], scale=1.0, R=[p, rwc], W=[a_])
                    kb.ts(tmp[:, :n], a_[:, :n], rwc[:, 18 + hp:19 + hp], omka[:, hp:hp + 1], ALU.mult, ALU.add, R=[a_, rwc, omka], W=[tmp])
                    kb.tt(kmod[:, :n], tmp[:, :n], k, ALU.mult, R=[tmp, psx], W=[kmod])
                    kb.tt(b_[:, :n], kk[:, :n], a_[:, :n], ALU.mult, R=[kk, a_], W=[b_])
                    if d == 0:
                        kb.copy(kbs[:, :n], kmod[:, :n], R=[kmod], W=[kbs], eng="act")
                    else:
                        kb.tt(kbs[:, :n], kbs[:, :n], kmod[:, :n], ALU.add, R=[kbs, kmod], W=[kbs])
                    kb.op("dve", lambda e: e.tensor_tensor_scan(out=ci[:, :n], data0=cmask[:, :n], data1=sg[:, :n], initial=0.0,
                                                                op0=ALU.mult, op1=ALU.add), R=[cmask, sg], W=[ci])
                    cc = ci
                    if d == 1:
                        kb.tt(tmp[:, :n], sg[:, :n], ci[:, :n], ALU.subtract, R=[sg, ci], W=[tmp])
                        civ = ci[:, :n].rearrange("p (c t) -> p c t", t=64)
                        kb.tt(cr[:, :n].rearrange("p (c t) -> p c t", t=64), tmp[:, :n].rearrange("p (c t) -> p c t", t=64),
                              civ[:, :, 63:64].to_broadcast([128, nch, 64]), ALU.add, R=[tmp, ci], W=[cr])
                        cc = cr
                    kb.tt(ce[:, :n], cc[:, :n], sg[:, :n], ALU.subtract, R=[cc, sg], W=[ce])
                    kb.act(e1[:, :n], cc[:, :n], AF.Exp, scale=KAPPA, R=[cc], W=[e1])
                    kb.act(e2[:, :n], cc[:, :n], AF.Exp, scale=-KAPPA, R=[cc], W=[e2])
                    kb.act(e3[:, :n], ce[:, :n], AF.Exp, scale=-KAPPA, R=[ce], W=[e3])
                    fa, fr, fb, fk = [fm16[i].next() for i in range(4)]
                    kb.stt(fa[:, :n], kk[:, :n], -1.0, e3[:, :n], ALU.mult, ALU.mult, R=[kk, e3], W=[fa])
                    kb.tt(fr[:, :n], r, e2[:, :n], ALU.mult, R=[psx, e2], W=[fr])
                    kb.tt(fb[:, :n], b_[:, :n], e1[:, :n], ALU.mult, R=[b_, e1], W=[fb])
                    kb.tt(fk[:, :n], kmod[:, :n], e1[:, :n], ALU.mult, R=[kmod, e1], W=[fk])
                    gs = gst.next()
                    e2v = e2[:, :n].rearrange("p (c t) -> p c t", t=64)
                    col = 63 if d == 0 else 0
                    kb.copy(gs[:, :nch], e2v[:, :, col], R=[e2], W=[gs], eng="pool")
                    kb.store(g.gam[d][hs, c0:c0 + nch], gs[:, :nch], R=[gs])
                    for kind, f in enumerate((fa, fr, fb, fk)):
                        kb.store(g.FM[d][kind][hs, s0:s0 + n], f[:, :n], R=[f])
                    for kind, f in ((0, fb), (1, fk)):
                        pt_ = ptr.next()
                        for tt_ in range(nt):
                            kb.tr(pt_[:, tt_ * 128:(tt_ + 1) * 128], f[:, tt_ * 128:(tt_ + 1) * 128], g.ident_bf[:], R=[f, g.ident_bf], W=[pt_])
                        tm = tms.next()
                        kb.copy(tm[:, :nt * 128], pt_[:, :nt * 128], R=[pt_], W=[tm], eng=("act" if kind else "dve"))
                        for tt_ in range(nt):
                            kb.store(g.TM[d][s0 + tt_ * 128:s0 + (tt_ + 1) * 128, kind, hs], tm[:, tt_ * 128:(tt_ + 1) * 128], R=[tm])
                kb.tt(rk[:, :n], r, kbs[:, :n], ALU.mult, R=[psx, kbs], W=[rk])
                kb.ts(rk[:, :n], rk[:, :n], hrk[:, hp:hp + 1], None, ALU.mult, R=[rk, hrk], W=[rk])
                p = pmm.next()
                kb.mm(p[:, :n], blk[:], rk[:, :n], R=[blk, rk], W=[p])
                kb.tt(bon[:, :n], p[:, :n], v, ALU.mult, R=[p, psx], W=[bon])
                kb.store(g.bonT[hs, s0:s0 + n], bon[:, :n], R=[bon])


def phase_C0(kb, g):
    NL = 5
    with phase(kb):
        mk = sbt(kb, "mk", [64, 2, 3, 64])
        kb.load(mk[:], g.masks_h[:, :, :, :], W=[mk])
        poolA = PPool([pst(kb, f"cpa{i}") for i in range(5)])
        poolB = PPool([pst(kb, f"cpb{i}") for i in range(3)])
        st = []
        for d in range(2):
            s = G()
            s.gam = sbt(kb, f"gam{d}", [64, 8, 68])
            kb.load(s.gam[:], g.gam[d].rearrange("(h k) c -> k h c", k=64), W=[s.gam])
            s.fm = Rot([sbt(kb, f"fm{d}_{i}", [64, 4, 8, 64], BF16) for i in range(2)])
            s.tm = Rot([sbt(kb, f"tm{d}_{i}", [64, 2, 512], BF16) for i in range(2)])
            s.vt = Rot([sbt(kb, f"vt{d}_{i}", [64, 512], BF16) for i in range(2)])
            s.Sf = sbt(kb, f"Sf{d}", [64, 8, 64])
            s.Sb = Rot([sbt(kb, f"Sb{d}_{i}", [64, 8, 64], BF16) for i in range(2)])
            s.Nm = Rot([sbt(kb, f"Nm{d}_{i}", [64, 8, 128], BF16) for i in range(2)])
            s.Nkm = Rot([sbt(kb, f"Nkm{d}_{i}", [64, 8, 128], BF16) for i in range(2)])
            s.Inv = Rot([sbt(kb, f"Inv{d}_{i}", [64, 8, 64], BF16) for i in range(2)])
            s.NT = sbt(kb, f"NT{d}", [64, 8, 64], BF16)
            s.X = sbt(kb, f"X{d}", [64, 8, 64])
            s.Xb = Rot([sbt(kb, f"Xb{d}_{i}", [64, 8, 64], BF16) for i in range(2)])
            s.P = Rot([sbt(kb, f"P{d}_{i}", [64, 8, 64], BF16) for i in range(2)])
            s.PT = Rot([sbt(kb, f"PT{d}_{i}", [64, 8, 64], BF16) for i in range(2)])
            s.W1 = sbt(kb, f"W1{d}", [64, 8, 64], BF16)
            s.UT = sbt(kb, f"UT{d}", [64, 8, 64], BF16)
            s.Yst = Rot([sbt(kb, f"Yst{d}_{i}", [64, 512]) for i in range(2)])
            kb.memset(s.Sf[:], 0.0, W=[s.Sf])
            s.sb = s.Sb.next()
            kb.memset(s.sb[:], 0.0, W=[s.sb])
            s.mAR = mk[:, d, 0:2, :].rearrange("p a t -> p (a t)")[:, None, :].to_broadcast([64, 4, 128])
            s.mT = mk[:, d, 2, :][:, None, :].to_broadcast([64, 8, 64])
            st.append(s)
        Ibc = g.ident[0:64, 0:64][:, None, :].to_broadcast([64, 8, 64])
        order = [list(range(68)), [3, 2, 1, 0] + list(range(67, 3, -1))]

        def v3(p):
            return p[0:64, :].rearrange("p (h t) -> p h t", t=64)

        def partA(d, c, rec):
            s = st[d]
            t0 = c * 64
            yield
            fm, tm, vt = s.fm.next(), s.tm.next(), s.vt.next()
            Nm, Nkm = s.Nm.next(), s.Nkm.next()
            for kind in range(4):
                kb.load(fm[:, kind, :, :], g.FM[d][kind].rearrange("(h k) t -> k h t", k=64)[:, :, t0:t0 + 64], W=[fm])
            kb.load(tm[:], g.TM[d][t0:t0 + 64, :, :], W=[tm])
            kb.load(vt[:], g.Vtm[t0:t0 + 64, :], W=[vt])
            for (lk, dst) in ((2, Nm), (3, Nkm)):
                for hh in range(2):
                    p = poolA.get()
                    for h4 in range(4):
                        h = hh * 4 + h4
                        kb.mm(p[0:64, h4 * 128:(h4 + 1) * 128], fm[:, lk, h, :], fm[:, 0:2, h, :], R=[fm], W=[p])
                    kb.tt(dst[:, hh * 4:(hh + 1) * 4, :], p[0:64, :].rearrange("p (h t) -> p h t", t=128), s.mAR, ALU.mult, R=[p, mk], W=[dst])
                    poolA.put(p)
                    yield
            p = poolA.get()
            for h in range(8):
                kb.mm(p[0:64, h * 64:(h + 1) * 64], fm[:, 0, h, :], fm[:, 2, h, :], R=[fm], W=[p])
            kb.tt(s.NT[:], v3(p), s.mT, ALU.mult, R=[p, mk], W=[s.NT])
            poolA.put(p)
            yield
            kb.tt(s.X[:], Nm[:, :, 0:64], Ibc, ALU.add, R=[Nm, g.ident], W=[s.X])
            xb = s.Xb.next()
            kb.copy(xb[:], s.X[:], R=[s.X], W=[xb], eng="act")
            P_ap = lambda h: Nm[:, h, 0:64]
            PT_ap = lambda h: s.NT[:, h, :]
            Pt, PTt = Nm, s.NT
            for lv in range(NL):
                last = lv == NL - 1
                p1 = None
                if not last:
                    p1 = poolA.get()
                    for h in range(8):
                        kb.mm(p1[0:64, h * 64:(h + 1) * 64], PT_ap(h), P_ap(h), R=[Pt, PTt], W=[p1])
                p2 = poolA.get()
                for h in range(8):
                    kb.mm(p2[0:64, h * 64:(h + 1) * 64], P_ap(h), PT_ap(h), R=[Pt, PTt], W=[p2])
                yield
                nPT = s.PT.next()
                kb.copy(nPT[:], v3(p2), R=[p2], W=[nPT], eng="act")
                poolA.put(p2)
                if not last:
                    nP = s.P.next()
                    kb.copy(nP[:], v3(p1), R=[p1], W=[nP], eng="act")
                    poolA.put(p1)
                    Pt = nP
                    P_ap = (lambda t_: (lambda h: t_[:, h, :]))(nP)
                PTt = nPT
                PT_ap = (lambda t_: (lambda h: t_[:, h, :]))(nPT)
                yield
                p3 = poolA.get()
                for h in range(8):
                    kb.mm(p3[0:64, h * 64:(h + 1) * 64], PT_ap(h), xb[:, h, :], R=[PTt, xb], W=[p3])
                yield
                kb.tt(s.X[:], s.X[:], v3(p3), ALU.add, R=[s.X, p3], W=[s.X])
                poolA.put(p3)
                xb = s.Inv.next() if last else s.Xb.next()
                kb.copy(xb[:], s.X[:], R=[s.X], W=[xb], eng="act")
                yield
            rec.update(fm=fm, tm=tm, vt=vt, Nm=Nm, Nkm=Nkm, inv=xb, c=c)

        def partB(d, rec):
            s = st[d]
            fm, tm, vt, Nm, Nkm, inv, c = (rec[k] for k in ("fm", "tm", "vt", "Nm", "Nkm", "inv", "c"))
            t0 = c * 64
            sb = s.sb
            yield
            pw = poolB.get()
            for h in range(8):
                hs = slice(h * 64, (h + 1) * 64)
                kb.mm(pw[0:64, hs], Nkm[:, h, 0:64], vt[:, hs], start=True, stop=False, R=[Nkm, vt], W=[pw])
                kb.mm(pw[0:64, hs], fm[:, 0, h, :], sb[:, h, :], start=False, stop=True, R=[fm, sb], W=[pw])
            yield
            kb.copy(s.W1[:], v3(pw), R=[pw], W=[s.W1], eng="act")
            poolB.put(pw)
            pu = poolB.get()
            for h in range(8):
                kb.mm(pu[0:64, h * 64:(h + 1) * 64], inv[:, h, :], s.W1[:, h, :], R=[inv, s.W1], W=[pu])
            yield
            kb.copy(s.UT[:], v3(pu), R=[pu], W=[s.UT], eng="act")
            poolB.put(pu)
            pn = poolB.get()
            for h in range(8):
                hs = slice(h * 64, (h + 1) * 64)
                kb.mm(pn[0:64, hs], tm[:, 0, hs], s.UT[:, h, :], start=True, stop=False, R=[tm, s.UT], W=[pn])
                kb.mm(pn[0:64, hs], tm[:, 1, hs], vt[:, hs], start=False, stop=True, R=[tm, vt], W=[pn])
            yield
            kb.tt(s.Sf[:], s.Sf[:], v3(pn), ALU.add, R=[s.Sf, pn], W=[s.Sf])
            poolB.put(pn)
            kb.tt(s.Sf[:], s.Sf[:], s.gam[:, :, c:c + 1].to_broadcast([64, 8, 64]), ALU.mult, R=[s.Sf, s.gam], W=[s.Sf])
            nsb = s.Sb.next()
            kb.copy(nsb[:], s.Sf[:], R=[s.Sf], W=[nsb], eng="act")
            s.sb = nsb
            yield
            py = poolB.get()
            for h in range(8):
                hs = slice(h * 64, (h + 1) * 64)
                kb.mm(py[0:64, hs], fm[:, 1, h, :], sb[:, h, :], start=True, stop=False, R=[fm, sb], W=[py])
                kb.mm(py[0:64, hs], Nm[:, h, 64:128], s.UT[:, h, :], start=False, stop=False, R=[Nm, s.UT], W=[py])
                kb.mm(py[0:64, hs], Nkm[:, h, 64:128], vt[:, hs], start=False, stop=True, R=[Nkm, vt], W=[py])
            ys = s.Yst.next()
            kb.copy(ys[:], py[0:64, :], R=[py], W=[ys])
            poolB.put(py)
            kb.store(g.Y[d][t0:t0 + 64, :], ys[:], R=[ys])

        recs = [{}, {}]
        run_interleaved([partA(0, order[0][0], recs[0]), partA(1, order[1][0], recs[1])])
        for i in range(68):
            cur = recs
            gens = [partB(0, cur[0]), partB(1, cur[1])]
            recs = [{}, {}]
            if i + 1 < 68:
                gens += [partA(0, order[0][i + 1], recs[0]), partA(1, order[1][i + 1], recs[1])]
            run_interleaved(gens)

def phase_C2(kb, g):
    with phase(kb):
        rwc = sbt(kb, "rwc2", [128, 58])
        kb.load(rwc[:], g.rw_cols[:, :], W=[rwc])
        yf = Rot([sbt(kb, f"yf{i}", [128, 512]) for i in range(2)])
        yb = Rot([sbt(kb, f"yb{i}", [128, 512]) for i in range(2)])
        y = sbt(kb, "y", [128, 512])
        sq = sbt(kb, "ysq", [128, 512])
        sm = sbt(kb, "ysm", [128, 8])
        vr = sbt(kb, "yvr", [128, 8])
        rt = sbt(kb, "yrt", [128, 8])
        rs = sbt(kb, "yrs", [128, 8])
        yn = Rot([sbt(kb, f"yn{i}", [128, 512]) for i in range(2)])
        pb = [pst(kb, f"c2p{i}") for i in range(4)]
        bon = Rot([sbt(kb, f"bon{i}", [128, 4, 512]) for i in range(2)])
        gt = Rot([sbt(kb, f"gt{i}", [128, 4, 512]) for i in range(2)])
        a1 = Rot([sbt(kb, f"a1_{i}", [128, 512]) for i in range(2)])
        mo = Rot([sbt(kb, f"mo{i}", [128, 512], BF16) for i in range(3)])
        for bi, (s0, n) in enumerate(TBS):
            nt = n // 128
            bo, gg = bon.next(), gt.next()
            kb.load(bo[:, :, :n], g.bonT.rearrange("(c p) t -> p c t", p=128)[:, :, s0:s0 + n], W=[bo])
            kb.load(gg[:, :, :n], g.gT.rearrange("(c p) t -> p c t", p=128)[:, :, s0:s0 + n], W=[gg])
            for tt_ in range(nt):
                t0 = s0 + tt_ * 128
                a, b = yf.next(), yb.next()
                kb.load(a[:], g.Y[0][t0:t0 + 128, :], W=[a])
                kb.load(b[:], g.Y[1][t0:t0 + 128, :], W=[b])
                kb.tt(y[:], a[:], b[:], ALU.add, R=[a, b], W=[y])
                y3 = y[:].rearrange("p (h v) -> p h v", v=64)
                kb.op("dve", lambda e: e.tensor_reduce(out=sm[:], in_=y3, axis=AX.X, op=ALU.add), R=[y], W=[sm])
                kb.ts(sm[:], sm[:], -1.0 / 64.0, None, ALU.mult, R=[sm], W=[sm])
                kb.tt(y3, y3, sm[:, :, None].to_broadcast([128, 8, 64]), ALU.add, R=[y, sm], W=[y])
                kb.tt(sq[:], y[:], y[:], ALU.mult, R=[y], W=[sq])
                kb.op("dve", lambda e: e.tensor_reduce(out=vr[:], in_=sq[:].rearrange("p (h v) -> p h v", v=64), axis=AX.X, op=ALU.add), R=[sq], W=[vr])
                kb.act(rt[:], vr[:], AF.Sqrt, bias=g.eps_t[:, 2:3], scale=1.0 / 64.0, R=[vr, g.eps_t], W=[rt])
                kb.op("dve", lambda e: e.reciprocal(out=rs[:], in_=rt[:]), R=[rt], W=[rs])
                yo = yn.next()
                kb.tt(yo[:].rearrange("p (h v) -> p h v", v=64), y3, rs[:, :, None].to_broadcast([128, 8, 64]), ALU.mult, R=[y, rs], W=[yo])
                for ct in range(4):
                    kb.tr(pb[ct][:, tt_ * 128:(tt_ + 1) * 128], yo[:, ct * 128:(ct + 1) * 128], g.ident[:], R=[yo, g.ident], W=[pb[ct]])
            for ct in range(4):
                t1 = a1.next()
                kb.act(t1[:, :n], pb[ct][:, :n], AF.Identity, bias=rwc[:, 30 + ct:31 + ct], scale=rwc[:, 26 + ct:27 + ct], R=[pb[ct], rwc], W=[t1])
                kb.tt(t1[:, :n], t1[:, :n], bo[:, ct, :n], ALU.add, R=[t1, bo], W=[t1])
                o = mo.next()
                kb.tt(o[:, :n], t1[:, :n], gg[:, ct, :n], ALU.mult, R=[t1, gg], W=[o])
                kb.store(g.mixT[ct * 128:(ct + 1) * 128, s0:s0 + n], o[:, :n], R=[o])


def phase_D0(kb, g):
    LAM_INIT = 0.2
    with phase(kb):
        dc = sbt(kb, "dc", [64, 4])
        pr = sbt(kb, "dpr", [64, 2])
        onesf = sbt(kb, "onesf", [64, 128])
        lam = sbt(kb, "lam", [128, 2])
        nlam = sbt(kb, "nlam", [128, 1])
        slg = sbt(kb, "slg", [128, 1])
        kb.load(dc[:], g.diff_cols[:, :], W=[dc])
        kb.load(slg[:], g.subln_g[:, :], W=[slg])
        kb.memset(onesf[:], 1.0, W=[onesf])
        kb.tt(pr[:, 0:1], dc[:, 0:1], dc[:, 1:2], ALU.mult, R=[dc], W=[pr])
        kb.tt(pr[:, 1:2], dc[:, 2:3], dc[:, 3:4], ALU.mult, R=[dc], W=[pr])
        ps = Rot([pst(kb, f"dps{i}") for i in range(3)])
        pfin = pst(kb, "dpfin")
        acc = [[pst(kb, f"dacc{m}{q}") for q in range(2)] for m in range(2)]
        pl = ps.next()
        kb.mm(pl[:, 0:2], onesf[:], pr[:], R=[onesf, pr], W=[pl])
        kb.act(lam[:], pl[:, 0:2], AF.Exp, R=[pl], W=[lam])
        kb.tt(nlam[:], lam[:, 1:2], lam[:, 0:1], ALU.subtract, R=[lam], W=[nlam])
        kb.ts(nlam[:], nlam[:], -LAM_INIT, None, ALU.add, R=[nlam], W=[nlam])
        KT = sbt(kb, "KT", [128, T], BF16)
        QR = [sbt(kb, f"QR{m}", [128, T], BF16) for m in range(2)]
        QP = [sbt(kb, f"QP{m}", [128, T], BF16) for m in range(2)]
        for m in range(2):
            zs = slice(64, 128) if m == 0 else slice(0, 64)
            kb.memset(QR[m][zs, :], 0.0, W=[QR[m]])
            kb.memset(QP[m][zs, :], 0.0, W=[QP[m]])
        VA = sbt(kb, "VA", [128, 34, 130], BF16)
        pT = Rot([sbt(kb, f"pT{i}", [128, 512], BF16) for i in range(4)])
        rz = sbt(kb, "rz", [128, 4])
        o0 = sbt(kb, "o0", [128, 128])
        o = sbt(kb, "o", [128, 128])
        osq = sbt(kb, "osq", [128, 128])
        ssq = sbt(kb, "ssq", [128, 1])
        rt = sbt(kb, "drt", [128, 1])
        rs = sbt(kb, "drs", [128, 1])
        on = Rot([sbt(kb, f"on{i}", [128, 128]) for i in range(2)])
        so = Rot([sbt(kb, f"dso{i}", [128, 256], BF16) for i in range(2)])
        accS = Rot([sbt(kb, f"accS{i}", [128, 4, 129]) for i in range(2)])

        def finalize(aS, h, q0):
            s_ = so.next()
            for qt in range(2):
                a0_, a1_ = aS[:, qt, :], aS[:, 2 + qt, :]
                kb.op("dve", lambda e_: e_.reciprocal(out=rz[:, 0:1], in_=a0_[:, 128:129]), R=[aS], W=[rz])
                kb.op("dve", lambda e_: e_.reciprocal(out=rz[:, 1:2], in_=a1_[:, 128:129]), R=[aS], W=[rz])
                kb.tt(rz[:, 2:3], rz[:, 1:2], nlam[:], ALU.mult, R=[rz, nlam], W=[rz])
                kb.ts(o0[:], a0_[:, 0:128], rz[:, 0:1], None, ALU.mult, R=[aS, rz], W=[o0])
                kb.stt(o[:], a1_[:, 0:128], rz[:, 2:3], o0[:], ALU.mult, ALU.add, R=[aS, rz, o0], W=[o])
                yield
                kb.act(osq[:], o[:], AF.Square, R=[o], W=[osq, ssq], accum_out=ssq[:])
                yield
                kb.act(rt[:], ssq[:], AF.Sqrt, bias=g.eps_t[:, 0:1], scale=1.0 / 128.0, R=[ssq, g.eps_t], W=[rt])
                yield
                kb.op("dve", lambda e_: e_.reciprocal(out=rs[:], in_=rt[:]), R=[rt], W=[rs])
                on_ = on.next()
                kb.ts(on_[:], o[:], rs[:, 0:1], 1.0 - LAM_INIT, ALU.mult, ALU.mult, R=[o, rs], W=[on_])
                yield
                kb.tr(pfin[:, 0:128], on_[:], g.ident[:], R=[on_, g.ident], W=[pfin])
                yield
                kb.act(s_[:, qt * 128:(qt + 1) * 128], pfin[:, 0:128], AF.Copy, scale=slg[:, 0:1], R=[pfin, slg], W=[s_])
                yield
            kb.store(g.mixT[512 + h * 128:512 + (h + 1) * 128, q0:q0 + 256], s_[:], R=[s_])

        fin = iter(())
        chunks = [(0, True)] + [(256 + 256 * i, False) for i in range(16)]
        import os
        DS = int(os.environ.get("D0_STAGE", "9"))
        if DS < 9:
            chunks = chunks[:2]
        for h in range(4 if DS == 9 else 1):
            hs = slice(h * 128, (h + 1) * 128)
            kb.load(KT[:], g.KrotT[hs, :], W=[KT])
            for m in range(2):
                ms = slice(m * 64, (m + 1) * 64)
                kb.load(QR[m][ms, :], g.QrotT[h * 128 + m * 64:h * 128 + (m + 1) * 64, :], W=[QR[m]])
                kb.load(QP[m][ms, :], g.QplT[h * 128 + m * 64:h * 128 + (m + 1) * 64, :], W=[QP[m]])
            kb.load(VA[:, :, 0:128], g.Vd.rearrange("(kt p) c -> p kt c", p=128)[:, :, hs], W=[VA])
            kb.memset(VA[:, :, 128:129], 1.0, W=[VA])
            for (q0, isctx) in chunks:
                if DS < 2:
                    break
                kts = [0, 1] if isctx else list(range(34))
                def score(kt):
                    Q = QP if (not isctx and kt < 2) else QR
                    p = ps.next()
                    ks = slice(kt * 128, (kt + 1) * 128)
                    kb.mm(p[:, 0:256], KT[:, ks], Q[0][:, q0:q0 + 256], R=[KT, Q[0]], W=[p])
                    kb.mm(p[:, 256:512], KT[:, ks], Q[1][:, q0:q0 + 256], R=[KT, Q[1]], W=[p])
                    return p
                LA = 2
                pq = [score(kts[k]) for k in range(min(LA, len(kts)))]
                for i, kt in enumerate(kts):
                    p = pq.pop(0)
                    if i + LA < len(kts):
                        pq.append(score(kts[i + LA]))
                    e = pT.next()
                    kb.act(e[:], p[:], AF.Exp, scale=0.125, R=[p], W=[e])
                    next(fin, None)
                    if DS < 3:
                        continue
                    for m in range(2):
                        for qt in range(2):
                            kb.mm(acc[m][qt][:, 0:129], e[:, m * 256 + qt * 128:m * 256 + (qt + 1) * 128], VA[:, kt, 0:129],
                                  start=(i == 0), stop=(i == len(kts) - 1), R=[e, VA], W=[acc[m][qt]])
                if DS < 4:
                    continue
                for _ in fin:
                    pass
                aS = accS.next()
                for m in range(2):
                    for qt in range(2):
                        kb.copy(aS[:, m * 2 + qt, :], acc[m][qt][:, 0:129], R=[acc[m][qt]], W=[aS])
                fin = finalize(aS, h, q0)
        for _ in fin:
            pass


def phase_E(kb, g, li, w_ap, kcn, mixT, blocks):
    with phase(kb):
        Wo = sbt(kb, "Wo", [128, kcn, 1024], BF16)
        stg = Rot([sbt(kb, f"estg{i}", [128, kcn * 128]) for i in range(2)])
        load_weight_bf16(kb, Wo, w_ap, 1024, stg, piece=128)
        xb = Rot([sbt(kb, f"exb{i}", [128, 8, 512]) for i in range(2)])
        mb = Rot([sbt(kb, f"emb{i}", [128, kcn, 512], BF16) for i in range(2)])
        pmm = Rot([pst(kb, f"ep{i}") for i in range(4)])
        xsrc = g.xT.rearrange("(kc p) t -> p kc t", p=128)
        msrc = mixT.rearrange("(kc p) t -> p kc t", p=128)
        for (s0, n, j) in blocks:
            x, m = xb.next(), mb.next()
            kb.load(x[:, :, :n], xsrc[:, :, s0:s0 + n], W=[x])
            kb.load(m[:, :, :n], msrc[:, :, s0:s0 + n], W=[m])
            for ct in range(8):
                p = pmm.next()
                for kc in range(kcn):
                    kb.mm(p[:, :n], Wo[:, kc, ct * 128:(ct + 1) * 128], m[:, kc, :n], start=(kc == 0), stop=(kc == kcn - 1), R=[Wo, m], W=[p])
                kb.stt(x[:, ct, :n], p[:, :n], g.mod[li][:, 16 + ct, j:j + 1], x[:, ct, :n], ALU.mult, ALU.add, R=[p, g.mod[li], x], W=[x])
            kb.store(xsrc[:, :, s0:s0 + n], x[:, :, :n], R=[x])


def phase_F(kb, g, li, blocks):
    with phase(kb):
        W1 = sbt(kb, "W1", [128, 8, 4096], BF16)
        W2 = sbt(kb, "W2", [128, 32, 1024], BF16)
        stg = Rot([sbt(kb, f"fstg{i}", [128, 1024]) for i in range(2)])
        load_weight_bf16(kb, W1, g.mlp_w1[li], 4096, stg, piece=128)
        load_weight_bf16(kb, W2, g.mlp_w2[li], 1024, stg, piece=32)
        nb = G()
        nb.sq = sbt(kb, "fsq", [128, 8, 256], BF16)
        nb.tmp = sbt(kb, "ftmp", [128, 8, 256])
        nb.rt = sbt(kb, "frt", [128, 256])
        nb.rstd = sbt(kb, "frstd", [128, 256])
        nb.ss = pst(kb, "fss")
        xb = Rot([sbt(kb, f"fxb{i}", [128, 8, 256]) for i in range(2)])
        hT = sbt(kb, "fhT", [128, 8, 256], BF16)
        hid = sbt(kb, "fhid", [128, 32, 256], BF16)
        rl = Rot([sbt(kb, f"frl{i}", [128, 256]) for i in range(3)])
        pmm = Rot([pst(kb, f"fp{i}") for i in range(6)])
        xsrc = g.xT.rearrange("(kc p) t -> p kc t", p=128)
        for (s0, n, j) in blocks:
            x = xb.next()
            kb.load(x[:, :, :n], xsrc[:, :, s0:s0 + n], W=[x])
            norm_mod(kb, g, nb, x, n, g.sc2[li], g.mod[li], j, hT, sh_off=24)
            for hc in range(32):
                p = pmm.next()
                for kc in range(8):
                    kb.mm(p[:, :n], W1[:, kc, hc * 128:(hc + 1) * 128], hT[:, kc, :n], start=(kc == 0), stop=(kc == 7), R=[W1, hT], W=[p])
                r = rl.next()
                kb.act(r[:, :n], p[:, :n], AF.Relu, R=[p], W=[r])
                kb.tt(hid[:, hc, :n], r[:, :n], p[:, :n], ALU.mult, R=[r, p], W=[hid])
            for ct in range(8):
                p = pmm.next()
                for hc in range(32):
                    kb.mm(p[:, :n], W2[:, hc, ct * 128:(ct + 1) * 128], hid[:, hc, :n], start=(hc == 0), stop=(hc == 31), R=[W2, hid], W=[p])
                kb.stt(x[:, ct, :n], p[:, :n], g.mod[li][:, 40 + ct, j:j + 1], x[:, ct, :n], ALU.mult, ALU.add, R=[p, g.mod[li], x], W=[x])
            kb.store(xsrc[:, :, s0:s0 + n], x[:, :, :n], R=[x])


BLK512 = [(s0, n, 1 if s0 == 0 else 0) for (s0, n) in TBS]
BLK256 = [(s0, 256, 1 if s0 == 0 else 0) for s0 in range(0, T, 256)]

def phase_A1(kb, g):
    with phase(kb):
        W = sbt(kb, "wA1", [128, 8, 6176], BF16)
        stg = Rot([sbt(kb, f"w1stg{i}", [128, 2048]) for i in range(2)])
        load_weight_bf16(kb, W, g.ssd_w_in, 6176, stg)
        nb = NormBufs(kb, "A1")
        xb = Rot([sbt(kb, f"a1x{i}", [128, 8, 512]) for i in range(2)])
        hTs = Rot([sbt(kb, f"a1h{i}", [128, 8, 512], BF16) for i in range(2)])
        pmm = Rot([pst(kb, f"a1p{i}") for i in range(6)])
        st32 = Rot([sbt(kb, f"a1s{i}", [128, 512]) for i in range(4)])
        st16 = Rot([sbt(kb, f"a1z{i}", [128, 512], BF16) for i in range(4)])
        xsrc = g.xT.rearrange("(kc p) t -> p kc t", p=128)
        for bi, (s0, n) in enumerate(TBS):
            j = 1 if bi == 0 else 0
            x = xb.next()
            kb.load(x[:, :, :n], xsrc[:, :, s0:s0 + n], W=[x])
            hT = hTs.next()
            norm_mod(kb, g, nb, x, n, g.sc1[1], g.mod[1], j, hT, sh_off=0)
            for ct in range(32):
                p = pmm.next()
                c0 = 2048 + ct * 128
                for kc in range(8):
                    kb.mm(p[:, :n], W[:, kc, c0:c0 + 128], hT[:, kc, :n], start=(kc == 0), stop=(kc == 7), R=[W, hT], W=[p])
                st = st32.next()
                kb.copy(st[:, :n], p[:, :n], R=[p], W=[st], eng=("act" if ct % 2 else "dve"))
                kb.store(g.xbcT[ct * 128:(ct + 1) * 128, s0:s0 + n], st[:, :n], R=[st])
            for tt_ in range(n // 128):
                ts_ = slice(tt_ * 128, (tt_ + 1) * 128)
                t0 = s0 + tt_ * 128
                for zc in range(4):
                    p = pmm.next()
                    for kc in range(8):
                        kb.mm(p[:, :], hT[:, kc, ts_], W[:, kc, zc * 512:(zc + 1) * 512], start=(kc == 0), stop=(kc == 7), R=[W, hT], W=[p])
                    so = st16.next()
                    kb.copy(so[:], p[:], R=[p], W=[so], eng=("act" if zc % 2 else "dve"))
                    kb.store(g.zTM[t0:t0 + 128, zc * 512:(zc + 1) * 512], so[:], R=[so])
                p = pmm.next()
                for kc in range(8):
                    kb.mm(p[:, 0:32], hT[:, kc, ts_], W[:, kc, 6144:6176], start=(kc == 0), stop=(kc == 7), R=[W, hT], W=[p])
                st = st32.next()
                kb.copy(st[:, 0:32], p[:, 0:32], R=[p], W=[st])
                kb.store(g.dtTM[t0:t0 + 128, :], st[:, 0:32], R=[st])


def phase_B1(kb, g):
    with phase(kb):
        cw = sbt(kb, "cw", [128, 32, 5])
        cb = sbt(kb, "cb", [128, 32])
        kb.load(cw[:], g.conv_wT[:, :, :], W=[cw])
        kb.load(cb[:], g.conv_bT[:, :], W=[cb])
        xin = Rot([sbt(kb, f"b1x{i}", [128, 8, 516]) for i in range(2)])
        acc = Rot([sbt(kb, f"b1a{i}", [128, 512]) for i in range(2)])
        u32 = Rot([sbt(kb, f"b1u{i}", [128, 512]) for i in range(2)])
        u16 = Rot([sbt(kb, f"b1v{i}", [128, 512], BF16) for i in range(3)])
        p32 = Rot([pst(kb, f"b1p{i}") for i in range(3)])
        p16 = Rot([pst(kb, f"b1q{i}", dt=BF16) for i in range(2)])
        t32 = Rot([sbt(kb, f"b1t{i}", [128, 512]) for i in range(3)])
        t16 = Rot([sbt(kb, f"b1s{i}", [128, 512], BF16) for i in range(3)])
        src = g.xbcT.rearrange("(ti p) t -> p ti t", p=128)
        for bi, (s0, n) in enumerate(TBS):
            nt = n // 128
            seq0, seq1 = (0, NCTX) if s0 < NCTX else (NCTX, T)
            lo = max(seq0, s0 - 2)
            hi = min(seq1, s0 + n + 2)
            for grp in range(4):
                xi = xin.next()
                kb.memset(xi[:, :, 0:2], 0.0, W=[xi])
                kb.memset(xi[:, :, n + 2:n + 4], 0.0, W=[xi])
                kb.load(xi[:, :, lo - s0 + 2:hi - s0 + 2], src[:, grp * 8:(grp + 1) * 8, lo:hi], W=[xi])
                for t8 in range(8):
                    ti = grp * 8 + t8
                    a = acc.next()
                    kb.act(a[:, :n], xi[:, t8, 0:n], AF.Identity, bias=cb[:, ti:ti + 1], scale=cw[:, ti, 0:1], R=[xi, cw, cb], W=[a])
                    for k in range(1, 5):
                        kb.stt(a[:, :n], xi[:, t8, k:k + n], cw[:, ti, k:k + 1], a[:, :n], ALU.mult, ALU.add, R=[xi, cw, a], W=[a])
                    if ti < 16:
                        u = u32.next()
                        kb.act(u[:, :n], a[:, :n], AF.Silu, R=[a], W=[u])
                        p = p32.next()
                        for tt_ in range(nt):
                            kb.tr(p[:, tt_ * 128:(tt_ + 1) * 128], u[:, tt_ * 128:(tt_ + 1) * 128], g.ident[:], R=[u, g.ident], W=[p])
                        t = t32.next()
                        kb.copy(t[:, :n], p[:, :n], R=[p], W=[t], eng=("act" if ti % 2 else "dve"))
                        for tt_ in range(nt):
                            kb.store(g.xsTM[s0 + tt_ * 128:s0 + (tt_ + 1) * 128, ti * 128:(ti + 1) * 128], t[:, tt_ * 128:(tt_ + 1) * 128], R=[t])
                    else:
                        u = u16.next()
                        kb.act(u[:, :n], a[:, :n], AF.Silu, R=[a], W=[u])
                        if ti < 24:
                            gi = ti - 16
                            kb.store(g.BT[gi * 128:(gi + 1) * 128, s0:s0 + n], u[:, :n], R=[u])
                            p = p16.next()
                            for tt_ in range(nt):
                                kb.tr(p[:, tt_ * 128:(tt_ + 1) * 128], u[:, tt_ * 128:(tt_ + 1) * 128], g.ident_bf[:], R=[u, g.ident_bf], W=[p])
                            t = t16.next()
                            kb.copy(t[:, :n], p[:, :n], R=[p], W=[t], eng="act")
                            for tt_ in range(nt):
                                kb.store(g.BTM[s0 + tt_ * 128:s0 + (tt_ + 1) * 128, gi * 128:(gi + 1) * 128], t[:, tt_ * 128:(tt_ + 1) * 128], R=[t])
                        else:
                            gi = ti - 24
                            kb.store(g.CT[gi * 128:(gi + 1) * 128, s0:s0 + n], u[:, :n], R=[u])


def phase_C1(kb, g):
    with phase(kb):
        UT1 = sbt(kb, "UT1", [128, 128])
        LT1 = sbt(kb, "LT1", [128, 128])
        onesf = sbt(kb, "c1ones", [128, 128])
        prm = sbt(kb, "prm", [128, 5, 32])
        aneg = sbt(kb, "aneg", [128, 2, 32])
        kb.load(UT1[:], g.ut1_h[:, :], W=[UT1])
        kb.load(LT1[:], g.lt1_h[:, :], W=[LT1])
        SLT = [sbt(kb, "sLT", [128, 128]), sbt(kb, "sUT", [128, 128])]
        kb.tt(SLT[0][:], LT1[:], g.ident[:], ALU.subtract, R=[LT1, g.ident], W=[SLT[0]])
        kb.tt(SLT[1][:], UT1[:], g.ident[:], ALU.subtract, R=[UT1, g.ident], W=[SLT[1]])
        kb.load(prm[:], g.ssd_prm[:, :, :], W=[prm])
        kb.memset(onesf[:], 1.0, W=[onesf])
        kb.act(aneg[:], prm[:, 2:4, :], AF.Exp, R=[prm], W=[aneg])
        kb.ts(aneg[:], aneg[:], -1.0, None, ALU.mult, R=[aneg], W=[aneg])
        tri = [UT1, LT1]
        triB = [sbt(kb, "UT1b", [128, 128], BF16), sbt(kb, "LT1b", [128, 128], BF16)]
        kb.copy(triB[0][:], UT1[:], R=[UT1], W=[triB[0]])
        kb.copy(triB[1][:], LT1[:], R=[LT1], W=[triB[1]])
        pydd = [[pst(kb, f"c1y{d}{i}") for i in range(2)] for d in range(2)]
        pbig = Rot([pst(kb, f"c1b{i}") for i in range(2)])
        psm = Rot([pst(kb, f"c1s{i}") for i in range(2)])
        st = []
        for d in range(2):
            s = G()
            s.xs = Rot([sbt(kb, f"xs{d}_{i}", [128, 32, 64]) for i in range(1)])
            s.D = sbt(kb, f"Dcs{d}", [128, 32, 128], BF16)
            s.bt = Rot([sbt(kb, f"bt{d}_{i}", [128, 8, 128], BF16) for i in range(2)])
            s.ct = Rot([sbt(kb, f"ct{d}_{i}", [128, 8, 128], BF16) for i in range(2)])
            s.btm = Rot([sbt(kb, f"btm{d}_{i}", [128, 1024], BF16) for i in range(2)])
            s.dt = Rot([sbt(kb, f"dt{d}_{i}", [128, 32]) for i in range(2)])
            s.hf = sbt(kb, f"hf{d}", [128, 32, 64])
            s.hb = Rot([sbt(kb, f"hb{d}_{i}", [128, 32, 64], BF16) for i in range(2)])
            s.xdt = sbt(kb, f"xdt{d}", [128, 32, 64], BF16)
            s.xdw = sbt(kb, f"xdw{d}", [128, 32, 64], BF16)
            s.yo = sbt(kb, f"yo{d}", [128, 8, 64])
            s.y = Rot([sbt(kb, f"y{d}_{i}", [128, 32, 64]) for i in range(1)])
            s.cbm = sbt(kb, f"cbm{d}", [128, 8, 128], BF16)
            kb.memset(s.hf[:], 0.0, W=[s.hf])
            s.h = s.hb.next()
            kb.memset(s.h[:], 0.0, W=[s.h])
            st.append(s)
        sm = lambda nm, w=32: sbt(kb, nm, [128, w])
        ex, dtd, dta, cs, ncs, ecs, wts, etot, csT = [[sm(f"{nm}{d}") for d in range(2)] for nm in
                                                       ("ex", "dtd", "dta", "cs", "ncs", "ecs", "wts", "etot", "csTx")]
        csTs = [sbt(kb, f"csT{d}", [32, 128]) for d in range(2)]
        E4 = Rot([sbt(kb, f"E4{i}", [128, 512], BF16) for i in range(3)])
        G4 = Rot([sbt(kb, f"G4{i}", [128, 4, 128], BF16) for i in range(3)])
        order = [list(range(34)), [1, 0] + list(range(33, 1, -1))]

        def chunk(d, c):
            s = st[d]
            t0 = c * 128
            yield
            xs, bt, ct, btm, dt = s.xs.next(), s.bt.next(), s.ct.next(), s.btm.next(), s.dt.next()
            kb.load(xs[:].rearrange("p h q -> p (h q)"), g.xsTM[t0:t0 + 128, :], W=[xs])
            kb.load(bt[:], g.BT.rearrange("(g n) t -> n g t", n=128)[:, :, t0:t0 + 128], W=[bt])
            kb.load(ct[:], g.CT.rearrange("(g n) t -> n g t", n=128)[:, :, t0:t0 + 128], W=[ct])
            kb.load(btm[:], g.BTM[t0:t0 + 128, :], W=[btm])
            kb.load(dt[:], g.dtTM[t0:t0 + 128, :], W=[dt])
            kb.tt(ex[d][:], dt[:], prm[:, d, :], ALU.add, R=[dt, prm], W=[ex[d]])
            kb.act(ex[d][:], ex[d][:], AF.Exp, R=[ex[d]], W=[ex[d]])
            kb.act(dtd[d][:], ex[d][:], AF.Ln, bias=g.eps_t[:, 4:5], scale=1.0, R=[ex[d], g.eps_t], W=[dtd[d]])
            kb.tt(dta[d][:], dtd[d][:], aneg[:, d, :], ALU.mult, R=[dtd[d], aneg], W=[dta[d]])
            p = psm.next()
            kb.mm(p[:, 0:32], tri[d][:], dta[d][:], R=[tri[d], dta[d]], W=[p])
            kb.mm(p[:, 32:64], onesf[:], dta[d][:], R=[onesf, dta[d]], W=[p])
            kb.copy(cs[d][:], p[:, 0:32], R=[p], W=[cs[d]])
            kb.act(ecs[d][:], p[:, 0:32], AF.Exp, R=[p], W=[ecs[d]])
            kb.act(etot[d][:], p[:, 32:64], AF.Exp, R=[p], W=[etot[d]])
            kb.tt(wts[d][:], p[:, 32:64], cs[d][:], ALU.subtract, R=[p, cs[d]], W=[wts[d]])
            kb.act(wts[d][:], wts[d][:], AF.Exp, R=[wts[d]], W=[wts[d]])
            for h in range(32):
                kb.act(s.D[:, h, :], SLT[d][:], AF.Copy, scale=dta[d][:, h:h + 1], R=[SLT[d], dta[d]], W=[s.D])
            yield
            kb.tt(s.xdt[:], xs[:], dtd[d][:, :, None].to_broadcast([128, 32, 64]), ALU.mult, R=[xs, dtd[d]], W=[s.xdt])
            kb.tt(s.xdw[:], s.xdt[:], wts[d][:, :, None].to_broadcast([128, 32, 64]), ALU.mult, R=[s.xdt, wts[d]], W=[s.xdw])
            for hf in range(2):
                p = pbig.next()
                for g4 in range(4):
                    gi = hf * 4 + g4
                    kb.mm(p[:, g4 * 128:(g4 + 1) * 128], bt[:, gi, :], ct[:, gi, :], R=[bt, ct], W=[p])
                kb.tt(s.cbm[:, hf * 4:(hf + 1) * 4, :], p[:, :].rearrange("p (g l) -> p g l", l=128),
                      tri[d][:, None, :].to_broadcast([128, 4, 128]), ALU.mult, R=[p, tri[d]], W=[s.cbm])
            h_old = s.h
            yt = s.y.next()
            for hf in range(2):
                pyd = pydd[d]
                for g4 in range(4):
                    gi = hf * 4 + g4
                    pc = psm.next()
                    for h4 in range(4):
                        kb.mm(pc[:, h4 * 128:(h4 + 1) * 128], s.D[:, gi * 4 + h4, :], triB[d][:], R=[s.D, triB[d]], W=[pc])
                    e4 = E4.next()
                    kb.act(e4[:], pc[:, :], AF.Exp, R=[pc], W=[e4])
                    g4t = G4.next()
                    kb.tt(g4t[:], e4[:].rearrange("p (h l) -> p h l", l=128), s.cbm[:, gi, None, :].to_broadcast([128, 4, 128]),
                          ALU.mult, R=[e4, s.cbm], W=[g4t])
                    for h4 in range(4):
                        h = gi * 4 + h4
                        h16 = h - hf * 16
                        pb = pyd[h16 // 8]
                        kb.mm(pb[:, (h16 % 8) * 64:(h16 % 8 + 1) * 64], g4t[:, h4, :], s.xdt[:, h, :], R=[g4t, s.xdt], W=[pb])
                    yield
                pyo = [pbig.next(), pbig.next()]
                for g4 in range(4):
                    gi = hf * 4 + g4
                    pb = pyo[g4 // 2]
                    kb.mm(pb[:, (g4 % 2) * 256:(g4 % 2 + 1) * 256], ct[:, gi, :], h_old[:, gi * 4:(gi + 1) * 4, :], R=[ct, h_old], W=[pb])
                for q in range(2):
                    hs = slice(hf * 16 + q * 8, hf * 16 + (q + 1) * 8)
                    kb.tt(s.yo[:], pyo[q][:, :].rearrange("p (h q) -> p h q", q=64), ecs[d][:, hs, None].to_broadcast([128, 8, 64]),
                          ALU.mult, R=[pyo[q], ecs[d]], W=[s.yo])
                    kb.tt(yt[:, hs, :], pyd[q][:, :].rearrange("p (h q) -> p h q", q=64), s.yo[:], ALU.add, R=[pyd[q], s.yo], W=[yt])
            kb.store(g.Yssd[d][t0:t0 + 128, :], yt[:].rearrange("p h q -> p (h q)"), R=[yt])
            yield
            nh = s.hb.next()
            for q in range(4):
                p = pbig.next()
                for g2 in range(2):
                    gi = q * 2 + g2
                    kb.mm(p[:, g2 * 256:(g2 + 1) * 256], btm[:, gi * 128:(gi + 1) * 128], s.xdw[:, gi * 4:(gi + 1) * 4, :], R=[btm, s.xdw], W=[p])
                hs = slice(q * 8, (q + 1) * 8)
                kb.tt(s.hf[:, hs, :], s.hf[:, hs, :], etot[d][:, hs, None].to_broadcast([128, 8, 64]), ALU.mult, R=[s.hf, etot[d]], W=[s.hf])
                kb.tt(s.hf[:, hs, :], s.hf[:, hs, :], p[:, :].rearrange("p (h q) -> p h q", q=64), ALU.add, R=[s.hf, p], W=[s.hf])
            kb.copy(nh[:], s.hf[:], R=[s.hf], W=[nh], eng="act")
            s.h = nh

        for i in range(34):
            run_interleaved([chunk(0, order[0][i]), chunk(1, order[1][i])])


def phase_C3(kb, g):
    with phase(kb):
        prm = sbt(kb, "c3prm", [128, 5, 32])
        ng = sbt(kb, "c3ng", [128, 16])
        kb.load(prm[:], g.ssd_prm[:, :, :], W=[prm])
        kb.load(ng[:], g.ssd_ngT[:, :], W=[ng])
        y0 = Rot([sbt(kb, f"c3a{i}", [128, 32, 64]) for i in range(2)])
        y1 = Rot([sbt(kb, f"c3b{i}", [128, 32, 64]) for i in range(2)])
        xs = Rot([sbt(kb, f"c3x{i}", [128, 32, 64]) for i in range(2)])
        z = Rot([sbt(kb, f"c3z{i}", [128, 2048], BF16) for i in range(2)])
        sz = sbt(kb, "c3sz", [128, 2048])
        sq = sbt(kb, "c3sq", [128, 2048])
        ss = sbt(kb, "c3ss", [128, 8])
        rt = sbt(kb, "c3rt", [128, 8])
        rs = sbt(kb, "c3rs", [128, 8])
        yn = sbt(kb, "c3yn", [128, 4, 2048])
        pb = Rot([pst(kb, f"c3p{i}") for i in range(4)])
        so = Rot([sbt(kb, f"c3o{i}", [128, 512], BF16) for i in range(3)])
        for bi, (s0, n) in enumerate(TBS):
            nt = n // 128
            for tt_ in range(nt):
                t0 = s0 + tt_ * 128
                a, b, x, zz = y0.next(), y1.next(), xs.next(), z.next()
                kb.load(a[:].rearrange("p h q -> p (h q)"), g.Yssd[0][t0:t0 + 128, :], W=[a])
                kb.load(b[:].rearrange("p h q -> p (h q)"), g.Yssd[1][t0:t0 + 128, :], W=[b])
                kb.load(x[:].rearrange("p h q -> p (h q)"), g.xsTM[t0:t0 + 128, :], W=[x])
                kb.load(zz[:], g.zTM[t0:t0 + 128, :], W=[zz])
                kb.tt(a[:], a[:], b[:], ALU.add, R=[a, b], W=[a])
                kb.tt(x[:], x[:], prm[:, 4, :, None].to_broadcast([128, 32, 64]), ALU.mult, R=[x, prm], W=[x])
                kb.tt(a[:], a[:], x[:], ALU.add, R=[a, x], W=[a])
                kb.act(sz[:], zz[:], AF.Silu, R=[zz], W=[sz])
                af = a[:].rearrange("p h q -> p (h q)")
                kb.tt(af, af, sz[:], ALU.mult, R=[a, sz], W=[a])
                for gi in range(8):
                    kb.act(sq[:, gi * 256:(gi + 1) * 256], af[:, gi * 256:(gi + 1) * 256], AF.Square, R=[a], W=[sq, ss], accum_out=ss[:, gi:gi + 1])
                kb.act(rt[:], ss[:], AF.Sqrt, bias=g.eps_t[:, 0:1], scale=1.0 / 256.0, R=[ss, g.eps_t], W=[rt])
                kb.op("dve", lambda e: e.reciprocal(out=rs[:], in_=rt[:]), R=[rt], W=[rs])
                kb.tt(yn[:, tt_, :].rearrange("p (g c) -> p g c", c=256), af.rearrange("p (g c) -> p g c", c=256),
                      rs[:, :, None].to_broadcast([128, 8, 256]), ALU.mult, R=[a, rs], W=[yn])
            for ct in range(16):
                p = pb.next()
                for tt_ in range(nt):
                    kb.tr(p[:, tt_ * 128:(tt_ + 1) * 128], yn[:, tt_, ct * 128:(ct + 1) * 128], g.ident[:], R=[yn, g.ident], W=[p])
                o = so.next()
                kb.act(o[:, :n], p[:, :n], AF.Copy, scale=ng[:, ct:ct + 1], R=[p, ng], W=[o])
                kb.store(g.mixT1[ct * 128:(ct + 1) * 128, s0:s0 + n], o[:, :n], R=[o])


def phase_G(kb, g):
    with phase(kb):
        gf = sbt(kb, "gf", [128, 8])
        kb.load(gf[:], g.gfT[:, :], W=[gf])
        xb = Rot([sbt(kb, f"gx{i}", [128, 8, 512]) for i in range(2)])
        sq = sbt(kb, "gsq", [128, 8, 512], BF16)
        ssp = pst(kb, "gss")
        rt = sbt(kb, "grt", [128, 512])
        rs = sbt(kb, "grs", [128, 512])
        xn = sbt(kb, "gxn", [128, 8, 512])
        pt = Rot([pst(kb, f"gp{i}") for i in range(4)])
        o = Rot([sbt(kb, f"go{i}", [128, 1024]) for i in range(2)])
        xsrc = g.xT.rearrange("(kc p) t -> p kc t", p=128)
        for (s0, n) in TBS[1:]:
            x = xb.next()
            kb.load(x[:, :, :n], xsrc[:, :, s0:s0 + n], W=[x])
            kb.act(sq[:, :, :n], x[:, :, :n], AF.Square, R=[x], W=[sq])
            for kc in range(8):
                kb.mm(ssp[:, :n], g.ones_bf[:], sq[:, kc, :n], start=(kc == 0), stop=(kc == 7), R=[sq, g.ones_bf], W=[ssp])
            kb.act(rt[:, :n], ssp[:, :n], AF.Sqrt, bias=g.eps_t[:, 0:1], scale=1.0 / 1024.0, R=[ssp, g.eps_t], W=[rt])
            kb.op("dve", lambda e: e.reciprocal(out=rs[:, :n], in_=rt[:, :n]), R=[rt], W=[rs])
            kb.tt(xn[:, :, :n], x[:, :, :n], rs[:, None, :n].to_broadcast([128, 8, n]), ALU.mult, R=[x, rs], W=[xn])
            for kc in range(8):
                if kc % 2:
                    kb.act(xn[:, kc, :n], xn[:, kc, :n], AF.Identity, scale=gf[:, kc:kc + 1], R=[xn, gf], W=[xn])
                else:
                    kb.ts(xn[:, kc, :n], xn[:, kc, :n], gf[:, kc:kc + 1], None, ALU.mult, R=[xn, gf], W=[xn])
            for tt_ in range(n // 128):
                oo = o.next()
                for hf in range(2):
                    p = pt.next()
                    for j in range(4):
                        kc = hf * 4 + j
                        kb.tr(p[:, j * 128:(j + 1) * 128], xn[:, kc, tt_ * 128:(tt_ + 1) * 128], g.ident[:], R=[xn, g.ident], W=[p])
                    kb.copy(oo[:, hf * 512:(hf + 1) * 512], p[:, :], R=[p], W=[oo], eng=("act" if hf else "dve"))
                t0 = s0 - NCTX + tt_ * 128
                kb.store(g.out[t0:t0 + 128, :], oo[:], R=[oo])


BLK512L = BLK512[1:]
BLK256L = BLK256[1:]

def declare_inputs(nc, g, shapes):
    for name, (shape, dt) in shapes.items():
        setattr(g, name, nc.dram_tensor(name, list(shape), dt, kind="ExternalInput").ap())


def input_shapes():
    S = {}
    S["x"] = ([4096, 1024], F32)
    S["ctx"] = ([256, 1024], F32)
    S["cT"] = ([128, 8, 2], F32)
    S["ada_w"] = ([2, 1024, 6144], F32)
    S["ada_bT"] = ([2, 128, 48], F32)
    S["g1T"] = ([2, 128, 8], F32)
    S["g2T"] = ([2, 128, 8], F32)
    S["gfT"] = ([128, 8], F32)
    S["w_in0"] = ([1024, 4352], F32)
    S["cosT"] = ([128, T], F32)
    S["sinT"] = ([128, T], F32)
    S["ident_h"] = ([128, 128], F32)
    S["hy_w_out"] = ([1024, 1024], F32)
    S["mlp_w1"] = ([2, 1024, 4096], F32)
    S["mlp_w2"] = ([2, 4096, 1024], F32)
    S["rw_cols"] = ([128, 14 + 4 * 7 + 16], F32)
    S["rw_lora"] = ([128, 2, 512], F32)
    S["rw_gup"] = ([128, 512], F32)
    S["blk_h"] = ([128, 128], F32)
    S["cmask_h"] = ([128, 512], F32)
    S["masks_h"] = ([64, 2, 3, 64], F32)
    S["diff_cols"] = ([64, 4], F32)
    S["subln_g"] = ([128, 1], F32)
    S["ssd_w_in"] = ([1024, 6176], F32)
    S["ssd_w_out"] = ([2048, 1024], F32)
    S["conv_wT"] = ([128, 32, 5], F32)
    S["conv_bT"] = ([128, 32], F32)
    S["ssd_prm"] = ([128, 5, 32], F32)
    S["ssd_ngT"] = ([128, 16], F32)
    S["ut1_h"] = ([128, 128], F32)
    S["lt1_h"] = ([128, 128], F32)
    S["sel_h"] = ([32, 32, 128], F32)
    return S


def build(debug=False, stop_after=None, only=None, as_input=()):
    nc = bass.Bass("TRN2", target_bir_lowering=False)
    _AS_INPUT.clear()
    _AS_INPUT.update(as_input)
    g = G()
    declare_inputs(nc, g, input_shapes())
    g.out = nc.dram_tensor("out", [4096, 1024], F32, kind="ExternalOutput").ap()
    dbg = debug
    g.xT = dram(nc, "xT", [1024, T], F32, dbg)
    g.PrT = dram(nc, "PrT", [1792, T], F32, dbg)
    g.QrotT = dram(nc, "QrotT", [512, T], BF16, dbg)
    g.QplT = dram(nc, "QplT", [512, T], BF16, dbg)
    g.KrotT = dram(nc, "KrotT", [512, T], BF16, dbg)
    g.Vd = dram(nc, "Vd", [T, 512], BF16, dbg)
    g.modD = dram(nc, "modD", [2, 128, 96], F32, dbg)
    g.gT = dram(nc, "gT", [512, T], F32, dbg)
    g.bonT = dram(nc, "bonT", [512, T], F32, dbg)
    g.Vtm = dram(nc, "Vtm", [T, 512], BF16, dbg)
    g.gamA = dram(nc, "gamA", [2, 512, 68], F32, dbg)
    g.gam = [g.gamA[0], g.gamA[1]]
    g.FMA = dram(nc, "FMA", [2, 4, 512, T], BF16, dbg)
    g.FM = [[g.FMA[d, k] for k in range(4)] for d in range(2)]
    g.TMA = dram(nc, "TMA", [2, T, 2, 512], BF16, dbg)
    g.TM = [g.TMA[0], g.TMA[1]]
    g.mixT = dram(nc, "mixT", [1024, T], BF16, dbg)
    g.xbcT = dram(nc, "xbcT", [4096, T], F32, False)
    g.zTM = dram(nc, "zTM", [T, 2048], BF16, False)
    g.dtTM = dram(nc, "dtTM", [T, 32], F32, dbg)
    g.xsTM = dram(nc, "xsTM", [T, 2048], F32, dbg)
    g.BT = dram(nc, "BT", [1024, T], BF16, dbg)
    g.CT = dram(nc, "CT", [1024, T], BF16, dbg)
    g.BTM = dram(nc, "BTM", [T, 1024], BF16, False)
    g.YsA = dram(nc, "YsA", [2, T, 2048], F32, dbg)
    g.Yssd = [g.YsA[0], g.YsA[1]]
    g.mixT1 = dram(nc, "mixT1", [2048, T], BF16, dbg)
    g.YA = dram(nc, "YA", [2, T, 512], F32, dbg)
    g.Y = [g.YA[0], g.YA[1]]
    with ExitStack() as es:
        kb = KB(nc, es)
        kb.es_t = None
        g.ident = kb.sb("ident", [128, 128])
        g.ones_bf = kb.sb("ones_bf", [128, 128], BF16)
        g.eps_t = kb.sb("eps_t", [128, 8])
        g.mod = [kb.sb(f"mod{li}", [128, 48, 2]) for li in range(2)]
        g.sc1 = [kb.sb(f"sc1_{li}", [128, 8, 2]) for li in range(2)]
        g.sc2 = [kb.sb(f"sc2_{li}", [128, 8, 2]) for li in range(2)]
        kb.load(g.ident[:], g.ident_h[:, :], W=[g.ident])
        kb.memset(g.ones_bf[:], 1.0, W=[g.ones_bf])
        g.ident_bf = kb.sb("ident_bf", [128, 128], BF16)
        kb.copy(g.ident_bf[:], g.ident[:], R=[g.ident], W=[g.ident_bf])
        kb.memset(g.eps_t[:, 0:1], EPS, W=[g.eps_t])
        kb.memset(g.eps_t[:, 1:2], 1e-12, W=[g.eps_t])
        kb.memset(g.eps_t[:, 2:3], 64e-5, W=[g.eps_t])
        kb.memset(g.eps_t[:, 3:4], 0.0, W=[g.eps_t])
        kb.memset(g.eps_t[:, 4:5], 1.0, W=[g.eps_t])
        phases = [("mods", phase_mods), ("xT", phase_xT), ("A0", phase_A0), ("B0", phase_B0), ("C0", phase_C0), ("C2", phase_C2), ("D0", phase_D0),
                  ("E0", lambda kb, g: phase_E(kb, g, 0, g.hy_w_out, 8, g.mixT, BLK512)),
                  ("F0", lambda kb, g: phase_F(kb, g, 0, BLK256)),
                  ("A1", phase_A1), ("B1", phase_B1), ("C1", phase_C1), ("C3", phase_C3),
                  ("E1", lambda kb, g: phase_E(kb, g, 1, g.ssd_w_out, 16, g.mixT1, BLK512L)),
                  ("F1", lambda kb, g: phase_F(kb, g, 1, BLK256L)), ("G", phase_G)]
        for name, fn in phases:
            if only is not None and name not in only:
                continue
            fn(kb, g)
            if stop_after == name:
                break
        if debug:
            for li in range(2):
                kb.store(g.modD[li], g.mod[li][:].rearrange("p a b -> p (a b)"), R=[g.mod[li]])
        kb.finish()
        print("instructions:", kb.nins)
    return nc


def rope_tables():
    inv = 10000.0 ** (-np.arange(0, 32, 2, dtype=np.float32) / 32.0)
    t = np.arange(4096)
    rows = (t // 64).astype(np.float32)
    cols = (t % 64).astype(np.float32)
    ar = rows[:, None] * inv[None, :]
    ac = cols[:, None] * inv[None, :]
    cosT = np.ones((128, T), np.float32)
    sinT = np.zeros((128, T), np.float32)
    for p in range(128):
        d = p % 64
        ang = ar if d < 32 else ac
        i = d % 16
        first = (d % 32) < 16
        cosT[p, 256:] = np.cos(ang[:, i])
        sinT[p, 256:] = (-np.sin(ang[:, i])) if first else np.sin(ang[:, i])
    return cosT, sinT


def swap_cols(w):
    idx = np.arange(512)
    d = idx % 32
    partner = np.where(d < 16, idx + 16, idx - 16)
    return w[:, partner]


def host_consts(inp):
    C = {}
    f = np.float32
    C["ada_w"] = np.ascontiguousarray(inp["ada_w"], dtype=f)
    C["ada_bT"] = np.ascontiguousarray(inp["ada_b"].reshape(2, 48, 128).transpose(0, 2, 1), dtype=f)
    C["g1T"] = np.ascontiguousarray(inp["norm1_g"].reshape(2, 8, 128).transpose(0, 2, 1), dtype=f)
    C["g2T"] = np.ascontiguousarray(inp["norm2_g"].reshape(2, 8, 128).transpose(0, 2, 1), dtype=f)
    C["gfT"] = np.ascontiguousarray(inp["norm_f_g"].reshape(8, 128).T, dtype=f)
    w = inp["hy_w_in"][0]
    q = w[:, 1792:2304]
    k = w[:, 2304:2816]
    v = w[:, 2816:3328]
    C["w_in0"] = np.ascontiguousarray(np.concatenate([w[:, :1792], q, swap_cols(q), k, swap_cols(k), v], axis=1), dtype=f)
    C["cosT"], C["sinT"] = rope_tables()
    C["ident_h"] = np.eye(128, dtype=f)
    C["hy_w_out"] = np.ascontiguousarray(inp["hy_w_out"][0], dtype=f)
    C["mlp_w1"] = np.ascontiguousarray(inp["mlp_w1"], dtype=f)
    C["mlp_w2"] = np.ascontiguousarray(inp["mlp_w2"], dtype=f)
    col = lambda a: np.ascontiguousarray(np.asarray(a, dtype=f).reshape(-1, 128).T)
    rw = np.zeros((128, 58), f)
    rw[:, 0:14] = col(inp["rwkv_mu"][0])
    rw[:, 14:18] = col(inp["rwkv_k_k"][0])
    rw[:, 18:22] = col(inp["rwkv_k_a"][0])
    rw[:, 22:26] = col(inp["rwkv_r_k"][0].reshape(-1))
    rw[:, 26:30] = col(inp["rwkv_ln_w"][0])
    rw[:, 30:34] = col(inp["rwkv_ln_b"][0])
    rw[:, 34:38] = col(inp["rwkv_w0"][0, 0])
    rw[:, 38:42] = col(inp["rwkv_w0"][0, 1])
    rw[:, 42:46] = col(inp["rwkv_a0"][0, 0])
    rw[:, 46:50] = col(inp["rwkv_a0"][0, 1])
    C["rw_cols"] = rw
    lora = np.zeros((128, 2, 512), f)
    lora[0:64] = inp["rwkv_w_up"][0].transpose(1, 0, 2)
    lora[64:128] = inp["rwkv_a_up"][0].transpose(1, 0, 2)
    C["rw_lora"] = lora
    C["rw_gup"] = np.ascontiguousarray(inp["rwkv_g_up"][0], dtype=f)
    blk = np.zeros((128, 128), f)
    blk[:64, :64] = 1
    blk[64:, 64:] = 1
    C["blk_h"] = blk
    cm = np.ones((128, 512), f)
    cm[:, ::64] = 0
    C["cmask_h"] = cm
    s = np.arange(64)[:, None]
    t = np.arange(64)[None, :]
    m = np.zeros((64, 2, 3, 64), f)
    m[:, 0, 0] = (s < t)
    m[:, 0, 1] = (s <= t)
    m[:, 0, 2] = (s > t)
    m[:, 1, 0] = (s > t)
    m[:, 1, 1] = (s >= t)
    m[:, 1, 2] = (s < t)
    C["masks_h"] = m
    C["diff_cols"] = np.stack([inp["diff_lq1"][0], inp["diff_lk1"][0], inp["diff_lq2"][0], inp["diff_lk2"][0]], axis=1).astype(f)
    C["subln_g"] = np.ascontiguousarray(inp["diff_subln_g"][0].reshape(128, 1), dtype=f)
    C["ssd_w_in"] = np.ascontiguousarray(inp["ssd_w_in"][0], dtype=f)
    C["ssd_w_out"] = np.ascontiguousarray(inp["ssd_w_out"][0], dtype=f)
    C["conv_wT"] = np.ascontiguousarray(inp["ssd_conv_w"][0].reshape(5, 32, 128).transpose(2, 1, 0), dtype=f)
    C["conv_bT"] = col(inp["ssd_conv_b"][0])
    prm = np.zeros((128, 5, 32), f)
    prm[:, 0] = inp["ssd_dt_bias"][0, 0][None, :]
    prm[:, 1] = inp["ssd_dt_bias"][0, 1][None, :]
    prm[:, 2] = inp["ssd_a_log"][0, 0][None, :]
    prm[:, 3] = inp["ssd_a_log"][0, 1][None, :]
    prm[:, 4] = inp["ssd_d"][0][None, :]
    C["ssd_prm"] = prm
    C["ssd_ngT"] = col(inp["ssd_norm_g"][0])
    jj = np.arange(128)[:, None]
    ll = np.arange(128)[None, :]
    C["ut1_h"] = (jj <= ll).astype(f)
    C["lt1_h"] = (jj >= ll).astype(f)
    sel = np.zeros((32, 32, 128), f)
    for h in range(32):
        sel[h, h, :] = 1.0
    C["sel_h"] = sel
    return C


def core_inputs(inp, C, b):
    m = dict(C)
    m["x"] = np.ascontiguousarray(inp["x"][b], dtype=np.float32)
    m["ctx"] = np.ascontiguousarray(inp["ctx"][b], dtype=np.float32)
    cv = np.stack([inp["c"][b], inp["c_ctx"]], axis=0).astype(np.float32)
    m["cT"] = np.ascontiguousarray(cv.reshape(2, 8, 128).transpose(2, 1, 0))
    return m


def kernel(**inputs):
    inp = {k: np.asarray(v) for k, v in inputs.items()}
    C = host_consts(inp)
    nc = build()
    in_maps = [core_inputs(inp, C, b) for b in range(8)]
    res = run_bass_kernel_spmd(nc, in_maps, core_ids=list(range(8)))
    return np.stack([np.asarray(r["out"]) for r in res.results], axis=0).astype(np.float32)
```

```python
import concourse.bass as bass
import concourse.mybir as mybir

F32 = mybir.dt.float32
BF16 = mybir.dt.bfloat16
AF = mybir.ActivationFunctionType
ALU = mybir.AluOpType
AX = mybir.AxisListType


class Src:
    def __init__(s, kb, name, inc, limit):
        s.kb, s.name, s.inc, s.limit = kb, name, inc, limit
        s.sems = []
        s.n = 0

    def sem_for(s, n):
        e = (n - 1) // s.limit
        while len(s.sems) <= e:
            s.sems.append(s.kb.es.enter_context(s.kb.nc.semaphore(f"{s.name}_{len(s.sems)}")))
        return s.sems[e], ((n - 1) % s.limit + 1) * s.inc


class Tk:
    __slots__ = ("w", "r")

    def __init__(s):
        s.w = {}
        s.r = {}


class TT:
    def __init__(s, t, k=None, ps=False):
        s.t = t
        s.k = k if k is not None else Tk()
        s.ps = ps

    def __getitem__(s, idx):
        return s.t[idx]


class KB:
    NSLOT = 20

    def __init__(s, nc, es):
        s.nc, s.es = nc, es
        s.eng = {"pe": nc.tensor, "dve": nc.vector, "act": nc.scalar, "pool": nc.gpsimd, "sp": nc.sync}
        s.src = {k: Src(s, "c" + k, 1, 30000) for k in s.eng}
        s.waited = {k: {} for k in s.eng}
        s.slots = {q: [Src(s, f"d{q}{i}", 16, 1800) for i in range(s.NSLOT)] for q in ("sp", "pool", "act")}
        s.rr = {q: 0 for q in s.slots}
        s.nins = 0
        s.same_engine_sync = True

    def sb(s, name, shape, dt=F32):
        return TT(s.es.enter_context(s.nc.sbuf_tensor("g_" + name, list(shape), dt)))

    def ps(s, name, shape, dt=F32):
        return TT(s.es.enter_context(s.nc.psum_tensor("gp_" + name, list(shape), dt)))

    def _deps(s, R, W, me=None):
        d = {}
        for r in R:
            k = r.k if isinstance(r, TT) else r
            for src, n in k.w.items():
                if d.get(src, 0) < n:
                    d[src] = n
            if isinstance(r, TT) and r.ps:
                for src, n in k.r.items():
                    if src is not me and d.get(src, 0) < n:
                        d[src] = n
        for w in W:
            k = w.k if isinstance(w, TT) else w
            for dd in (k.w, k.r):
                for src, n in dd.items():
                    if d.get(src, 0) < n:
                        d[src] = n
        return d

    def _wait(s, eng, d):
        wd = s.waited[eng]
        for src, n in d.items():
            if src is s.src[eng] and (eng == "pe" or not s.same_engine_sync):
                continue
            if wd.get(src, 0) >= n:
                continue
            sem, val = src.sem_for(n)
            s.eng[eng].wait_ge(sem, val)
            wd[src] = n

    def _mark(s, src, n, R, W):
        for w in W:
            k = w.k if isinstance(w, TT) else w
            k.w = {src: n}
            k.r = {}
        for r in R:
            k = r.k if isinstance(r, TT) else r
            if k.r.get(src, 0) < n:
                k.r[src] = n

    def op(s, eng, fn, R=(), W=()):
        d = s._deps(R, W, s.src[eng])
        s._wait(eng, d)
        src = s.src[eng]
        src.n += 1
        sem, _ = src.sem_for(src.n)
        ins = fn(s.eng[eng])
        ins.then_inc(sem, 1)
        s._mark(src, src.n, R, W)
        s.nins += 1

    def dma(s, q, out, in_, R=(), W=(), **kw):
        i = s.rr[q]
        s.rr[q] = (i + 1) % s.NSLOT
        slot = s.slots[q][i]
        d = s._deps(R, W)
        if slot.n > 0 and d.get(slot, 0) < slot.n:
            d[slot] = slot.n
        s._wait(q, d)
        slot.n += 1
        sem, _ = slot.sem_for(slot.n)
        s.eng[q].dma_start(out=out, in_=in_, **kw).then_inc(sem, 16)
        s._mark(slot, slot.n, R, W)
        s.nins += 1

    def load(s, out, in_, R=(), W=(), **kw):
        s.dma("sp", out, in_, R, W, **kw)

    def store(s, out, in_, R=(), W=(), **kw):
        s.dma("pool", out, in_, R, W, **kw)

    def finish(s):
        d = {}
        for q in s.slots:
            for sl in s.slots[q]:
                if sl.n:
                    d[sl] = sl.n
        for k, src in s.src.items():
            if src.n and k != "sp":
                d[src] = src.n
        s._wait("sp", d)

    def mm(s, out, lhsT, rhs, start=True, stop=True, R=(), W=()):
        s.op("pe", lambda e: e.matmul(out, lhsT=lhsT, rhs=rhs, start=start, stop=stop), R, W)

    def tr(s, out, in_, ident, R=(), W=()):
        s.op("pe", lambda e: e.transpose(out, in_, ident), R, W)

    def act(s, out, in_, func, bias=None, scale=None, R=(), W=(), accum_out=None):
        kw = {}
        if bias is not None:
            kw["bias"] = bias
        if scale is not None:
            kw["scale"] = scale
        if accum_out is not None:
            kw["accum_out"] = accum_out
        s.op("act", lambda e: e.activation(out=out, in_=in_, func=func, **kw), R, W)

    def tt(s, out, in0, in1, op, R=(), W=(), eng="dve"):
        s.op(eng, lambda e: e.tensor_tensor(out=out, in0=in0, in1=in1, op=op), R, W)

    def ts(s, out, in0, s1, s2, op0, op1=None, R=(), W=(), eng="dve"):
        if op1 is None:
            s.op(eng, lambda e: e.tensor_scalar(out=out, in0=in0, scalar1=s1, scalar2=None, op0=op0), R, W)
        else:
            s.op(eng, lambda e: e.tensor_scalar(out=out, in0=in0, scalar1=s1, scalar2=s2, op0=op0, op1=op1), R, W)

    def stt(s, out, in0, scalar, in1, op0, op1, R=(), W=()):
        s.op("dve", lambda e: e.scalar_tensor_tensor(out=out, in0=in0, scalar=scalar, in1=in1, op0=op0, op1=op1), R, W)

    def copy(s, out, in_, R=(), W=(), eng="dve"):
        if eng == "act":
            s.op("act", lambda e: e.copy(out=out, in_=in_), R, W)
        else:
            s.op(eng, lambda e: e.tensor_copy(out=out, in_=in_), R, W)

    def memset(s, ap, val, W=(), eng="dve"):
        s.op(eng, lambda e: e.memset(ap, val), (), W)
import math
import numpy as np
from contextlib import ExitStack, contextmanager
from concourse.bass_utils import run_bass_kernel_spmd

T = 4352
NCTX = 256
TBS = [(0, 256)] + [(256 + 512 * i, 512) for i in range(8)]
KAPPA = math.exp(-0.5)
EPS = 1e-6


class G:
    pass


@contextmanager
def phase(kb):
    old = kb.es
    barrier(kb)
    with ExitStack() as es:
        kb.es_t = es
        yield
        barrier(kb)
    kb.es_t = None


def barrier(kb):
    d = {}
    for q in kb.slots:
        for sl in kb.slots[q]:
            if sl.n:
                d[sl] = sl.n
    for k, src in kb.src.items():
        if src.n:
            d[src] = src.n
    for e in ("pe", "dve", "act", "pool", "sp"):
        dd = {s_: n for s_, n in d.items() if s_ is not kb.src[e]}
        kb._wait(e, dd)


_uid = [0]


def sbt(kb, name, shape, dt=F32):
    _uid[0] += 1
    return TT(kb.es_t.enter_context(kb.nc.sbuf_tensor(f"s{_uid[0]}_{name}", list(shape), dt)))


def pst(kb, name, shape=None, dt=F32):
    _uid[0] += 1
    full = [128, 512] if dt == F32 else [128, 1024]
    return TT(kb.es_t.enter_context(kb.nc.psum_tensor(f"p{_uid[0]}_{name}", full, dt)), ps=True)


def run_interleaved(gens):
    gens = list(gens)
    while gens:
        for g_ in list(gens):
            try:
                next(g_)
            except StopIteration:
                gens.remove(g_)


class PPool:
    def __init__(s, banks):
        s.b = banks
        s.live = [False] * len(banks)
        s.i = 0

    def get(s):
        n = len(s.b)
        for k in range(n):
            j = (s.i + k) % n
            if not s.live[j]:
                s.live[j] = True
                s.i = (j + 1) % n
                return s.b[j]
        raise RuntimeError("PSUM pool exhausted")

    def put(s, bank):
        s.live[s.b.index(bank)] = False


class Rot:
    def __init__(s, items):
        s.items = items
        s.i = 0

    def next(s):
        x = s.items[s.i]
        s.i = (s.i + 1) % len(s.items)
        return x


_AS_INPUT = set()


def dram(nc, name, shape, dt, debug):
    kind = "ExternalInput" if name in _AS_INPUT else ("ExternalOutput" if debug else "Internal")
    return nc.dram_tensor(name, list(shape), dt, kind=kind).ap()


def phase_mods(kb, g):
    nc = kb.nc
    with phase(kb):
        cT = sbt(kb, "cT", [128, 8, 2])
        scT = sbt(kb, "scT", [128, 8, 2])
        sg_ = sbt(kb, "sgc", [128, 8, 2])
        kb.load(cT[:], g.cT[:, :, :], W=[cT])
        kb.act(sg_[:], cT[:], AF.Sigmoid, R=[cT], W=[sg_])
        kb.tt(scT[:], cT[:], sg_[:], ALU.mult, R=[cT, sg_], W=[scT])
        wb = Rot([sbt(kb, f"adaw{i}", [128, 8, 1024]) for i in range(2)])
        pmb = pst(kb, "pmod")
        pm = TT(pmb.t[:, 0:96].rearrange("p (a b) -> p a b", b=2), pmb.k, ps=True)
        adab = sbt(kb, "adab", [128, 48])
        g1 = sbt(kb, "g1", [128, 8])
        g2 = sbt(kb, "g2", [128, 8])
        for li in range(2):
            kb.load(adab[:], g.ada_bT[li], W=[adab])
            kb.load(g1[:], g.g1T[li], W=[g1])
            kb.load(g2[:], g.g2T[li], W=[g2])
            src = g.ada_w[li].rearrange("(kc p) n -> p kc n", p=128)
            for pc in range(6):
                w = wb.next()
                kb.load(w[:], src[:, :, pc * 1024:(pc + 1) * 1024], W=[w])
                for cc in range(8):
                    col = pc * 8 + cc
                    for kc in range(8):
                        kb.mm(pm[:, col, :], w[:, kc, cc * 128:(cc + 1) * 128], scT[:, kc, :],
                              start=(kc == 0), stop=(kc == 7), R=[w, scT], W=[pm])
            mod = g.mod[li]
            kb.tt(mod[:], pm[:], adab[:, :, None].to_broadcast([128, 48, 2]), ALU.add, R=[pm, adab], W=[mod])
            for (sc, gi, m) in ((g.sc1[li], g1, 1), (g.sc2[li], g2, 4)):
                kb.ts(sc[:], mod[:, m * 8:(m + 1) * 8, :], 1.0, None, ALU.add, R=[mod], W=[sc])
                kb.tt(sc[:], sc[:], gi[:, :, None].to_broadcast([128, 8, 2]), ALU.mult, R=[sc, gi], W=[sc])


def phase_xT(kb, g):
    with phase(kb):
        xin = Rot([sbt(kb, f"xin{i}", [128, 1024]) for i in range(2)])
        xo = Rot([sbt(kb, f"xo{i}", [128, 8, 128]) for i in range(2)])
        pt = Rot([pst(kb, f"pT{i}") for i in range(4)])
        dst = g.xT.rearrange("(kc p) t -> p kc t", p=128)
        for i in range(34):
            xi = xin.next()
            src = g.ctx[i * 128:(i + 1) * 128, :] if i < 2 else g.x[(i - 2) * 128:(i - 1) * 128, :]
            kb.load(xi[:], src, W=[xi])
            o = xo.next()
            for hf in range(2):
                p = pt.next()
                for j in range(4):
                    kc = hf * 4 + j
                    kb.tr(p[:, j * 128:(j + 1) * 128], xi[:, kc * 128:(kc + 1) * 128], g.ident[:], R=[xi, g.ident], W=[p])
                kb.copy(o[:, hf * 4:(hf + 1) * 4, :], p[:, :].rearrange("p (a b) -> p a b", b=128), R=[p], W=[o], eng=("act" if hf else "dve"))
            kb.store(dst[:, :, i * 128:(i + 1) * 128], o[:], R=[o])


class NormBufs:
    def __init__(s, kb, tag):
        s.sq = sbt(kb, f"nsq{tag}", [128, 8, 512], BF16)
        s.tmp = sbt(kb, f"ntmp{tag}", [128, 8, 512])
        s.rt = sbt(kb, f"nrt{tag}", [128, 512])
        s.rstd = sbt(kb, f"nrstd{tag}", [128, 512])
        s.ss = pst(kb, f"nss{tag}", [128, 512])


def norm_mod(kb, g, nb, xTb, n, sc, sh, j, hT, sh_off=0):
    kb.act(nb.sq[:, :, :n], xTb[:, :, :n], AF.Square, R=[xTb], W=[nb.sq])
    for kc in range(8):
        kb.mm(nb.ss[:, :n], g.ones_bf[:], nb.sq[:, kc, :n], start=(kc == 0), stop=(kc == 7), R=[nb.sq, g.ones_bf], W=[nb.ss])
    kb.act(nb.rt[:, :n], nb.ss[:, :n], AF.Sqrt, bias=g.eps_t[:, 0:1], scale=1.0 / 1024.0, R=[nb.ss, g.eps_t], W=[nb.rt])
    kb.op("dve", lambda e: e.reciprocal(out=nb.rstd[:, :n], in_=nb.rt[:, :n]), R=[nb.rt], W=[nb.rstd])
    kb.tt(nb.tmp[:, :, :n], xTb[:, :, :n], nb.rstd[:, None, :n].to_broadcast([128, 8, n]), ALU.mult, R=[xTb, nb.rstd], W=[nb.tmp])
    for kc in range(8):
        kb.act(hT[:, kc, :n], nb.tmp[:, kc, :n], AF.Identity, bias=sh[:, sh_off + kc, j:j + 1], scale=sc[:, kc, j:j + 1],
               R=[nb.tmp, sc, sh], W=[hT])


def load_weight_bf16(kb, W, src_ap, ncols, stg, piece=256):
    src = src_ap.rearrange("(kc p) n -> p kc n", p=128)
    kcn = src.shape[1]
    i = 0
    for c0 in range(0, ncols, piece):
        c1 = min(ncols, c0 + piece)
        w = c1 - c0
        st = stg.next()
        sv = st.t[:, 0:kcn * w].rearrange("p (k n) -> p k n", n=w)
        kb.load(sv, src[:, :, c0:c1], W=[st])
        kb.copy(W[:, :kcn, c0:c1], sv, R=[st], W=[W], eng=("act" if i % 2 else "dve"))
        i += 1


def phase_A0(kb, g):
    with phase(kb):
        W = sbt(kb, "wA", [128, 8, 4352], BF16)
        stg = Rot([sbt(kb, f"wstg{i}", [128, 2048]) for i in range(2)])
        import os
        STG = int(os.environ.get("A0_STAGE", "9"))
        load_weight_bf16(kb, W, g.w_in0, 4352, stg)
        nb = NormBufs(kb, "A")
        xb = Rot([sbt(kb, f"xTb{i}", [128, 8, 512]) for i in range(2)])
        hTs = Rot([sbt(kb, f"hT{i}", [128, 8, 512], BF16) for i in range(2)])
        cosb = Rot([sbt(kb, f"cos{i}", [128, 512]) for i in range(2)])
        sinb = Rot([sbt(kb, f"sin{i}", [128, 512]) for i in range(2)])
        pmm = Rot([pst(kb, f"pmm{i}", [128, 512]) for i in range(6)])
        st32 = Rot([sbt(kb, f"st32_{i}", [128, 512]) for i in range(4)])
        st16 = Rot([sbt(kb, f"st16_{i}", [128, 512], BF16) for i in range(6)])
        t1s = Rot([sbt(kb, f"t1_{i}", [128, 512]) for i in range(2)])
        t2s = Rot([sbt(kb, f"t2_{i}", [128, 512]) for i in range(2)])
        xsrc = g.xT.rearrange("(kc p) t -> p kc t", p=128)
        for bi, (s0, n) in enumerate(TBS):
            if STG < 2 or (STG < 9 and bi > 0):
                break
            j = 1 if bi == 0 else 0
            x = xb.next()
            kb.load(x[:, :, :n], xsrc[:, :, s0:s0 + n], W=[x])
            cs, sn = cosb.next(), sinb.next()
            kb.load(cs[:, :n], g.cosT[:, s0:s0 + n], W=[cs])
            kb.load(sn[:, :n], g.sinT[:, s0:s0 + n], W=[sn])
            hT = hTs.next()
            norm_mod(kb, g, nb, x, n, g.sc1[0], g.mod[0], j, hT, sh_off=0)

            def proj(ct):
                p = pmm.next()
                for kc in range(8):
                    kb.mm(p[:, :n], W[:, kc, ct * 128:(ct + 1) * 128], hT[:, kc, :n], start=(kc == 0), stop=(kc == 7), R=[W, hT], W=[p])
                return p
            if STG < 3:
                continue
            for ct in range(14):
                p = proj(ct)
                st = st32.next()
                kb.copy(st[:, :n], p[:, :n], R=[p], W=[st], eng=("act" if ct % 2 else "dve"))
                kb.store(g.PrT[ct * 128:(ct + 1) * 128, s0:s0 + n], st[:, :n], R=[st])
            if STG < 4:
                continue
            for h in range(4):
                for (base, dst_rot, dst_pl) in ((14, g.QrotT, g.QplT), (22, g.KrotT, None)):
                    pq = proj(base + h)
                    pw = proj(base + 4 + h)
                    t1, t2 = t1s.next(), t2s.next()
                    kb.tt(t1[:, :n], pq[:, :n], cs[:, :n], ALU.mult, R=[pq, cs], W=[t1])
                    kb.tt(t2[:, :n], pw[:, :n], sn[:, :n], ALU.mult, R=[pw, sn], W=[t2])
                    so = st16.next()
                    kb.tt(so[:, :n], t1[:, :n], t2[:, :n], ALU.add, R=[t1, t2], W=[so])
                    if not os.environ.get("NOSTORE4"):
                        kb.store(dst_rot[h * 128:(h + 1) * 128, s0:s0 + n], so[:, :n], R=[so])
                    if dst_pl is not None:
                        sp_ = st16.next()
                        kb.copy(sp_[:, :n], pq[:, :n], R=[pq], W=[sp_], eng="act")
                        if not os.environ.get("NOSTORE4"):
                            kb.store(dst_pl[h * 128:(h + 1) * 128, s0:s0 + n], sp_[:, :n], R=[sp_])
            if STG < 5:
                continue
            for tt_ in range(n // 128):
                p = pmm.next()
                for kc in range(8):
                    kb.mm(p[:, :], hT[:, kc, tt_ * 128:(tt_ + 1) * 128], W[:, kc, 3840:4352], start=(kc == 0), stop=(kc == 7), R=[W, hT], W=[p])
                so = st16.next()
                kb.copy(so[:], p[:], R=[p], W=[so], eng=("act" if tt_ % 2 else "dve"))
                kb.store(g.Vd[s0 + tt_ * 128:s0 + (tt_ + 1) * 128, :], so[:], R=[so])

def phase_B0(kb, g):
    with phase(kb):
        rwc = sbt(kb, "rwc", [128, 58])
        hmu = sbt(kb, "hmu", [128, 14])
        omm = sbt(kb, "omm", [128, 14])
        omka = sbt(kb, "omka", [128, 4])
        hrk = sbt(kb, "hrk", [128, 4])
        lora = sbt(kb, "lora", [128, 2, 512])
        gup = sbt(kb, "gup", [128, 512])
        blk = sbt(kb, "blk", [128, 128])
        cmask = sbt(kb, "cmask", [128, 512])
        kb.load(rwc[:], g.rw_cols[:, :], W=[rwc])
        kb.load(lora[:], g.rw_lora[:, :, :], W=[lora])
        kb.load(gup[:], g.rw_gup[:, :], W=[gup])
        kb.load(blk[:], g.blk_h[:, :], W=[blk])
        kb.load(cmask[:], g.cmask_h[:, :], W=[cmask])
        kb.ts(hmu[:], rwc[:, 0:14], 0.5, None, ALU.mult, R=[rwc], W=[hmu])
        kb.ts(omm[:], rwc[:, 0:14], -1.0, 1.0, ALU.mult, ALU.add, R=[rwc], W=[omm])
        kb.ts(omka[:], rwc[:, 18:22], -1.0, 1.0, ALU.mult, ALU.add, R=[rwc], W=[omka])
        kb.ts(hrk[:], rwc[:, 22:26], 0.5, None, ALU.mult, R=[rwc], W=[hrk])

        pin = sbt(kb, "pin", [128, 14, 514])
        psx = sbt(kb, "psx", [128, 14, 512])
        lwin = sbt(kb, "lwin", [128, 512])
        sgd = sbt(kb, "sgd", [128, 512])
        F = lambda nm, dt=F32: sbt(kb, nm, [128, 512], dt)
        R2 = lambda nm: Rot([F(f"{nm}{i}") for i in range(2)])
        hp_rots = [R2(nm) for nm in ("kku", "sq", "rt", "rs", "kk", "rk", "bon", "kbs")]
        d_rots = [R2(nm) for nm in ("sg", "a_", "tmp", "kmod", "b_", "ci", "cr", "ce", "e1", "e2", "e3")]
        fm16 = [Rot([F(f"fm{k}_{i}", BF16) for i in range(2)]) for k in range(4)]
        vb = F("vb", BF16)
        st32 = Rot([F(f"bst{i}") for i in range(2)])
        gst = Rot([sbt(kb, f"gst{i}", [128, 8]) for i in range(2)])
        tms = Rot([sbt(kb, f"tms{i}", [128, 1024], BF16) for i in range(3)])
        pmm = Rot([pst(kb, f"bp{i}") for i in range(4)])
        ptr = Rot([pst(kb, f"bt{i}", dt=BF16) for i in range(3)])
        psrc = g.PrT.rearrange("(ti p) t -> p ti t", p=128)
        for bi, (s0, n) in enumerate(TBS):
            nt = n // 128
            nch = n // 64
            c0 = s0 // 64
            lo = s0 if s0 in (0, NCTX) else s0 - 1
            hi = s0 + n if (s0 + n) in (NCTX, T) else s0 + n + 1
            kb.memset(pin[:, :, 0:1], 0.0, W=[pin])
            kb.memset(pin[:, :, n + 1:n + 2], 0.0, W=[pin])
            kb.load(pin[:, :, lo - s0 + 1:hi - s0 + 1], psrc[:, :, lo:hi], W=[pin])
            kb.tt(psx[:, :, :n], pin[:, :, 0:n], pin[:, :, 2:n + 2], ALU.add, R=[pin], W=[psx])
            for ti in range(14):
                kb.act(psx[:, ti, :n], psx[:, ti, :n], AF.Identity, scale=hmu[:, ti:ti + 1], R=[psx, hmu], W=[psx])
            for ti in range(14):
                kb.stt(psx[:, ti, :n], pin[:, ti, 1:n + 1], omm[:, ti:ti + 1], psx[:, ti, :n], ALU.mult, ALU.add, R=[pin, psx, omm], W=[psx])
            kb.act(lwin[0:64, :n], psx[0:64, 12, :n], AF.Tanh, R=[psx], W=[lwin])
            kb.copy(lwin[64:128, :n], psx[64:128, 12, :n], R=[psx], W=[lwin])
            kb.act(sgd[:, :n], psx[:, 13, :n], AF.Sigmoid, R=[psx], W=[sgd])
            for hp in range(4):
                kku, sq, rt, rs, kk, rk, bon, kbs = [R_.next() for R_ in hp_rots]
                r = psx[:, hp, :n]
                k = psx[:, 4 + hp, :n]
                v = psx[:, 8 + hp, :n]
                hs = slice(hp * 128, (hp + 1) * 128)
                kb.ts(kku[:, :n], k, rwc[:, 14 + hp:15 + hp], None, ALU.mult, R=[psx, rwc], W=[kku])
                kb.tt(sq[:, :n], kku[:, :n], kku[:, :n], ALU.mult, R=[kku], W=[sq])
                p = pmm.next()
                kb.mm(p[:, :n], blk[:], sq[:, :n], R=[blk, sq], W=[p])
                kb.act(rt[:, :n], p[:, :n], AF.Sqrt, bias=g.eps_t[:, 1:2], scale=1.0, R=[p, g.eps_t], W=[rt])
                kb.op("dve", lambda e: e.reciprocal(out=rs[:, :n], in_=rt[:, :n]), R=[rt], W=[rs])
                kb.tt(kk[:, :n], kku[:, :n], rs[:, :n], ALU.mult, R=[kku, rs], W=[kk])
                p = pmm.next()
                kb.mm(p[:, :n], gup[:, hs], sgd[:, :n], R=[gup, sgd], W=[p])
                st = st32.next()
                kb.copy(st[:, :n], p[:, :n], R=[p], W=[st], eng="act")
                kb.store(g.gT[hs, s0:s0 + n], st[:, :n], R=[st])
                kb.copy(vb[:, :n], v, R=[psx], W=[vb], eng="act")
                pt_ = ptr.next()
                for tt_ in range(nt):
                    kb.tr(pt_[:, tt_ * 128:(tt_ + 1) * 128], vb[:, tt_ * 128:(tt_ + 1) * 128], g.ident_bf[:], R=[vb, g.ident_bf], W=[pt_])
                tm = tms.next()
                kb.copy(tm[:, :nt * 128], pt_[:, :nt * 128], R=[pt_], W=[tm], eng="act")
                for tt_ in range(nt):
                    kb.store(g.Vtm[s0 + tt_ * 128:s0 + (tt_ + 1) * 128, hs], tm[:, tt_ * 128:(tt_ + 1) * 128], R=[tm])
                for d in range(2):
                    sg, a_, tmp, kmod, b_, ci, cr, ce, e1, e2, e3 = [R_.next() for R_ in d_rots]
                    p = pmm.next()
                    kb.mm(p[:, :n], lora[0:64, d, hs], lwin[0:64, :n], R=[lora, lwin], W=[p])
                    kb.act(sg[:, :n], p[:, :n], AF.Sigmoid, bias=rwc[:, 34 + 4 * d + hp:35 + 4 * d + hp], scale=1.0, R=[p, rwc], W=[sg])
                    p = pmm.next()
                    kb.mm(p[:, :n], lora[64:128, d, hs], lwin[64:128, :n], R=[lora, lwin], W=[p])
                    kb.act(a_[:, :n], p[:, :n], AF.Sigmoid, bias=rwc[:, 42 + 4 * d + hp:43 + 4 * d + hp], scale=1.0, R=[p, rwc], W=[a_])
                    kb.ts(tmp[:, :n], a_[:, :n], rwc[:, 18 + hp:19 + hp], omka[:, hp:hp + 1], ALU.mult, ALU.add, R=[a_, rwc, omka], W=[tmp])
                    kb.tt(kmod[:, :n], tmp[:, :n], k, ALU.mult, R=[tmp, psx], W=[kmod])
                    kb.tt(b_[:, :n], kk[:, :n], a_[:, :n], ALU.mult, R=[kk, a_], W=[b_])
                    if d == 0:
                        kb.copy(kbs[:, :n], kmod[:, :n], R=[kmod], W=[kbs], eng="act")
                    else:
                        kb.tt(kbs[:, :n], kbs[:, :n], kmod[:, :n], ALU.add, R=[kbs, kmod], W=[kbs])
                    kb.op("dve", lambda e: e.tensor_tensor_scan(out=ci[:, :n], data0=cmask[:, :n], data1=sg[:, :n], initial=0.0,
                                                                op0=ALU.mult, op1=ALU.add), R=[cmask, sg], W=[ci])
                    cc = ci
                    if d == 1:
                        kb.tt(tmp[:, :n], sg[:, :n], ci[:, :n], ALU.subtract, R=[sg, ci], W=[tmp])
                        civ = ci[:, :n].rearrange("p (c t) -> p c t", t=64)
                        kb.tt(cr[:, :n].rearrange("p (c t) -> p c t", t=64), tmp[:, :n].rearrange("p (c t) -> p c t", t=64),
                              civ[:, :, 63:64].to_broadcast([128, nch, 64]), ALU.add, R=[tmp, ci], W=[cr])
                        cc = cr
                    kb.tt(ce[:, :n], cc[:, :n], sg[:, :n], ALU.subtract, R=[cc, sg], W=[ce])
                    kb.act(e1[:, :n], cc[:, :n], AF.Exp, scale=KAPPA, R=[cc], W=[e1])
                    kb.act(e2[:, :n], cc[:, :n], AF.Exp, scale=-KAPPA, R=[cc], W=[e2])
                    kb.act(e3[:, :n], ce[:, :n], AF.Exp, scale=-KAPPA, R=[ce], W=[e3])
                    fa, fr, fb, fk = [fm16[i].next() for i in range(4)]
                    kb.stt(fa[:, :n], kk[:, :n], -1.0, e3[:, :n], ALU.mult, ALU.mult, R=[kk, e3], W=[fa])
                    kb.tt(fr[:, :n], r, e2[:, :n], ALU.mult, R=[psx, e2], W=[fr])
                    kb.tt(fb[:, :n], b_[:, :n], e1[:, :n], ALU.mult, R=[b_, e1], W=[fb])
                    kb.tt(fk[:, :n], kmod[:, :n], e1[:, :n], ALU.mult, R=[kmod, e1], W=[fk])
                    gs = gst.next()
                    e2v = e2[:, :n].rearrange("p (c t) -> p c t", t=64)
                    col = 63 if d == 0 else 0
                    kb.copy(gs[:, :nch], e2v[:, :, col], R=[e2], W=[gs], eng="pool")
                    kb.store(g.gam[d][hs, c0:c0 + nch], gs[:, :nch], R=[gs])
                    for kind, f in enumerate((fa, fr, fb, fk)):
                        kb.store(g.FM[d][kind][hs, s0:s0 + n], f[:, :n], R=[f])
                    for kind, f in ((0, fb), (1, fk)):
                        pt_ = ptr.next()
                        for tt_ in range(nt):
                            kb.tr(pt_[:, tt_ * 128:(tt_ + 1) * 128], f[:, tt_ * 128:(tt_ + 1) * 128], g.ident_bf[:], R=[f, g.ident_bf], W=[pt_])
                        tm = tms.next()
                        kb.copy(tm[:, :nt * 128], pt_[:, :nt * 128], R=[pt_], W=[tm], eng=("act" if kind else "dve"))
                        for tt_ in range(nt):
                            kb.store(g.TM[d][s0 + tt_ * 128:s0 + (tt_ + 1) * 128, kind, hs], tm[:, tt_ * 128:(tt_ + 1) * 128], R=[tm])
                kb.tt(rk[:, :n], r, kbs[:, :n], ALU.mult, R=[psx, kbs], W=[rk])
                kb.ts(rk[:, :n], rk[:, :n], hrk[:, hp:hp + 1], None, ALU.mult, R=[rk, hrk], W=[rk])
                p = pmm.next()
                kb.mm(p[:, :n], blk[:], rk[:, :n], R=[blk, rk], W=[p])
                kb.tt(bon[:, :n], p[:, :n], v, ALU.mult, R=[p, psx], W=[bon])
                kb.store(g.bonT[hs, s0:s0 + n], bon[:, :n], R=[bon])


def phase_C0(kb, g):
    NL = 5
    with phase(kb):
        mk = sbt(kb, "mk", [64, 2, 3, 64])
        kb.load(mk[:], g.masks_h[:, :, :, :], W=[mk])
        poolA = PPool([pst(kb, f"cpa{i}") for i in range(5)])
        poolB = PPool([pst(kb, f"cpb{i}") for i in range(3)])
        st = []
        for d in range(2):
            s = G()
            s.gam = sbt(kb, f"gam{d}", [64, 8, 68])
            kb.load(s.gam[:], g.gam[d].rearrange("(h k) c -> k h c", k=64), W=[s.gam])
            s.fm = Rot([sbt(kb, f"fm{d}_{i}", [64, 4, 8, 64], BF16) for i in range(2)])
            s.tm = Rot([sbt(kb, f"tm{d}_{i}", [64, 2, 512], BF16) for i in range(2)])
            s.vt = Rot([sbt(kb, f"vt{d}_{i}", [64, 512], BF16) for i in range(2)])
            s.Sf = sbt(kb, f"Sf{d}", [64, 8, 64])
            s.Sb = Rot([sbt(kb, f"Sb{d}_{i}", [64, 8, 64], BF16) for i in range(2)])
            s.Nm = Rot([sbt(kb, f"Nm{d}_{i}", [64, 8, 128], BF16) for i in range(2)])
            s.Nkm = Rot([sbt(kb, f"Nkm{d}_{i}", [64, 8, 128], BF16) for i in range(2)])
            s.Inv = Rot([sbt(kb, f"Inv{d}_{i}", [64, 8, 64], BF16) for i in range(2)])
            s.NT = sbt(kb, f"NT{d}", [64, 8, 64], BF16)
            s.X = sbt(kb, f"X{d}", [64, 8, 64])
            s.Xb = Rot([sbt(kb, f"Xb{d}_{i}", [64, 8, 64], BF16) for i in range(2)])
            s.P = Rot([sbt(kb, f"P{d}_{i}", [64, 8, 64], BF16) for i in range(2)])
            s.PT = Rot([sbt(kb, f"PT{d}_{i}", [64, 8, 64], BF16) for i in range(2)])
            s.W1 = sbt(kb, f"W1{d}", [64, 8, 64], BF16)
            s.UT = sbt(kb, f"UT{d}", [64, 8, 64], BF16)
            s.Yst = Rot([sbt(kb, f"Yst{d}_{i}", [64, 512]) for i in range(2)])
            kb.memset(s.Sf[:], 0.0, W=[s.Sf])
            s.sb = s.Sb.next()
            kb.memset(s.sb[:], 0.0, W=[s.sb])
            s.mAR = mk[:, d, 0:2, :].rearrange("p a t -> p (a t)")[:, None, :].to_broadcast([64, 4, 128])
            s.mT = mk[:, d, 2, :][:, None, :].to_broadcast([64, 8, 64])
            st.append(s)
        Ibc = g.ident[0:64, 0:64][:, None, :].to_broadcast([64, 8, 64])
        order = [list(range(68)), [3, 2, 1, 0] + list(range(67, 3, -1))]

        def v3(p):
            return p[0:64, :].rearrange("p (h t) -> p h t", t=64)

        def partA(d, c, rec):
            s = st[d]
            t0 = c * 64
            yield
            fm, tm, vt = s.fm.next(), s.tm.next(), s.vt.next()
            Nm, Nkm = s.Nm.next(), s.Nkm.next()
            for kind in range(4):
                kb.load(fm[:, kind, :, :], g.FM[d][kind].rearrange("(h k) t -> k h t", k=64)[:, :, t0:t0 + 64], W=[fm])
            kb.load(tm[:], g.TM[d][t0:t0 + 64, :, :], W=[tm])
            kb.load(vt[:], g.Vtm[t0:t0 + 64, :], W=[vt])
            for (lk, dst) in ((2, Nm), (3, Nkm)):
                for hh in range(2):
                    p = poolA.get()
                    for h4 in range(4):
                        h = hh * 4 + h4
                        kb.mm(p[0:64, h4 * 128:(h4 + 1) * 128], fm[:, lk, h, :], fm[:, 0:2, h, :], R=[fm], W=[p])
                    kb.tt(dst[:, hh * 4:(hh + 1) * 4, :], p[0:64, :].rearrange("p (h t) -> p h t", t=128), s.mAR, ALU.mult, R=[p, mk], W=[dst])
                    poolA.put(p)
                    yield
            p = poolA.get()
            for h in range(8):
                kb.mm(p[0:64, h * 64:(h + 1) * 64], fm[:, 0, h, :], fm[:, 2, h, :], R=[fm], W=[p])
            kb.tt(s.NT[:], v3(p), s.mT, ALU.mult, R=[p, mk], W=[s.NT])
            poolA.put(p)
            yield
            kb.tt(s.X[:], Nm[:, :, 0:64], Ibc, ALU.add, R=[Nm, g.ident], W=[s.X])
            xb = s.Xb.next()
            kb.copy(xb[:], s.X[:], R=[s.X], W=[xb], eng="act")
            P_ap = lambda h: Nm[:, h, 0:64]
            PT_ap = lambda h: s.NT[:, h, :]
            Pt, PTt = Nm, s.NT
            for lv in range(NL):
                last = lv == NL - 1
                p1 = None
                if not last:
                    p1 = poolA.get()
                    for h in range(8):
                        kb.mm(p1[0:64, h * 64:(h + 1) * 64], PT_ap(h), P_ap(h), R=[Pt, PTt], W=[p1])
                p2 = poolA.get()
                for h in range(8):
                    kb.mm(p2[0:64, h * 64:(h + 1) * 64], P_ap(h), PT_ap(h), R=[Pt, PTt], W=[p2])
                yield
                nPT = s.PT.next()
                kb.copy(nPT[:], v3(p2), R=[p2], W=[nPT], eng="act")
                poolA.put(p2)
                if not last:
                    nP = s.P.next()
                    kb.copy(nP[:], v3(p1), R=[p1], W=[nP], eng="act")
                    poolA.put(p1)
                    Pt = nP
                    P_ap = (lambda t_: (lambda h: t_[:, h, :]))(nP)
                PTt = nPT
                PT_ap = (lambda t_: (lambda h: t_[:, h, :]))(nPT)
                yield
                p3 = poolA.get()
                for h in range(8):
                    kb.mm(p3[0:64, h * 64:(h + 1) * 64], PT_ap(h), xb[:, h, :], R=[PTt, xb], W=[p3])
                yield
                kb.tt(s.X[:], s.X[:], v3(p3), ALU.add, R=[s.X, p3], W=[s.X])
                poolA.put(p3)
                xb = s.Inv.next() if last else s.Xb.next()
                kb.copy(xb[:], s.X[:], R=[s.X], W=[xb], eng="act")
                yield
            rec.update(fm=fm, tm=tm, vt=vt, Nm=Nm, Nkm=Nkm, inv=xb, c=c)

        def partB(d, rec):
            s = st[d]
            fm, tm, vt, Nm, Nkm, inv, c = (rec[k] for k in ("fm", "tm", "vt", "Nm", "Nkm", "inv", "c"))
            t0 = c * 64
            sb = s.sb
            yield
            pw = poolB.get()
            for h in range(8):
                hs = slice(h * 64, (h + 1) * 64)
                kb.mm(pw[0:64, hs], Nkm[:, h, 0:64], vt[:, hs], start=True, stop=False, R=[Nkm, vt], W=[pw])
                kb.mm(pw[0:64, hs], fm[:, 0, h, :], sb[:, h, :], start=False, stop=True, R=[fm, sb], W=[pw])
            yield
            kb.copy(s.W1[:], v3(pw), R=[pw], W=[s.W1], eng="act")
            poolB.put(pw)
            pu = poolB.get()
            for h in range(8):
                kb.mm(pu[0:64, h * 64:(h + 1) * 64], inv[:, h, :], s.W1[:, h, :], R=[inv, s.W1], W=[pu])
            yield
            kb.copy(s.UT[:], v3(pu), R=[pu], W=[s.UT], eng="act")
            poolB.put(pu)
            pn = poolB.get()
            for h in range(8):
                hs = slice(h * 64, (h + 1) * 64)
                kb.mm(pn[0:64, hs], tm[:, 0, hs], s.UT[:, h, :], start=True, stop=False, R=[tm, s.UT], W=[pn])
                kb.mm(pn[0:64, hs], tm[:, 1, hs], vt[:, hs], start=False, stop=True, R=[tm, vt], W=[pn])
            yield
            kb.tt(s.Sf[:], s.Sf[:], v3(pn), ALU.add, R=[s.Sf, pn], W=[s.Sf])
            poolB.put(pn)
            kb.tt(s.Sf[:], s.Sf[:], s.gam[:, :, c:c + 1].to_broadcast([64, 8, 64]), ALU.mult, R=[s.Sf, s.gam], W=[s.Sf])
            nsb = s.Sb.next()
            kb.copy(nsb[:], s.Sf[:], R=[s.Sf], W=[nsb], eng="act")
            s.sb = nsb
            yield
            py = poolB.get()
            for h in range(8):
                hs = slice(h * 64, (h + 1) * 64)
                kb.mm(py[0:64, hs], fm[:, 1, h, :], sb[:, h, :], start=True, stop=False, R=[fm, sb], W=[py])
                kb.mm(py[0:64, hs], Nm[:, h, 64:128], s.UT[:, h, :], start=False, stop=False, R=[Nm, s.UT], W=[py])
                kb.mm(py[0:64, hs], Nkm[:, h, 64:128], vt[:, hs], start=False, stop=True, R=[Nkm, vt], W=[py])
            ys = s.Yst.next()
            kb.copy(ys[:], py[0:64, :], R=[py], W=[ys])
            poolB.put(py)
            kb.store(g.Y[d][t0:t0 + 64, :], ys[:], R=[ys])

        recs = [{}, {}]
        run_interleaved([partA(0, order[0][0], recs[0]), partA(1, order[1][0], recs[1])])
        for i in range(68):
            cur = recs
            gens = [partB(0, cur[0]), partB(1, cur[1])]
            recs = [{}, {}]
            if i + 1 < 68:
                gens += [partA(0, order[0][i + 1], recs[0]), partA(1, order[1][i + 1], recs[1])]
            run_interleaved(gens)

def phase_C2(kb, g):
    with phase(kb):
        rwc = sbt(kb, "rwc2", [128, 58])
        kb.load(rwc[:], g.rw_cols[:, :], W=[rwc])
        yf = Rot([sbt(kb, f"yf{i}", [128, 512]) for i in range(2)])
        yb = Rot([sbt(kb, f"yb{i}", [128, 512]) for i in range(2)])
        y = sbt(kb, "y", [128, 512])
        sq = sbt(kb, "ysq", [128, 512])
        sm = sbt(kb, "ysm", [128, 8])
        vr = sbt(kb, "yvr", [128, 8])
        rt = sbt(kb, "yrt", [128, 8])
        rs = sbt(kb, "yrs", [128, 8])
        yn = Rot([sbt(kb, f"yn{i}", [128, 512]) for i in range(2)])
        pb = [pst(kb, f"c2p{i}") for i in range(4)]
        bon = Rot([sbt(kb, f"bon{i}", [128, 4, 512]) for i in range(2)])
        gt = Rot([sbt(kb, f"gt{i}", [128, 4, 512]) for i in range(2)])
        a1 = Rot([sbt(kb, f"a1_{i}", [128, 512]) for i in range(2)])
        mo = Rot([sbt(kb, f"mo{i}", [128, 512], BF16) for i in range(3)])
        for bi, (s0, n) in enumerate(TBS):
            nt = n // 128
            bo, gg = bon.next(), gt.next()
            kb.load(bo[:, :, :n], g.bonT.rearrange("(c p) t -> p c t", p=128)[:, :, s0:s0 + n], W=[bo])
            kb.load(gg[:, :, :n], g.gT.rearrange("(c p) t -> p c t", p=128)[:, :, s0:s0 + n], W=[gg])
            for tt_ in range(nt):
                t0 = s0 + tt_ * 128
                a, b = yf.next(), yb.next()
                kb.load(a[:], g.Y[0][t0:t0 + 128, :], W=[a])
                kb.load(b[:], g.Y[1][t0:t0 + 128, :], W=[b])
                kb.tt(y[:], a[:], b[:], ALU.add, R=[a, b], W=[y])
                y3 = y[:].rearrange("p (h v) -> p h v", v=64)
                kb.op("dve", lambda e: e.tensor_reduce(out=sm[:], in_=y3, axis=AX.X, op=ALU.add), R=[y], W=[sm])
                kb.ts(sm[:], sm[:], -1.0 / 64.0, None, ALU.mult, R=[sm], W=[sm])
                kb.tt(y3, y3, sm[:, :, None].to_broadcast([128, 8, 64]), ALU.add, R=[y, sm], W=[y])
                kb.tt(sq[:], y[:], y[:], ALU.mult, R=[y], W=[sq])
                kb.op("dve", lambda e: e.tensor_reduce(out=vr[:], in_=sq[:].rearrange("p (h v) -> p h v", v=64), axis=AX.X, op=ALU.add), R=[sq], W=[vr])
                kb.act(rt[:], vr[:], AF.Sqrt, bias=g.eps_t[:, 2:3], scale=1.0 / 64.0, R=[vr, g.eps_t], W=[rt])
                kb.op("dve", lambda e: e.reciprocal(out=rs[:], in_=rt[:]), R=[rt], W=[rs])
                yo = yn.next()
                kb.tt(yo[:].rearrange("p (h v) -> p h v", v=64), y3, rs[:, :, None].to_broadcast([128, 8, 64]), ALU.mult, R=[y, rs], W=[yo])
                for ct in range(4):
                    kb.tr(pb[ct][:, tt_ * 128:(tt_ + 1) * 128], yo[:, ct * 128:(ct + 1) * 128], g.ident[:], R=[yo, g.ident], W=[pb[ct]])
            for ct in range(4):
                t1 = a1.next()
                kb.act(t1[:, :n], pb[ct][:, :n], AF.Identity, bias=rwc[:, 30 + ct:31 + ct], scale=rwc[:, 26 + ct:27 + ct], R=[pb[ct], rwc], W=[t1])
                kb.tt(t1[:, :n], t1[:, :n], bo[:, ct, :n], ALU.add, R=[t1, bo], W=[t1])
                o = mo.next()
                kb.tt(o[:, :n], t1[:, :n], gg[:, ct, :n], ALU.mult, R=[t1, gg], W=[o])
                kb.store(g.mixT[ct * 128:(ct + 1) * 128, s0:s0 + n], o[:, :n], R=[o])


def phase_D0(kb, g):
    LAM_INIT = 0.2
    with phase(kb):
        dc = sbt(kb, "dc", [64, 4])
        pr = sbt(kb, "dpr", [64, 2])
        onesf = sbt(kb, "onesf", [64, 128])
        lam = sbt(kb, "lam", [128, 2])
        nlam = sbt(kb, "nlam", [128, 1])
        slg = sbt(kb, "slg", [128, 1])
        kb.load(dc[:], g.diff_cols[:, :], W=[dc])
        kb.load(slg[:], g.subln_g[:, :], W=[slg])
        kb.memset(onesf[:], 1.0, W=[onesf])
        kb.tt(pr[:, 0:1], dc[:, 0:1], dc[:, 1:2], ALU.mult, R=[dc], W=[pr])
        kb.tt(pr[:, 1:2], dc[:, 2:3], dc[:, 3:4], ALU.mult, R=[dc], W=[pr])
        ps = Rot([pst(kb, f"dps{i}") for i in range(3)])
        pfin = pst(kb, "dpfin")
        acc = [[pst(kb, f"dacc{m}{q}") for q in range(2)] for m in range(2)]
        pl = ps.next()
        kb.mm(pl[:, 0:2], onesf[:], pr[:], R=[onesf, pr], W=[pl])
        kb.act(lam[:], pl[:, 0:2], AF.Exp, R=[pl], W=[lam])
        kb.tt(nlam[:], lam[:, 1:2], lam[:, 0:1], ALU.subtract, R=[lam], W=[nlam])
        kb.ts(nlam[:], nlam[:], -LAM_INIT, None, ALU.add, R=[nlam], W=[nlam])
        KT = sbt(kb, "KT", [128, T], BF16)
        QR = [sbt(kb, f"QR{m}", [128, T], BF16) for m in range(2)]
        QP = [sbt(kb, f"QP{m}", [128, T], BF16) for m in range(2)]
        for m in range(2):
            zs = slice(64, 128) if m == 0 else slice(0, 64)
            kb.memset(QR[m][zs, :], 0.0, W=[QR[m]])
            kb.memset(QP[m][zs, :], 0.0, W=[QP[m]])
        VA = sbt(kb, "VA", [128, 34, 130], BF16)
        pT = Rot([sbt(kb, f"pT{i}", [128, 512], BF16) for i in range(4)])
        rz = sbt(kb, "rz", [128, 4])
        o0 = sbt(kb, "o0", [128, 128])
        o = sbt(kb, "o", [128, 128])
        osq = sbt(kb, "osq", [128, 128])
        ssq = sbt(kb, "ssq", [128, 1])
        rt = sbt(kb, "drt", [128, 1])
        rs = sbt(kb, "drs", [128, 1])
        on = Rot([sbt(kb, f"on{i}", [128, 128]) for i in range(2)])
        so = Rot([sbt(kb, f"dso{i}", [128, 256], BF16) for i in range(2)])
        accS = Rot([sbt(kb, f"accS{i}", [128, 4, 129]) for i in range(2)])

        def finalize(aS, h, q0):
            s_ = so.next()
            for qt in range(2):
                a0_, a1_ = aS[:, qt, :], aS[:, 2 + qt, :]
                kb.op("dve", lambda e_: e_.reciprocal(out=rz[:, 0:1], in_=a0_[:, 128:129]), R=[aS], W=[rz])
                kb.op("dve", lambda e_: e_.reciprocal(out=rz[:, 1:2], in_=a1_[:, 128:129]), R=[aS], W=[rz])
                kb.tt(rz[:, 2:3], rz[:, 1:2], nlam[:], ALU.mult, R=[rz, nlam], W=[rz])
                kb.ts(o0[:], a0_[:, 0:128], rz[:, 0:1], None, ALU.mult, R=[aS, rz], W=[o0])
                kb.stt(o[:], a1_[:, 0:128], rz[:, 2:3], o0[:], ALU.mult, ALU.add, R=[aS, rz, o0], W=[o])
                yield
                kb.act(osq[:], o[:], AF.Square, R=[o], W=[osq, ssq], accum_out=ssq[:])
                yield
                kb.act(rt[:], ssq[:], AF.Sqrt, bias=g.eps_t[:, 0:1], scale=1.0 / 128.0, R=[ssq, g.eps_t], W=[rt])
                yield
                kb.op("dve", lambda e_: e_.reciprocal(out=rs[:], in_=rt[:]), R=[rt], W=[rs])
                on_ = on.next()
                kb.ts(on_[:], o[:], rs[:, 0:1], 1.0 - LAM_INIT, ALU.mult, ALU.mult, R=[o, rs], W=[on_])
                yield
                kb.tr(pfin[:, 0:128], on_[:], g.ident[:], R=[on_, g.ident], W=[pfin])
                yield
                kb.act(s_[:, qt * 128:(qt + 1) * 128], pfin[:, 0:128], AF.Copy, scale=slg[:, 0:1], R=[pfin, slg], W=[s_])
                yield
            kb.store(g.mixT[512 + h * 128:512 + (h + 1) * 128, q0:q0 + 256], s_[:], R=[s_])

        fin = iter(())
        chunks = [(0, True)] + [(256 + 256 * i, False) for i in range(16)]
        import os
        DS = int(os.environ.get("D0_STAGE", "9"))
        if DS < 9:
            chunks = chunks[:2]
        for h in range(4 if DS == 9 else 1):
            hs = slice(h * 128, (h + 1) * 128)
            kb.load(KT[:], g.KrotT[hs, :], W=[KT])
            for m in range(2):
                ms = slice(m * 64, (m + 1) * 64)
                kb.load(QR[m][ms, :], g.QrotT[h * 128 + m * 64:h * 128 + (m + 1) * 64, :], W=[QR[m]])
                kb.load(QP[m][ms, :], g.QplT[h * 128 + m * 64:h * 128 + (m + 1) * 64, :], W=[QP[m]])
            kb.load(VA[:, :, 0:128], g.Vd.rearrange("(kt p) c -> p kt c", p=128)[:, :, hs], W=[VA])
            kb.memset(VA[:, :, 128:129], 1.0, W=[VA])
            for (q0, isctx) in chunks:
                if DS < 2:
                    break
                kts = [0, 1] if isctx else list(range(34))
                def score(kt):
                    Q = QP if (not isctx and kt < 2) else QR
                    p = ps.next()
                    ks = slice(kt * 128, (kt + 1) * 128)
                    kb.mm(p[:, 0:256], KT[:, ks], Q[0][:, q0:q0 + 256], R=[KT, Q[0]], W=[p])
                    kb.mm(p[:, 256:512], KT[:, ks], Q[1][:, q0:q0 + 256], R=[KT, Q[1]], W=[p])
                    return p
                LA = 2
                pq = [score(kts[k]) for k in range(min(LA, len(kts)))]
                for i, kt in enumerate(kts):
                    p = pq.pop(0)
                    if i + LA < len(kts):
                        pq.append(score(kts[i + LA]))
                    e = pT.next()
                    kb.act(e[:], p[:], AF.Exp, scale=0.125, R=[p], W=[e])
                    next(fin, None)
                    if DS < 3:
                        continue
                    for m in range(2):
                        for qt in range(2):
                            kb.mm(acc[m][qt][:, 0:129], e[:, m * 256 + qt * 128:m * 256 + (qt + 1) * 128], VA[:, kt, 0:129],
                                  start=(i == 0), stop=(i == len(kts) - 1), R=[e, VA], W=[acc[m][qt]])
                if DS < 4:
                    continue
                for _ in fin:
                    pass
                aS = accS.next()
                for m in range(2):
                    for qt in range(2):
                        kb.copy(aS[:, m * 2 + qt, :], acc[m][qt][:, 0:129], R=[acc[m][qt]], W=[aS])
                fin = finalize(aS, h, q0)
        for _ in fin:
            pass


def phase_E(kb, g, li, w_ap, kcn, mixT, blocks):
    with phase(kb):
        Wo = sbt(kb, "Wo", [128, kcn, 1024], BF16)
        stg = Rot([sbt(kb, f"estg{i}", [128, kcn * 128]) for i in range(2)])
        load_weight_bf16(kb, Wo, w_ap, 1024, stg, piece=128)
        xb = Rot([sbt(kb, f"exb{i}", [128, 8, 512]) for i in range(2)])
        mb = Rot([sbt(kb, f"emb{i}", [128, kcn, 512], BF16) for i in range(2)])
        pmm = Rot([pst(kb, f"ep{i}") for i in range(4)])
        xsrc = g.xT.rearrange("(kc p) t -> p kc t", p=128)
        msrc = mixT.rearrange("(kc p) t -> p kc t", p=128)
        for (s0, n, j) in blocks:
            x, m = xb.next(), mb.next()
            kb.load(x[:, :, :n], xsrc[:, :, s0:s0 + n], W=[x])
            kb.load(m[:, :, :n], msrc[:, :, s0:s0 + n], W=[m])
            for ct in range(8):
                p = pmm.next()
                for kc in range(kcn):
                    kb.mm(p[:, :n], Wo[:, kc, ct * 128:(ct + 1) * 128], m[:, kc, :n], start=(kc == 0), stop=(kc == kcn - 1), R=[Wo, m], W=[p])
                kb.stt(x[:, ct, :n], p[:, :n], g.mod[li][:, 16 + ct, j:j + 1], x[:, ct, :n], ALU.mult, ALU.add, R=[p, g.mod[li], x], W=[x])
            kb.store(xsrc[:, :, s0:s0 + n], x[:, :, :n], R=[x])


def phase_F(kb, g, li, blocks):
    with phase(kb):
        W1 = sbt(kb, "W1", [128, 8, 4096], BF16)
        W2 = sbt(kb, "W2", [128, 32, 1024], BF16)
        stg = Rot([sbt(kb, f"fstg{i}", [128, 1024]) for i in range(1)])
        load_weight_bf16(kb, W1, g.mlp_w1[li], 4096, stg, piece=128)
        load_weight_bf16(kb, W2, g.mlp_w2[li], 1024, stg, piece=32)
        nb = G()
        nb.sq = sbt(kb, "fsq", [128, 8, 256], BF16)
        nb.tmp = sbt(kb, "ftmp", [128, 8, 256])
        nb.rt = sbt(kb, "frt", [128, 256])
        nb.rstd = sbt(kb, "frstd", [128, 256])
        nb.ss = pst(kb, "fss")
        xb = Rot([sbt(kb, f"fxb{i}", [128, 8, 256]) for i in range(2)])
        hTs = Rot([sbt(kb, f"fhT{i}", [128, 8, 256], BF16) for i in range(2)])
        hid = sbt(kb, "fhid", [128, 32, 256], BF16)
        rl = Rot([sbt(kb, f"frl{i}", [128, 256]) for i in range(3)])
        pmm = Rot([pst(kb, f"fp{i}") for i in range(6)])
        xsrc = g.xT.rearrange("(kc p) t -> p kc t", p=128)

        def prep(blk):
            s0, n, j = blk
            x = xb.next()
            kb.load(x[:, :, :n], xsrc[:, :, s0:s0 + n], W=[x])
            hT = hTs.next()
            norm_mod(kb, g, nb, x, n, g.sc2[li], g.mod[li], j, hT, sh_off=24)
            return x, hT

        nxt = prep(blocks[0])
        for bi, (s0, n, j) in enumerate(blocks):
            x, hT = nxt
            for hc in range(32):
                p = pmm.next()
                for kc in range(8):
                    kb.mm(p[:, :n], W1[:, kc, hc * 128:(hc + 1) * 128], hT[:, kc, :n], start=(kc == 0), stop=(kc == 7), R=[W1, hT], W=[p])
                r = rl.next()
                kb.act(r[:, :n], p[:, :n], AF.Relu, R=[p], W=[r])
                kb.tt(hid[:, hc, :n], r[:, :n], p[:, :n], ALU.mult, R=[r, p], W=[hid])
            if bi + 1 < len(blocks):
                nxt = prep(blocks[bi + 1])
            for ct in range(8):
                p = pmm.next()
                for hc in range(32):
                    kb.mm(p[:, :n], W2[:, hc, ct * 128:(ct + 1) * 128], hid[:, hc, :n], start=(hc == 0), stop=(hc == 31), R=[W2, hid], W=[p])
                kb.stt(x[:, ct, :n], p[:, :n], g.mod[li][:, 40 + ct, j:j + 1], x[:, ct, :n], ALU.mult, ALU.add, R=[p, g.mod[li], x], W=[x])
            kb.store(xsrc[:, :, s0:s0 + n], x[:, :, :n], R=[x])


BLK512 = [(s0, n, 1 if s0 == 0 else 0) for (s0, n) in TBS]
BLK256 = [(s0, 256, 1 if s0 == 0 else 0) for s0 in range(0, T, 256)]

def phase_A1(kb, g):
    with phase(kb):
        W = sbt(kb, "wA1", [128, 8, 6176], BF16)
        stg = Rot([sbt(kb, f"w1stg{i}", [128, 2048]) for i in range(2)])
        load_weight_bf16(kb, W, g.ssd_w_in, 6176, stg)
        nb = NormBufs(kb, "A1")
        xb = Rot([sbt(kb, f"a1x{i}", [128, 8, 512]) for i in range(2)])
        hTs = Rot([sbt(kb, f"a1h{i}", [128, 8, 512], BF16) for i in range(2)])
        pmm = Rot([pst(kb, f"a1p{i}") for i in range(6)])
        st32 = Rot([sbt(kb, f"a1s{i}", [128, 512]) for i in range(4)])
        st16 = Rot([sbt(kb, f"a1z{i}", [128, 512], BF16) for i in range(4)])
        xsrc = g.xT.rearrange("(kc p) t -> p kc t", p=128)
        for bi, (s0, n) in enumerate(TBS):
            j = 1 if bi == 0 else 0
            x = xb.next()
            kb.load(x[:, :, :n], xsrc[:, :, s0:s0 + n], W=[x])
            hT = hTs.next()
            norm_mod(kb, g, nb, x, n, g.sc1[1], g.mod[1], j, hT, sh_off=0)
            for ct in range(32):
                p = pmm.next()
                c0 = 2048 + ct * 128
                for kc in range(8):
                    kb.mm(p[:, :n], W[:, kc, c0:c0 + 128], hT[:, kc, :n], start=(kc == 0), stop=(kc == 7), R=[W, hT], W=[p])
                st = st32.next()
                kb.copy(st[:, :n], p[:, :n], R=[p], W=[st], eng=("act" if ct % 2 else "dve"))
                kb.store(g.xbcT[ct * 128:(ct + 1) * 128, s0:s0 + n], st[:, :n], R=[st])
            for tt_ in range(n // 128):
                ts_ = slice(tt_ * 128, (tt_ + 1) * 128)
                t0 = s0 + tt_ * 128
                for zc in range(4):
                    p = pmm.next()
                    for kc in range(8):
                        kb.mm(p[:, :], hT[:, kc, ts_], W[:, kc, zc * 512:(zc + 1) * 512], start=(kc == 0), stop=(kc == 7), R=[W, hT], W=[p])
                    so = st16.next()
                    kb.copy(so[:], p[:], R=[p], W=[so], eng=("act" if zc % 2 else "dve"))
                    kb.store(g.zTM[t0:t0 + 128, zc * 512:(zc + 1) * 512], so[:], R=[so])
                p = pmm.next()
                for kc in range(8):
                    kb.mm(p[:, 0:32], hT[:, kc, ts_], W[:, kc, 6144:6176], start=(kc == 0), stop=(kc == 7), R=[W, hT], W=[p])
                st = st32.next()
                kb.copy(st[:, 0:32], p[:, 0:32], R=[p], W=[st])
                kb.store(g.dtTM[t0:t0 + 128, :], st[:, 0:32], R=[st])


def phase_B1(kb, g):
    with phase(kb):
        cw = sbt(kb, "cw", [128, 32, 5])
        cb = sbt(kb, "cb", [128, 32])
        kb.load(cw[:], g.conv_wT[:, :, :], W=[cw])
        kb.load(cb[:], g.conv_bT[:, :], W=[cb])
        xin = Rot([sbt(kb, f"b1x{i}", [128, 8, 516]) for i in range(2)])
        acc = Rot([sbt(kb, f"b1a{i}", [128, 512]) for i in range(2)])
        u32 = Rot([sbt(kb, f"b1u{i}", [128, 512]) for i in range(2)])
        u16 = Rot([sbt(kb, f"b1v{i}", [128, 512], BF16) for i in range(3)])
        p32 = Rot([pst(kb, f"b1p{i}") for i in range(3)])
        p16 = Rot([pst(kb, f"b1q{i}", dt=BF16) for i in range(2)])
        t32 = Rot([sbt(kb, f"b1t{i}", [128, 512]) for i in range(3)])
        t16 = Rot([sbt(kb, f"b1s{i}", [128, 512], BF16) for i in range(3)])
        src = g.xbcT.rearrange("(ti p) t -> p ti t", p=128)
        for bi, (s0, n) in enumerate(TBS):
            nt = n // 128
            seq0, seq1 = (0, NCTX) if s0 < NCTX else (NCTX, T)
            lo = max(seq0, s0 - 2)
            hi = min(seq1, s0 + n + 2)
            for grp in range(4):
                xi = xin.next()
                kb.memset(xi[:, :, 0:2], 0.0, W=[xi])
                kb.memset(xi[:, :, n + 2:n + 4], 0.0, W=[xi])
                kb.load(xi[:, :, lo - s0 + 2:hi - s0 + 2], src[:, grp * 8:(grp + 1) * 8, lo:hi], W=[xi])
                for t8 in range(8):
                    ti = grp * 8 + t8
                    a = acc.next()
                    kb.ts(a[:, :n], xi[:, t8, 0:n], cw[:, ti, 0:1], cb[:, ti:ti + 1], ALU.mult, ALU.add, R=[xi, cw, cb], W=[a])
                    for k in range(1, 5):
                        kb.stt(a[:, :n], xi[:, t8, k:k + n], cw[:, ti, k:k + 1], a[:, :n], ALU.mult, ALU.add, R=[xi, cw, a], W=[a])
                    if ti < 16:
                        u = u32.next()
                        kb.act(u[:, :n], a[:, :n], AF.Silu, R=[a], W=[u])
                        p = p32.next()
                        for tt_ in range(nt):
                            kb.tr(p[:, tt_ * 128:(tt_ + 1) * 128], u[:, tt_ * 128:(tt_ + 1) * 128], g.ident[:], R=[u, g.ident], W=[p])
                        t = t32.next()
                        kb.copy(t[:, :n], p[:, :n], R=[p], W=[t], eng=("act" if ti % 2 else "dve"))
                        for tt_ in range(nt):
                            kb.store(g.xsTM[s0 + tt_ * 128:s0 + (tt_ + 1) * 128, ti * 128:(ti + 1) * 128], t[:, tt_ * 128:(tt_ + 1) * 128], R=[t])
                    else:
                        u = u16.next()
                        kb.act(u[:, :n], a[:, :n], AF.Silu, R=[a], W=[u])
                        if ti < 24:
                            gi = ti - 16
                            kb.store(g.BT[gi * 128:(gi + 1) * 128, s0:s0 + n], u[:, :n], R=[u])
                            p = p16.next()
                            for tt_ in range(nt):
                                kb.tr(p[:, tt_ * 128:(tt_ + 1) * 128], u[:, tt_ * 128:(tt_ + 1) * 128], g.ident_bf[:], R=[u, g.ident_bf], W=[p])
                            t = t16.next()
                            kb.copy(t[:, :n], p[:, :n], R=[p], W=[t], eng="act")
                            for tt_ in range(nt):
                                kb.store(g.BTM[s0 + tt_ * 128:s0 + (tt_ + 1) * 128, gi * 128:(gi + 1) * 128], t[:, tt_ * 128:(tt_ + 1) * 128], R=[t])
                        else:
                            gi = ti - 24
                            kb.store(g.CT[gi * 128:(gi + 1) * 128, s0:s0 + n], u[:, :n], R=[u])


def phase_C1(kb, g):
    with phase(kb):
        UT1 = sbt(kb, "UT1", [128, 128])
        LT1 = sbt(kb, "LT1", [128, 128])
        onesf = sbt(kb, "c1ones", [128, 128])
        prm = sbt(kb, "prm", [128, 5, 32])
        aneg = sbt(kb, "aneg", [128, 2, 32])
        kb.load(UT1[:], g.ut1_h[:, :], W=[UT1])
        kb.load(LT1[:], g.lt1_h[:, :], W=[LT1])
        SLT = [sbt(kb, "sLT", [128, 128]), sbt(kb, "sUT", [128, 128])]
        kb.tt(SLT[0][:], LT1[:], g.ident[:], ALU.subtract, R=[LT1, g.ident], W=[SLT[0]])
        kb.tt(SLT[1][:], UT1[:], g.ident[:], ALU.subtract, R=[UT1, g.ident], W=[SLT[1]])
        kb.load(prm[:], g.ssd_prm[:, :, :], W=[prm])
        kb.memset(onesf[:], 1.0, W=[onesf])
        kb.act(aneg[:], prm[:, 2:4, :], AF.Exp, R=[prm], W=[aneg])
        kb.ts(aneg[:], aneg[:], -1.0, None, ALU.mult, R=[aneg], W=[aneg])
        tri = [UT1, LT1]
        triB = [sbt(kb, "UT1b", [128, 128], BF16), sbt(kb, "LT1b", [128, 128], BF16)]
        kb.copy(triB[0][:], UT1[:], R=[UT1], W=[triB[0]])
        kb.copy(triB[1][:], LT1[:], R=[LT1], W=[triB[1]])
        pydd = [[pst(kb, f"c1y{d}{i}") for i in range(2)] for d in range(2)]
        pbig = Rot([pst(kb, f"c1b{i}") for i in range(2)])
        psm = Rot([pst(kb, f"c1s{i}") for i in range(2)])
        st = []
        for d in range(2):
            s = G()
            s.xs = Rot([sbt(kb, f"xs{d}_{i}", [128, 32, 64]) for i in range(1)])
            s.D = sbt(kb, f"Dcs{d}", [128, 32, 128], BF16)
            s.bt = Rot([sbt(kb, f"bt{d}_{i}", [128, 8, 128], BF16) for i in range(2)])
            s.ct = Rot([sbt(kb, f"ct{d}_{i}", [128, 8, 128], BF16) for i in range(2)])
            s.btm = Rot([sbt(kb, f"btm{d}_{i}", [128, 1024], BF16) for i in range(2)])
            s.dt = Rot([sbt(kb, f"dt{d}_{i}", [128, 32]) for i in range(2)])
            s.hf = sbt(kb, f"hf{d}", [128, 32, 64])
            s.hb = Rot([sbt(kb, f"hb{d}_{i}", [128, 32, 64], BF16) for i in range(2)])
            s.xdt = sbt(kb, f"xdt{d}", [128, 32, 64], BF16)
            s.xdw = sbt(kb, f"xdw{d}", [128, 32, 64], BF16)
            s.yo = sbt(kb, f"yo{d}", [128, 8, 64])
            s.y = Rot([sbt(kb, f"y{d}_{i}", [128, 32, 64]) for i in range(1)])
            s.cbm = sbt(kb, f"cbm{d}", [128, 8, 128], BF16)
            kb.memset(s.hf[:], 0.0, W=[s.hf])
            s.h = s.hb.next()
            kb.memset(s.h[:], 0.0, W=[s.h])
            st.append(s)
        sm = lambda nm, w=32: sbt(kb, nm, [128, w])
        ex, dtd, dta, cs, ncs, ecs, wts, etot, csT = [[sm(f"{nm}{d}") for d in range(2)] for nm in
                                                       ("ex", "dtd", "dta", "cs", "ncs", "ecs", "wts", "etot", "csTx")]
        csTs = [sbt(kb, f"csT{d}", [32, 128]) for d in range(2)]
        E4 = Rot([sbt(kb, f"E4{i}", [128, 512], BF16) for i in range(3)])
        G4 = Rot([sbt(kb, f"G4{i}", [128, 4, 128], BF16) for i in range(3)])
        order = [list(range(34)), [1, 0] + list(range(33, 1, -1))]

        def chunk(d, c):
            s = st[d]
            t0 = c * 128
            yield
            xs, bt, ct, btm, dt = s.xs.next(), s.bt.next(), s.ct.next(), s.btm.next(), s.dt.next()
            kb.load(xs[:].rearrange("p h q -> p (h q)"), g.xsTM[t0:t0 + 128, :], W=[xs])
            kb.load(bt[:], g.BT.rearrange("(g n) t -> n g t", n=128)[:, :, t0:t0 + 128], W=[bt])
            kb.load(ct[:], g.CT.rearrange("(g n) t -> n g t", n=128)[:, :, t0:t0 + 128], W=[ct])
            kb.load(btm[:], g.BTM[t0:t0 + 128, :], W=[btm])
            kb.load(dt[:], g.dtTM[t0:t0 + 128, :], W=[dt])
            kb.tt(ex[d][:], dt[:], prm[:, d, :], ALU.add, R=[dt, prm], W=[ex[d]])
            kb.act(ex[d][:], ex[d][:], AF.Exp, R=[ex[d]], W=[ex[d]])
            kb.act(dtd[d][:], ex[d][:], AF.Ln, bias=g.eps_t[:, 4:5], scale=1.0, R=[ex[d], g.eps_t], W=[dtd[d]])
            kb.tt(dta[d][:], dtd[d][:], aneg[:, d, :], ALU.mult, R=[dtd[d], aneg], W=[dta[d]])
            p = psm.next()
            kb.mm(p[:, 0:32], tri[d][:], dta[d][:], R=[tri[d], dta[d]], W=[p])
            kb.mm(p[:, 32:64], onesf[:], dta[d][:], R=[onesf, dta[d]], W=[p])
            kb.copy(cs[d][:], p[:, 0:32], R=[p], W=[cs[d]])
            kb.act(ecs[d][:], p[:, 0:32], AF.Exp, R=[p], W=[ecs[d]])
            kb.act(etot[d][:], p[:, 32:64], AF.Exp, R=[p], W=[etot[d]])
            kb.tt(wts[d][:], p[:, 32:64], cs[d][:], ALU.subtract, R=[p, cs[d]], W=[wts[d]])
            kb.act(wts[d][:], wts[d][:], AF.Exp, R=[wts[d]], W=[wts[d]])
            kb.tt(s.D[:], SLT[d][:, None, :].to_broadcast([128, 32, 128]), dta[d][:, :, None].to_broadcast([128, 32, 128]), ALU.mult,
                  R=[SLT[d], dta[d]], W=[s.D])
            yield
            kb.tt(s.xdt[:], xs[:], dtd[d][:, :, None].to_broadcast([128, 32, 64]), ALU.mult, R=[xs, dtd[d]], W=[s.xdt])
            kb.tt(s.xdw[:], s.xdt[:], wts[d][:, :, None].to_broadcast([128, 32, 64]), ALU.mult, R=[s.xdt, wts[d]], W=[s.xdw])
            for hf in range(2):
                p = pbig.next()
                for g4 in range(4):
                    gi = hf * 4 + g4
                    kb.mm(p[:, g4 * 128:(g4 + 1) * 128], bt[:, gi, :], ct[:, gi, :], R=[bt, ct], W=[p])
                kb.tt(s.cbm[:, hf * 4:(hf + 1) * 4, :], p[:, :].rearrange("p (g l) -> p g l", l=128),
                      tri[d][:, None, :].to_broadcast([128, 4, 128]), ALU.mult, R=[p, tri[d]], W=[s.cbm])
            h_old = s.h
            yt = s.y.next()
            for hf in range(2):
                pyd = pydd[d]
                for g4 in range(4):
                    gi = hf * 4 + g4
                    pc = psm.next()
                    for h4 in range(4):
                        kb.mm(pc[:, h4 * 128:(h4 + 1) * 128], s.D[:, gi * 4 + h4, :], triB[d][:], R=[s.D, triB[d]], W=[pc])
                    e4 = E4.next()
                    kb.act(e4[:], pc[:, :], AF.Exp, R=[pc], W=[e4])
                    g4t = G4.next()
                    kb.tt(g4t[:], e4[:].rearrange("p (h l) -> p h l", l=128), s.cbm[:, gi, None, :].to_broadcast([128, 4, 128]),
                          ALU.mult, R=[e4, s.cbm], W=[g4t])
                    for h4 in range(4):
                        h = gi * 4 + h4
                        h16 = h - hf * 16
                        pb = pyd[h16 // 8]
                        kb.mm(pb[:, (h16 % 8) * 64:(h16 % 8 + 1) * 64], g4t[:, h4, :], s.xdt[:, h, :], R=[g4t, s.xdt], W=[pb])
                    yield
                pyo = [pbig.next(), pbig.next()]
                for g4 in range(4):
                    gi = hf * 4 + g4
                    pb = pyo[g4 // 2]
                    kb.mm(pb[:, (g4 % 2) * 256:(g4 % 2 + 1) * 256], ct[:, gi, :], h_old[:, gi * 4:(gi + 1) * 4, :], R=[ct, h_old], W=[pb])
                for q in range(2):
                    hs = slice(hf * 16 + q * 8, hf * 16 + (q + 1) * 8)
                    kb.tt(s.yo[:], pyo[q][:, :].rearrange("p (h q) -> p h q", q=64), ecs[d][:, hs, None].to_broadcast([128, 8, 64]),
                          ALU.mult, R=[pyo[q], ecs[d]], W=[s.yo])
                    kb.tt(yt[:, hs, :], pyd[q][:, :].rearrange("p (h q) -> p h q", q=64), s.yo[:], ALU.add, R=[pyd[q], s.yo], W=[yt])
            kb.store(g.Yssd[d][t0:t0 + 128, :], yt[:].rearrange("p h q -> p (h q)"), R=[yt])
            yield
            nh = s.hb.next()
            for q in range(4):
                p = pbig.next()
                for g2 in range(2):
                    gi = q * 2 + g2
                    kb.mm(p[:, g2 * 256:(g2 + 1) * 256], btm[:, gi * 128:(gi + 1) * 128], s.xdw[:, gi * 4:(gi + 1) * 4, :], R=[btm, s.xdw], W=[p])
                hs = slice(q * 8, (q + 1) * 8)
                kb.tt(s.hf[:, hs, :], s.hf[:, hs, :], etot[d][:, hs, None].to_broadcast([128, 8, 64]), ALU.mult, R=[s.hf, etot[d]], W=[s.hf])
                kb.tt(s.hf[:, hs, :], s.hf[:, hs, :], p[:, :].rearrange("p (h q) -> p h q", q=64), ALU.add, R=[s.hf, p], W=[s.hf])
            kb.copy(nh[:], s.hf[:], R=[s.hf], W=[nh], eng="act")
            s.h = nh

        for i in range(34):
            run_interleaved([chunk(0, order[0][i]), chunk(1, order[1][i])])


def phase_C3(kb, g):
    with phase(kb):
        prm = sbt(kb, "c3prm", [128, 5, 32])
        ng = sbt(kb, "c3ng", [128, 16])
        kb.load(prm[:], g.ssd_prm[:, :, :], W=[prm])
        kb.load(ng[:], g.ssd_ngT[:, :], W=[ng])
        y0 = Rot([sbt(kb, f"c3a{i}", [128, 32, 64]) for i in range(2)])
        y1 = Rot([sbt(kb, f"c3b{i}", [128, 32, 64]) for i in range(2)])
        xs = Rot([sbt(kb, f"c3x{i}", [128, 32, 64]) for i in range(2)])
        z = Rot([sbt(kb, f"c3z{i}", [128, 2048], BF16) for i in range(2)])
        sz = sbt(kb, "c3sz", [128, 2048])
        sq = sbt(kb, "c3sq", [128, 2048])
        ss = sbt(kb, "c3ss", [128, 8])
        rt = sbt(kb, "c3rt", [128, 8])
        rs = sbt(kb, "c3rs", [128, 8])
        yn = sbt(kb, "c3yn", [128, 4, 2048])
        pb = Rot([pst(kb, f"c3p{i}") for i in range(4)])
        so = Rot([sbt(kb, f"c3o{i}", [128, 512], BF16) for i in range(3)])
        for bi, (s0, n) in enumerate(TBS):
            nt = n // 128
            for tt_ in range(nt):
                t0 = s0 + tt_ * 128
                a, b, x, zz = y0.next(), y1.next(), xs.next(), z.next()
                kb.load(a[:].rearrange("p h q -> p (h q)"), g.Yssd[0][t0:t0 + 128, :], W=[a])
                kb.load(b[:].rearrange("p h q -> p (h q)"), g.Yssd[1][t0:t0 + 128, :], W=[b])
                kb.load(x[:].rearrange("p h q -> p (h q)"), g.xsTM[t0:t0 + 128, :], W=[x])
                kb.load(zz[:], g.zTM[t0:t0 + 128, :], W=[zz])
                kb.tt(a[:], a[:], b[:], ALU.add, R=[a, b], W=[a])
                kb.tt(x[:], x[:], prm[:, 4, :, None].to_broadcast([128, 32, 64]), ALU.mult, R=[x, prm], W=[x])
                kb.tt(a[:], a[:], x[:], ALU.add, R=[a, x], W=[a])
                kb.act(sz[:], zz[:], AF.Silu, R=[zz], W=[sz])
                af = a[:].rearrange("p h q -> p (h q)")
                kb.tt(af, af, sz[:], ALU.mult, R=[a, sz], W=[a])
                kb.tt(sq[:], af, af, ALU.mult, R=[a], W=[sq])
                kb.op("dve", lambda e: e.tensor_reduce(out=ss[:], in_=sq[:].rearrange("p (g c) -> p g c", c=256), axis=AX.X, op=ALU.add), R=[sq], W=[ss])
                kb.act(rt[:], ss[:], AF.Sqrt, bias=g.eps_t[:, 0:1], scale=1.0 / 256.0, R=[ss, g.eps_t], W=[rt])
                kb.op("dve", lambda e: e.reciprocal(out=rs[:], in_=rt[:]), R=[rt], W=[rs])
                kb.tt(yn[:, tt_, :].rearrange("p (g c) -> p g c", c=256), af.rearrange("p (g c) -> p g c", c=256),
                      rs[:, :, None].to_broadcast([128, 8, 256]), ALU.mult, R=[a, rs], W=[yn])
            for ct in range(16):
                p = pb.next()
                for tt_ in range(nt):
                    kb.tr(p[:, tt_ * 128:(tt_ + 1) * 128], yn[:, tt_, ct * 128:(ct + 1) * 128], g.ident[:], R=[yn, g.ident], W=[p])
                o = so.next()
                kb.act(o[:, :n], p[:, :n], AF.Copy, scale=ng[:, ct:ct + 1], R=[p, ng], W=[o])
                kb.store(g.mixT1[ct * 128:(ct + 1) * 128, s0:s0 + n], o[:, :n], R=[o])


def phase_G(kb, g):
    with phase(kb):
        gf = sbt(kb, "gf", [128, 8])
        kb.load(gf[:], g.gfT[:, :], W=[gf])
        xb = Rot([sbt(kb, f"gx{i}", [128, 8, 512]) for i in range(2)])
        sq = sbt(kb, "gsq", [128, 8, 512], BF16)
        ssp = pst(kb, "gss")
        rt = sbt(kb, "grt", [128, 512])
        rs = sbt(kb, "grs", [128, 512])
        xn = sbt(kb, "gxn", [128, 8, 512])
        pt = Rot([pst(kb, f"gp{i}") for i in range(4)])
        o = Rot([sbt(kb, f"go{i}", [128, 1024]) for i in range(2)])
        xsrc = g.xT.rearrange("(kc p) t -> p kc t", p=128)
        for (s0, n) in TBS[1:]:
            x = xb.next()
            kb.load(x[:, :, :n], xsrc[:, :, s0:s0 + n], W=[x])
            kb.act(sq[:, :, :n], x[:, :, :n], AF.Square, R=[x], W=[sq])
            for kc in range(8):
                kb.mm(ssp[:, :n], g.ones_bf[:], sq[:, kc, :n], start=(kc == 0), stop=(kc == 7), R=[sq, g.ones_bf], W=[ssp])
            kb.act(rt[:, :n], ssp[:, :n], AF.Sqrt, bias=g.eps_t[:, 0:1], scale=1.0 / 1024.0, R=[ssp, g.eps_t], W=[rt])
            kb.op("dve", lambda e: e.reciprocal(out=rs[:, :n], in_=rt[:, :n]), R=[rt], W=[rs])
            kb.tt(xn[:, :, :n], x[:, :, :n], rs[:, None, :n].to_broadcast([128, 8, n]), ALU.mult, R=[x, rs], W=[xn])
            for kc in range(8):
                if kc % 2:
                    kb.act(xn[:, kc, :n], xn[:, kc, :n], AF.Identity, scale=gf[:, kc:kc + 1], R=[xn, gf], W=[xn])
                else:
                    kb.ts(xn[:, kc, :n], xn[:, kc, :n], gf[:, kc:kc + 1], None, ALU.mult, R=[xn, gf], W=[xn])
            for tt_ in range(n // 128):
                oo = o.next()
                for hf in range(2):
                    p = pt.next()
                    for j in range(4):
                        kc = hf * 4 + j
                        kb.tr(p[:, j * 128:(j + 1) * 128], xn[:, kc, tt_ * 128:(tt_ + 1) * 128], g.ident[:], R=[xn, g.ident], W=[p])
                    kb.copy(oo[:, hf * 512:(hf + 1) * 512], p[:, :], R=[p], W=[oo], eng=("act" if hf else "dve"))
                t0 = s0 - NCTX + tt_ * 128
                kb.store(g.out[t0:t0 + 128, :], oo[:], R=[oo])


BLK512L = BLK512[1:]
BLK256L = BLK256[1:]

def declare_inputs(nc, g, shapes):
    for name, (shape, dt) in shapes.items():
        setattr(g, name, nc.dram_tensor(name, list(shape), dt, kind="ExternalInput").ap())


def input_shapes():
    S = {}
    S["x"] = ([4096, 1024], F32)
    S["ctx"] = ([256, 1024], F32)
    S["cT"] = ([128, 8, 2], F32)
    S["ada_w"] = ([2, 1024, 6144], F32)
    S["ada_bT"] = ([2, 128, 48], F32)
    S["g1T"] = ([2, 128, 8], F32)
    S["g2T"] = ([2, 128, 8], F32)
    S["gfT"] = ([128, 8], F32)
    S["w_in0"] = ([1024, 4352], F32)
    S["cosT"] = ([128, T], F32)
    S["sinT"] = ([128, T], F32)
    S["ident_h"] = ([128, 128], F32)
    S["hy_w_out"] = ([1024, 1024], F32)
    S["mlp_w1"] = ([2, 1024, 4096], F32)
    S["mlp_w2"] = ([2, 4096, 1024], F32)
    S["rw_cols"] = ([128, 14 + 4 * 7 + 16], F32)
    S["rw_lora"] = ([128, 2, 512], F32)
    S["rw_gup"] = ([128, 512], F32)
    S["blk_h"] = ([128, 128], F32)
    S["cmask_h"] = ([128, 512], F32)
    S["masks_h"] = ([64, 2, 3, 64], F32)
    S["diff_cols"] = ([64, 4], F32)
    S["subln_g"] = ([128, 1], F32)
    S["ssd_w_in"] = ([1024, 6176], F32)
    S["ssd_w_out"] = ([2048, 1024], F32)
    S["conv_wT"] = ([128, 32, 5], F32)
    S["conv_bT"] = ([128, 32], F32)
    S["ssd_prm"] = ([128, 5, 32], F32)
    S["ssd_ngT"] = ([128, 16], F32)
    S["ut1_h"] = ([128, 128], F32)
    S["lt1_h"] = ([128, 128], F32)
    S["sel_h"] = ([32, 32, 128], F32)
    return S


def build(debug=False, stop_after=None, only=None, as_input=()):
    nc = bass.Bass("TRN2", target_bir_lowering=False)
    _AS_INPUT.clear()
    _AS_INPUT.update(as_input)
    g = G()
    declare_inputs(nc, g, input_shapes())
    g.out = nc.dram_tensor("out", [4096, 1024], F32, kind="ExternalOutput").ap()
    dbg = debug
    g.xT = dram(nc, "xT", [1024, T], F32, dbg)
    g.PrT = dram(nc, "PrT", [1792, T], F32, dbg)
    g.QrotT = dram(nc, "QrotT", [512, T], BF16, dbg)
    g.QplT = dram(nc, "QplT", [512, T], BF16, dbg)
    g.KrotT = dram(nc, "KrotT", [512, T], BF16, dbg)
    g.Vd = dram(nc, "Vd", [T, 512], BF16, dbg)
    g.modD = dram(nc, "modD", [2, 128, 96], F32, dbg)
    g.gT = dram(nc, "gT", [512, T], F32, dbg)
    g.bonT = dram(nc, "bonT", [512, T], F32, dbg)
    g.Vtm = dram(nc, "Vtm", [T, 512], BF16, dbg)
    g.gamA = dram(nc, "gamA", [2, 512, 68], F32, dbg)
    g.gam = [g.gamA[0], g.gamA[1]]
    g.FMA = dram(nc, "FMA", [2, 4, 512, T], BF16, dbg)
    g.FM = [[g.FMA[d, k] for k in range(4)] for d in range(2)]
    g.TMA = dram(nc, "TMA", [2, T, 2, 512], BF16, dbg)
    g.TM = [g.TMA[0], g.TMA[1]]
    g.mixT = dram(nc, "mixT", [1024, T], BF16, dbg)
    g.xbcT = dram(nc, "xbcT", [4096, T], F32, False)
    g.zTM = dram(nc, "zTM", [T, 2048], BF16, False)
    g.dtTM = dram(nc, "dtTM", [T, 32], F32, dbg)
    g.xsTM = dram(nc, "xsTM", [T, 2048], F32, dbg)
    g.BT = dram(nc, "BT", [1024, T], BF16, dbg)
    g.CT = dram(nc, "CT", [1024, T], BF16, dbg)
    g.BTM = dram(nc, "BTM", [T, 1024], BF16, False)
    g.YsA = dram(nc, "YsA", [2, T, 2048], F32, dbg)
    g.Yssd = [g.YsA[0], g.YsA[1]]
    g.mixT1 = dram(nc, "mixT1", [2048, T], BF16, dbg)
    g.YA = dram(nc, "YA", [2, T, 512], F32, dbg)
    g.Y = [g.YA[0], g.YA[1]]
    with ExitStack() as es:
        kb = KB(nc, es)
        kb.es_t = None
        g.ident = kb.sb("ident", [128, 128])
        g.ones_bf = kb.sb("ones_bf", [128, 128], BF16)
        g.eps_t = kb.sb("eps_t", [128, 8])
        g.mod = [kb.sb(f"mod{li}", [128, 48, 2]) for li in range(2)]
        g.sc1 = [kb.sb(f"sc1_{li}", [128, 8, 2]) for li in range(2)]
        g.sc2 = [kb.sb(f"sc2_{li}", [128, 8, 2]) for li in range(2)]
        kb.load(g.ident[:], g.ident_h[:, :], W=[g.ident])
        kb.memset(g.ones_bf[:], 1.0, W=[g.ones_bf])
        g.ident_bf = kb.sb("ident_bf", [128, 128], BF16)
        kb.copy(g.ident_bf[:], g.ident[:], R=[g.ident], W=[g.ident_bf])
        kb.memset(g.eps_t[:, 0:1], EPS, W=[g.eps_t])
        kb.memset(g.eps_t[:, 1:2], 1e-12, W=[g.eps_t])
        kb.memset(g.eps_t[:, 2:3], 64e-5, W=[g.eps_t])
        kb.memset(g.eps_t[:, 3:4], 0.0, W=[g.eps_t])
        kb.memset(g.eps_t[:, 4:5], 1.0, W=[g.eps_t])
        phases = [("mods", phase_mods), ("xT", phase_xT), ("A0", phase_A0), ("B0", phase_B0), ("C0", phase_C0), ("C2", phase_C2), ("D0", phase_D0),
                  ("E0", lambda kb, g: phase_E(kb, g, 0, g.hy_w_out, 8, g.mixT, BLK512)),
                  ("F0", lambda kb, g: phase_F(kb, g, 0, BLK256)),
                  ("A1", phase_A1), ("B1", phase_B1), ("C1", phase_C1), ("C3", phase_C3),
                  ("E1", lambda kb, g: phase_E(kb, g, 1, g.ssd_w_out, 16, g.mixT1, BLK512L)),
                  ("F1", lambda kb, g: phase_F(kb, g, 1, BLK256L)), ("G", phase_G)]
        for name, fn in phases:
            if only is not None and name not in only:
                continue
            fn(kb, g)
            if stop_after == name:
                break
        if debug:
            for li in range(2):
                kb.store(g.modD[li], g.mod[li][:].rearrange("p a b -> p (a b)"), R=[g.mod[li]])
        kb.finish()
        print("instructions:", kb.nins)
    return nc


def rope_tables():
    inv = 10000.0 ** (-np.arange(0, 32, 2, dtype=np.float32) / 32.0)
    t = np.arange(4096)
    rows = (t // 64).astype(np.float32)
    cols = (t % 64).astype(np.float32)
    ar = rows[:, None] * inv[None, :]
    ac = cols[:, None] * inv[None, :]
    cosT = np.ones((128, T), np.float32)
    sinT = np.zeros((128, T), np.float32)
    for p in range(128):
        d = p % 64
        ang = ar if d < 32 else ac
        i = d % 16
        first = (d % 32) < 16
        cosT[p, 256:] = np.cos(ang[:, i])
        sinT[p, 256:] = (-np.sin(ang[:, i])) if first else np.sin(ang[:, i])
    return cosT, sinT


def swap_cols(w):
    idx = np.arange(512)
    d = idx % 32
    partner = np.where(d < 16, idx + 16, idx - 16)
    return w[:, partner]


def host_consts(inp):
    C = {}
    f = np.float32
    C["ada_w"] = np.ascontiguousarray(inp["ada_w"], dtype=f)
    C["ada_bT"] = np.ascontiguousarray(inp["ada_b"].reshape(2, 48, 128).transpose(0, 2, 1), dtype=f)
    C["g1T"] = np.ascontiguousarray(inp["norm1_g"].reshape(2, 8, 128).transpose(0, 2, 1), dtype=f)
    C["g2T"] = np.ascontiguousarray(inp["norm2_g"].reshape(2, 8, 128).transpose(0, 2, 1), dtype=f)
    C["gfT"] = np.ascontiguousarray(inp["norm_f_g"].reshape(8, 128).T, dtype=f)
    w = inp["hy_w_in"][0]
    q = w[:, 1792:2304]
    k = w[:, 2304:2816]
    v = w[:, 2816:3328]
    C["w_in0"] = np.ascontiguousarray(np.concatenate([w[:, :1792], q, swap_cols(q), k, swap_cols(k), v], axis=1), dtype=f)
    C["cosT"], C["sinT"] = rope_tables()
    C["ident_h"] = np.eye(128, dtype=f)
    C["hy_w_out"] = np.ascontiguousarray(inp["hy_w_out"][0], dtype=f)
    C["mlp_w1"] = np.ascontiguousarray(inp["mlp_w1"], dtype=f)
    C["mlp_w2"] = np.ascontiguousarray(inp["mlp_w2"], dtype=f)
    col = lambda a: np.ascontiguousarray(np.asarray(a, dtype=f).reshape(-1, 128).T)
    rw = np.zeros((128, 58), f)
    rw[:, 0:14] = col(inp["rwkv_mu"][0])
    rw[:, 14:18] = col(inp["rwkv_k_k"][0])
    rw[:, 18:22] = col(inp["rwkv_k_a"][0])
    rw[:, 22:26] = col(inp["rwkv_r_k"][0].reshape(-1))
    rw[:, 26:30] = col(inp["rwkv_ln_w"][0])
    rw[:, 30:34] = col(inp["rwkv_ln_b"][0])
    rw[:, 34:38] = col(inp["rwkv_w0"][0, 0])
    rw[:, 38:42] = col(inp["rwkv_w0"][0, 1])
    rw[:, 42:46] = col(inp["rwkv_a0"][0, 0])
    rw[:, 46:50] = col(inp["rwkv_a0"][0, 1])
    C["rw_cols"] = rw
    lora = np.zeros((128, 2, 512), f)
    lora[0:64] = inp["rwkv_w_up"][0].transpose(1, 0, 2)
    lora[64:128] = inp["rwkv_a_up"][0].transpose(1, 0, 2)
    C["rw_lora"] = lora
    C["rw_gup"] = np.ascontiguousarray(inp["rwkv_g_up"][0], dtype=f)
    blk = np.zeros((128, 128), f)
    blk[:64, :64] = 1
    blk[64:, 64:] = 1
    C["blk_h"] = blk
    cm = np.ones((128, 512), f)
    cm[:, ::64] = 0
    C["cmask_h"] = cm
    s = np.arange(64)[:, None]
    t = np.arange(64)[None, :]
    m = np.zeros((64, 2, 3, 64), f)
    m[:, 0, 0] = (s < t)
    m[:, 0, 1] = (s <= t)
    m[:, 0, 2] = (s > t)
    m[:, 1, 0] = (s > t)
    m[:, 1, 1] = (s >= t)
    m[:, 1, 2] = (s < t)
    C["masks_h"] = m
    C["diff_cols"] = np.stack([inp["diff_lq1"][0], inp["diff_lk1"][0], inp["diff_lq2"][0], inp["diff_lk2"][0]], axis=1).astype(f)
    C["subln_g"] = np.ascontiguousarray(inp["diff_subln_g"][0].reshape(128, 1), dtype=f)
    C["ssd_w_in"] = np.ascontiguousarray(inp["ssd_w_in"][0], dtype=f)
    C["ssd_w_out"] = np.ascontiguousarray(inp["ssd_w_out"][0], dtype=f)
    C["conv_wT"] = np.ascontiguousarray(inp["ssd_conv_w"][0].reshape(5, 32, 128).transpose(2, 1, 0), dtype=f)
    C["conv_bT"] = col(inp["ssd_conv_b"][0])
    prm = np.zeros((128, 5, 32), f)
    prm[:, 0] = inp["ssd_dt_bias"][0, 0][None, :]
    prm[:, 1] = inp["ssd_dt_bias"][0, 1][None, :]
    prm[:, 2] = inp["ssd_a_log"][0, 0][None, :]
    prm[:, 3] = inp["ssd_a_log"][0, 1][None, :]
    prm[:, 4] = inp["ssd_d"][0][None, :]
    C["ssd_prm"] = prm
    C["ssd_ngT"] = col(inp["ssd_norm_g"][0])
    jj = np.arange(128)[:, None]
    ll = np.arange(128)[None, :]
    C["ut1_h"] = (jj <= ll).astype(f)
    C["lt1_h"] = (jj >= ll).astype(f)
    sel = np.zeros((32, 32, 128), f)
    for h in range(32):
        sel[h, h, :] = 1.0
    C["sel_h"] = sel
    return C


def core_inputs(inp, C, b):
    m = dict(C)
    m["x"] = np.ascontiguousarray(inp["x"][b], dtype=np.float32)
    m["ctx"] = np.ascontiguousarray(inp["ctx"][b], dtype=np.float32)
    cv = np.stack([inp["c"][b], inp["c_ctx"]], axis=0).astype(np.float32)
    m["cT"] = np.ascontiguousarray(cv.reshape(2, 8, 128).transpose(2, 1, 0))
    return m


def kernel(**inputs):
    inp = {k: np.asarray(v) for k, v in inputs.items()}
    C = host_consts(inp)
    nc = build()
    in_maps = [core_inputs(inp, C, b) for b in range(8)]
    res = run_bass_kernel_spmd(nc, in_maps, core_ids=list(range(8)))
    return np.stack([np.asarray(r["out"]) for r in res.results], axis=0).astype(np.float32)
```

```python
import concourse.bass as bass
import concourse.mybir as mybir

F32 = mybir.dt.float32
BF16 = mybir.dt.bfloat16
AF = mybir.ActivationFunctionType
ALU = mybir.AluOpType
AX = mybir.AxisListType


class Src:
    def __init__(s, kb, name, inc, limit):
        s.kb, s.name, s.inc, s.limit = kb, name, inc, limit
        s.sems = []
        s.n = 0

    def sem_for(s, n):
        e = (n - 1) // s.limit
        while len(s.sems) <= e:
            s.sems.append(s.kb.es.enter_context(s.kb.nc.semaphore(f"{s.name}_{len(s.sems)}")))
        return s.sems[e], ((n - 1) % s.limit + 1) * s.inc


class Tk:
    __slots__ = ("w", "r")

    def __init__(s):
        s.w = {}
        s.r = {}


class TT:
    def __init__(s, t, k=None, ps=False):
        s.t = t
        s.k = k if k is not None else Tk()
        s.ps = ps

    def __getitem__(s, idx):
        return s.t[idx]


class KB:
    NSLOT = 20

    def __init__(s, nc, es):
        s.nc, s.es = nc, es
        s.eng = {"pe": nc.tensor, "dve": nc.vector, "act": nc.scalar, "pool": nc.gpsimd, "sp": nc.sync}
        s.src = {k: Src(s, "c" + k, 1, 30000) for k in s.eng}
        s.waited = {k: {} for k in s.eng}
        s.slots = {q: [Src(s, f"d{q}{i}", 16, 1800) for i in range(s.NSLOT)] for q in ("sp", "pool", "act")}
        s.rr = {q: 0 for q in s.slots}
        s.nins = 0
        s.same_engine_sync = True

    def sb(s, name, shape, dt=F32):
        return TT(s.es.enter_context(s.nc.sbuf_tensor("g_" + name, list(shape), dt)))

    def ps(s, name, shape, dt=F32):
        return TT(s.es.enter_context(s.nc.psum_tensor("gp_" + name, list(shape), dt)))

    def _deps(s, R, W, me=None):
        d = {}
        for r in R:
            k = r.k if isinstance(r, TT) else r
            for src, n in k.w.items():
                if d.get(src, 0) < n:
                    d[src] = n
            if isinstance(r, TT) and r.ps:
                for src, n in k.r.items():
                    if src is not me and d.get(src, 0) < n:
                        d[src] = n
        for w in W:
            k = w.k if isinstance(w, TT) else w
            for dd in (k.w, k.r):
                for src, n in dd.items():
                    if d.get(src, 0) < n:
                        d[src] = n
        return d

    def _wait(s, eng, d):
        wd = s.waited[eng]
        for src, n in d.items():
            if src is s.src[eng] and (eng == "pe" or not s.same_engine_sync):
                continue
            if wd.get(src, 0) >= n:
                continue
            sem, val = src.sem_for(n)
            s.eng[eng].wait_ge(sem, val)
            wd[src] = n

    def _mark(s, src, n, R, W):
        for w in W:
            k = w.k if isinstance(w, TT) else w
            k.w = {src: n}
            k.r = {}
        for r in R:
            k = r.k if isinstance(r, TT) else r
            if k.r.get(src, 0) < n:
                k.r[src] = n

    def op(s, eng, fn, R=(), W=()):
        d = s._deps(R, W, s.src[eng])
        s._wait(eng, d)
        src = s.src[eng]
        src.n += 1
        sem, _ = src.sem_for(src.n)
        ins = fn(s.eng[eng])
        ins.then_inc(sem, 1)
        s._mark(src, src.n, R, W)
        s.nins += 1

    def dma(s, q, out, in_, R=(), W=(), **kw):
        i = s.rr[q]
        s.rr[q] = (i + 1) % s.NSLOT
        slot = s.slots[q][i]
        d = s._deps(R, W)
        if slot.n > 0 and d.get(slot, 0) < slot.n:
            d[slot] = slot.n
        s._wait(q, d)
        slot.n += 1
        sem, _ = slot.sem_for(slot.n)
        s.eng[q].dma_start(out=out, in_=in_, **kw).then_inc(sem, 16)
        s._mark(slot, slot.n, R, W)
        s.nins += 1

    def load(s, out, in_, R=(), W=(), **kw):
        s.dma("sp", out, in_, R, W, **kw)

    def store(s, out, in_, R=(), W=(), **kw):
        s.dma("pool", out, in_, R, W, **kw)

    def finish(s):
        d = {}
        for q in s.slots:
            for sl in s.slots[q]:
                if sl.n:
                    d[sl] = sl.n
        for k, src in s.src.items():
            if src.n and k != "sp":
                d[src] = src.n
        s._wait("sp", d)

    def mm(s, out, lhsT, rhs, start=True, stop=True, R=(), W=()):
        s.op("pe", lambda e: e.matmul(out, lhsT=lhsT, rhs=rhs, start=start, stop=stop), R, W)

    def tr(s, out, in_, ident, R=(), W=()):
        s.op("pe", lambda e: e.transpose(out, in_, ident), R, W)

    def act(s, out, in_, func, bias=None, scale=None, R=(), W=(), accum_out=None):
        kw = {}
        if bias is not None:
            kw["bias"] = bias
        if scale is not None:
            kw["scale"] = scale
        if accum_out is not None:
            kw["accum_out"] = accum_out
        s.op("act", lambda e: e.activation(out=out, in_=in_, func=func, **kw), R, W)

    def tt(s, out, in0, in1, op, R=(), W=(), eng="dve"):
        s.op(eng, lambda e: e.tensor_tensor(out=out, in0=in0, in1=in1, op=op), R, W)

    def ts(s, out, in0, s1, s2, op0, op1=None, R=(), W=(), eng="dve"):
        if op1 is None:
            s.op(eng, lambda e: e.tensor_scalar(out=out, in0=in0, scalar1=s1, scalar2=None, op0=op0), R, W)
        else:
            s.op(eng, lambda e: e.tensor_scalar(out=out, in0=in0, scalar1=s1, scalar2=s2, op0=op0, op1=op1), R, W)

    def stt(s, out, in0, scalar, in1, op0, op1, R=(), W=()):
        s.op("dve", lambda e: e.scalar_tensor_tensor(out=out, in0=in0, scalar=scalar, in1=in1, op0=op0, op1=op1), R, W)

    def copy(s, out, in_, R=(), W=(), eng="dve"):
        if eng == "act":
            s.op("act", lambda e: e.copy(out=out, in_=in_), R, W)
        else:
            s.op(eng, lambda e: e.tensor_copy(out=out, in_=in_), R, W)

    def memset(s, ap, val, W=(), eng="dve"):
        s.op(eng, lambda e: e.memset(ap, val), (), W)
import math
import numpy as np
from contextlib import ExitStack, contextmanager
from concourse.bass_utils import run_bass_kernel_spmd

T = 4352
NCTX = 256
TBS = [(0, 256)] + [(256 + 512 * i, 512) for i in range(8)]
KAPPA = math.exp(-0.5)
EPS = 1e-6


class G:
    pass


@contextmanager
def phase(kb):
    old = kb.es
    barrier(kb)
    with ExitStack() as es:
        kb.es_t = es
        yield
        barrier(kb)
    kb.es_t = None


def barrier(kb):
    d = {}
    for q in kb.slots:
        for sl in kb.slots[q]:
            if sl.n:
                d[sl] = sl.n
    for k, src in kb.src.items():
        if src.n:
            d[src] = src.n
    for e in ("pe", "dve", "act", "pool", "sp"):
        dd = {s_: n for s_, n in d.items() if s_ is not kb.src[e]}
        kb._wait(e, dd)


_uid = [0]


def sbt(kb, name, shape, dt=F32):
    _uid[0] += 1
    return TT(kb.es_t.enter_context(kb.nc.sbuf_tensor(f"s{_uid[0]}_{name}", list(shape), dt)))


def pst(kb, name, shape=None, dt=F32):
    _uid[0] += 1
    full = [128, 512] if dt == F32 else [128, 1024]
    return TT(kb.es_t.enter_context(kb.nc.psum_tensor(f"p{_uid[0]}_{name}", full, dt)), ps=True)


def run_interleaved(gens):
    gens = list(gens)
    while gens:
        for g_ in list(gens):
            try:
                next(g_)
            except StopIteration:
                gens.remove(g_)


class PPool:
    def __init__(s, banks):
        s.b = banks
        s.live = [False] * len(banks)
        s.i = 0

    def get(s):
        n = len(s.b)
        for k in range(n):
            j = (s.i + k) % n
            if not s.live[j]:
                s.live[j] = True
                s.i = (j + 1) % n
                return s.b[j]
        raise RuntimeError("PSUM pool exhausted")

    def put(s, bank):
        s.live[s.b.index(bank)] = False


class Rot:
    def __init__(s, items):
        s.items = items
        s.i = 0

    def next(s):
        x = s.items[s.i]
        s.i = (s.i + 1) % len(s.items)
        return x


_AS_INPUT = set()


def dram(nc, name, shape, dt, debug):
    kind = "ExternalInput" if name in _AS_INPUT else ("ExternalOutput" if debug else "Internal")
    return nc.dram_tensor(name, list(shape), dt, kind=kind).ap()


def phase_mods(kb, g):
    nc = kb.nc
    with phase(kb):
        cT = sbt(kb, "cT", [128, 8, 2])
        scT = sbt(kb, "scT", [128, 8, 2])
        sg_ = sbt(kb, "sgc", [128, 8, 2])
        kb.load(cT[:], g.cT[:, :, :], W=[cT])
        kb.act(sg_[:], cT[:], AF.Sigmoid, R=[cT], W=[sg_])
        kb.tt(scT[:], cT[:], sg_[:], ALU.mult, R=[cT, sg_], W=[scT])
        wb = Rot([sbt(kb, f"adaw{i}", [128, 8, 1024]) for i in range(2)])
        pmb = pst(kb, "pmod")
        pm = TT(pmb.t[:, 0:96].rearrange("p (a b) -> p a b", b=2), pmb.k, ps=True)
        adab = sbt(kb, "adab", [128, 48])
        g1 = sbt(kb, "g1", [128, 8])
        g2 = sbt(kb, "g2", [128, 8])
        for li in range(2):
            kb.load(adab[:], g.ada_bT[li], W=[adab])
            kb.load(g1[:], g.g1T[li], W=[g1])
            kb.load(g2[:], g.g2T[li], W=[g2])
            src = g.ada_w[li].rearrange("(kc p) n -> p kc n", p=128)
            for pc in range(6):
                w = wb.next()
                kb.load(w[:], src[:, :, pc * 1024:(pc + 1) * 1024], W=[w])
                for cc in range(8):
                    col = pc * 8 + cc
                    for kc in range(8):
                        kb.mm(pm[:, col, :], w[:, kc, cc * 128:(cc + 1) * 128], scT[:, kc, :],
                              start=(kc == 0), stop=(kc == 7), R=[w, scT], W=[pm])
            mod = g.mod[li]
            kb.tt(mod[:], pm[:], adab[:, :, None].to_broadcast([128, 48, 2]), ALU.add, R=[pm, adab], W=[mod])
            for (sc, gi, m) in ((g.sc1[li], g1, 1), (g.sc2[li], g2, 4)):
                kb.ts(sc[:], mod[:, m * 8:(m + 1) * 8, :], 1.0, None, ALU.add, R=[mod], W=[sc])
                kb.tt(sc[:], sc[:], gi[:, :, None].to_broadcast([128, 8, 2]), ALU.mult, R=[sc, gi], W=[sc])


def phase_xT(kb, g):
    with phase(kb):
        xin = Rot([sbt(kb, f"xin{i}", [128, 1024]) for i in range(2)])
        xo = Rot([sbt(kb, f"xo{i}", [128, 8, 128]) for i in range(2)])
        pt = Rot([pst(kb, f"pT{i}") for i in range(4)])
        dst = g.xT.rearrange("(kc p) t -> p kc t", p=128)
        for i in range(34):
            xi = xin.next()
            src = g.ctx[i * 128:(i + 1) * 128, :] if i < 2 else g.x[(i - 2) * 128:(i - 1) * 128, :]
            kb.load(xi[:], src, W=[xi])
            o = xo.next()
            for hf in range(2):
                p = pt.next()
                for j in range(4):
                    kc = hf * 4 + j
                    kb.tr(p[:, j * 128:(j + 1) * 128], xi[:, kc * 128:(kc + 1) * 128], g.ident[:], R=[xi, g.ident], W=[p])
                kb.copy(o[:, hf * 4:(hf + 1) * 4, :], p[:, :].rearrange("p (a b) -> p a b", b=128), R=[p], W=[o], eng=("act" if hf else "dve"))
            kb.store(dst[:, :, i * 128:(i + 1) * 128], o[:], R=[o])


class NormBufs:
    def __init__(s, kb, tag):
        s.sq = sbt(kb, f"nsq{tag}", [128, 8, 512], BF16)
        s.tmp = sbt(kb, f"ntmp{tag}", [128, 8, 512])
        s.rt = sbt(kb, f"nrt{tag}", [128, 512])
        s.rstd = sbt(kb, f"nrstd{tag}", [128, 512])
        s.ss = pst(kb, f"nss{tag}", [128, 512])


def norm_mod(kb, g, nb, xTb, n, sc, sh, j, hT, sh_off=0):
    kb.act(nb.sq[:, :, :n], xTb[:, :, :n], AF.Square, R=[xTb], W=[nb.sq])
    for kc in range(8):
        kb.mm(nb.ss[:, :n], g.ones_bf[:], nb.sq[:, kc, :n], start=(kc == 0), stop=(kc == 7), R=[nb.sq, g.ones_bf], W=[nb.ss])
    kb.act(nb.rt[:, :n], nb.ss[:, :n], AF.Sqrt, bias=g.eps_t[:, 0:1], scale=1.0 / 1024.0, R=[nb.ss, g.eps_t], W=[nb.rt])
    kb.op("dve", lambda e: e.reciprocal(out=nb.rstd[:, :n], in_=nb.rt[:, :n]), R=[nb.rt], W=[nb.rstd])
    kb.tt(nb.tmp[:, :, :n], xTb[:, :, :n], nb.rstd[:, None, :n].to_broadcast([128, 8, n]), ALU.mult, R=[xTb, nb.rstd], W=[nb.tmp])
    for kc in range(8):
        kb.act(hT[:, kc, :n], nb.tmp[:, kc, :n], AF.Identity, bias=sh[:, sh_off + kc, j:j + 1], scale=sc[:, kc, j:j + 1],
               R=[nb.tmp, sc, sh], W=[hT])


def load_weight_bf16(kb, W, src_ap, ncols, stg, piece=256):
    src = src_ap.rearrange("(kc p) n -> p kc n", p=128)
    kcn = src.shape[1]
    i = 0
    for c0 in range(0, ncols, piece):
        c1 = min(ncols, c0 + piece)
        w = c1 - c0
        st = stg.next()
        sv = st.t[:, 0:kcn * w].rearrange("p (k n) -> p k n", n=w)
        kb.load(sv, src[:, :, c0:c1], W=[st])
        kb.copy(W[:, :kcn, c0:c1], sv, R=[st], W=[W], eng=("act" if i % 2 else "dve"))
        i += 1


def phase_A0(kb, g):
    with phase(kb):
        W = sbt(kb, "wA", [128, 8, 4352], BF16)
        stg = Rot([sbt(kb, f"wstg{i}", [128, 2048]) for i in range(2)])
        import os
        STG = int(os.environ.get("A0_STAGE", "9"))
        load_weight_bf16(kb, W, g.w_in0, 4352, stg)
        nb = NormBufs(kb, "A")
        xb = Rot([sbt(kb, f"xTb{i}", [128, 8, 512]) for i in range(2)])
        hTs = Rot([sbt(kb, f"hT{i}", [128, 8, 512], BF16) for i in range(2)])
        cosb = Rot([sbt(kb, f"cos{i}", [128, 512]) for i in range(2)])
        sinb = Rot([sbt(kb, f"sin{i}", [128, 512]) for i in range(2)])
        pmm = Rot([pst(kb, f"pmm{i}", [128, 512]) for i in range(6)])
        st32 = Rot([sbt(kb, f"st32_{i}", [128, 512]) for i in range(4)])
        st16 = Rot([sbt(kb, f"st16_{i}", [128, 512], BF16) for i in range(6)])
        t1s = Rot([sbt(kb, f"t1_{i}", [128, 512]) for i in range(2)])
        t2s = Rot([sbt(kb, f"t2_{i}", [128, 512]) for i in range(2)])
        xsrc = g.xT.rearrange("(kc p) t -> p kc t", p=128)
        for bi, (s0, n) in enumerate(TBS):
            if STG < 2 or (STG < 9 and bi > 0):
                break
            j = 1 if bi == 0 else 0
            x = xb.next()
            kb.load(x[:, :, :n], xsrc[:, :, s0:s0 + n], W=[x])
            cs, sn = cosb.next(), sinb.next()
            kb.load(cs[:, :n], g.cosT[:, s0:s0 + n], W=[cs])
            kb.load(sn[:, :n], g.sinT[:, s0:s0 + n], W=[sn])
            hT = hTs.next()
            norm_mod(kb, g, nb, x, n, g.sc1[0], g.mod[0], j, hT, sh_off=0)

            def proj(ct):
                p = pmm.next()
                for kc in range(8):
                    kb.mm(p[:, :n], W[:, kc, ct * 128:(ct + 1) * 128], hT[:, kc, :n], start=(kc == 0), stop=(kc == 7), R=[W, hT], W=[p])
                return p
            if STG < 3:
                continue
            for ct in range(14):
                p = proj(ct)
                st = st32.next()
                kb.copy(st[:, :n], p[:, :n], R=[p], W=[st], eng=("act" if ct % 2 else "dve"))
                kb.store(g.PrT[ct * 128:(ct + 1) * 128, s0:s0 + n], st[:, :n], R=[st])
            if STG < 4:
                continue
            for h in range(4):
                for (base, dst_rot, dst_pl) in ((14, g.QrotT, g.QplT), (22, g.KrotT, None)):
                    pq = proj(base + h)
                    pw = proj(base + 4 + h)
                    t1, t2 = t1s.next(), t2s.next()
                    kb.tt(t1[:, :n], pq[:, :n], cs[:, :n], ALU.mult, R=[pq, cs], W=[t1])
                    kb.tt(t2[:, :n], pw[:, :n], sn[:, :n], ALU.mult, R=[pw, sn], W=[t2])
                    so = st16.next()
                    kb.tt(so[:, :n], t1[:, :n], t2[:, :n], ALU.add, R=[t1, t2], W=[so])
                    if not os.environ.get("NOSTORE4"):
                        kb.store(dst_rot[h * 128:(h + 1) * 128, s0:s0 + n], so[:, :n], R=[so])
                    if dst_pl is not None:
                        sp_ = st16.next()
                        kb.copy(sp_[:, :n], pq[:, :n], R=[pq], W=[sp_], eng="act")
                        if not os.environ.get("NOSTORE4"):
                            kb.store(dst_pl[h * 128:(h + 1) * 128, s0:s0 + n], sp_[:, :n], R=[sp_])
            if STG < 5:
                continue
            for tt_ in range(n // 128):
                p = pmm.next()
                for kc in range(8):
                    kb.mm(p[:, :], hT[:, kc, tt_ * 128:(tt_ + 1) * 128], W[:, kc, 3840:4352], start=(kc == 0), stop=(kc == 7), R=[W, hT], W=[p])
                so = st16.next()
                kb.copy(so[:], p[:], R=[p], W=[so], eng=("act" if tt_ % 2 else "dve"))
                kb.store(g.Vd[s0 + tt_ * 128:s0 + (tt_ + 1) * 128, :], so[:], R=[so])

def phase_B0(kb, g):
    with phase(kb):
        rwc = sbt(kb, "rwc", [128, 58])
        hmu = sbt(kb, "hmu", [128, 14])
        omm = sbt(kb, "omm", [128, 14])
        omka = sbt(kb, "omka", [128, 4])
        hrk = sbt(kb, "hrk", [128, 4])
        lora = sbt(kb, "lora", [128, 2, 512])
        gup = sbt(kb, "gup", [128, 512])
        blk = sbt(kb, "blk", [128, 128])
        cmask = sbt(kb, "cmask", [128, 512])
        kb.load(rwc[:], g.rw_cols[:, :], W=[rwc])
        kb.load(lora[:], g.rw_lora[:, :, :], W=[lora])
        kb.load(gup[:], g.rw_gup[:, :], W=[gup])
        kb.load(blk[:], g.blk_h[:, :], W=[blk])
        kb.load(cmask[:], g.cmask_h[:, :], W=[cmask])
        kb.ts(hmu[:], rwc[:, 0:14], 0.5, None, ALU.mult, R=[rwc], W=[hmu])
        kb.ts(omm[:], rwc[:, 0:14], -1.0, 1.0, ALU.mult, ALU.add, R=[rwc], W=[omm])
        kb.ts(omka[:], rwc[:, 18:22], -1.0, 1.0, ALU.mult, ALU.add, R=[rwc], W=[omka])
        kb.ts(hrk[:], rwc[:, 22:26], 0.5, None, ALU.mult, R=[rwc], W=[hrk])

        pin = sbt(kb, "pin", [128, 14, 514])
        psx = sbt(kb, "psx", [128, 14, 512])
        lwin = sbt(kb, "lwin", [128, 512])
        sgd = sbt(kb, "sgd", [128, 512])
        F = lambda nm, dt=F32: sbt(kb, nm, [128, 512], dt)
        R2 = lambda nm: Rot([F(f"{nm}{i}") for i in range(2)])
        hp_rots = [R2(nm) for nm in ("kku", "sq", "rt", "rs", "kk", "rk", "bon", "kbs")]
        d_rots = [R2(nm) for nm in ("sg", "a_", "tmp", "kmod", "b_", "ci", "cr", "ce", "e1", "e2", "e3")]
        fm16 = [Rot([F(f"fm{k}_{i}", BF16) for i in range(2)]) for k in range(4)]
        vb = F("vb", BF16)
        st32 = Rot([F(f"bst{i}") for i in range(2)])
        gst = Rot([sbt(kb, f"gst{i}", [128, 8]) for i in range(2)])
        tms = Rot([sbt(kb, f"tms{i}", [128, 1024], BF16) for i in range(3)])
        pmm = Rot([pst(kb, f"bp{i}") for i in range(4)])
        ptr = Rot([pst(kb, f"bt{i}", dt=BF16) for i in range(3)])
        psrc = g.PrT.rearrange("(ti p) t -> p ti t", p=128)
        for bi, (s0, n) in enumerate(TBS):
            nt = n // 128
            nch = n // 64
            c0 = s0 // 64
            lo = s0 if s0 in (0, NCTX) else s0 - 1
            hi = s0 + n if (s0 + n) in (NCTX, T) else s0 + n + 1
            kb.memset(pin[:, :, 0:1], 0.0, W=[pin])
            kb.memset(pin[:, :, n + 1:n + 2], 0.0, W=[pin])
            kb.load(pin[:, :, lo - s0 + 1:hi - s0 + 1], psrc[:, :, lo:hi], W=[pin])
            kb.tt(psx[:, :, :n], pin[:, :, 0:n], pin[:, :, 2:n + 2], ALU.add, R=[pin], W=[psx])
            for ti in range(14):
                kb.act(psx[:, ti, :n], psx[:, ti, :n], AF.Identity, scale=hmu[:, ti:ti + 1], R=[psx, hmu], W=[psx])
            for ti in range(14):
                kb.stt(psx[:, ti, :n], pin[:, ti, 1:n + 1], omm[:, ti:ti + 1], psx[:, ti, :n], ALU.mult, ALU.add, R=[pin, psx, omm], W=[psx])
            kb.act(lwin[0:64, :n], psx[0:64, 12, :n], AF.Tanh, R=[psx], W=[lwin])
            kb.copy(lwin[64:128, :n], psx[64:128, 12, :n], R=[psx], W=[lwin])
            kb.act(sgd[:, :n], psx[:, 13, :n], AF.Sigmoid, R=[psx], W=[sgd])
            for hp in range(4):
                kku, sq, rt, rs, kk, rk, bon, kbs = [R_.next() for R_ in hp_rots]
                r = psx[:, hp, :n]
                k = psx[:, 4 + hp, :n]
                v = psx[:, 8 + hp, :n]
                hs = slice(hp * 128, (hp + 1) * 128)
                kb.ts(kku[:, :n], k, rwc[:, 14 + hp:15 + hp], None, ALU.mult, R=[psx, rwc], W=[kku])
                kb.tt(sq[:, :n], kku[:, :n], kku[:, :n], ALU.mult, R=[kku], W=[sq])
                p = pmm.next()
                kb.mm(p[:, :n], blk[:], sq[:, :n], R=[blk, sq], W=[p])
                kb.act(rt[:, :n], p[:, :n], AF.Sqrt, bias=g.eps_t[:, 1:2], scale=1.0, R=[p, g.eps_t], W=[rt])
                kb.op("dve", lambda e: e.reciprocal(out=rs[:, :n], in_=rt[:, :n]), R=[rt], W=[rs])
                kb.tt(kk[:, :n], kku[:, :n], rs[:, :n], ALU.mult, R=[kku, rs], W=[kk])
                p = pmm.next()
                kb.mm(p[:, :n], gup[:, hs], sgd[:, :n], R=[gup, sgd], W=[p])
                st = st32.next()
                kb.copy(st[:, :n], p[:, :n], R=[p], W=[st], eng="act")
                kb.dma("sp", g.gT[hs, s0:s0 + n], st[:, :n], R=[st])
                kb.copy(vb[:, :n], v, R=[psx], W=[vb], eng="act")
                pt_ = ptr.next()
                for tt_ in range(nt):
                    kb.tr(pt_[:, tt_ * 128:(tt_ + 1) * 128], vb[:, tt_ * 128:(tt_ + 1) * 128], g.ident_bf[:], R=[vb, g.ident_bf], W=[pt_])
                tm = tms.next()
                kb.copy(tm[:, :nt * 128], pt_[:, :nt * 128], R=[pt_], W=[tm], eng="act")
                for tt_ in range(nt):
                    kb.dma("sp", g.Vtm[s0 + tt_ * 128:s0 + (tt_ + 1) * 128, hs], tm[:, tt_ * 128:(tt_ + 1) * 128], R=[tm])
                for d in range(2):
                    sg, a_, tmp, kmod, b_, ci, cr, ce, e1, e2, e3 = [R_.next() for R_ in d_rots]
                    p = pmm.next()
                    kb.mm(p[:, :n], lora[0:64, d, hs], lwin[0:64, :n], R=[lora, lwin], W=[p])
                    kb.act(sg[:, :n], p[:, :n], AF.Sigmoid, bias=rwc[:, 34 + 4 * d + hp:35 + 4 * d + hp], scale=1.0, R=[p, rwc], W=[sg])
                    p = pmm.next()
                    kb.mm(p[:, :n], lora[64:128, d, hs], lwin[64:128, :n], R=[lora, lwin], W=[p])
                    kb.act(a_[:, :n], p[:, :n], AF.Sigmoid, bias=rwc[:, 42 + 4 * d + hp:43 + 4 * d + hp], scale=1.0, R=[p, rwc], W=[a_])
                    kb.ts(tmp[:, :n], a_[:, :n], rwc[:, 18 + hp:19 + hp], omka[:, hp:hp + 1], ALU.mult, ALU.add, R=[a_, rwc, omka], W=[tmp])
                    kb.tt(kmod[:, :n], tmp[:, :n], k, ALU.mult, R=[tmp, psx], W=[kmod])
                    kb.tt(b_[:, :n], kk[:, :n], a_[:, :n], ALU.mult, R=[kk, a_], W=[b_])
                    if d == 0:
                        kb.copy(kbs[:, :n], kmod[:, :n], R=[kmod], W=[kbs], eng="act")
                    else:
                        kb.tt(kbs[:, :n], kbs[:, :n], kmod[:, :n], ALU.add, R=[kbs, kmod], W=[kbs])
                    kb.op("dve", lambda e: e.tensor_tensor_scan(out=ci[:, :n], data0=cmask[:, :n], data1=sg[:, :n], initial=0.0,
                                                                op0=ALU.mult, op1=ALU.add), R=[cmask, sg], W=[ci])
                    cc = ci
                    if d == 1:
                        kb.tt(tmp[:, :n], sg[:, :n], ci[:, :n], ALU.subtract, R=[sg, ci], W=[tmp])
                        civ = ci[:, :n].rearrange("p (c t) -> p c t", t=64)
                        kb.tt(cr[:, :n].rearrange("p (c t) -> p c t", t=64), tmp[:, :n].rearrange("p (c t) -> p c t", t=64),
                              civ[:, :, 63:64].to_broadcast([128, nch, 64]), ALU.add, R=[tmp, ci], W=[cr])
                        cc = cr
                    kb.tt(ce[:, :n], cc[:, :n], sg[:, :n], ALU.subtract, R=[cc, sg], W=[ce])
                    kb.act(e1[:, :n], cc[:, :n], AF.Exp, scale=KAPPA, R=[cc], W=[e1])
                    kb.act(e2[:, :n], cc[:, :n], AF.Exp, scale=-KAPPA, R=[cc], W=[e2])
                    kb.act(e3[:, :n], ce[:, :n], AF.Exp, scale=-KAPPA, R=[ce], W=[e3])
                    fa, fr, fb, fk = [fm16[i].next() for i in range(4)]
                    kb.stt(fa[:, :n], kk[:, :n], -1.0, e3[:, :n], ALU.mult, ALU.mult, R=[kk, e3], W=[fa])
                    kb.tt(fr[:, :n], r, e2[:, :n], ALU.mult, R=[psx, e2], W=[fr])
                    kb.tt(fb[:, :n], b_[:, :n], e1[:, :n], ALU.mult, R=[b_, e1], W=[fb])
                    kb.tt(fk[:, :n], kmod[:, :n], e1[:, :n], ALU.mult, R=[kmod, e1], W=[fk])
                    gs = gst.next()
                    e2v = e2[:, :n].rearrange("p (c t) -> p c t", t=64)
                    col = 63 if d == 0 else 0
                    kb.copy(gs[:, :nch], e2v[:, :, col], R=[e2], W=[gs], eng="pool")
                    kb.dma("sp", g.gam[d][hs, c0:c0 + nch], gs[:, :nch], R=[gs])
                    for kind, f in enumerate((fa, fr, fb, fk)):
                        kb.dma("sp", g.FM[d][kind][hs, s0:s0 + n], f[:, :n], R=[f])
                    for kind, f in ((0, fb), (1, fk)):
                        pt_ = ptr.next()
                        for tt_ in range(nt):
                            kb.tr(pt_[:, tt_ * 128:(tt_ + 1) * 128], f[:, tt_ * 128:(tt_ + 1) * 128], g.ident_bf[:], R=[f, g.ident_bf], W=[pt_])
                        tm = tms.next()
                        kb.copy(tm[:, :nt * 128], pt_[:, :nt * 128], R=[pt_], W=[tm], eng=("act" if kind else "dve"))
                        for tt_ in range(nt):
                            kb.dma("sp", g.TM[d][s0 + tt_ * 128:s0 + (tt_ + 1) * 128, kind, hs], tm[:, tt_ * 128:(tt_ + 1) * 128], R=[tm])
                kb.tt(rk[:, :n], r, kbs[:, :n], ALU.mult, R=[psx, kbs], W=[rk])
                kb.ts(rk[:, :n], rk[:, :n], hrk[:, hp:hp + 1], None, ALU.mult, R=[rk, hrk], W=[rk])
                p = pmm.next()
                kb.mm(p[:, :n], blk[:], rk[:, :n], R=[blk, rk], W=[p])
                kb.tt(bon[:, :n], p[:, :n], v, ALU.mult, R=[p, psx], W=[bon])
                kb.dma("sp", g.bonT[hs, s0:s0 + n], bon[:, :n], R=[bon])


def phase_C0(kb, g):
    NL = 5
    with phase(kb):
        mk = sbt(kb, "mk", [64, 2, 3, 64])
        kb.load(mk[:], g.masks_h[:, :, :, :], W=[mk])
        poolA = PPool([pst(kb, f"cpa{i}") for i in range(5)])
        poolB = PPool([pst(kb, f"cpb{i}") for i in range(3)])
        st = []
        for d in range(2):
            s = G()
            s.gam = sbt(kb, f"gam{d}", [64, 8, 68])
            kb.load(s.gam[:], g.gam[d].rearrange("(h k) c -> k h c", k=64), W=[s.gam])
            s.fm = Rot([sbt(kb, f"fm{d}_{i}", [64, 4, 8, 64], BF16) for i in range(2)])
            s.tm = Rot([sbt(kb, f"tm{d}_{i}", [64, 2, 512], BF16) for i in range(2)])
            s.vt = Rot([sbt(kb, f"vt{d}_{i}", [64, 512], BF16) for i in range(2)])
            s.Sf = sbt(kb, f"Sf{d}", [64, 8, 64])
            s.Sb = Rot([sbt(kb, f"Sb{d}_{i}", [64, 8, 64], BF16) for i in range(2)])
            s.Nm = Rot([sbt(kb, f"Nm{d}_{i}", [64, 8, 128], BF16) for i in range(2)])
            s.Nkm = Rot([sbt(kb, f"Nkm{d}_{i}", [64, 8, 128], BF16) for i in range(2)])
            s.Inv = Rot([sbt(kb, f"Inv{d}_{i}", [64, 8, 64], BF16) for i in range(2)])
            s.NT = sbt(kb, f"NT{d}", [64, 8, 64], BF16)
            s.X = sbt(kb, f"X{d}", [64, 8, 64])
            s.Xb = Rot([sbt(kb, f"Xb{d}_{i}", [64, 8, 64], BF16) for i in range(2)])
            s.P = Rot([sbt(kb, f"P{d}_{i}", [64, 8, 64], BF16) for i in range(2)])
            s.PT = Rot([sbt(kb, f"PT{d}_{i}", [64, 8, 64], BF16) for i in range(2)])
            s.W1 = sbt(kb, f"W1{d}", [64, 8, 64], BF16)
            s.UT = sbt(kb, f"UT{d}", [64, 8, 64], BF16)
            s.Yst = Rot([sbt(kb, f"Yst{d}_{i}", [64, 512]) for i in range(2)])
            kb.memset(s.Sf[:], 0.0, W=[s.Sf])
            s.sb = s.Sb.next()
            kb.memset(s.sb[:], 0.0, W=[s.sb])
            s.mAR = mk[:, d, 0:2, :].rearrange("p a t -> p (a t)")[:, None, :].to_broadcast([64, 4, 128])
            s.mT = mk[:, d, 2, :][:, None, :].to_broadcast([64, 8, 64])
            st.append(s)
        Ibc = g.ident[0:64, 0:64][:, None, :].to_broadcast([64, 8, 64])
        order = [list(range(68)), [3, 2, 1, 0] + list(range(67, 3, -1))]

        def v3(p):
            return p[0:64, :].rearrange("p (h t) -> p h t", t=64)

        def partA(d, c, rec):
            s = st[d]
            t0 = c * 64
            yield
            fm, tm, vt = s.fm.next(), s.tm.next(), s.vt.next()
            Nm, Nkm = s.Nm.next(), s.Nkm.next()
            for kind in range(4):
                kb.load(fm[:, kind, :, :], g.FM[d][kind].rearrange("(h k) t -> k h t", k=64)[:, :, t0:t0 + 64], W=[fm])
            kb.load(tm[:], g.TM[d][t0:t0 + 64, :, :], W=[tm])
            kb.load(vt[:], g.Vtm[t0:t0 + 64, :], W=[vt])
            for (lk, dst) in ((2, Nm), (3, Nkm)):
                for hh in range(2):
                    p = poolA.get()
                    for h4 in range(4):
                        h = hh * 4 + h4
                        kb.mm(p[0:64, h4 * 128:(h4 + 1) * 128], fm[:, lk, h, :], fm[:, 0:2, h, :], R=[fm], W=[p])
                    kb.tt(dst[:, hh * 4:(hh + 1) * 4, :], p[0:64, :].rearrange("p (h t) -> p h t", t=128), s.mAR, ALU.mult, R=[p, mk], W=[dst])
                    poolA.put(p)
                    yield
            p = poolA.get()
            for h in range(8):
                kb.mm(p[0:64, h * 64:(h + 1) * 64], fm[:, 0, h, :], fm[:, 2, h, :], R=[fm], W=[p])
            kb.tt(s.NT[:], v3(p), s.mT, ALU.mult, R=[p, mk], W=[s.NT])
            poolA.put(p)
            yield
            kb.tt(s.X[:], Nm[:, :, 0:64], Ibc, ALU.add, R=[Nm, g.ident], W=[s.X])
            xb = s.Xb.next()
            kb.copy(xb[:], s.X[:], R=[s.X], W=[xb], eng="act")
            P_ap = lambda h: Nm[:, h, 0:64]
            PT_ap = lambda h: s.NT[:, h, :]
            Pt, PTt = Nm, s.NT
            for lv in range(NL):
                last = lv == NL - 1
                p1 = None
                if not last:
                    p1 = poolA.get()
                    for h in range(8):
                        kb.mm(p1[0:64, h * 64:(h + 1) * 64], PT_ap(h), P_ap(h), R=[Pt, PTt], W=[p1])
                p2 = poolA.get()
                for h in range(8):
                    kb.mm(p2[0:64, h * 64:(h + 1) * 64], P_ap(h), PT_ap(h), R=[Pt, PTt], W=[p2])
                yield
                nPT = s.PT.next()
                kb.copy(nPT[:], v3(p2), R=[p2], W=[nPT], eng="act")
                poolA.put(p2)
                if not last:
                    nP = s.P.next()
                    kb.copy(nP[:], v3(p1), R=[p1], W=[nP], eng="act")
                    poolA.put(p1)
                    Pt = nP
                    P_ap = (lambda t_: (lambda h: t_[:, h, :]))(nP)
                PTt = nPT
                PT_ap = (lambda t_: (lambda h: t_[:, h, :]))(nPT)
                yield
                p3 = poolA.get()
                for h in range(8):
                    kb.mm(p3[0:64, h * 64:(h + 1) * 64], PT_ap(h), xb[:, h, :], R=[PTt, xb], W=[p3])
                yield
                kb.tt(s.X[:], s.X[:], v3(p3), ALU.add, R=[s.X, p3], W=[s.X])
                poolA.put(p3)
                xb = s.Inv.next() if last else s.Xb.next()
                kb.copy(xb[:], s.X[:], R=[s.X], W=[xb], eng="act")
                yield
            rec.update(fm=fm, tm=tm, vt=vt, Nm=Nm, Nkm=Nkm, inv=xb, c=c)

        def partB(d, rec):
            s = st[d]
            fm, tm, vt, Nm, Nkm, inv, c = (rec[k] for k in ("fm", "tm", "vt", "Nm", "Nkm", "inv", "c"))
            t0 = c * 64
            sb = s.sb
            yield
            pw = poolB.get()
            for h in range(8):
                hs = slice(h * 64, (h + 1) * 64)
                kb.mm(pw[0:64, hs], Nkm[:, h, 0:64], vt[:, hs], start=True, stop=False, R=[Nkm, vt], W=[pw])
                kb.mm(pw[0:64, hs], fm[:, 0, h, :], sb[:, h, :], start=False, stop=True, R=[fm, sb], W=[pw])
            yield
            kb.copy(s.W1[:], v3(pw), R=[pw], W=[s.W1], eng="act")
            poolB.put(pw)
            pu = poolB.get()
            for h in range(8):
                kb.mm(pu[0:64, h * 64:(h + 1) * 64], inv[:, h, :], s.W1[:, h, :], R=[inv, s.W1], W=[pu])
            yield
            kb.copy(s.UT[:], v3(pu), R=[pu], W=[s.UT], eng="act")
            poolB.put(pu)
            pn = poolB.get()
            for h in range(8):
                hs = slice(h * 64, (h + 1) * 64)
                kb.mm(pn[0:64, hs], tm[:, 0, hs], s.UT[:, h, :], start=True, stop=False, R=[tm, s.UT], W=[pn])
                kb.mm(pn[0:64, hs], tm[:, 1, hs], vt[:, hs], start=False, stop=True, R=[tm, vt], W=[pn])
            yield
            kb.tt(s.Sf[:], s.Sf[:], v3(pn), ALU.add, R=[s.Sf, pn], W=[s.Sf])
            poolB.put(pn)
            kb.tt(s.Sf[:], s.Sf[:], s.gam[:, :, c:c + 1].to_broadcast([64, 8, 64]), ALU.mult, R=[s.Sf, s.gam], W=[s.Sf])
            nsb = s.Sb.next()
            kb.copy(nsb[:], s.Sf[:], R=[s.Sf], W=[nsb], eng="act")
            s.sb = nsb
            yield
            py = poolB.get()
            for h in range(8):
                hs = slice(h * 64, (h + 1) * 64)
                kb.mm(py[0:64, hs], fm[:, 1, h, :], sb[:, h, :], start=True, stop=False, R=[fm, sb], W=[py])
                kb.mm(py[0:64, hs], Nm[:, h, 64:128], s.UT[:, h, :], start=False, stop=False, R=[Nm, s.UT], W=[py])
                kb.mm(py[0:64, hs], Nkm[:, h, 64:128], vt[:, hs], start=False, stop=True, R=[Nkm, vt], W=[py])
            ys = s.Yst.next()
            kb.copy(ys[:], py[0:64, :], R=[py], W=[ys])
            poolB.put(py)
            kb.store(g.Y[d][t0:t0 + 64, :], ys[:], R=[ys])

        recs = [{}, {}]
        run_interleaved([partA(0, order[0][0], recs[0]), partA(1, order[1][0], recs[1])])
        for i in range(68):
            cur = recs
            gens = [partB(0, cur[0]), partB(1, cur[1])]
            recs = [{}, {}]
            if i + 1 < 68:
                gens += [partA(0, order[0][i + 1], recs[0]), partA(1, order[1][i + 1], recs[1])]
            run_interleaved(gens)

def phase_C2(kb, g):
    with phase(kb):
        rwc = sbt(kb, "rwc2", [128, 58])
        kb.load(rwc[:], g.rw_cols[:, :], W=[rwc])
        yf = Rot([sbt(kb, f"yf{i}", [128, 512]) for i in range(2)])
        yb = Rot([sbt(kb, f"yb{i}", [128, 512]) for i in range(2)])
        y = sbt(kb, "y", [128, 512])
        sq = sbt(kb, "ysq", [128, 512])
        sm = sbt(kb, "ysm", [128, 8])
        vr = sbt(kb, "yvr", [128, 8])
        rt = sbt(kb, "yrt", [128, 8])
        rs = sbt(kb, "yrs", [128, 8])
        yn = Rot([sbt(kb, f"yn{i}", [128, 512]) for i in range(2)])
        pb = [pst(kb, f"c2p{i}") for i in range(4)]
        bon = Rot([sbt(kb, f"bon{i}", [128, 4, 512]) for i in range(2)])
        gt = Rot([sbt(kb, f"gt{i}", [128, 4, 512]) for i in range(2)])
        a1 = Rot([sbt(kb, f"a1_{i}", [128, 512]) for i in range(2)])
        mo = Rot([sbt(kb, f"mo{i}", [128, 512], BF16) for i in range(3)])
        for bi, (s0, n) in enumerate(TBS):
            nt = n // 128
            bo, gg = bon.next(), gt.next()
            kb.load(bo[:, :, :n], g.bonT.rearrange("(c p) t -> p c t", p=128)[:, :, s0:s0 + n], W=[bo])
            kb.load(gg[:, :, :n], g.gT.rearrange("(c p) t -> p c t", p=128)[:, :, s0:s0 + n], W=[gg])
            for tt_ in range(nt):
                t0 = s0 + tt_ * 128
                a, b = yf.next(), yb.next()
                kb.load(a[:], g.Y[0][t0:t0 + 128, :], W=[a])
                kb.load(b[:], g.Y[1][t0:t0 + 128, :], W=[b])
                kb.tt(y[:], a[:], b[:], ALU.add, R=[a, b], W=[y])
                y3 = y[:].rearrange("p (h v) -> p h v", v=64)
                kb.op("dve", lambda e: e.tensor_reduce(out=sm[:], in_=y3, axis=AX.X, op=ALU.add), R=[y], W=[sm])
                kb.ts(sm[:], sm[:], -1.0 / 64.0, None, ALU.mult, R=[sm], W=[sm])
                kb.tt(y3, y3, sm[:, :, None].to_broadcast([128, 8, 64]), ALU.add, R=[y, sm], W=[y])
                kb.tt(sq[:], y[:], y[:], ALU.mult, R=[y], W=[sq])
                kb.op("dve", lambda e: e.tensor_reduce(out=vr[:], in_=sq[:].rearrange("p (h v) -> p h v", v=64), axis=AX.X, op=ALU.add), R=[sq], W=[vr])
                kb.act(rt[:], vr[:], AF.Sqrt, bias=g.eps_t[:, 2:3], scale=1.0 / 64.0, R=[vr, g.eps_t], W=[rt])
                kb.op("dve", lambda e: e.reciprocal(out=rs[:], in_=rt[:]), R=[rt], W=[rs])
                yo = yn.next()
                kb.tt(yo[:].rearrange("p (h v) -> p h v", v=64), y3, rs[:, :, None].to_broadcast([128, 8, 64]), ALU.mult, R=[y, rs], W=[yo])
                for ct in range(4):
                    kb.tr(pb[ct][:, tt_ * 128:(tt_ + 1) * 128], yo[:, ct * 128:(ct + 1) * 128], g.ident[:], R=[yo, g.ident], W=[pb[ct]])
            for ct in range(4):
                t1 = a1.next()
                kb.act(t1[:, :n], pb[ct][:, :n], AF.Identity, bias=rwc[:, 30 + ct:31 + ct], scale=rwc[:, 26 + ct:27 + ct], R=[pb[ct], rwc], W=[t1])
                kb.tt(t1[:, :n], t1[:, :n], bo[:, ct, :n], ALU.add, R=[t1, bo], W=[t1])
                o = mo.next()
                kb.tt(o[:, :n], t1[:, :n], gg[:, ct, :n], ALU.mult, R=[t1, gg], W=[o])
                kb.store(g.mixT[ct * 128:(ct + 1) * 128, s0:s0 + n], o[:, :n], R=[o])


def phase_D0(kb, g):
    LAM_INIT = 0.2
    with phase(kb):
        dc = sbt(kb, "dc", [64, 4])
        pr = sbt(kb, "dpr", [64, 2])
        onesf = sbt(kb, "onesf", [64, 128])
        lam = sbt(kb, "lam", [128, 2])
        nlam = sbt(kb, "nlam", [128, 1])
        slg = sbt(kb, "slg", [128, 1])
        kb.load(dc[:], g.diff_cols[:, :], W=[dc])
        kb.load(slg[:], g.subln_g[:, :], W=[slg])
        kb.memset(onesf[:], 1.0, W=[onesf])
        kb.tt(pr[:, 0:1], dc[:, 0:1], dc[:, 1:2], ALU.mult, R=[dc], W=[pr])
        kb.tt(pr[:, 1:2], dc[:, 2:3], dc[:, 3:4], ALU.mult, R=[dc], W=[pr])
        ps = Rot([pst(kb, f"dps{i}") for i in range(3)])
        pfin = pst(kb, "dpfin")
        acc = [[pst(kb, f"dacc{m}{q}") for q in range(2)] for m in range(2)]
        pl = ps.next()
        kb.mm(pl[:, 0:2], onesf[:], pr[:], R=[onesf, pr], W=[pl])
        kb.act(lam[:], pl[:, 0:2], AF.Exp, R=[pl], W=[lam])
        kb.tt(nlam[:], lam[:, 1:2], lam[:, 0:1], ALU.subtract, R=[lam], W=[nlam])
        kb.ts(nlam[:], nlam[:], -LAM_INIT, None, ALU.add, R=[nlam], W=[nlam])
        KT = sbt(kb, "KT", [128, T], BF16)
        QR = [sbt(kb, f"QR{m}", [128, T], BF16) for m in range(2)]
        QP = [sbt(kb, f"QP{m}", [128, T], BF16) for m in range(2)]
        for m in range(2):
            zs = slice(64, 128) if m == 0 else slice(0, 64)
            kb.memset(QR[m][zs, :], 0.0, W=[QR[m]])
            kb.memset(QP[m][zs, :], 0.0, W=[QP[m]])
        VA = sbt(kb, "VA", [128, 34, 130], BF16)
        pT = Rot([sbt(kb, f"pT{i}", [128, 512], BF16) for i in range(4)])
        rz = sbt(kb, "rz", [128, 4])
        o0 = sbt(kb, "o0", [128, 128])
        o = sbt(kb, "o", [128, 128])
        osq = sbt(kb, "osq", [128, 128])
        ssq = sbt(kb, "ssq", [128, 1])
        rt = sbt(kb, "drt", [128, 1])
        rs = sbt(kb, "drs", [128, 1])
        on = Rot([sbt(kb, f"on{i}", [128, 128]) for i in range(2)])
        so = Rot([sbt(kb, f"dso{i}", [128, 256], BF16) for i in range(2)])
        accS = Rot([sbt(kb, f"accS{i}", [128, 4, 129]) for i in range(2)])

        def finalize(aS, h, q0):
            s_ = so.next()
            for qt in range(2):
                a0_, a1_ = aS[:, qt, :], aS[:, 2 + qt, :]
                kb.op("dve", lambda e_: e_.reciprocal(out=rz[:, 0:1], in_=a0_[:, 128:129]), R=[aS], W=[rz])
                kb.op("dve", lambda e_: e_.reciprocal(out=rz[:, 1:2], in_=a1_[:, 128:129]), R=[aS], W=[rz])
                kb.tt(rz[:, 2:3], rz[:, 1:2], nlam[:], ALU.mult, R=[rz, nlam], W=[rz])
                kb.ts(o0[:], a0_[:, 0:128], rz[:, 0:1], None, ALU.mult, R=[aS, rz], W=[o0])
                kb.stt(o[:], a1_[:, 0:128], rz[:, 2:3], o0[:], ALU.mult, ALU.add, R=[aS, rz, o0], W=[o])
                yield
                kb.act(osq[:], o[:], AF.Square, R=[o], W=[osq, ssq], accum_out=ssq[:])
                yield
                kb.act(rt[:], ssq[:], AF.Sqrt, bias=g.eps_t[:, 0:1], scale=1.0 / 128.0, R=[ssq, g.eps_t], W=[rt])
                yield
                kb.op("dve", lambda e_: e_.reciprocal(out=rs[:], in_=rt[:]), R=[rt], W=[rs])
                on_ = on.next()
                kb.ts(on_[:], o[:], rs[:, 0:1], 1.0 - LAM_INIT, ALU.mult, ALU.mult, R=[o, rs], W=[on_])
                yield
                kb.tr(pfin[:, 0:128], on_[:], g.ident[:], R=[on_, g.ident], W=[pfin])
                yield
                kb.act(s_[:, qt * 128:(qt + 1) * 128], pfin[:, 0:128], AF.Copy, scale=slg[:, 0:1], R=[pfin, slg], W=[s_])
                yield
            kb.store(g.mixT[512 + h * 128:512 + (h + 1) * 128, q0:q0 + 256], s_[:], R=[s_])

        fin = iter(())
        chunks = [(0, True)] + [(256 + 256 * i, False) for i in range(16)]
        import os
        DS = int(os.environ.get("D0_STAGE", "9"))
        if DS < 9:
            chunks = chunks[:2]
        for h in range(4 if DS == 9 else 1):
            hs = slice(h * 128, (h + 1) * 128)
            kb.load(KT[:], g.KrotT[hs, :], W=[KT])
            for m in range(2):
                ms = slice(m * 64, (m + 1) * 64)
                kb.load(QR[m][ms, :], g.QrotT[h * 128 + m * 64:h * 128 + (m + 1) * 64, :], W=[QR[m]])
                kb.load(QP[m][ms, :], g.QplT[h * 128 + m * 64:h * 128 + (m + 1) * 64, :], W=[QP[m]])
            kb.load(VA[:, :, 0:128], g.Vd.rearrange("(kt p) c -> p kt c", p=128)[:, :, hs], W=[VA])
            kb.memset(VA[:, :, 128:129], 1.0, W=[VA])
            for (q0, isctx) in chunks:
                if DS < 2:
                    break
                kts = [0, 1] if isctx else list(range(34))
                def score(kt):
                    Q = QP if (not isctx and kt < 2) else QR
                    p = ps.next()
                    ks = slice(kt * 128, (kt + 1) * 128)
                    kb.mm(p[:, 0:256], KT[:, ks], Q[0][:, q0:q0 + 256], R=[KT, Q[0]], W=[p])
                    kb.mm(p[:, 256:512], KT[:, ks], Q[1][:, q0:q0 + 256], R=[KT, Q[1]], W=[p])
                    return p
                LA = 2
                pq = [score(kts[k]) for k in range(min(LA, len(kts)))]
                for i, kt in enumerate(kts):
                    p = pq.pop(0)
                    if i + LA < len(kts):
                        pq.append(score(kts[i + LA]))
                    e = pT.next()
                    kb.act(e[:], p[:], AF.Exp, scale=0.125, R=[p], W=[e])
                    next(fin, None)
                    if DS < 3:
                        continue
                    for m in range(2):
                        for qt in range(2):
                            kb.mm(acc[m][qt][:, 0:129], e[:, m * 256 + qt * 128:m * 256 + (qt + 1) * 128], VA[:, kt, 0:129],
                                  start=(i == 0), stop=(i == len(kts) - 1), R=[e, VA], W=[acc[m][qt]])
                if DS < 4:
                    continue
                for _ in fin:
                    pass
                aS = accS.next()
                for m in range(2):
                    for qt in range(2):
                        kb.copy(aS[:, m * 2 + qt, :], acc[m][qt][:, 0:129], R=[acc[m][qt]], W=[aS])
                fin = finalize(aS, h, q0)
        for _ in fin:
            pass


def phase_E(kb, g, li, w_ap, kcn, mixT, blocks):
    with phase(kb):
        Wo = sbt(kb, "Wo", [128, kcn, 1024], BF16)
        stg = Rot([sbt(kb, f"estg{i}", [128, kcn * 128]) for i in range(2)])
        load_weight_bf16(kb, Wo, w_ap, 1024, stg, piece=128)
        xb = Rot([sbt(kb, f"exb{i}", [128, 8, 512]) for i in range(2)])
        mb = Rot([sbt(kb, f"emb{i}", [128, kcn, 512], BF16) for i in range(2)])
        pmm = Rot([pst(kb, f"ep{i}") for i in range(4)])
        xsrc = g.xT.rearrange("(kc p) t -> p kc t", p=128)
        msrc = mixT.rearrange("(kc p) t -> p kc t", p=128)
        for (s0, n, j) in blocks:
            x, m = xb.next(), mb.next()
            kb.load(x[:, :, :n], xsrc[:, :, s0:s0 + n], W=[x])
            kb.load(m[:, :, :n], msrc[:, :, s0:s0 + n], W=[m])
            for ct in range(8):
                p = pmm.next()
                for kc in range(kcn):
                    kb.mm(p[:, :n], Wo[:, kc, ct * 128:(ct + 1) * 128], m[:, kc, :n], start=(kc == 0), stop=(kc == kcn - 1), R=[Wo, m], W=[p])
                kb.stt(x[:, ct, :n], p[:, :n], g.mod[li][:, 16 + ct, j:j + 1], x[:, ct, :n], ALU.mult, ALU.add, R=[p, g.mod[li], x], W=[x])
            kb.store(xsrc[:, :, s0:s0 + n], x[:, :, :n], R=[x])


def phase_F(kb, g, li, blocks):
    with phase(kb):
        W1 = sbt(kb, "W1", [128, 8, 4096], BF16)
        W2 = sbt(kb, "W2", [128, 32, 1024], BF16)
        stg = Rot([sbt(kb, f"fstg{i}", [128, 1024]) for i in range(1)])
        load_weight_bf16(kb, W1, g.mlp_w1[li], 4096, stg, piece=128)
        load_weight_bf16(kb, W2, g.mlp_w2[li], 1024, stg, piece=32)
        nb = G()
        nb.sq = sbt(kb, "fsq", [128, 8, 256], BF16)
        nb.tmp = sbt(kb, "ftmp", [128, 8, 256])
        nb.rt = sbt(kb, "frt", [128, 256])
        nb.rstd = sbt(kb, "frstd", [128, 256])
        nb.ss = pst(kb, "fss")
        xb = Rot([sbt(kb, f"fxb{i}", [128, 8, 256]) for i in range(2)])
        hTs = Rot([sbt(kb, f"fhT{i}", [128, 8, 256], BF16) for i in range(2)])
        hid = sbt(kb, "fhid", [128, 32, 256], BF16)
        rl = Rot([sbt(kb, f"frl{i}", [128, 256]) for i in range(3)])
        pmm = Rot([pst(kb, f"fp{i}") for i in range(6)])
        xsrc = g.xT.rearrange("(kc p) t -> p kc t", p=128)

        def prep(blk):
            s0, n, j = blk
            x = xb.next()
            kb.load(x[:, :, :n], xsrc[:, :, s0:s0 + n], W=[x])
            hT = hTs.next()
            norm_mod(kb, g, nb, x, n, g.sc2[li], g.mod[li], j, hT, sh_off=24)
            return x, hT

        nxt = prep(blocks[0])
        for bi, (s0, n, j) in enumerate(blocks):
            x, hT = nxt
            for hc in range(32):
                p = pmm.next()
                for kc in range(8):
                    kb.mm(p[:, :n], W1[:, kc, hc * 128:(hc + 1) * 128], hT[:, kc, :n], start=(kc == 0), stop=(kc == 7), R=[W1, hT], W=[p])
                r = rl.next()
                kb.act(r[:, :n], p[:, :n], AF.Relu, R=[p], W=[r])
                kb.tt(hid[:, hc, :n], r[:, :n], p[:, :n], ALU.mult, R=[r, p], W=[hid])
            if bi + 1 < len(blocks):
                nxt = prep(blocks[bi + 1])
            for ct in range(8):
                p = pmm.next()
                for hc in range(32):
                    kb.mm(p[:, :n], W2[:, hc, ct * 128:(ct + 1) * 128], hid[:, hc, :n], start=(hc == 0), stop=(hc == 31), R=[W2, hid], W=[p])
                kb.stt(x[:, ct, :n], p[:, :n], g.mod[li][:, 40 + ct, j:j + 1], x[:, ct, :n], ALU.mult, ALU.add, R=[p, g.mod[li], x], W=[x])
            kb.store(xsrc[:, :, s0:s0 + n], x[:, :, :n], R=[x])


BLK512 = [(s0, n, 1 if s0 == 0 else 0) for (s0, n) in TBS]
BLK256 = [(s0, 256, 1 if s0 == 0 else 0) for s0 in range(0, T, 256)]

def phase_A1(kb, g):
    with phase(kb):
        W = sbt(kb, "wA1", [128, 8, 6176], BF16)
        stg = Rot([sbt(kb, f"w1stg{i}", [128, 2048]) for i in range(2)])
        load_weight_bf16(kb, W, g.ssd_w_in, 6176, stg)
        nb = NormBufs(kb, "A1")
        xb = Rot([sbt(kb, f"a1x{i}", [128, 8, 512]) for i in range(2)])
        hTs = Rot([sbt(kb, f"a1h{i}", [128, 8, 512], BF16) for i in range(2)])
        pmm = Rot([pst(kb, f"a1p{i}") for i in range(6)])
        st32 = Rot([sbt(kb, f"a1s{i}", [128, 512]) for i in range(4)])
        st16 = Rot([sbt(kb, f"a1z{i}", [128, 512], BF16) for i in range(4)])
        xsrc = g.xT.rearrange("(kc p) t -> p kc t", p=128)
        for bi, (s0, n) in enumerate(TBS):
            j = 1 if bi == 0 else 0
            x = xb.next()
            kb.load(x[:, :, :n], xsrc[:, :, s0:s0 + n], W=[x])
            hT = hTs.next()
            norm_mod(kb, g, nb, x, n, g.sc1[1], g.mod[1], j, hT, sh_off=0)
            for ct in range(32):
                p = pmm.next()
                c0 = 2048 + ct * 128
                for kc in range(8):
                    kb.mm(p[:, :n], W[:, kc, c0:c0 + 128], hT[:, kc, :n], start=(kc == 0), stop=(kc == 7), R=[W, hT], W=[p])
                st = st32.next()
                kb.copy(st[:, :n], p[:, :n], R=[p], W=[st], eng=("act" if ct % 2 else "dve"))
                kb.store(g.xbcT[ct * 128:(ct + 1) * 128, s0:s0 + n], st[:, :n], R=[st])
            for tt_ in range(n // 128):
                ts_ = slice(tt_ * 128, (tt_ + 1) * 128)
                t0 = s0 + tt_ * 128
                for zc in range(4):
                    p = pmm.next()
                    for kc in range(8):
                        kb.mm(p[:, :], hT[:, kc, ts_], W[:, kc, zc * 512:(zc + 1) * 512], start=(kc == 0), stop=(kc == 7), R=[W, hT], W=[p])
                    so = st16.next()
                    kb.copy(so[:], p[:], R=[p], W=[so], eng=("act" if zc % 2 else "dve"))
                    kb.store(g.zTM[t0:t0 + 128, zc * 512:(zc + 1) * 512], so[:], R=[so])
                p = pmm.next()
                for kc in range(8):
                    kb.mm(p[:, 0:32], hT[:, kc, ts_], W[:, kc, 6144:6176], start=(kc == 0), stop=(kc == 7), R=[W, hT], W=[p])
                st = st32.next()
                kb.copy(st[:, 0:32], p[:, 0:32], R=[p], W=[st])
                kb.store(g.dtTM[t0:t0 + 128, :], st[:, 0:32], R=[st])


def phase_B1(kb, g):
    with phase(kb):
        cw = sbt(kb, "cw", [128, 32, 5])
        cb = sbt(kb, "cb", [128, 32])
        kb.load(cw[:], g.conv_wT[:, :, :], W=[cw])
        kb.load(cb[:], g.conv_bT[:, :], W=[cb])
        xin = Rot([sbt(kb, f"b1x{i}", [128, 8, 516]) for i in range(2)])
        acc = Rot([sbt(kb, f"b1a{i}", [128, 512]) for i in range(2)])
        u32 = Rot([sbt(kb, f"b1u{i}", [128, 512]) for i in range(2)])
        u16 = Rot([sbt(kb, f"b1v{i}", [128, 512], BF16) for i in range(3)])
        p32 = Rot([pst(kb, f"b1p{i}") for i in range(3)])
        p16 = Rot([pst(kb, f"b1q{i}", dt=BF16) for i in range(2)])
        t32 = Rot([sbt(kb, f"b1t{i}", [128, 512]) for i in range(3)])
        t16 = Rot([sbt(kb, f"b1s{i}", [128, 512], BF16) for i in range(3)])
        src = g.xbcT.rearrange("(ti p) t -> p ti t", p=128)
        for bi, (s0, n) in enumerate(TBS):
            nt = n // 128
            seq0, seq1 = (0, NCTX) if s0 < NCTX else (NCTX, T)
            lo = max(seq0, s0 - 2)
            hi = min(seq1, s0 + n + 2)
            for grp in range(4):
                xi = xin.next()
                kb.memset(xi[:, :, 0:2], 0.0, W=[xi])
                kb.memset(xi[:, :, n + 2:n + 4], 0.0, W=[xi])
                kb.load(xi[:, :, lo - s0 + 2:hi - s0 + 2], src[:, grp * 8:(grp + 1) * 8, lo:hi], W=[xi])
                for t8 in range(8):
                    ti = grp * 8 + t8
                    a = acc.next()
                    kb.ts(a[:, :n], xi[:, t8, 0:n], cw[:, ti, 0:1], cb[:, ti:ti + 1], ALU.mult, ALU.add, R=[xi, cw, cb], W=[a])
                    for k in range(1, 5):
                        kb.stt(a[:, :n], xi[:, t8, k:k + n], cw[:, ti, k:k + 1], a[:, :n], ALU.mult, ALU.add, R=[xi, cw, a], W=[a])
                    if ti < 16:
                        u = u32.next()
                        kb.act(u[:, :n], a[:, :n], AF.Silu, R=[a], W=[u])
                        p = p32.next()
                        for tt_ in range(nt):
                            kb.tr(p[:, tt_ * 128:(tt_ + 1) * 128], u[:, tt_ * 128:(tt_ + 1) * 128], g.ident[:], R=[u, g.ident], W=[p])
                        t = t32.next()
                        kb.copy(t[:, :n], p[:, :n], R=[p], W=[t], eng=("act" if ti % 2 else "dve"))
                        for tt_ in range(nt):
                            kb.store(g.xsTM[s0 + tt_ * 128:s0 + (tt_ + 1) * 128, ti * 128:(ti + 1) * 128], t[:, tt_ * 128:(tt_ + 1) * 128], R=[t])
                    else:
                        u = u16.next()
                        kb.act(u[:, :n], a[:, :n], AF.Silu, R=[a], W=[u])
                        if ti < 24:
                            gi = ti - 16
                            kb.store(g.BT[gi * 128:(gi + 1) * 128, s0:s0 + n], u[:, :n], R=[u])
                            p = p16.next()
                            for tt_ in range(nt):
                                kb.tr(p[:, tt_ * 128:(tt_ + 1) * 128], u[:, tt_ * 128:(tt_ + 1) * 128], g.ident_bf[:], R=[u, g.ident_bf], W=[p])
                            t = t16.next()
                            kb.copy(t[:, :n], p[:, :n], R=[p], W=[t], eng="act")
                            for tt_ in range(nt):
                                kb.store(g.BTM[s0 + tt_ * 128:s0 + (tt_ + 1) * 128, gi * 128:(gi + 1) * 128], t[:, tt_ * 128:(tt_ + 1) * 128], R=[t])
                        else:
                            gi = ti - 24
                            kb.store(g.CT[gi * 128:(gi + 1) * 128, s0:s0 + n], u[:, :n], R=[u])


def phase_C1(kb, g):
    with phase(kb):
        UT1 = sbt(kb, "UT1", [128, 128])
        LT1 = sbt(kb, "LT1", [128, 128])
        onesf = sbt(kb, "c1ones", [128, 128])
        prm = sbt(kb, "prm", [128, 5, 32])
        aneg = sbt(kb, "aneg", [128, 2, 32])
        kb.load(UT1[:], g.ut1_h[:, :], W=[UT1])
        kb.load(LT1[:], g.lt1_h[:, :], W=[LT1])
        SLT = [sbt(kb, "sLT", [128, 128]), sbt(kb, "sUT", [128, 128])]
        kb.tt(SLT[0][:], LT1[:], g.ident[:], ALU.subtract, R=[LT1, g.ident], W=[SLT[0]])
        kb.tt(SLT[1][:], UT1[:], g.ident[:], ALU.subtract, R=[UT1, g.ident], W=[SLT[1]])
        kb.load(prm[:], g.ssd_prm[:, :, :], W=[prm])
        kb.memset(onesf[:], 1.0, W=[onesf])
        kb.act(aneg[:], prm[:, 2:4, :], AF.Exp, R=[prm], W=[aneg])
        kb.ts(aneg[:], aneg[:], -1.0, None, ALU.mult, R=[aneg], W=[aneg])
        tri = [UT1, LT1]
        triB = [sbt(kb, "UT1b", [128, 128], BF16), sbt(kb, "LT1b", [128, 128], BF16)]
        kb.copy(triB[0][:], UT1[:], R=[UT1], W=[triB[0]])
        kb.copy(triB[1][:], LT1[:], R=[LT1], W=[triB[1]])
        pydd = [[pst(kb, f"c1y{d}{i}") for i in range(2)] for d in range(2)]
        pbig = Rot([pst(kb, f"c1b{i}") for i in range(2)])
        psm = Rot([pst(kb, f"c1s{i}") for i in range(2)])
        st = []
        for d in range(2):
            s = G()
            s.xs = Rot([sbt(kb, f"xs{d}_{i}", [128, 32, 64]) for i in range(1)])
            s.D = sbt(kb, f"Dcs{d}", [128, 32, 128], BF16)
            s.bt = Rot([sbt(kb, f"bt{d}_{i}", [128, 8, 128], BF16) for i in range(2)])
            s.ct = Rot([sbt(kb, f"ct{d}_{i}", [128, 8, 128], BF16) for i in range(2)])
            s.btm = Rot([sbt(kb, f"btm{d}_{i}", [128, 1024], BF16) for i in range(2)])
            s.dt = Rot([sbt(kb, f"dt{d}_{i}", [128, 32]) for i in range(2)])
            s.hf = sbt(kb, f"hf{d}", [128, 32, 64])
            s.hb = Rot([sbt(kb, f"hb{d}_{i}", [128, 32, 64], BF16) for i in range(2)])
            s.xdt = sbt(kb, f"xdt{d}", [128, 32, 64], BF16)
            s.xdw = sbt(kb, f"xdw{d}", [128, 32, 64], BF16)
            s.yo = sbt(kb, f"yo{d}", [128, 8, 64])
            s.y = Rot([sbt(kb, f"y{d}_{i}", [128, 32, 64]) for i in range(1)])
            s.cbm = sbt(kb, f"cbm{d}", [128, 8, 128], BF16)
            kb.memset(s.hf[:], 0.0, W=[s.hf])
            s.h = s.hb.next()
            kb.memset(s.h[:], 0.0, W=[s.h])
            st.append(s)
        sm = lambda nm, w=32: sbt(kb, nm, [128, w])
        ex, dtd, dta, cs, ncs, ecs, wts, etot, csT = [[sm(f"{nm}{d}") for d in range(2)] for nm in
                                                       ("ex", "dtd", "dta", "cs", "ncs", "ecs", "wts", "etot", "csTx")]
        csTs = [sbt(kb, f"csT{d}", [32, 128]) for d in range(2)]
        E4 = Rot([sbt(kb, f"E4{i}", [128, 512], BF16) for i in range(3)])
        G4 = Rot([sbt(kb, f"G4{i}", [128, 4, 128], BF16) for i in range(3)])
        order = [list(range(34)), [1, 0] + list(range(33, 1, -1))]

        def chunk(d, c):
            s = st[d]
            t0 = c * 128
            yield
            xs, bt, ct, btm, dt = s.xs.next(), s.bt.next(), s.ct.next(), s.btm.next(), s.dt.next()
            kb.load(xs[:].rearrange("p h q -> p (h q)"), g.xsTM[t0:t0 + 128, :], W=[xs])
            kb.load(bt[:], g.BT.rearrange("(g n) t -> n g t", n=128)[:, :, t0:t0 + 128], W=[bt])
            kb.load(ct[:], g.CT.rearrange("(g n) t -> n g t", n=128)[:, :, t0:t0 + 128], W=[ct])
            kb.load(btm[:], g.BTM[t0:t0 + 128, :], W=[btm])
            kb.load(dt[:], g.dtTM[t0:t0 + 128, :], W=[dt])
            kb.tt(ex[d][:], dt[:], prm[:, d, :], ALU.add, R=[dt, prm], W=[ex[d]])
            kb.act(ex[d][:], ex[d][:], AF.Exp, R=[ex[d]], W=[ex[d]])
            kb.act(dtd[d][:], ex[d][:], AF.Ln, bias=g.eps_t[:, 4:5], scale=1.0, R=[ex[d], g.eps_t], W=[dtd[d]])
            kb.tt(dta[d][:], dtd[d][:], aneg[:, d, :], ALU.mult, R=[dtd[d], aneg], W=[dta[d]])
            p = psm.next()
            kb.mm(p[:, 0:32], tri[d][:], dta[d][:], R=[tri[d], dta[d]], W=[p])
            kb.mm(p[:, 32:64], onesf[:], dta[d][:], R=[onesf, dta[d]], W=[p])
            kb.copy(cs[d][:], p[:, 0:32], R=[p], W=[cs[d]])
            kb.act(ecs[d][:], p[:, 0:32], AF.Exp, R=[p], W=[ecs[d]])
            kb.act(etot[d][:], p[:, 32:64], AF.Exp, R=[p], W=[etot[d]])
            kb.tt(wts[d][:], p[:, 32:64], cs[d][:], ALU.subtract, R=[p, cs[d]], W=[wts[d]])
            kb.act(wts[d][:], wts[d][:], AF.Exp, R=[wts[d]], W=[wts[d]])
            kb.tt(s.D[:], SLT[d][:, None, :].to_broadcast([128, 32, 128]), dta[d][:, :, None].to_broadcast([128, 32, 128]), ALU.mult,
                  R=[SLT[d], dta[d]], W=[s.D])
            yield
            kb.tt(s.xdt[:], xs[:], dtd[d][:, :, None].to_broadcast([128, 32, 64]), ALU.mult, R=[xs, dtd[d]], W=[s.xdt])
            kb.tt(s.xdw[:], s.xdt[:], wts[d][:, :, None].to_broadcast([128, 32, 64]), ALU.mult, R=[s.xdt, wts[d]], W=[s.xdw])
            for hf in range(2):
                p = pbig.next()
                for g4 in range(4):
                    gi = hf * 4 + g4
                    kb.mm(p[:, g4 * 128:(g4 + 1) * 128], bt[:, gi, :], ct[:, gi, :], R=[bt, ct], W=[p])
                kb.tt(s.cbm[:, hf * 4:(hf + 1) * 4, :], p[:, :].rearrange("p (g l) -> p g l", l=128),
                      tri[d][:, None, :].to_broadcast([128, 4, 128]), ALU.mult, R=[p, tri[d]], W=[s.cbm])
            h_old = s.h
            yt = s.y.next()
            for hf in range(2):
                pyd = pydd[d]
                for g4 in range(4):
                    gi = hf * 4 + g4
                    pc = psm.next()
                    for h4 in range(4):
                        kb.mm(pc[:, h4 * 128:(h4 + 1) * 128], s.D[:, gi * 4 + h4, :], triB[d][:], R=[s.D, triB[d]], W=[pc])
                    e4 = E4.next()
                    kb.act(e4[:], pc[:, :], AF.Exp, R=[pc], W=[e4])
                    g4t = G4.next()
                    kb.tt(g4t[:], e4[:].rearrange("p (h l) -> p h l", l=128), s.cbm[:, gi, None, :].to_broadcast([128, 4, 128]),
                          ALU.mult, R=[e4, s.cbm], W=[g4t])
                    for h4 in range(4):
                        h = gi * 4 + h4
                        h16 = h - hf * 16
                        pb = pyd[h16 // 8]
                        kb.mm(pb[:, (h16 % 8) * 64:(h16 % 8 + 1) * 64], g4t[:, h4, :], s.xdt[:, h, :], R=[g4t, s.xdt], W=[pb])
                    yield
                pyo = [pbig.next(), pbig.next()]
                for g4 in range(4):
                    gi = hf * 4 + g4
                    pb = pyo[g4 // 2]
                    kb.mm(pb[:, (g4 % 2) * 256:(g4 % 2 + 1) * 256], ct[:, gi, :], h_old[:, gi * 4:(gi + 1) * 4, :], R=[ct, h_old], W=[pb])
                for q in range(2):
                    hs = slice(hf * 16 + q * 8, hf * 16 + (q + 1) * 8)
                    kb.tt(s.yo[:], pyo[q][:, :].rearrange("p (h q) -> p h q", q=64), ecs[d][:, hs, None].to_broadcast([128, 8, 64]),
                          ALU.mult, R=[pyo[q], ecs[d]], W=[s.yo])
                    kb.tt(yt[:, hs, :], pyd[q][:, :].rearrange("p (h q) -> p h q", q=64), s.yo[:], ALU.add, R=[pyd[q], s.yo], W=[yt])
            kb.store(g.Yssd[d][t0:t0 + 128, :], yt[:].rearrange("p h q -> p (h q)"), R=[yt])
            yield
            nh = s.hb.next()
            for q in range(4):
                p = pbig.next()
                for g2 in range(2):
                    gi = q * 2 + g2
                    kb.mm(p[:, g2 * 256:(g2 + 1) * 256], btm[:, gi * 128:(gi + 1) * 128], s.xdw[:, gi * 4:(gi + 1) * 4, :], R=[btm, s.xdw], W=[p])
                hs = slice(q * 8, (q + 1) * 8)
                kb.tt(s.hf[:, hs, :], s.hf[:, hs, :], etot[d][:, hs, None].to_broadcast([128, 8, 64]), ALU.mult, R=[s.hf, etot[d]], W=[s.hf])
                kb.tt(s.hf[:, hs, :], s.hf[:, hs, :], p[:, :].rearrange("p (h q) -> p h q", q=64), ALU.add, R=[s.hf, p], W=[s.hf])
            kb.copy(nh[:], s.hf[:], R=[s.hf], W=[nh], eng="act")
            s.h = nh

        for i in range(34):
            run_interleaved([chunk(0, order[0][i]), chunk(1, order[1][i])])


def phase_C3(kb, g):
    with phase(kb):
        prm = sbt(kb, "c3prm", [128, 5, 32])
        ng = sbt(kb, "c3ng", [128, 16])
        kb.load(prm[:], g.ssd_prm[:, :, :], W=[prm])
        kb.load(ng[:], g.ssd_ngT[:, :], W=[ng])
        y0 = Rot([sbt(kb, f"c3a{i}", [128, 32, 64]) for i in range(2)])
        y1 = Rot([sbt(kb, f"c3b{i}", [128, 32, 64]) for i in range(2)])
        xs = Rot([sbt(kb, f"c3x{i}", [128, 32, 64]) for i in range(2)])
        z = Rot([sbt(kb, f"c3z{i}", [128, 2048], BF16) for i in range(2)])
        sz = sbt(kb, "c3sz", [128, 2048])
        sq = sbt(kb, "c3sq", [128, 2048])
        ss = sbt(kb, "c3ss", [128, 8])
        rt = sbt(kb, "c3rt", [128, 8])
        rs = sbt(kb, "c3rs", [128, 8])
        yn = sbt(kb, "c3yn", [128, 4, 2048])
        pb = Rot([pst(kb, f"c3p{i}") for i in range(4)])
        so = Rot([sbt(kb, f"c3o{i}", [128, 512], BF16) for i in range(3)])
        for bi, (s0, n) in enumerate(TBS):
            nt = n // 128
            for tt_ in range(nt):
                t0 = s0 + tt_ * 128
                a, b, x, zz = y0.next(), y1.next(), xs.next(), z.next()
                kb.load(a[:].rearrange("p h q -> p (h q)"), g.Yssd[0][t0:t0 + 128, :], W=[a])
                kb.load(b[:].rearrange("p h q -> p (h q)"), g.Yssd[1][t0:t0 + 128, :], W=[b])
                kb.load(x[:].rearrange("p h q -> p (h q)"), g.xsTM[t0:t0 + 128, :], W=[x])
                kb.load(zz[:], g.zTM[t0:t0 + 128, :], W=[zz])
                kb.tt(a[:], a[:], b[:], ALU.add, R=[a, b], W=[a])
                kb.tt(x[:], x[:], prm[:, 4, :, None].to_broadcast([128, 32, 64]), ALU.mult, R=[x, prm], W=[x])
                kb.tt(a[:], a[:], x[:], ALU.add, R=[a, x], W=[a])
                kb.act(sz[:], zz[:], AF.Silu, R=[zz], W=[sz])
                af = a[:].rearrange("p h q -> p (h q)")
                kb.tt(af, af, sz[:], ALU.mult, R=[a, sz], W=[a])
                kb.tt(sq[:], af, af, ALU.mult, R=[a], W=[sq])
                kb.op("dve", lambda e: e.tensor_reduce(out=ss[:], in_=sq[:].rearrange("p (g c) -> p g c", c=256), axis=AX.X, op=ALU.add), R=[sq], W=[ss])
                kb.act(rt[:], ss[:], AF.Sqrt, bias=g.eps_t[:, 0:1], scale=1.0 / 256.0, R=[ss, g.eps_t], W=[rt])
                kb.op("dve", lambda e: e.reciprocal(out=rs[:], in_=rt[:]), R=[rt], W=[rs])
                kb.tt(yn[:, tt_, :].rearrange("p (g c) -> p g c", c=256), af.rearrange("p (g c) -> p g c", c=256),
                      rs[:, :, None].to_broadcast([128, 8, 256]), ALU.mult, R=[a, rs], W=[yn])
            for ct in range(16):
                p = pb.next()
                for tt_ in range(nt):
                    kb.tr(p[:, tt_ * 128:(tt_ + 1) * 128], yn[:, tt_, ct * 128:(ct + 1) * 128], g.ident[:], R=[yn, g.ident], W=[p])
                o = so.next()
                kb.act(o[:, :n], p[:, :n], AF.Copy, scale=ng[:, ct:ct + 1], R=[p, ng], W=[o])
                kb.store(g.mixT1[ct * 128:(ct + 1) * 128, s0:s0 + n], o[:, :n], R=[o])


def phase_G(kb, g):
    with phase(kb):
        gf = sbt(kb, "gf", [128, 8])
        kb.load(gf[:], g.gfT[:, :], W=[gf])
        xb = Rot([sbt(kb, f"gx{i}", [128, 8, 512]) for i in range(2)])
        sq = sbt(kb, "gsq", [128, 8, 512], BF16)
        ssp = pst(kb, "gss")
        rt = sbt(kb, "grt", [128, 512])
        rs = sbt(kb, "grs", [128, 512])
        xn = sbt(kb, "gxn", [128, 8, 512])
        pt = Rot([pst(kb, f"gp{i}") for i in range(4)])
        o = Rot([sbt(kb, f"go{i}", [128, 1024]) for i in range(2)])
        xsrc = g.xT.rearrange("(kc p) t -> p kc t", p=128)
        for (s0, n) in TBS[1:]:
            x = xb.next()
            kb.load(x[:, :, :n], xsrc[:, :, s0:s0 + n], W=[x])
            kb.act(sq[:, :, :n], x[:, :, :n], AF.Square, R=[x], W=[sq])
            for kc in range(8):
                kb.mm(ssp[:, :n], g.ones_bf[:], sq[:, kc, :n], start=(kc == 0), stop=(kc == 7), R=[sq, g.ones_bf], W=[ssp])
            kb.act(rt[:, :n], ssp[:, :n], AF.Sqrt, bias=g.eps_t[:, 0:1], scale=1.0 / 1024.0, R=[ssp, g.eps_t], W=[rt])
            kb.op("dve", lambda e: e.reciprocal(out=rs[:, :n], in_=rt[:, :n]), R=[rt], W=[rs])
            kb.tt(xn[:, :, :n], x[:, :, :n], rs[:, None, :n].to_broadcast([128, 8, n]), ALU.mult, R=[x, rs], W=[xn])
            for kc in range(8):
                if kc % 2:
                    kb.act(xn[:, kc, :n], xn[:, kc, :n], AF.Identity, scale=gf[:, kc:kc + 1], R=[xn, gf], W=[xn])
                else:
                    kb.ts(xn[:, kc, :n], xn[:, kc, :n], gf[:, kc:kc + 1], None, ALU.mult, R=[xn, gf], W=[xn])
            for tt_ in range(n // 128):
                oo = o.next()
                for hf in range(2):
                    p = pt.next()
                    for j in range(4):
                        kc = hf * 4 + j
                        kb.tr(p[:, j * 128:(j + 1) * 128], xn[:, kc, tt_ * 128:(tt_ + 1) * 128], g.ident[:], R=[xn, g.ident], W=[p])
                    kb.copy(oo[:, hf * 512:(hf + 1) * 512], p[:, :], R=[p], W=[oo], eng=("act" if hf else "dve"))
                t0 = s0 - NCTX + tt_ * 128
                kb.store(g.out[t0:t0 + 128, :], oo[:], R=[oo])


BLK512L = BLK512[1:]
BLK256L = BLK256[1:]

def declare_inputs(nc, g, shapes):
    for name, (shape, dt) in shapes.items():
        setattr(g, name, nc.dram_tensor(name, list(shape), dt, kind="ExternalInput").ap())


def input_shapes():
    S = {}
    S["x"] = ([4096, 1024], F32)
    S["ctx"] = ([256, 1024], F32)
    S["cT"] = ([128, 8, 2], F32)
    S["ada_w"] = ([2, 1024, 6144], F32)
    S["ada_bT"] = ([2, 128, 48], F32)
    S["g1T"] = ([2, 128, 8], F32)
    S["g2T"] = ([2, 128, 8], F32)
    S["gfT"] = ([128, 8], F32)
    S["w_in0"] = ([1024, 4352], F32)
    S["cosT"] = ([128, T], F32)
    S["sinT"] = ([128, T], F32)
    S["ident_h"] = ([128, 128], F32)
    S["hy_w_out"] = ([1024, 1024], F32)
    S["mlp_w1"] = ([2, 1024, 4096], F32)
    S["mlp_w2"] = ([2, 4096, 1024], F32)
    S["rw_cols"] = ([128, 14 + 4 * 7 + 16], F32)
    S["rw_lora"] = ([128, 2, 512], F32)
    S["rw_gup"] = ([128, 512], F32)
    S["blk_h"] = ([128, 128], F32)
    S["cmask_h"] = ([128, 512], F32)
    S["masks_h"] = ([64, 2, 3, 64], F32)
    S["diff_cols"] = ([64, 4], F32)
    S["subln_g"] = ([128, 1], F32)
    S["ssd_w_in"] = ([1024, 6176], F32)
    S["ssd_w_out"] = ([2048, 1024], F32)
    S["conv_wT"] = ([128, 32, 5], F32)
    S["conv_bT"] = ([128, 32], F32)
    S["ssd_prm"] = ([128, 5, 32], F32)
    S["ssd_ngT"] = ([128, 16], F32)
    S["ut1_h"] = ([128, 128], F32)
    S["lt1_h"] = ([128, 128], F32)
    S["sel_h"] = ([32, 32, 128], F32)
    return S


def build(debug=False, stop_after=None, only=None, as_input=()):
    nc = bass.Bass("TRN2", target_bir_lowering=False)
    _AS_INPUT.clear()
    _AS_INPUT.update(as_input)
    g = G()
    declare_inputs(nc, g, input_shapes())
    g.out = nc.dram_tensor("out", [4096, 1024], F32, kind="ExternalOutput").ap()
    dbg = debug
    g.xT = dram(nc, "xT", [1024, T], F32, dbg)
    g.PrT = dram(nc, "PrT", [1792, T], F32, dbg)
    g.QrotT = dram(nc, "QrotT", [512, T], BF16, dbg)
    g.QplT = dram(nc, "QplT", [512, T], BF16, dbg)
    g.KrotT = dram(nc, "KrotT", [512, T], BF16, dbg)
    g.Vd = dram(nc, "Vd", [T, 512], BF16, dbg)
    g.modD = dram(nc, "modD", [2, 128, 96], F32, dbg)
    g.gT = dram(nc, "gT", [512, T], F32, dbg)
    g.bonT = dram(nc, "bonT", [512, T], F32, dbg)
    g.Vtm = dram(nc, "Vtm", [T, 512], BF16, dbg)
    g.gamA = dram(nc, "gamA", [2, 512, 68], F32, dbg)
    g.gam = [g.gamA[0], g.gamA[1]]
    g.FMA = dram(nc, "FMA", [2, 4, 512, T], BF16, dbg)
    g.FM = [[g.FMA[d, k] for k in range(4)] for d in range(2)]
    g.TMA = dram(nc, "TMA", [2, T, 2, 512], BF16, dbg)
    g.TM = [g.TMA[0], g.TMA[1]]
    g.mixT = dram(nc, "mixT", [1024, T], BF16, dbg)
    g.xbcT = dram(nc, "xbcT", [4096, T], F32, False)
    g.zTM = dram(nc, "zTM", [T, 2048], BF16, False)
    g.dtTM = dram(nc, "dtTM", [T, 32], F32, dbg)
    g.xsTM = dram(nc, "xsTM", [T, 2048], F32, dbg)
    g.BT = dram(nc, "BT", [1024, T], BF16, dbg)
    g.CT = dram(nc, "CT", [1024, T], BF16, dbg)
    g.BTM = dram(nc, "BTM", [T, 1024], BF16, False)
    g.YsA = dram(nc, "YsA", [2, T, 2048], F32, dbg)
    g.Yssd = [g.YsA[0], g.YsA[1]]
    g.mixT1 = dram(nc, "mixT1", [2048, T], BF16, dbg)
    g.YA = dram(nc, "YA", [2, T, 512], F32, dbg)
    g.Y = [g.YA[0], g.YA[1]]
    with ExitStack() as es:
        kb = KB(nc, es)
        kb.es_t = None
        g.ident = kb.sb("ident", [128, 128])
        g.ones_bf = kb.sb("ones_bf", [128, 128], BF16)
        g.eps_t = kb.sb("eps_t", [128, 8])
        g.mod = [kb.sb(f"mod{li}", [128, 48, 2]) for li in range(2)]
        g.sc1 = [kb.sb(f"sc1_{li}", [128, 8, 2]) for li in range(2)]
        g.sc2 = [kb.sb(f"sc2_{li}", [128, 8, 2]) for li in range(2)]
        kb.load(g.ident[:], g.ident_h[:, :], W=[g.ident])
        kb.memset(g.ones_bf[:], 1.0, W=[g.ones_bf])
        g.ident_bf = kb.sb("ident_bf", [128, 128], BF16)
        kb.copy(g.ident_bf[:], g.ident[:], R=[g.ident], W=[g.ident_bf])
        kb.memset(g.eps_t[:, 0:1], EPS, W=[g.eps_t])
        kb.memset(g.eps_t[:, 1:2], 1e-12, W=[g.eps_t])
        kb.memset(g.eps_t[:, 2:3], 64e-5, W=[g.eps_t])
        kb.memset(g.eps_t[:, 3:4], 0.0, W=[g.eps_t])
        kb.memset(g.eps_t[:, 4:5], 1.0, W=[g.eps_t])
        phases = [("mods", phase_mods), ("xT", phase_xT), ("A0", phase_A0), ("B0", phase_B0), ("C0", phase_C0), ("C2", phase_C2), ("D0", phase_D0),
                  ("E0", lambda kb, g: phase_E(kb, g, 0, g.hy_w_out, 8, g.mixT, BLK512)),
                  ("F0", lambda kb, g: phase_F(kb, g, 0, BLK256)),
                  ("A1", phase_A1), ("B1", phase_B1), ("C1", phase_C1), ("C3", phase_C3),
                  ("E1", lambda kb, g: phase_E(kb, g, 1, g.ssd_w_out, 16, g.mixT1, BLK512L)),
                  ("F1", lambda kb, g: phase_F(kb, g, 1, BLK256L)), ("G", phase_G)]
        for name, fn in phases:
            if only is not None and name not in only:
                continue
            fn(kb, g)
            if stop_after == name:
                break
        if debug:
            for li in range(2):
                kb.store(g.modD[li], g.mod[li][:].rearrange("p a b -> p (a b)"), R=[g.mod[li]])
        kb.finish()
        print("instructions:", kb.nins)
    return nc


def rope_tables():
    inv = 10000.0 ** (-np.arange(0, 32, 2, dtype=np.float32) / 32.0)
    t = np.arange(4096)
    rows = (t // 64).astype(np.float32)
    cols = (t % 64).astype(np.float32)
    ar = rows[:, None] * inv[None, :]
    ac = cols[:, None] * inv[None, :]
    cosT = np.ones((128, T), np.float32)
    sinT = np.zeros((128, T), np.float32)
    for p in range(128):
        d = p % 64
        ang = ar if d < 32 else ac
        i = d % 16
        first = (d % 32) < 16
        cosT[p, 256:] = np.cos(ang[:, i])
        sinT[p, 256:] = (-np.sin(ang[:, i])) if first else np.sin(ang[:, i])
    return cosT, sinT


def swap_cols(w):
    idx = np.arange(512)
    d = idx % 32
    partner = np.where(d < 16, idx + 16, idx - 16)
    return w[:, partner]


def host_consts(inp):
    C = {}
    f = np.float32
    C["ada_w"] = np.ascontiguousarray(inp["ada_w"], dtype=f)
    C["ada_bT"] = np.ascontiguousarray(inp["ada_b"].reshape(2, 48, 128).transpose(0, 2, 1), dtype=f)
    C["g1T"] = np.ascontiguousarray(inp["norm1_g"].reshape(2, 8, 128).transpose(0, 2, 1), dtype=f)
    C["g2T"] = np.ascontiguousarray(inp["norm2_g"].reshape(2, 8, 128).transpose(0, 2, 1), dtype=f)
    C["gfT"] = np.ascontiguousarray(inp["norm_f_g"].reshape(8, 128).T, dtype=f)
    w = inp["hy_w_in"][0]
    q = w[:, 1792:2304]
    k = w[:, 2304:2816]
    v = w[:, 2816:3328]
    C["w_in0"] = np.ascontiguousarray(np.concatenate([w[:, :1792], q, swap_cols(q), k, swap_cols(k), v], axis=1), dtype=f)
    C["cosT"], C["sinT"] = rope_tables()
    C["ident_h"] = np.eye(128, dtype=f)
    C["hy_w_out"] = np.ascontiguousarray(inp["hy_w_out"][0], dtype=f)
    C["mlp_w1"] = np.ascontiguousarray(inp["mlp_w1"], dtype=f)
    C["mlp_w2"] = np.ascontiguousarray(inp["mlp_w2"], dtype=f)
    col = lambda a: np.ascontiguousarray(np.asarray(a, dtype=f).reshape(-1, 128).T)
    rw = np.zeros((128, 58), f)
    rw[:, 0:14] = col(inp["rwkv_mu"][0])
    rw[:, 14:18] = col(inp["rwkv_k_k"][0])
    rw[:, 18:22] = col(inp["rwkv_k_a"][0])
    rw[:, 22:26] = col(inp["rwkv_r_k"][0].reshape(-1))
    rw[:, 26:30] = col(inp["rwkv_ln_w"][0])
    rw[:, 30:34] = col(inp["rwkv_ln_b"][0])
    rw[:, 34:38] = col(inp["rwkv_w0"][0, 0])
    rw[:, 38:42] = col(inp["rwkv_w0"][0, 1])
    rw[:, 42:46] = col(inp["rwkv_a0"][0, 0])
    rw[:, 46:50] = col(inp["rwkv_a0"][0, 1])
    C["rw_cols"] = rw
    lora = np.zeros((128, 2, 512), f)
    lora[0:64] = inp["rwkv_w_up"][0].transpose(1, 0, 2)
    lora[64:128] = inp["rwkv_a_up"][0].transpose(1, 0, 2)
    C["rw_lora"] = lora
    C["rw_gup"] = np.ascontiguousarray(inp["rwkv_g_up"][0], dtype=f)
    blk = np.zeros((128, 128), f)
    blk[:64, :64] = 1
    blk[64:, 64:] = 1
    C["blk_h"] = blk
    cm = np.ones((128, 512), f)
    cm[:, ::64] = 0
    C["cmask_h"] = cm
    s = np.arange(64)[:, None]
    t = np.arange(64)[None, :]
    m = np.zeros((64, 2, 3, 64), f)
    m[:, 0, 0] = (s < t)
    m[:, 0, 1] = (s <= t)
    m[:, 0, 2] = (s > t)
    m[:, 1, 0] = (s > t)
    m[:, 1, 1] = (s >= t)
    m[:, 1, 2] = (s < t)
    C["masks_h"] = m
    C["diff_cols"] = np.stack([inp["diff_lq1"][0], inp["diff_lk1"][0], inp["diff_lq2"][0], inp["diff_lk2"][0]], axis=1).astype(f)
    C["subln_g"] = np.ascontiguousarray(inp["diff_subln_g"][0].reshape(128, 1), dtype=f)
    C["ssd_w_in"] = np.ascontiguousarray(inp["ssd_w_in"][0], dtype=f)
    C["ssd_w_out"] = np.ascontiguousarray(inp["ssd_w_out"][0], dtype=f)
    C["conv_wT"] = np.ascontiguousarray(inp["ssd_conv_w"][0].reshape(5, 32, 128).transpose(2, 1, 0), dtype=f)
    C["conv_bT"] = col(inp["ssd_conv_b"][0])
    prm = np.zeros((128, 5, 32), f)
    prm[:, 0] = inp["ssd_dt_bias"][0, 0][None, :]
    prm[:, 1] = inp["ssd_dt_bias"][0, 1][None, :]
    prm[:, 2] = inp["ssd_a_log"][0, 0][None, :]
    prm[:, 3] = inp["ssd_a_log"][0, 1][None, :]
    prm[:, 4] = inp["ssd_d"][0][None, :]
    C["ssd_prm"] = prm
    C["ssd_ngT"] = col(inp["ssd_norm_g"][0])
    jj = np.arange(128)[:, None]
    ll = np.arange(128)[None, :]
    C["ut1_h"] = (jj <= ll).astype(f)
    C["lt1_h"] = (jj >= ll).astype(f)
    sel = np.zeros((32, 32, 128), f)
    for h in range(32):
        sel[h, h, :] = 1.0
    C["sel_h"] = sel
    return C


def core_inputs(inp, C, b):
    m = dict(C)
    m["x"] = np.ascontiguousarray(inp["x"][b], dtype=np.float32)
    m["ctx"] = np.ascontiguousarray(inp["ctx"][b], dtype=np.float32)
    cv = np.stack([inp["c"][b], inp["c_ctx"]], axis=0).astype(np.float32)
    m["cT"] = np.ascontiguousarray(cv.reshape(2, 8, 128).transpose(2, 1, 0))
    return m


def kernel(**inputs):
    inp = {k: np.asarray(v) for k, v in inputs.items()}
    C = host_consts(inp)
    nc = build()
    in_maps = [core_inputs(inp, C, b) for b in range(8)]
    res = run_bass_kernel_spmd(nc, in_maps, core_ids=list(range(8)))
    return np.stack([np.asarray(r["out"]) for r in res.results], axis=0).astype(np.float32)
```

```python
import concourse.bass as bass
import concourse.mybir as mybir

F32 = mybir.dt.float32
BF16 = mybir.dt.bfloat16
AF = mybir.ActivationFunctionType
ALU = mybir.AluOpType
AX = mybir.AxisListType


class Src:
    def __init__(s, kb, name, inc, limit):
        s.kb, s.name, s.inc, s.limit = kb, name, inc, limit
        s.sems = []
        s.n = 0

    def sem_for(s, n):
        e = (n - 1) // s.limit
        while len(s.sems) <= e:
            s.sems.append(s.kb.es.enter_context(s.kb.nc.semaphore(f"{s.name}_{len(s.sems)}")))
        return s.sems[e], ((n - 1) % s.limit + 1) * s.inc


class Tk:
    __slots__ = ("w", "r")

    def __init__(s):
        s.w = {}
        s.r = {}


class TT:
    def __init__(s, t, k=None, ps=False):
        s.t = t
        s.k = k if k is not None else Tk()
        s.ps = ps

    def __getitem__(s, idx):
        return s.t[idx]


class KB:
    NSLOT = 20

    def __init__(s, nc, es):
        s.nc, s.es = nc, es
        s.eng = {"pe": nc.tensor, "dve": nc.vector, "act": nc.scalar, "pool": nc.gpsimd, "sp": nc.sync}
        s.src = {k: Src(s, "c" + k, 1, 30000) for k in s.eng}
        s.waited = {k: {} for k in s.eng}
        s.slots = {q: [Src(s, f"d{q}{i}", 16, 1800) for i in range(s.NSLOT)] for q in ("sp", "pool", "act")}
        s.rr = {q: 0 for q in s.slots}
        s.nins = 0
        s.same_engine_sync = True

    def sb(s, name, shape, dt=F32):
        return TT(s.es.enter_context(s.nc.sbuf_tensor("g_" + name, list(shape), dt)))

    def ps(s, name, shape, dt=F32):
        return TT(s.es.enter_context(s.nc.psum_tensor("gp_" + name, list(shape), dt)))

    def _deps(s, R, W, me=None):
        d = {}
        for r in R:
            k = r.k if isinstance(r, TT) else r
            for src, n in k.w.items():
                if d.get(src, 0) < n:
                    d[src] = n
            if isinstance(r, TT) and r.ps:
                for src, n in k.r.items():
                    if src is not me and d.get(src, 0) < n:
                        d[src] = n
        for w in W:
            k = w.k if isinstance(w, TT) else w
            for dd in (k.w, k.r):
                for src, n in dd.items():
                    if d.get(src, 0) < n:
                        d[src] = n
        return d

    def _wait(s, eng, d):
        wd = s.waited[eng]
        for src, n in d.items():
            if src is s.src[eng] and (eng == "pe" or not s.same_engine_sync):
                continue
            if wd.get(src, 0) >= n:
                continue
            sem, val = src.sem_for(n)
            s.eng[eng].wait_ge(sem, val)
            wd[src] = n

    def _mark(s, src, n, R, W):
        for w in W:
            k = w.k if isinstance(w, TT) else w
            k.w = {src: n}
            k.r = {}
        for r in R:
            k = r.k if isinstance(r, TT) else r
            if k.r.get(src, 0) < n:
                k.r[src] = n

    def op(s, eng, fn, R=(), W=()):
        d = s._deps(R, W, s.src[eng])
        s._wait(eng, d)
        src = s.src[eng]
        src.n += 1
        sem, _ = src.sem_for(src.n)
        ins = fn(s.eng[eng])
        ins.then_inc(sem, 1)
        s._mark(src, src.n, R, W)
        s.nins += 1

    def dma(s, q, out, in_, R=(), W=(), **kw):
        i = s.rr[q]
        s.rr[q] = (i + 1) % s.NSLOT
        slot = s.slots[q][i]
        d = s._deps(R, W)
        if slot.n > 0 and d.get(slot, 0) < slot.n:
            d[slot] = slot.n
        s._wait(q, d)
        slot.n += 1
        sem, _ = slot.sem_for(slot.n)
        s.eng[q].dma_start(out=out, in_=in_, **kw).then_inc(sem, 16)
        s._mark(slot, slot.n, R, W)
        s.nins += 1

    def load(s, out, in_, R=(), W=(), **kw):
        s.dma("sp", out, in_, R, W, **kw)

    def store(s, out, in_, R=(), W=(), **kw):
        s.dma("pool", out, in_, R, W, **kw)

    def finish(s):
        d = {}
        for q in s.slots:
            for sl in s.slots[q]:
                if sl.n:
                    d[sl] = sl.n
        for k, src in s.src.items():
            if src.n and k != "sp":
                d[src] = src.n
        s._wait("sp", d)

    def mm(s, out, lhsT, rhs, start=True, stop=True, R=(), W=()):
        s.op("pe", lambda e: e.matmul(out, lhsT=lhsT, rhs=rhs, start=start, stop=stop), R, W)

    def tr(s, out, in_, ident, R=(), W=()):
        s.op("pe", lambda e: e.transpose(out, in_, ident), R, W)

    def act(s, out, in_, func, bias=None, scale=None, R=(), W=(), accum_out=None):
        kw = {}
        if bias is not None:
            kw["bias"] = bias
        if scale is not None:
            kw["scale"] = scale
        if accum_out is not None:
            kw["accum_out"] = accum_out
        s.op("act", lambda e: e.activation(out=out, in_=in_, func=func, **kw), R, W)

    def tt(s, out, in0, in1, op, R=(), W=(), eng="dve"):
        s.op(eng, lambda e: e.tensor_tensor(out=out, in0=in0, in1=in1, op=op), R, W)

    def ts(s, out, in0, s1, s2, op0, op1=None, R=(), W=(), eng="dve"):
        if op1 is None:
            s.op(eng, lambda e: e.tensor_scalar(out=out, in0=in0, scalar1=s1, scalar2=None, op0=op0), R, W)
        else:
            s.op(eng, lambda e: e.tensor_scalar(out=out, in0=in0, scalar1=s1, scalar2=s2, op0=op0, op1=op1), R, W)

    def stt(s, out, in0, scalar, in1, op0, op1, R=(), W=()):
        s.op("dve", lambda e: e.scalar_tensor_tensor(out=out, in0=in0, scalar=scalar, in1=in1, op0=op0, op1=op1), R, W)

    def copy(s, out, in_, R=(), W=(), eng="dve"):
        if eng == "act":
            s.op("act", lambda e: e.copy(out=out, in_=in_), R, W)
        else:
            s.op(eng, lambda e: e.tensor_copy(out=out, in_=in_), R, W)

    def memset(s, ap, val, W=(), eng="dve"):
        s.op(eng, lambda e: e.memset(ap, val), (), W)
import math
import numpy as np
from contextlib import ExitStack, contextmanager
from concourse.bass_utils import run_bass_kernel_spmd

T = 4352
NCTX = 256
TBS = [(0, 256)] + [(256 + 512 * i, 512) for i in range(8)]
KAPPA = math.exp(-0.5)
EPS = 1e-6


class G:
    pass


@contextmanager
def phase(kb):
    old = kb.es
    barrier(kb)
    with ExitStack() as es:
        kb.es_t = es
        yield
        barrier(kb)
    kb.es_t = None


def barrier(kb):
    d = {}
    for q in kb.slots:
        for sl in kb.slots[q]:
            if sl.n:
                d[sl] = sl.n
    for k, src in kb.src.items():
        if src.n:
            d[src] = src.n
    for e in ("pe", "dve", "act", "pool", "sp"):
        dd = {s_: n for s_, n in d.items() if s_ is not kb.src[e]}
        kb._wait(e, dd)


_uid = [0]


def sbt(kb, name, shape, dt=F32):
    _uid[0] += 1
    return TT(kb.es_t.enter_context(kb.nc.sbuf_tensor(f"s{_uid[0]}_{name}", list(shape), dt)))


def pst(kb, name, shape=None, dt=F32):
    _uid[0] += 1
    full = [128, 512] if dt == F32 else [128, 1024]
    return TT(kb.es_t.enter_context(kb.nc.psum_tensor(f"p{_uid[0]}_{name}", full, dt)), ps=True)


def run_interleaved(gens):
    gens = list(gens)
    while gens:
        for g_ in list(gens):
            try:
                next(g_)
            except StopIteration:
                gens.remove(g_)


class PPool:
    def __init__(s, banks):
        s.b = banks
        s.live = [False] * len(banks)
        s.i = 0

    def get(s):
        n = len(s.b)
        for k in range(n):
            j = (s.i + k) % n
            if not s.live[j]:
                s.live[j] = True
                s.i = (j + 1) % n
                return s.b[j]
        raise RuntimeError("PSUM pool exhausted")

    def put(s, bank):
        s.live[s.b.index(bank)] = False


class Rot:
    def __init__(s, items):
        s.items = items
        s.i = 0

    def next(s):
        x = s.items[s.i]
        s.i = (s.i + 1) % len(s.items)
        return x


_AS_INPUT = set()


def dram(nc, name, shape, dt, debug):
    kind = "ExternalInput" if name in _AS_INPUT else ("ExternalOutput" if debug else "Internal")
    return nc.dram_tensor(name, list(shape), dt, kind=kind).ap()


def phase_mods(kb, g):
    with phase(kb):
        run_interleaved([_mods_gen(kb, g), _xT_gen(kb, g)])


def _mods_gen(kb, g):
    if True:
        cT = sbt(kb, "cT", [128, 8, 2])
        scT = sbt(kb, "scT", [128, 8, 2])
        sg_ = sbt(kb, "sgc", [128, 8, 2])
        kb.load(cT[:], g.cT[:, :, :], W=[cT])
        kb.act(sg_[:], cT[:], AF.Sigmoid, R=[cT], W=[sg_])
        kb.tt(scT[:], cT[:], sg_[:], ALU.mult, R=[cT, sg_], W=[scT])
        wb = Rot([sbt(kb, f"adaw{i}", [128, 8, 1024]) for i in range(2)])
        pmb = pst(kb, "pmod")
        pm = TT(pmb.t[:, 0:96].rearrange("p (a b) -> p a b", b=2), pmb.k, ps=True)
        adab = sbt(kb, "adab", [128, 48])
        g1 = sbt(kb, "g1", [128, 8])
        g2 = sbt(kb, "g2", [128, 8])
        for li in range(2):
            kb.load(adab[:], g.ada_bT[li], W=[adab])
            kb.load(g1[:], g.g1T[li], W=[g1])
            kb.load(g2[:], g.g2T[li], W=[g2])
            src = g.ada_w[li].rearrange("(kc p) n -> p kc n", p=128)
            for pc in range(6):
                w = wb.next()
                kb.load(w[:], src[:, :, pc * 1024:(pc + 1) * 1024], W=[w])
                yield
                for cc in range(8):
                    col = pc * 8 + cc
                    for kc in range(8):
                        kb.mm(pm[:, col, :], w[:, kc, cc * 128:(cc + 1) * 128], scT[:, kc, :],
                              start=(kc == 0), stop=(kc == 7), R=[w, scT], W=[pm])
            mod = g.mod[li]
            kb.tt(mod[:], pm[:], adab[:, :, None].to_broadcast([128, 48, 2]), ALU.add, R=[pm, adab], W=[mod])
            for (sc, gi, m) in ((g.sc1[li], g1, 1), (g.sc2[li], g2, 4)):
                kb.ts(sc[:], mod[:, m * 8:(m + 1) * 8, :], 1.0, None, ALU.add, R=[mod], W=[sc])
                kb.tt(sc[:], sc[:], gi[:, :, None].to_broadcast([128, 8, 2]), ALU.mult, R=[sc, gi], W=[sc])


def phase_xT(kb, g):
    return


def _xT_gen(kb, g):
    if True:
        xin = Rot([sbt(kb, f"xin{i}", [128, 1024]) for i in range(2)])
        xo = Rot([sbt(kb, f"xo{i}", [128, 8, 128]) for i in range(2)])
        pt = Rot([pst(kb, f"pT{i}") for i in range(4)])
        dst = g.xT.rearrange("(kc p) t -> p kc t", p=128)
        for i in range(34):
            xi = xin.next()
            src = g.ctx[i * 128:(i + 1) * 128, :] if i < 2 else g.x[(i - 2) * 128:(i - 1) * 128, :]
            kb.load(xi[:], src, W=[xi])
            o = xo.next()
            for hf in range(2):
                p = pt.next()
                for j in range(4):
                    kc = hf * 4 + j
                    kb.tr(p[:, j * 128:(j + 1) * 128], xi[:, kc * 128:(kc + 1) * 128], g.ident[:], R=[xi, g.ident], W=[p])
                kb.copy(o[:, hf * 4:(hf + 1) * 4, :], p[:, :].rearrange("p (a b) -> p a b", b=128), R=[p], W=[o], eng=("act" if hf else "dve"))
            kb.store(dst[:, :, i * 128:(i + 1) * 128], o[:], R=[o])
            if i % 3 == 2:
                yield


class NormBufs:
    def __init__(s, kb, tag):
        s.sq = sbt(kb, f"nsq{tag}", [128, 8, 512], BF16)
        s.tmp = sbt(kb, f"ntmp{tag}", [128, 8, 512])
        s.rt = sbt(kb, f"nrt{tag}", [128, 512])
        s.rstd = sbt(kb, f"nrstd{tag}", [128, 512])
        s.ss = pst(kb, f"nss{tag}", [128, 512])


def norm_mod(kb, g, nb, xTb, n, sc, sh, j, hT, sh_off=0):
    kb.act(nb.sq[:, :, :n], xTb[:, :, :n], AF.Square, R=[xTb], W=[nb.sq])
    for kc in range(8):
        kb.mm(nb.ss[:, :n], g.ones_bf[:], nb.sq[:, kc, :n], start=(kc == 0), stop=(kc == 7), R=[nb.sq, g.ones_bf], W=[nb.ss])
    kb.act(nb.rt[:, :n], nb.ss[:, :n], AF.Sqrt, bias=g.eps_t[:, 0:1], scale=1.0 / 1024.0, R=[nb.ss, g.eps_t], W=[nb.rt])
    kb.op("dve", lambda e: e.reciprocal(out=nb.rstd[:, :n], in_=nb.rt[:, :n]), R=[nb.rt], W=[nb.rstd])
    kb.tt(nb.tmp[:, :, :n], xTb[:, :, :n], nb.rstd[:, None, :n].to_broadcast([128, 8, n]), ALU.mult, R=[xTb, nb.rstd], W=[nb.tmp])
    for kc in range(8):
        kb.act(hT[:, kc, :n], nb.tmp[:, kc, :n], AF.Identity, bias=sh[:, sh_off + kc, j:j + 1], scale=sc[:, kc, j:j + 1],
               R=[nb.tmp, sc, sh], W=[hT])


def load_weight_bf16(kb, W, src_ap, ncols, stg, piece=256):
    src = src_ap.rearrange("(kc p) n -> p kc n", p=128)
    kcn = src.shape[1]
    i = 0
    for c0 in range(0, ncols, piece):
        c1 = min(ncols, c0 + piece)
        w = c1 - c0
        st = stg.next()
        sv = st.t[:, 0:kcn * w].rearrange("p (k n) -> p k n", n=w)
        kb.load(sv, src[:, :, c0:c1], W=[st])
        kb.copy(W[:, :kcn, c0:c1], sv, R=[st], W=[W], eng=("act" if i % 2 else "dve"))
        i += 1


def phase_A0(kb, g):
    with phase(kb):
        W = sbt(kb, "wA", [128, 8, 4352], BF16)
        stg = Rot([sbt(kb, f"wstg{i}", [128, 2048]) for i in range(2)])
        import os
        STG = int(os.environ.get("A0_STAGE", "9"))
        load_weight_bf16(kb, W, g.w_in0, 4352, stg)
        nb = NormBufs(kb, "A")
        xb = Rot([sbt(kb, f"xTb{i}", [128, 8, 512]) for i in range(2)])
        hTs = Rot([sbt(kb, f"hT{i}", [128, 8, 512], BF16) for i in range(2)])
        cosb = Rot([sbt(kb, f"cos{i}", [128, 512]) for i in range(2)])
        sinb = Rot([sbt(kb, f"sin{i}", [128, 512]) for i in range(2)])
        pmm = Rot([pst(kb, f"pmm{i}", [128, 512]) for i in range(6)])
        st32 = Rot([sbt(kb, f"st32_{i}", [128, 512]) for i in range(4)])
        st16 = Rot([sbt(kb, f"st16_{i}", [128, 512], BF16) for i in range(6)])
        t1s = Rot([sbt(kb, f"t1_{i}", [128, 512]) for i in range(2)])
        t2s = Rot([sbt(kb, f"t2_{i}", [128, 512]) for i in range(2)])
        xsrc = g.xT.rearrange("(kc p) t -> p kc t", p=128)
        for bi, (s0, n) in enumerate(TBS):
            if STG < 2 or (STG < 9 and bi > 0):
                break
            j = 1 if bi == 0 else 0
            x = xb.next()
            kb.load(x[:, :, :n], xsrc[:, :, s0:s0 + n], W=[x])
            cs, sn = cosb.next(), sinb.next()
            kb.load(cs[:, :n], g.cosT[:, s0:s0 + n], W=[cs])
            kb.load(sn[:, :n], g.sinT[:, s0:s0 + n], W=[sn])
            hT = hTs.next()
            norm_mod(kb, g, nb, x, n, g.sc1[0], g.mod[0], j, hT, sh_off=0)

            def proj(ct):
                p = pmm.next()
                for kc in range(8):
                    kb.mm(p[:, :n], W[:, kc, ct * 128:(ct + 1) * 128], hT[:, kc, :n], start=(kc == 0), stop=(kc == 7), R=[W, hT], W=[p])
                return p
            if STG < 3:
                continue
            for ct in range(14):
                p = proj(ct)
                st = st32.next()
                kb.copy(st[:, :n], p[:, :n], R=[p], W=[st], eng=("act" if ct % 2 else "dve"))
                kb.store(g.PrT[ct * 128:(ct + 1) * 128, s0:s0 + n], st[:, :n], R=[st])
            if STG < 4:
                continue
            for h in range(4):
                for (base, dst_rot, dst_pl) in ((14, g.QrotT, g.QplT), (22, g.KrotT, None)):
                    pq = proj(base + h)
                    pw = proj(base + 4 + h)
                    t1, t2 = t1s.next(), t2s.next()
                    kb.tt(t1[:, :n], pq[:, :n], cs[:, :n], ALU.mult, R=[pq, cs], W=[t1])
                    kb.tt(t2[:, :n], pw[:, :n], sn[:, :n], ALU.mult, R=[pw, sn], W=[t2])
                    so = st16.next()
                    kb.tt(so[:, :n], t1[:, :n], t2[:, :n], ALU.add, R=[t1, t2], W=[so])
                    if not os.environ.get("NOSTORE4"):
                        kb.store(dst_rot[h * 128:(h + 1) * 128, s0:s0 + n], so[:, :n], R=[so])
                    if dst_pl is not None:
                        sp_ = st16.next()
                        kb.copy(sp_[:, :n], pq[:, :n], R=[pq], W=[sp_], eng="act")
                        if not os.environ.get("NOSTORE4"):
                            kb.store(dst_pl[h * 128:(h + 1) * 128, s0:s0 + n], sp_[:, :n], R=[sp_])
            if STG < 5:
                continue
            for tt_ in range(n // 128):
                p = pmm.next()
                for kc in range(8):
                    kb.mm(p[:, :], hT[:, kc, tt_ * 128:(tt_ + 1) * 128], W[:, kc, 3840:4352], start=(kc == 0), stop=(kc == 7), R=[W, hT], W=[p])
                so = st16.next()
                kb.copy(so[:], p[:], R=[p], W=[so], eng=("act" if tt_ % 2 else "dve"))
                kb.store(g.Vd[s0 + tt_ * 128:s0 + (tt_ + 1) * 128, :], so[:], R=[so])

def phase_B0(kb, g):
    with phase(kb):
        rwc = sbt(kb, "rwc", [128, 58])
        hmu = sbt(kb, "hmu", [128, 14])
        omm = sbt(kb, "omm", [128, 14])
        omka = sbt(kb, "omka", [128, 4])
        hrk = sbt(kb, "hrk", [128, 4])
        lora = sbt(kb, "lora", [128, 2, 512])
        gup = sbt(kb, "gup", [128, 512])
        blk = sbt(kb, "blk", [128, 128])
        cmask = sbt(kb, "cmask", [128, 512])
        kb.load(rwc[:], g.rw_cols[:, :], W=[rwc])
        kb.load(lora[:], g.rw_lora[:, :, :], W=[lora])
        kb.load(gup[:], g.rw_gup[:, :], W=[gup])
        kb.load(blk[:], g.blk_h[:, :], W=[blk])
        kb.load(cmask[:], g.cmask_h[:, :], W=[cmask])
        kb.ts(hmu[:], rwc[:, 0:14], 0.5, None, ALU.mult, R=[rwc], W=[hmu])
        kb.ts(omm[:], rwc[:, 0:14], -1.0, 1.0, ALU.mult, ALU.add, R=[rwc], W=[omm])
        kb.ts(omka[:], rwc[:, 18:22], -1.0, 1.0, ALU.mult, ALU.add, R=[rwc], W=[omka])
        kb.ts(hrk[:], rwc[:, 22:26], 0.5, None, ALU.mult, R=[rwc], W=[hrk])

        pin = sbt(kb, "pin", [128, 14, 514])
        psx = sbt(kb, "psx", [128, 14, 512])
        lwin = sbt(kb, "lwin", [128, 512])
        sgd = sbt(kb, "sgd", [128, 512])
        F = lambda nm, dt=F32: sbt(kb, nm, [128, 512], dt)
        R2 = lambda nm: Rot([F(f"{nm}{i}") for i in range(2)])
        hp_rots = [R2(nm) for nm in ("kku", "sq", "rt", "rs", "kk", "rk", "bon", "kbs")]
        d_rots = [R2(nm) for nm in ("sg", "a_", "tmp", "kmod", "b_", "ci", "cr", "ce", "e1", "e2", "e3")]
        fm16 = [Rot([F(f"fm{k}_{i}", BF16) for i in range(2)]) for k in range(4)]
        vb = F("vb", BF16)
        st32 = Rot([F(f"bst{i}") for i in range(2)])
        gst = Rot([sbt(kb, f"gst{i}", [128, 8]) for i in range(2)])
        tms = Rot([sbt(kb, f"tms{i}", [128, 1024], BF16) for i in range(3)])
        pmm = Rot([pst(kb, f"bp{i}") for i in range(4)])
        ptr = Rot([pst(kb, f"bt{i}", dt=BF16) for i in range(3)])
        psrc = g.PrT.rearrange("(ti p) t -> p ti t", p=128)
        for bi, (s0, n) in enumerate(TBS):
            nt = n // 128
            nch = n // 64
            c0 = s0 // 64
            lo = s0 if s0 in (0, NCTX) else s0 - 1
            hi = s0 + n if (s0 + n) in (NCTX, T) else s0 + n + 1
            kb.memset(pin[:, :, 0:1], 0.0, W=[pin])
            kb.memset(pin[:, :, n + 1:n + 2], 0.0, W=[pin])
            kb.load(pin[:, :, lo - s0 + 1:hi - s0 + 1], psrc[:, :, lo:hi], W=[pin])
            kb.tt(psx[:, :, :n], pin[:, :, 0:n], pin[:, :, 2:n + 2], ALU.add, R=[pin], W=[psx])
            for ti in range(14):
                kb.act(psx[:, ti, :n], psx[:, ti, :n], AF.Identity, scale=hmu[:, ti:ti + 1], R=[psx, hmu], W=[psx])
            for ti in range(14):
                kb.stt(psx[:, ti, :n], pin[:, ti, 1:n + 1], omm[:, ti:ti + 1], psx[:, ti, :n], ALU.mult, ALU.add, R=[pin, psx, omm], W=[psx])
            kb.act(lwin[0:64, :n], psx[0:64, 12, :n], AF.Tanh, R=[psx], W=[lwin])
            kb.copy(lwin[64:128, :n], psx[64:128, 12, :n], R=[psx], W=[lwin])
            kb.act(sgd[:, :n], psx[:, 13, :n], AF.Sigmoid, R=[psx], W=[sgd])
            for hp in range(4):
                kku, sq, rt, rs, kk, rk, bon, kbs = [R_.next() for R_ in hp_rots]
                r = psx[:, hp, :n]
                k = psx[:, 4 + hp, :n]
                v = psx[:, 8 + hp, :n]
                hs = slice(hp * 128, (hp + 1) * 128)
                kb.ts(kku[:, :n], k, rwc[:, 14 + hp:15 + hp], None, ALU.mult, R=[psx, rwc], W=[kku])
                kb.tt(sq[:, :n], kku[:, :n], kku[:, :n], ALU.mult, R=[kku], W=[sq])
                p = pmm.next()
                kb.mm(p[:, :n], blk[:], sq[:, :n], R=[blk, sq], W=[p])
                kb.act(rt[:, :n], p[:, :n], AF.Sqrt, bias=g.eps_t[:, 1:2], scale=1.0, R=[p, g.eps_t], W=[rt])
                kb.op("dve", lambda e: e.reciprocal(out=rs[:, :n], in_=rt[:, :n]), R=[rt], W=[rs])
                kb.tt(kk[:, :n], kku[:, :n], rs[:, :n], ALU.mult, R=[kku, rs], W=[kk])
                p = pmm.next()
                kb.mm(p[:, :n], gup[:, hs], sgd[:, :n], R=[gup, sgd], W=[p])
                st = st32.next()
                kb.copy(st[:, :n], p[:, :n], R=[p], W=[st], eng="act")
                kb.dma("sp", g.gT[hs, s0:s0 + n], st[:, :n], R=[st])
                kb.copy(vb[:, :n], v, R=[psx], W=[vb], eng="act")
                pt_ = ptr.next()
                for tt_ in range(nt):
                    kb.tr(pt_[:, tt_ * 128:(tt_ + 1) * 128], vb[:, tt_ * 128:(tt_ + 1) * 128], g.ident_bf[:], R=[vb, g.ident_bf], W=[pt_])
                tm = tms.next()
                kb.copy(tm[:, :nt * 128], pt_[:, :nt * 128], R=[pt_], W=[tm], eng="act")
                for tt_ in range(nt):
                    kb.dma("sp", g.Vtm[s0 + tt_ * 128:s0 + (tt_ + 1) * 128, hs], tm[:, tt_ * 128:(tt_ + 1) * 128], R=[tm])
                for d in range(2):
                    sg, a_, tmp, kmod, b_, ci, cr, ce, e1, e2, e3 = [R_.next() for R_ in d_rots]
                    p = pmm.next()
                    kb.mm(p[:, :n], lora[0:64, d, hs], lwin[0:64, :n], R=[lora, lwin], W=[p])
                    kb.act(sg[:, :n], p[:, :n], AF.Sigmoid, bias=rwc[:, 34 + 4 * d + hp:35 + 4 * d + hp], scale=1.0, R=[p, rwc], W=[sg])
                    p = pmm.next()
                    kb.mm(p[:, :n], lora[64:128, d, hs], lwin[64:128, :n], R=[lora, lwin], W=[p])
                    kb.act(a_[:, :n], p[:, :n], AF.Sigmoid, bias=rwc[:, 42 + 4 * d + hp:43 + 4 * d + hp], scale=1.0, R=[p, rwc], W=[a_])
                    kb.ts(tmp[:, :n], a_[:, :n], rwc[:, 18 + hp:19 + hp], omka[:, hp:hp + 1], ALU.mult, ALU.add, R=[a_, rwc, omka], W=[tmp])
                    kb.tt(kmod[:, :n], tmp[:, :n], k, ALU.mult, R=[tmp, psx], W=[kmod])
                    kb.tt(b_[:, :n], kk[:, :n], a_[:, :n], ALU.mult, R=[kk, a_], W=[b_])
                    if d == 0:
                        kb.copy(kbs[:, :n], kmod[:, :n], R=[kmod], W=[kbs], eng="act")
                    else:
                        kb.tt(kbs[:, :n], kbs[:, :n], kmod[:, :n], ALU.add, R=[kbs, kmod], W=[kbs])
                    kb.op("dve", lambda e: e.tensor_tensor_scan(out=ci[:, :n], data0=cmask[:, :n], data1=sg[:, :n], initial=0.0,
                                                                op0=ALU.mult, op1=ALU.add), R=[cmask, sg], W=[ci])
                    cc = ci
                    if d == 1:
                        kb.tt(tmp[:, :n], sg[:, :n], ci[:, :n], ALU.subtract, R=[sg, ci], W=[tmp])
                        civ = ci[:, :n].rearrange("p (c t) -> p c t", t=64)
                        kb.tt(cr[:, :n].rearrange("p (c t) -> p c t", t=64), tmp[:, :n].rearrange("p (c t) -> p c t", t=64),
                              civ[:, :, 63:64].to_broadcast([128, nch, 64]), ALU.add, R=[tmp, ci], W=[cr])
                        cc = cr
                    kb.tt(ce[:, :n], cc[:, :n], sg[:, :n], ALU.subtract, R=[cc, sg], W=[ce])
                    kb.act(e1[:, :n], cc[:, :n], AF.Exp, scale=KAPPA, R=[cc], W=[e1])
                    kb.act(e2[:, :n], cc[:, :n], AF.Exp, scale=-KAPPA, R=[cc], W=[e2])
                    kb.act(e3[:, :n], ce[:, :n], AF.Exp, scale=-KAPPA, R=[ce], W=[e3])
                    fa, fr, fb, fk = [fm16[i].next() for i in range(4)]
                    kb.stt(fa[:, :n], kk[:, :n], -1.0, e3[:, :n], ALU.mult, ALU.mult, R=[kk, e3], W=[fa])
                    kb.tt(fr[:, :n], r, e2[:, :n], ALU.mult, R=[psx, e2], W=[fr])
                    kb.tt(fb[:, :n], b_[:, :n], e1[:, :n], ALU.mult, R=[b_, e1], W=[fb])
                    kb.tt(fk[:, :n], kmod[:, :n], e1[:, :n], ALU.mult, R=[kmod, e1], W=[fk])
                    gs = gst.next()
                    e2v = e2[:, :n].rearrange("p (c t) -> p c t", t=64)
                    col = 63 if d == 0 else 0
                    kb.copy(gs[:, :nch], e2v[:, :, col], R=[e2], W=[gs], eng="pool")
                    kb.dma("sp", g.gam[d][hs, c0:c0 + nch], gs[:, :nch], R=[gs])
                    for kind, f in enumerate((fa, fr, fb, fk)):
                        kb.dma("sp", g.FM[d][kind][hs, s0:s0 + n], f[:, :n], R=[f])
                    for kind, f in ((0, fb), (1, fk)):
                        pt_ = ptr.next()
                        for tt_ in range(nt):
                            kb.tr(pt_[:, tt_ * 128:(tt_ + 1) * 128], f[:, tt_ * 128:(tt_ + 1) * 128], g.ident_bf[:], R=[f, g.ident_bf], W=[pt_])
                        tm = tms.next()
                        kb.copy(tm[:, :nt * 128], pt_[:, :nt * 128], R=[pt_], W=[tm], eng=("act" if kind else "dve"))
                        for tt_ in range(nt):
                            kb.dma("sp", g.TM[d][s0 + tt_ * 128:s0 + (tt_ + 1) * 128, kind, hs], tm[:, tt_ * 128:(tt_ + 1) * 128], R=[tm])
                kb.tt(rk[:, :n], r, kbs[:, :n], ALU.mult, R=[psx, kbs], W=[rk])
                kb.ts(rk[:, :n], rk[:, :n], hrk[:, hp:hp + 1], None, ALU.mult, R=[rk, hrk], W=[rk])
                p = pmm.next()
                kb.mm(p[:, :n], blk[:], rk[:, :n], R=[blk, rk], W=[p])
                kb.tt(bon[:, :n], p[:, :n], v, ALU.mult, R=[p, psx], W=[bon])
                kb.dma("sp", g.bonT[hs, s0:s0 + n], bon[:, :n], R=[bon])


def phase_C0(kb, g):
    NL = 5
    with phase(kb):
        mk = sbt(kb, "mk", [64, 2, 3, 64])
        kb.load(mk[:], g.masks_h[:, :, :, :], W=[mk])
        poolA = PPool([pst(kb, f"cpa{i}") for i in range(5)])
        poolB = PPool([pst(kb, f"cpb{i}") for i in range(3)])
        st = []
        for d in range(2):
            s = G()
            s.gam = sbt(kb, f"gam{d}", [64, 8, 68])
            kb.load(s.gam[:], g.gam[d].rearrange("(h k) c -> k h c", k=64), W=[s.gam])
            s.fm = Rot([sbt(kb, f"fm{d}_{i}", [64, 4, 8, 64], BF16) for i in range(2)])
            s.tm = Rot([sbt(kb, f"tm{d}_{i}", [64, 2, 512], BF16) for i in range(2)])
            s.vt = Rot([sbt(kb, f"vt{d}_{i}", [64, 512], BF16) for i in range(2)])
            s.Sf = sbt(kb, f"Sf{d}", [64, 8, 64])
            s.Sb = Rot([sbt(kb, f"Sb{d}_{i}", [64, 8, 64], BF16) for i in range(2)])
            s.Nm = Rot([sbt(kb, f"Nm{d}_{i}", [64, 8, 128], BF16) for i in range(2)])
            s.Nkm = Rot([sbt(kb, f"Nkm{d}_{i}", [64, 8, 128], BF16) for i in range(2)])
            s.Inv = Rot([sbt(kb, f"Inv{d}_{i}", [64, 8, 64], BF16) for i in range(2)])
            s.NT = sbt(kb, f"NT{d}", [64, 8, 64], BF16)
            s.X = sbt(kb, f"X{d}", [64, 8, 64])
            s.Xb = Rot([sbt(kb, f"Xb{d}_{i}", [64, 8, 64], BF16) for i in range(2)])
            s.P = Rot([sbt(kb, f"P{d}_{i}", [64, 8, 64], BF16) for i in range(2)])
            s.PT = Rot([sbt(kb, f"PT{d}_{i}", [64, 8, 64], BF16) for i in range(2)])
            s.W1 = sbt(kb, f"W1{d}", [64, 8, 64], BF16)
            s.UT = sbt(kb, f"UT{d}", [64, 8, 64], BF16)
            s.Yst = Rot([sbt(kb, f"Yst{d}_{i}", [64, 512]) for i in range(2)])
            kb.memset(s.Sf[:], 0.0, W=[s.Sf])
            s.sb = s.Sb.next()
            kb.memset(s.sb[:], 0.0, W=[s.sb])
            s.mAR = mk[:, d, 0:2, :].rearrange("p a t -> p (a t)")[:, None, :].to_broadcast([64, 4, 128])
            s.mT = mk[:, d, 2, :][:, None, :].to_broadcast([64, 8, 64])
            st.append(s)
        Ibc = g.ident[0:64, 0:64][:, None, :].to_broadcast([64, 8, 64])
        order = [list(range(68)), [3, 2, 1, 0] + list(range(67, 3, -1))]

        def v3(p):
            return p[0:64, :].rearrange("p (h t) -> p h t", t=64)

        def partA(d, c, rec):
            s = st[d]
            t0 = c * 64
            yield
            fm, tm, vt = s.fm.next(), s.tm.next(), s.vt.next()
            Nm, Nkm = s.Nm.next(), s.Nkm.next()
            for kind in range(4):
                kb.load(fm[:, kind, :, :], g.FM[d][kind].rearrange("(h k) t -> k h t", k=64)[:, :, t0:t0 + 64], W=[fm])
            kb.load(tm[:], g.TM[d][t0:t0 + 64, :, :], W=[tm])
            kb.load(vt[:], g.Vtm[t0:t0 + 64, :], W=[vt])
            for (lk, dst) in ((2, Nm), (3, Nkm)):
                for hh in range(2):
                    p = poolA.get()
                    for h4 in range(4):
                        h = hh * 4 + h4
                        kb.mm(p[0:64, h4 * 128:(h4 + 1) * 128], fm[:, lk, h, :], fm[:, 0:2, h, :], R=[fm], W=[p])
                    kb.tt(dst[:, hh * 4:(hh + 1) * 4, :], p[0:64, :].rearrange("p (h t) -> p h t", t=128), s.mAR, ALU.mult, R=[p, mk], W=[dst])
                    poolA.put(p)
                    yield
            p = poolA.get()
            for h in range(8):
                kb.mm(p[0:64, h * 64:(h + 1) * 64], fm[:, 0, h, :], fm[:, 2, h, :], R=[fm], W=[p])
            kb.tt(s.NT[:], v3(p), s.mT, ALU.mult, R=[p, mk], W=[s.NT])
            poolA.put(p)
            yield
            kb.tt(s.X[:], Nm[:, :, 0:64], Ibc, ALU.add, R=[Nm, g.ident], W=[s.X])
            xb = s.Xb.next()
            kb.copy(xb[:], s.X[:], R=[s.X], W=[xb], eng="act")
            P_ap = lambda h: Nm[:, h, 0:64]
            PT_ap = lambda h: s.NT[:, h, :]
            Pt, PTt = Nm, s.NT
            for lv in range(NL):
                last = lv == NL - 1
                p1 = None
                if not last:
                    p1 = poolA.get()
                    for h in range(8):
                        kb.mm(p1[0:64, h * 64:(h + 1) * 64], PT_ap(h), P_ap(h), R=[Pt, PTt], W=[p1])
                p2 = poolA.get()
                for h in range(8):
                    kb.mm(p2[0:64, h * 64:(h + 1) * 64], P_ap(h), PT_ap(h), R=[Pt, PTt], W=[p2])
                yield
                nPT = s.PT.next()
                kb.copy(nPT[:], v3(p2), R=[p2], W=[nPT], eng="act")
                poolA.put(p2)
                if not last:
                    nP = s.P.next()
                    kb.copy(nP[:], v3(p1), R=[p1], W=[nP], eng="act")
                    poolA.put(p1)
                    Pt = nP
                    P_ap = (lambda t_: (lambda h: t_[:, h, :]))(nP)
                PTt = nPT
                PT_ap = (lambda t_: (lambda h: t_[:, h, :]))(nPT)
                yield
                p3 = poolA.get()
                for h in range(8):
                    kb.mm(p3[0:64, h * 64:(h + 1) * 64], PT_ap(h), xb[:, h, :], R=[PTt, xb], W=[p3])
                yield
                kb.tt(s.X[:], s.X[:], v3(p3), ALU.add, R=[s.X, p3], W=[s.X])
                poolA.put(p3)
                xb = s.Inv.next() if last else s.Xb.next()
                kb.copy(xb[:], s.X[:], R=[s.X], W=[xb], eng="act")
                yield
            rec.update(fm=fm, tm=tm, vt=vt, Nm=Nm, Nkm=Nkm, inv=xb, c=c)

        def partB(d, rec):
            s = st[d]
            fm, tm, vt, Nm, Nkm, inv, c = (rec[k] for k in ("fm", "tm", "vt", "Nm", "Nkm", "inv", "c"))
            t0 = c * 64
            sb = s.sb
            yield
            pw = poolB.get()
            for h in range(8):
                hs = slice(h * 64, (h + 1) * 64)
                kb.mm(pw[0:64, hs], Nkm[:, h, 0:64], vt[:, hs], start=True, stop=False, R=[Nkm, vt], W=[pw])
                kb.mm(pw[0:64, hs], fm[:, 0, h, :], sb[:, h, :], start=False, stop=True, R=[fm, sb], W=[pw])
            yield
            kb.copy(s.W1[:], v3(pw), R=[pw], W=[s.W1], eng="act")
            poolB.put(pw)
            pu = poolB.get()
            for h in range(8):
                kb.mm(pu[0:64, h * 64:(h + 1) * 64], inv[:, h, :], s.W1[:, h, :], R=[inv, s.W1], W=[pu])
            yield
            kb.copy(s.UT[:], v3(pu), R=[pu], W=[s.UT], eng="act")
            poolB.put(pu)
            pn = poolB.get()
            for h in range(8):
                hs = slice(h * 64, (h + 1) * 64)
                kb.mm(pn[0:64, hs], tm[:, 0, hs], s.UT[:, h, :], start=True, stop=False, R=[tm, s.UT], W=[pn])
                kb.mm(pn[0:64, hs], tm[:, 1, hs], vt[:, hs], start=False, stop=True, R=[tm, vt], W=[pn])
            yield
            kb.tt(s.Sf[:], s.Sf[:], v3(pn), ALU.add, R=[s.Sf, pn], W=[s.Sf])
            poolB.put(pn)
            kb.tt(s.Sf[:], s.Sf[:], s.gam[:, :, c:c + 1].to_broadcast([64, 8, 64]), ALU.mult, R=[s.Sf, s.gam], W=[s.Sf])
            nsb = s.Sb.next()
            kb.copy(nsb[:], s.Sf[:], R=[s.Sf], W=[nsb], eng="act")
            s.sb = nsb
            yield
            py = poolB.get()
            for h in range(8):
                hs = slice(h * 64, (h + 1) * 64)
                kb.mm(py[0:64, hs], fm[:, 1, h, :], sb[:, h, :], start=True, stop=False, R=[fm, sb], W=[py])
                kb.mm(py[0:64, hs], Nm[:, h, 64:128], s.UT[:, h, :], start=False, stop=False, R=[Nm, s.UT], W=[py])
                kb.mm(py[0:64, hs], Nkm[:, h, 64:128], vt[:, hs], start=False, stop=True, R=[Nkm, vt], W=[py])
            ys = s.Yst.next()
            kb.copy(ys[:], py[0:64, :], R=[py], W=[ys])
            poolB.put(py)
            kb.store(g.Y[d][t0:t0 + 64, :], ys[:], R=[ys])

        recs = [{}, {}]
        run_interleaved([partA(0, order[0][0], recs[0]), partA(1, order[1][0], recs[1])])
        for i in range(68):
            cur = recs
            gens = [partB(0, cur[0]), partB(1, cur[1])]
            recs = [{}, {}]
            if i + 1 < 68:
                gens += [partA(0, order[0][i + 1], recs[0]), partA(1, order[1][i + 1], recs[1])]
            run_interleaved(gens)

def phase_C2(kb, g):
    with phase(kb):
        rwc = sbt(kb, "rwc2", [128, 58])
        kb.load(rwc[:], g.rw_cols[:, :], W=[rwc])
        yf = Rot([sbt(kb, f"yf{i}", [128, 512]) for i in range(2)])
        yb = Rot([sbt(kb, f"yb{i}", [128, 512]) for i in range(2)])
        y = sbt(kb, "y", [128, 512])
        sq = sbt(kb, "ysq", [128, 512])
        sm = sbt(kb, "ysm", [128, 8])
        vr = sbt(kb, "yvr", [128, 8])
        rt = sbt(kb, "yrt", [128, 8])
        rs = sbt(kb, "yrs", [128, 8])
        yn = Rot([sbt(kb, f"yn{i}", [128, 512]) for i in range(2)])
        pb = [pst(kb, f"c2p{i}") for i in range(4)]
        bon = Rot([sbt(kb, f"bon{i}", [128, 4, 512]) for i in range(2)])
        gt = Rot([sbt(kb, f"gt{i}", [128, 4, 512]) for i in range(2)])
        a1 = Rot([sbt(kb, f"a1_{i}", [128, 512]) for i in range(2)])
        mo = Rot([sbt(kb, f"mo{i}", [128, 512], BF16) for i in range(3)])
        for bi, (s0, n) in enumerate(TBS):
            nt = n // 128
            bo, gg = bon.next(), gt.next()
            kb.load(bo[:, :, :n], g.bonT.rearrange("(c p) t -> p c t", p=128)[:, :, s0:s0 + n], W=[bo])
            kb.load(gg[:, :, :n], g.gT.rearrange("(c p) t -> p c t", p=128)[:, :, s0:s0 + n], W=[gg])
            for tt_ in range(nt):
                t0 = s0 + tt_ * 128
                a, b = yf.next(), yb.next()
                kb.load(a[:], g.Y[0][t0:t0 + 128, :], W=[a])
                kb.load(b[:], g.Y[1][t0:t0 + 128, :], W=[b])
                kb.tt(y[:], a[:], b[:], ALU.add, R=[a, b], W=[y])
                y3 = y[:].rearrange("p (h v) -> p h v", v=64)
                kb.op("dve", lambda e: e.tensor_reduce(out=sm[:], in_=y3, axis=AX.X, op=ALU.add), R=[y], W=[sm])
                kb.ts(sm[:], sm[:], -1.0 / 64.0, None, ALU.mult, R=[sm], W=[sm])
                kb.tt(y3, y3, sm[:, :, None].to_broadcast([128, 8, 64]), ALU.add, R=[y, sm], W=[y])
                kb.tt(sq[:], y[:], y[:], ALU.mult, R=[y], W=[sq])
                kb.op("dve", lambda e: e.tensor_reduce(out=vr[:], in_=sq[:].rearrange("p (h v) -> p h v", v=64), axis=AX.X, op=ALU.add), R=[sq], W=[vr])
                kb.act(rt[:], vr[:], AF.Sqrt, bias=g.eps_t[:, 2:3], scale=1.0 / 64.0, R=[vr, g.eps_t], W=[rt])
                kb.op("dve", lambda e: e.reciprocal(out=rs[:], in_=rt[:]), R=[rt], W=[rs])
                yo = yn.next()
                kb.tt(yo[:].rearrange("p (h v) -> p h v", v=64), y3, rs[:, :, None].to_broadcast([128, 8, 64]), ALU.mult, R=[y, rs], W=[yo])
                for ct in range(4):
                    kb.tr(pb[ct][:, tt_ * 128:(tt_ + 1) * 128], yo[:, ct * 128:(ct + 1) * 128], g.ident[:], R=[yo, g.ident], W=[pb[ct]])
            for ct in range(4):
                t1 = a1.next()
                kb.act(t1[:, :n], pb[ct][:, :n], AF.Identity, bias=rwc[:, 30 + ct:31 + ct], scale=rwc[:, 26 + ct:27 + ct], R=[pb[ct], rwc], W=[t1])
                kb.tt(t1[:, :n], t1[:, :n], bo[:, ct, :n], ALU.add, R=[t1, bo], W=[t1])
                o = mo.next()
                kb.tt(o[:, :n], t1[:, :n], gg[:, ct, :n], ALU.mult, R=[t1, gg], W=[o])
                kb.store(g.mixT[ct * 128:(ct + 1) * 128, s0:s0 + n], o[:, :n], R=[o])


def phase_D0(kb, g):
    LAM_INIT = 0.2
    with phase(kb):
        dc = sbt(kb, "dc", [64, 4])
        pr = sbt(kb, "dpr", [64, 2])
        onesf = sbt(kb, "onesf", [64, 128])
        lam = sbt(kb, "lam", [128, 2])
        nlam = sbt(kb, "nlam", [128, 1])
        slg = sbt(kb, "slg", [128, 1])
        kb.load(dc[:], g.diff_cols[:, :], W=[dc])
        kb.load(slg[:], g.subln_g[:, :], W=[slg])
        kb.memset(onesf[:], 1.0, W=[onesf])
        kb.tt(pr[:, 0:1], dc[:, 0:1], dc[:, 1:2], ALU.mult, R=[dc], W=[pr])
        kb.tt(pr[:, 1:2], dc[:, 2:3], dc[:, 3:4], ALU.mult, R=[dc], W=[pr])
        ps = Rot([pst(kb, f"dps{i}") for i in range(3)])
        pfin = pst(kb, "dpfin")
        acc = [[pst(kb, f"dacc{m}{q}") for q in range(2)] for m in range(2)]
        pl = ps.next()
        kb.mm(pl[:, 0:2], onesf[:], pr[:], R=[onesf, pr], W=[pl])
        kb.act(lam[:], pl[:, 0:2], AF.Exp, R=[pl], W=[lam])
        kb.tt(nlam[:], lam[:, 1:2], lam[:, 0:1], ALU.subtract, R=[lam], W=[nlam])
        kb.ts(nlam[:], nlam[:], -LAM_INIT, None, ALU.add, R=[nlam], W=[nlam])
        KT = sbt(kb, "KT", [128, T], BF16)
        QR = [sbt(kb, f"QR{m}", [128, T], BF16) for m in range(2)]
        QP = [sbt(kb, f"QP{m}", [128, T], BF16) for m in range(2)]
        for m in range(2):
            zs = slice(64, 128) if m == 0 else slice(0, 64)
            kb.memset(QR[m][zs, :], 0.0, W=[QR[m]])
            kb.memset(QP[m][zs, :], 0.0, W=[QP[m]])
        VA = sbt(kb, "VA", [128, 34, 130], BF16)
        pT = Rot([sbt(kb, f"pT{i}", [128, 512], BF16) for i in range(4)])
        rz = sbt(kb, "rz", [128, 4])
        o0 = sbt(kb, "o0", [128, 128])
        o = sbt(kb, "o", [128, 128])
        osq = sbt(kb, "osq", [128, 128])
        ssq = sbt(kb, "ssq", [128, 1])
        rt = sbt(kb, "drt", [128, 1])
        rs = sbt(kb, "drs", [128, 1])
        on = Rot([sbt(kb, f"on{i}", [128, 128]) for i in range(2)])
        so = Rot([sbt(kb, f"dso{i}", [128, 256], BF16) for i in range(2)])
        accS = Rot([sbt(kb, f"accS{i}", [128, 4, 129]) for i in range(2)])

        def finalize(aS, h, q0):
            s_ = so.next()
            for qt in range(2):
                a0_, a1_ = aS[:, qt, :], aS[:, 2 + qt, :]
                kb.op("dve", lambda e_: e_.reciprocal(out=rz[:, 0:1], in_=a0_[:, 128:129]), R=[aS], W=[rz])
                kb.op("dve", lambda e_: e_.reciprocal(out=rz[:, 1:2], in_=a1_[:, 128:129]), R=[aS], W=[rz])
                kb.tt(rz[:, 2:3], rz[:, 1:2], nlam[:], ALU.mult, R=[rz, nlam], W=[rz])
                kb.ts(o0[:], a0_[:, 0:128], rz[:, 0:1], None, ALU.mult, R=[aS, rz], W=[o0])
                kb.stt(o[:], a1_[:, 0:128], rz[:, 2:3], o0[:], ALU.mult, ALU.add, R=[aS, rz, o0], W=[o])
                yield
                kb.act(osq[:], o[:], AF.Square, R=[o], W=[osq, ssq], accum_out=ssq[:])
                yield
                kb.act(rt[:], ssq[:], AF.Sqrt, bias=g.eps_t[:, 0:1], scale=1.0 / 128.0, R=[ssq, g.eps_t], W=[rt])
                yield
                kb.op("dve", lambda e_: e_.reciprocal(out=rs[:], in_=rt[:]), R=[rt], W=[rs])
                on_ = on.next()
                kb.ts(on_[:], o[:], rs[:, 0:1], 1.0 - LAM_INIT, ALU.mult, ALU.mult, R=[o, rs], W=[on_])
                yield
                kb.tr(pfin[:, 0:128], on_[:], g.ident[:], R=[on_, g.ident], W=[pfin])
                yield
                kb.act(s_[:, qt * 128:(qt + 1) * 128], pfin[:, 0:128], AF.Copy, scale=slg[:, 0:1], R=[pfin, slg], W=[s_])
                yield
            kb.store(g.mixT[512 + h * 128:512 + (h + 1) * 128, q0:q0 + 256], s_[:], R=[s_])

        fin = iter(())
        chunks = [(0, True)] + [(256 + 256 * i, False) for i in range(16)]
        import os
        DS = int(os.environ.get("D0_STAGE", "9"))
        if DS < 9:
            chunks = chunks[:2]
        for h in range(4 if DS == 9 else 1):
            hs = slice(h * 128, (h + 1) * 128)
            kb.load(KT[:], g.KrotT[hs, :], W=[KT])
            for m in range(2):
                ms = slice(m * 64, (m + 1) * 64)
                kb.load(QR[m][ms, :], g.QrotT[h * 128 + m * 64:h * 128 + (m + 1) * 64, :], W=[QR[m]])
                kb.load(QP[m][ms, :], g.QplT[h * 128 + m * 64:h * 128 + (m + 1) * 64, :], W=[QP[m]])
            kb.load(VA[:, :, 0:128], g.Vd.rearrange("(kt p) c -> p kt c", p=128)[:, :, hs], W=[VA])
            kb.memset(VA[:, :, 128:129], 1.0, W=[VA])
            for (q0, isctx) in chunks:
                if DS < 2:
                    break
                kts = [0, 1] if isctx else list(range(34))
                def score(kt):
                    Q = QP if (not isctx and kt < 2) else QR
                    p = ps.next()
                    ks = slice(kt * 128, (kt + 1) * 128)
                    kb.mm(p[:, 0:256], KT[:, ks], Q[0][:, q0:q0 + 256], R=[KT, Q[0]], W=[p])
                    kb.mm(p[:, 256:512], KT[:, ks], Q[1][:, q0:q0 + 256], R=[KT, Q[1]], W=[p])
                    return p
                LA = 2
                pq = [score(kts[k]) for k in range(min(LA, len(kts)))]
                for i, kt in enumerate(kts):
                    p = pq.pop(0)
                    if i + LA < len(kts):
                        pq.append(score(kts[i + LA]))
                    e = pT.next()
                    kb.act(e[:], p[:], AF.Exp, scale=0.125, R=[p], W=[e])
                    next(fin, None)
                    if DS < 3:
                        continue
                    for m in range(2):
                        for qt in range(2):
                            kb.mm(acc[m][qt][:, 0:129], e[:, m * 256 + qt * 128:m * 256 + (qt + 1) * 128], VA[:, kt, 0:129],
                                  start=(i == 0), stop=(i == len(kts) - 1), R=[e, VA], W=[acc[m][qt]])
                if DS < 4:
                    continue
                for _ in fin:
                    pass
                aS = accS.next()
                for m in range(2):
                    for qt in range(2):
                        kb.copy(aS[:, m * 2 + qt, :], acc[m][qt][:, 0:129], R=[acc[m][qt]], W=[aS])
                fin = finalize(aS, h, q0)
        for _ in fin:
            pass


def phase_E(kb, g, li, w_ap, kcn, mixT, blocks):
    with phase(kb):
        Wo = sbt(kb, "Wo", [128, kcn, 1024], BF16)
        stg = Rot([sbt(kb, f"estg{i}", [128, kcn * 128]) for i in range(2)])
        load_weight_bf16(kb, Wo, w_ap, 1024, stg, piece=128)
        xb = Rot([sbt(kb, f"exb{i}", [128, 8, 512]) for i in range(2)])
        mb = Rot([sbt(kb, f"emb{i}", [128, kcn, 512], BF16) for i in range(2)])
        pmm = Rot([pst(kb, f"ep{i}") for i in range(4)])
        xsrc = g.xT.rearrange("(kc p) t -> p kc t", p=128)
        msrc = mixT.rearrange("(kc p) t -> p kc t", p=128)
        for (s0, n, j) in blocks:
            x, m = xb.next(), mb.next()
            kb.load(x[:, :, :n], xsrc[:, :, s0:s0 + n], W=[x])
            kb.load(m[:, :, :n], msrc[:, :, s0:s0 + n], W=[m])
            for ct in range(8):
                p = pmm.next()
                for kc in range(kcn):
                    kb.mm(p[:, :n], Wo[:, kc, ct * 128:(ct + 1) * 128], m[:, kc, :n], start=(kc == 0), stop=(kc == kcn - 1), R=[Wo, m], W=[p])
                kb.stt(x[:, ct, :n], p[:, :n], g.mod[li][:, 16 + ct, j:j + 1], x[:, ct, :n], ALU.mult, ALU.add, R=[p, g.mod[li], x], W=[x])
            kb.store(xsrc[:, :, s0:s0 + n], x[:, :, :n], R=[x])


def phase_F(kb, g, li, blocks):
    with phase(kb):
        W1 = sbt(kb, "W1", [128, 8, 4096], BF16)
        W2 = sbt(kb, "W2", [128, 32, 1024], BF16)
        stg = Rot([sbt(kb, f"fstg{i}", [128, 1024]) for i in range(1)])
        load_weight_bf16(kb, W1, g.mlp_w1[li], 4096, stg, piece=128)
        load_weight_bf16(kb, W2, g.mlp_w2[li], 1024, stg, piece=32)
        nb = G()
        nb.sq = sbt(kb, "fsq", [128, 8, 256], BF16)
        nb.tmp = sbt(kb, "ftmp", [128, 8, 256])
        nb.rt = sbt(kb, "frt", [128, 256])
        nb.rstd = sbt(kb, "frstd", [128, 256])
        nb.ss = pst(kb, "fss")
        xb = Rot([sbt(kb, f"fxb{i}", [128, 8, 256]) for i in range(2)])
        hTs = Rot([sbt(kb, f"fhT{i}", [128, 8, 256], BF16) for i in range(2)])
        hid = sbt(kb, "fhid", [128, 32, 256], BF16)
        rl = Rot([sbt(kb, f"frl{i}", [128, 256]) for i in range(3)])
        pmm = Rot([pst(kb, f"fp{i}") for i in range(6)])
        xsrc = g.xT.rearrange("(kc p) t -> p kc t", p=128)

        def prep(blk):
            s0, n, j = blk
            x = xb.next()
            kb.load(x[:, :, :n], xsrc[:, :, s0:s0 + n], W=[x])
            hT = hTs.next()
            norm_mod(kb, g, nb, x, n, g.sc2[li], g.mod[li], j, hT, sh_off=24)
            return x, hT

        nxt = prep(blocks[0])
        for bi, (s0, n, j) in enumerate(blocks):
            x, hT = nxt
            for hc in range(32):
                p = pmm.next()
                for kc in range(8):
                    kb.mm(p[:, :n], W1[:, kc, hc * 128:(hc + 1) * 128], hT[:, kc, :n], start=(kc == 0), stop=(kc == 7), R=[W1, hT], W=[p])
                r = rl.next()
                kb.act(r[:, :n], p[:, :n], AF.Relu, R=[p], W=[r])
                kb.tt(hid[:, hc, :n], r[:, :n], p[:, :n], ALU.mult, R=[r, p], W=[hid])
            if bi + 1 < len(blocks):
                nxt = prep(blocks[bi + 1])
            for ct in range(8):
                p = pmm.next()
                for hc in range(32):
                    kb.mm(p[:, :n], W2[:, hc, ct * 128:(ct + 1) * 128], hid[:, hc, :n], start=(hc == 0), stop=(hc == 31), R=[W2, hid], W=[p])
                kb.stt(x[:, ct, :n], p[:, :n], g.mod[li][:, 40 + ct, j:j + 1], x[:, ct, :n], ALU.mult, ALU.add, R=[p, g.mod[li], x], W=[x])
            kb.store(xsrc[:, :, s0:s0 + n], x[:, :, :n], R=[x])


BLK512 = [(s0, n, 1 if s0 == 0 else 0) for (s0, n) in TBS]
BLK256 = [(s0, 256, 1 if s0 == 0 else 0) for s0 in range(0, T, 256)]

def phase_A1(kb, g):
    with phase(kb):
        W = sbt(kb, "wA1", [128, 8, 6176], BF16)
        stg = Rot([sbt(kb, f"w1stg{i}", [128, 2048]) for i in range(2)])
        load_weight_bf16(kb, W, g.ssd_w_in, 6176, stg)
        nb = NormBufs(kb, "A1")
        xb = Rot([sbt(kb, f"a1x{i}", [128, 8, 512]) for i in range(2)])
        hTs = Rot([sbt(kb, f"a1h{i}", [128, 8, 512], BF16) for i in range(2)])
        pmm = Rot([pst(kb, f"a1p{i}") for i in range(6)])
        st32 = Rot([sbt(kb, f"a1s{i}", [128, 512]) for i in range(4)])
        st16 = Rot([sbt(kb, f"a1z{i}", [128, 512], BF16) for i in range(4)])
        xsrc = g.xT.rearrange("(kc p) t -> p kc t", p=128)
        for bi, (s0, n) in enumerate(TBS):
            j = 1 if bi == 0 else 0
            x = xb.next()
            kb.load(x[:, :, :n], xsrc[:, :, s0:s0 + n], W=[x])
            hT = hTs.next()
            norm_mod(kb, g, nb, x, n, g.sc1[1], g.mod[1], j, hT, sh_off=0)
            for ct in range(32):
                p = pmm.next()
                c0 = 2048 + ct * 128
                for kc in range(8):
                    kb.mm(p[:, :n], W[:, kc, c0:c0 + 128], hT[:, kc, :n], start=(kc == 0), stop=(kc == 7), R=[W, hT], W=[p])
                st = st32.next()
                kb.copy(st[:, :n], p[:, :n], R=[p], W=[st], eng=("act" if ct % 2 else "dve"))
                kb.store(g.xbcT[ct * 128:(ct + 1) * 128, s0:s0 + n], st[:, :n], R=[st])
            for tt_ in range(n // 128):
                ts_ = slice(tt_ * 128, (tt_ + 1) * 128)
                t0 = s0 + tt_ * 128
                for zc in range(4):
                    p = pmm.next()
                    for kc in range(8):
                        kb.mm(p[:, :], hT[:, kc, ts_], W[:, kc, zc * 512:(zc + 1) * 512], start=(kc == 0), stop=(kc == 7), R=[W, hT], W=[p])
                    so = st16.next()
                    kb.copy(so[:], p[:], R=[p], W=[so], eng=("act" if zc % 2 else "dve"))
                    kb.store(g.zTM[t0:t0 + 128, zc * 512:(zc + 1) * 512], so[:], R=[so])
                p = pmm.next()
                for kc in range(8):
                    kb.mm(p[:, 0:32], hT[:, kc, ts_], W[:, kc, 6144:6176], start=(kc == 0), stop=(kc == 7), R=[W, hT], W=[p])
                st = st32.next()
                kb.copy(st[:, 0:32], p[:, 0:32], R=[p], W=[st])
                kb.store(g.dtTM[t0:t0 + 128, :], st[:, 0:32], R=[st])


def phase_B1(kb, g):
    with phase(kb):
        cw = sbt(kb, "cw", [128, 32, 5])
        cb = sbt(kb, "cb", [128, 32])
        kb.load(cw[:], g.conv_wT[:, :, :], W=[cw])
        kb.load(cb[:], g.conv_bT[:, :], W=[cb])
        xin = Rot([sbt(kb, f"b1x{i}", [128, 8, 516]) for i in range(2)])
        acc = Rot([sbt(kb, f"b1a{i}", [128, 512]) for i in range(2)])
        u32 = Rot([sbt(kb, f"b1u{i}", [128, 512]) for i in range(2)])
        u16 = Rot([sbt(kb, f"b1v{i}", [128, 512], BF16) for i in range(3)])
        p32 = Rot([pst(kb, f"b1p{i}") for i in range(3)])
        p16 = Rot([pst(kb, f"b1q{i}", dt=BF16) for i in range(2)])
        t32 = Rot([sbt(kb, f"b1t{i}", [128, 512]) for i in range(3)])
        t16 = Rot([sbt(kb, f"b1s{i}", [128, 512], BF16) for i in range(3)])
        src = g.xbcT.rearrange("(ti p) t -> p ti t", p=128)
        for bi, (s0, n) in enumerate(TBS):
            nt = n // 128
            seq0, seq1 = (0, NCTX) if s0 < NCTX else (NCTX, T)
            lo = max(seq0, s0 - 2)
            hi = min(seq1, s0 + n + 2)
            for grp in range(4):
                xi = xin.next()
                kb.memset(xi[:, :, 0:2], 0.0, W=[xi])
                kb.memset(xi[:, :, n + 2:n + 4], 0.0, W=[xi])
                kb.load(xi[:, :, lo - s0 + 2:hi - s0 + 2], src[:, grp * 8:(grp + 1) * 8, lo:hi], W=[xi])
                for t8 in range(8):
                    ti = grp * 8 + t8
                    a = acc.next()
                    kb.ts(a[:, :n], xi[:, t8, 0:n], cw[:, ti, 0:1], cb[:, ti:ti + 1], ALU.mult, ALU.add, R=[xi, cw, cb], W=[a])
                    for k in range(1, 5):
                        kb.stt(a[:, :n], xi[:, t8, k:k + n], cw[:, ti, k:k + 1], a[:, :n], ALU.mult, ALU.add, R=[xi, cw, a], W=[a])
                    if ti < 16:
                        u = u32.next()
                        kb.act(u[:, :n], a[:, :n], AF.Silu, R=[a], W=[u])
                        p = p32.next()
                        for tt_ in range(nt):
                            kb.tr(p[:, tt_ * 128:(tt_ + 1) * 128], u[:, tt_ * 128:(tt_ + 1) * 128], g.ident[:], R=[u, g.ident], W=[p])
                        t = t32.next()
                        kb.copy(t[:, :n], p[:, :n], R=[p], W=[t], eng=("act" if ti % 2 else "dve"))
                        for tt_ in range(nt):
                            kb.store(g.xsTM[s0 + tt_ * 128:s0 + (tt_ + 1) * 128, ti * 128:(ti + 1) * 128], t[:, tt_ * 128:(tt_ + 1) * 128], R=[t])
                    else:
                        u = u16.next()
                        kb.act(u[:, :n], a[:, :n], AF.Silu, R=[a], W=[u])
                        if ti < 24:
                            gi = ti - 16
                            kb.store(g.BT[gi * 128:(gi + 1) * 128, s0:s0 + n], u[:, :n], R=[u])
                            p = p16.next()
                            for tt_ in range(nt):
                                kb.tr(p[:, tt_ * 128:(tt_ + 1) * 128], u[:, tt_ * 128:(tt_ + 1) * 128], g.ident_bf[:], R=[u, g.ident_bf], W=[p])
                            t = t16.next()
                            kb.copy(t[:, :n], p[:, :n], R=[p], W=[t], eng="act")
                            for tt_ in range(nt):
                                kb.store(g.BTM[s0 + tt_ * 128:s0 + (tt_ + 1) * 128, gi * 128:(gi + 1) * 128], t[:, tt_ * 128:(tt_ + 1) * 128], R=[t])
                        else:
                            gi = ti - 24
                            kb.store(g.CT[gi * 128:(gi + 1) * 128, s0:s0 + n], u[:, :n], R=[u])


def phase_C1(kb, g):
    with phase(kb):
        UT1 = sbt(kb, "UT1", [128, 128])
        LT1 = sbt(kb, "LT1", [128, 128])
        onesf = sbt(kb, "c1ones", [128, 128])
        prm = sbt(kb, "prm", [128, 5, 32])
        aneg = sbt(kb, "aneg", [128, 2, 32])
        kb.load(UT1[:], g.ut1_h[:, :], W=[UT1])
        kb.load(LT1[:], g.lt1_h[:, :], W=[LT1])
        SLT = [sbt(kb, "sLT", [128, 128]), sbt(kb, "sUT", [128, 128])]
        kb.tt(SLT[0][:], LT1[:], g.ident[:], ALU.subtract, R=[LT1, g.ident], W=[SLT[0]])
        kb.tt(SLT[1][:], UT1[:], g.ident[:], ALU.subtract, R=[UT1, g.ident], W=[SLT[1]])
        kb.load(prm[:], g.ssd_prm[:, :, :], W=[prm])
        kb.memset(onesf[:], 1.0, W=[onesf])
        kb.act(aneg[:], prm[:, 2:4, :], AF.Exp, R=[prm], W=[aneg])
        kb.ts(aneg[:], aneg[:], -1.0, None, ALU.mult, R=[aneg], W=[aneg])
        tri = [UT1, LT1]
        triB = [sbt(kb, "UT1b", [128, 128], BF16), sbt(kb, "LT1b", [128, 128], BF16)]
        kb.copy(triB[0][:], UT1[:], R=[UT1], W=[triB[0]])
        kb.copy(triB[1][:], LT1[:], R=[LT1], W=[triB[1]])
        pydd = [[pst(kb, f"c1y{d}{i}") for i in range(2)] for d in range(2)]
        pbig = Rot([pst(kb, f"c1b{i}") for i in range(2)])
        psm = Rot([pst(kb, f"c1s{i}") for i in range(2)])
        st = []
        for d in range(2):
            s = G()
            s.xs = Rot([sbt(kb, f"xs{d}_{i}", [128, 32, 64]) for i in range(1)])
            s.D = sbt(kb, f"Dcs{d}", [128, 32, 128], BF16)
            s.bt = Rot([sbt(kb, f"bt{d}_{i}", [128, 8, 128], BF16) for i in range(2)])
            s.ct = Rot([sbt(kb, f"ct{d}_{i}", [128, 8, 128], BF16) for i in range(2)])
            s.btm = Rot([sbt(kb, f"btm{d}_{i}", [128, 1024], BF16) for i in range(2)])
            s.dt = Rot([sbt(kb, f"dt{d}_{i}", [128, 32]) for i in range(2)])
            s.hf = sbt(kb, f"hf{d}", [128, 32, 64])
            s.hb = Rot([sbt(kb, f"hb{d}_{i}", [128, 32, 64], BF16) for i in range(2)])
            s.xdt = sbt(kb, f"xdt{d}", [128, 32, 64], BF16)
            s.xdw = sbt(kb, f"xdw{d}", [128, 32, 64], BF16)
            s.yo = sbt(kb, f"yo{d}", [128, 8, 64])
            s.y = Rot([sbt(kb, f"y{d}_{i}", [128, 32, 64]) for i in range(1)])
            s.cbm = sbt(kb, f"cbm{d}", [128, 8, 128], BF16)
            kb.memset(s.hf[:], 0.0, W=[s.hf])
            s.h = s.hb.next()
            kb.memset(s.h[:], 0.0, W=[s.h])
            st.append(s)
        sm = lambda nm, w=32: sbt(kb, nm, [128, w])
        ex, dtd, dta, cs, ncs, ecs, wts, etot, csT = [[sm(f"{nm}{d}") for d in range(2)] for nm in
                                                       ("ex", "dtd", "dta", "cs", "ncs", "ecs", "wts", "etot", "csTx")]
        csTs = [sbt(kb, f"csT{d}", [32, 128]) for d in range(2)]
        E4 = Rot([sbt(kb, f"E4{i}", [128, 512], BF16) for i in range(3)])
        G4 = Rot([sbt(kb, f"G4{i}", [128, 4, 128], BF16) for i in range(3)])
        order = [list(range(34)), [1, 0] + list(range(33, 1, -1))]

        def chunk(d, c):
            s = st[d]
            t0 = c * 128
            yield
            xs, bt, ct, btm, dt = s.xs.next(), s.bt.next(), s.ct.next(), s.btm.next(), s.dt.next()
            kb.load(xs[:].rearrange("p h q -> p (h q)"), g.xsTM[t0:t0 + 128, :], W=[xs])
            kb.load(bt[:], g.BT.rearrange("(g n) t -> n g t", n=128)[:, :, t0:t0 + 128], W=[bt])
            kb.load(ct[:], g.CT.rearrange("(g n) t -> n g t", n=128)[:, :, t0:t0 + 128], W=[ct])
            kb.load(btm[:], g.BTM[t0:t0 + 128, :], W=[btm])
            kb.load(dt[:], g.dtTM[t0:t0 + 128, :], W=[dt])
            kb.tt(ex[d][:], dt[:], prm[:, d, :], ALU.add, R=[dt, prm], W=[ex[d]])
            kb.act(ex[d][:], ex[d][:], AF.Exp, R=[ex[d]], W=[ex[d]])
            kb.act(dtd[d][:], ex[d][:], AF.Ln, bias=g.eps_t[:, 4:5], scale=1.0, R=[ex[d], g.eps_t], W=[dtd[d]])
            kb.tt(dta[d][:], dtd[d][:], aneg[:, d, :], ALU.mult, R=[dtd[d], aneg], W=[dta[d]])
            p = psm.next()
            kb.mm(p[:, 0:32], tri[d][:], dta[d][:], R=[tri[d], dta[d]], W=[p])
            kb.mm(p[:, 32:64], onesf[:], dta[d][:], R=[onesf, dta[d]], W=[p])
            kb.copy(cs[d][:], p[:, 0:32], R=[p], W=[cs[d]])
            kb.act(ecs[d][:], p[:, 0:32], AF.Exp, R=[p], W=[ecs[d]])
            kb.act(etot[d][:], p[:, 32:64], AF.Exp, R=[p], W=[etot[d]])
            kb.tt(wts[d][:], p[:, 32:64], cs[d][:], ALU.subtract, R=[p, cs[d]], W=[wts[d]])
            kb.act(wts[d][:], wts[d][:], AF.Exp, R=[wts[d]], W=[wts[d]])
            kb.tt(s.D[:], SLT[d][:, None, :].to_broadcast([128, 32, 128]), dta[d][:, :, None].to_broadcast([128, 32, 128]), ALU.mult,
                  R=[SLT[d], dta[d]], W=[s.D])
            yield
            kb.tt(s.xdt[:], xs[:], dtd[d][:, :, None].to_broadcast([128, 32, 64]), ALU.mult, R=[xs, dtd[d]], W=[s.xdt])
            kb.tt(s.xdw[:], s.xdt[:], wts[d][:, :, None].to_broadcast([128, 32, 64]), ALU.mult, R=[s.xdt, wts[d]], W=[s.xdw])
            for hf in range(2):
                p = pbig.next()
                for g4 in range(4):
                    gi = hf * 4 + g4
                    kb.mm(p[:, g4 * 128:(g4 + 1) * 128], bt[:, gi, :], ct[:, gi, :], R=[bt, ct], W=[p])
                kb.tt(s.cbm[:, hf * 4:(hf + 1) * 4, :], p[:, :].rearrange("p (g l) -> p g l", l=128),
                      tri[d][:, None, :].to_broadcast([128, 4, 128]), ALU.mult, R=[p, tri[d]], W=[s.cbm])
            h_old = s.h
            yt = s.y.next()
            for hf in range(2):
                pyd = pydd[d]
                for g4 in range(4):
                    gi = hf * 4 + g4
                    pc = psm.next()
                    for h4 in range(4):
                        kb.mm(pc[:, h4 * 128:(h4 + 1) * 128], s.D[:, gi * 4 + h4, :], triB[d][:], R=[s.D, triB[d]], W=[pc])
                    e4 = E4.next()
                    kb.act(e4[:], pc[:, :], AF.Exp, R=[pc], W=[e4])
                    g4t = G4.next()
                    kb.tt(g4t[:], e4[:].rearrange("p (h l) -> p h l", l=128), s.cbm[:, gi, None, :].to_broadcast([128, 4, 128]),
                          ALU.mult, R=[e4, s.cbm], W=[g4t])
                    for h4 in range(4):
                        h = gi * 4 + h4
                        h16 = h - hf * 16
                        pb = pyd[h16 // 8]
                        kb.mm(pb[:, (h16 % 8) * 64:(h16 % 8 + 1) * 64], g4t[:, h4, :], s.xdt[:, h, :], R=[g4t, s.xdt], W=[pb])
                    yield
                pyo = [pbig.next(), pbig.next()]
                for g4 in range(4):
                    gi = hf * 4 + g4
                    pb = pyo[g4 // 2]
                    kb.mm(pb[:, (g4 % 2) * 256:(g4 % 2 + 1) * 256], ct[:, gi, :], h_old[:, gi * 4:(gi + 1) * 4, :], R=[ct, h_old], W=[pb])
                for q in range(2):
                    hs = slice(hf * 16 + q * 8, hf * 16 + (q + 1) * 8)
                    kb.tt(s.yo[:], pyo[q][:, :].rearrange("p (h q) -> p h q", q=64), ecs[d][:, hs, None].to_broadcast([128, 8, 64]),
                          ALU.mult, R=[pyo[q], ecs[d]], W=[s.yo])
                    kb.tt(yt[:, hs, :], pyd[q][:, :].rearrange("p (h q) -> p h q", q=64), s.yo[:], ALU.add, R=[pyd[q], s.yo], W=[yt])
            kb.store(g.Yssd[d][t0:t0 + 128, :], yt[:].rearrange("p h q -> p (h q)"), R=[yt])
            yield
            nh = s.hb.next()
            for q in range(4):
                p = pbig.next()
                for g2 in range(2):
                    gi = q * 2 + g2
                    kb.mm(p[:, g2 * 256:(g2 + 1) * 256], btm[:, gi * 128:(gi + 1) * 128], s.xdw[:, gi * 4:(gi + 1) * 4, :], R=[btm, s.xdw], W=[p])
                hs = slice(q * 8, (q + 1) * 8)
                kb.tt(s.hf[:, hs, :], s.hf[:, hs, :], etot[d][:, hs, None].to_broadcast([128, 8, 64]), ALU.mult, R=[s.hf, etot[d]], W=[s.hf])
                kb.tt(s.hf[:, hs, :], s.hf[:, hs, :], p[:, :].rearrange("p (h q) -> p h q", q=64), ALU.add, R=[s.hf, p], W=[s.hf])
            kb.copy(nh[:], s.hf[:], R=[s.hf], W=[nh], eng="act")
            s.h = nh

        for i in range(34):
            run_interleaved([chunk(0, order[0][i]), chunk(1, order[1][i])])


def phase_C3(kb, g):
    with phase(kb):
        prm = sbt(kb, "c3prm", [128, 5, 32])
        ng = sbt(kb, "c3ng", [128, 16])
        kb.load(prm[:], g.ssd_prm[:, :, :], W=[prm])
        kb.load(ng[:], g.ssd_ngT[:, :], W=[ng])
        y0 = Rot([sbt(kb, f"c3a{i}", [128, 32, 64]) for i in range(2)])
        y1 = Rot([sbt(kb, f"c3b{i}", [128, 32, 64]) for i in range(2)])
        xs = Rot([sbt(kb, f"c3x{i}", [128, 32, 64]) for i in range(2)])
        z = Rot([sbt(kb, f"c3z{i}", [128, 2048], BF16) for i in range(2)])
        sz = sbt(kb, "c3sz", [128, 2048])
        sq = sbt(kb, "c3sq", [128, 2048])
        ss = sbt(kb, "c3ss", [128, 8])
        rt = sbt(kb, "c3rt", [128, 8])
        rs = sbt(kb, "c3rs", [128, 8])
        yn = sbt(kb, "c3yn", [128, 4, 2048])
        pb = Rot([pst(kb, f"c3p{i}") for i in range(4)])
        so = Rot([sbt(kb, f"c3o{i}", [128, 512], BF16) for i in range(3)])
        for bi, (s0, n) in enumerate(TBS):
            nt = n // 128
            for tt_ in range(nt):
                t0 = s0 + tt_ * 128
                a, b, x, zz = y0.next(), y1.next(), xs.next(), z.next()
                kb.load(a[:].rearrange("p h q -> p (h q)"), g.Yssd[0][t0:t0 + 128, :], W=[a])
                kb.load(b[:].rearrange("p h q -> p (h q)"), g.Yssd[1][t0:t0 + 128, :], W=[b])
                kb.load(x[:].rearrange("p h q -> p (h q)"), g.xsTM[t0:t0 + 128, :], W=[x])
                kb.load(zz[:], g.zTM[t0:t0 + 128, :], W=[zz])
                kb.tt(a[:], a[:], b[:], ALU.add, R=[a, b], W=[a])
                kb.tt(x[:], x[:], prm[:, 4, :, None].to_broadcast([128, 32, 64]), ALU.mult, R=[x, prm], W=[x])
                kb.tt(a[:], a[:], x[:], ALU.add, R=[a, x], W=[a])
                kb.act(sz[:], zz[:], AF.Silu, R=[zz], W=[sz])
                af = a[:].rearrange("p h q -> p (h q)")
                kb.tt(af, af, sz[:], ALU.mult, R=[a, sz], W=[a])
                kb.tt(sq[:], af, af, ALU.mult, R=[a], W=[sq])
                kb.op("dve", lambda e: e.tensor_reduce(out=ss[:], in_=sq[:].rearrange("p (g c) -> p g c", c=256), axis=AX.X, op=ALU.add), R=[sq], W=[ss])
                kb.act(rt[:], ss[:], AF.Sqrt, bias=g.eps_t[:, 0:1], scale=1.0 / 256.0, R=[ss, g.eps_t], W=[rt])
                kb.op("dve", lambda e: e.reciprocal(out=rs[:], in_=rt[:]), R=[rt], W=[rs])
                kb.tt(yn[:, tt_, :].rearrange("p (g c) -> p g c", c=256), af.rearrange("p (g c) -> p g c", c=256),
                      rs[:, :, None].to_broadcast([128, 8, 256]), ALU.mult, R=[a, rs], W=[yn])
            for ct in range(16):
                p = pb.next()
                for tt_ in range(nt):
                    kb.tr(p[:, tt_ * 128:(tt_ + 1) * 128], yn[:, tt_, ct * 128:(ct + 1) * 128], g.ident[:], R=[yn, g.ident], W=[p])
                o = so.next()
                kb.act(o[:, :n], p[:, :n], AF.Copy, scale=ng[:, ct:ct + 1], R=[p, ng], W=[o])
                kb.store(g.mixT1[ct * 128:(ct + 1) * 128, s0:s0 + n], o[:, :n], R=[o])


def phase_G(kb, g):
    with phase(kb):
        gf = sbt(kb, "gf", [128, 8])
        kb.load(gf[:], g.gfT[:, :], W=[gf])
        xb = Rot([sbt(kb, f"gx{i}", [128, 8, 512]) for i in range(2)])
        sq = sbt(kb, "gsq", [128, 8, 512], BF16)
        ssp = pst(kb, "gss")
        rt = sbt(kb, "grt", [128, 512])
        rs = sbt(kb, "grs", [128, 512])
        xn = sbt(kb, "gxn", [128, 8, 512])
        pt = Rot([pst(kb, f"gp{i}") for i in range(4)])
        o = Rot([sbt(kb, f"go{i}", [128, 1024]) for i in range(2)])
        xsrc = g.xT.rearrange("(kc p) t -> p kc t", p=128)
        for (s0, n) in TBS[1:]:
            x = xb.next()
            kb.load(x[:, :, :n], xsrc[:, :, s0:s0 + n], W=[x])
            kb.act(sq[:, :, :n], x[:, :, :n], AF.Square, R=[x], W=[sq])
            for kc in range(8):
                kb.mm(ssp[:, :n], g.ones_bf[:], sq[:, kc, :n], start=(kc == 0), stop=(kc == 7), R=[sq, g.ones_bf], W=[ssp])
            kb.act(rt[:, :n], ssp[:, :n], AF.Sqrt, bias=g.eps_t[:, 0:1], scale=1.0 / 1024.0, R=[ssp, g.eps_t], W=[rt])
            kb.op("dve", lambda e: e.reciprocal(out=rs[:, :n], in_=rt[:, :n]), R=[rt], W=[rs])
            kb.tt(xn[:, :, :n], x[:, :, :n], rs[:, None, :n].to_broadcast([128, 8, n]), ALU.mult, R=[x, rs], W=[xn])
            for kc in range(8):
                if kc % 2:
                    kb.act(xn[:, kc, :n], xn[:, kc, :n], AF.Identity, scale=gf[:, kc:kc + 1], R=[xn, gf], W=[xn])
                else:
                    kb.ts(xn[:, kc, :n], xn[:, kc, :n], gf[:, kc:kc + 1], None, ALU.mult, R=[xn, gf], W=[xn])
            for tt_ in range(n // 128):
                oo = o.next()
                for hf in range(2):
                    p = pt.next()
                    for j in range(4):
                        kc = hf * 4 + j
                        kb.tr(p[:, j * 128:(j + 1) * 128], xn[:, kc, tt_ * 128:(tt_ + 1) * 128], g.ident[:], R=[xn, g.ident], W=[p])
                    kb.copy(oo[:, hf * 512:(hf + 1) * 512], p[:, :], R=[p], W=[oo], eng=("act" if hf else "dve"))
                t0 = s0 - NCTX + tt_ * 128
                kb.store(g.out[t0:t0 + 128, :], oo[:], R=[oo])


BLK512L = BLK512[1:]
BLK256L = BLK256[1:]

def declare_inputs(nc, g, shapes):
    for name, (shape, dt) in shapes.items():
        setattr(g, name, nc.dram_tensor(name, list(shape), dt, kind="ExternalInput").ap())


def input_shapes():
    S = {}
    S["x"] = ([4096, 1024], F32)
    S["ctx"] = ([256, 1024], F32)
    S["cT"] = ([128, 8, 2], F32)
    S["ada_w"] = ([2, 1024, 6144], F32)
    S["ada_bT"] = ([2, 128, 48], F32)
    S["g1T"] = ([2, 128, 8], F32)
    S["g2T"] = ([2, 128, 8], F32)
    S["gfT"] = ([128, 8], F32)
    S["w_in0"] = ([1024, 4352], F32)
    S["cosT"] = ([128, T], F32)
    S["sinT"] = ([128, T], F32)
    S["ident_h"] = ([128, 128], F32)
    S["hy_w_out"] = ([1024, 1024], F32)
    S["mlp_w1"] = ([2, 1024, 4096], F32)
    S["mlp_w2"] = ([2, 4096, 1024], F32)
    S["rw_cols"] = ([128, 14 + 4 * 7 + 16], F32)
    S["rw_lora"] = ([128, 2, 512], F32)
    S["rw_gup"] = ([128, 512], F32)
    S["blk_h"] = ([128, 128], F32)
    S["cmask_h"] = ([128, 512], F32)
    S["masks_h"] = ([64, 2, 3, 64], F32)
    S["diff_cols"] = ([64, 4], F32)
    S["subln_g"] = ([128, 1], F32)
    S["ssd_w_in"] = ([1024, 6176], F32)
    S["ssd_w_out"] = ([2048, 1024], F32)
    S["conv_wT"] = ([128, 32, 5], F32)
    S["conv_bT"] = ([128, 32], F32)
    S["ssd_prm"] = ([128, 5, 32], F32)
    S["ssd_ngT"] = ([128, 16], F32)
    S["ut1_h"] = ([128, 128], F32)
    S["lt1_h"] = ([128, 128], F32)
    S["sel_h"] = ([32, 32, 128], F32)
    return S


def build(debug=False, stop_after=None, only=None, as_input=()):
    nc = bass.Bass("TRN2", target_bir_lowering=False)
    _AS_INPUT.clear()
    _AS_INPUT.update(as_input)
    g = G()
    declare_inputs(nc, g, input_shapes())
    g.out = nc.dram_tensor("out", [4096, 1024], F32, kind="ExternalOutput").ap()
    dbg = debug
    g.xT = dram(nc, "xT", [1024, T], F32, dbg)
    g.PrT = dram(nc, "PrT", [1792, T], F32, dbg)
    g.QrotT = dram(nc, "QrotT", [512, T], BF16, dbg)
    g.QplT = dram(nc, "QplT", [512, T], BF16, dbg)
    g.KrotT = dram(nc, "KrotT", [512, T], BF16, dbg)
    g.Vd = dram(nc, "Vd", [T, 512], BF16, dbg)
    g.modD = dram(nc, "modD", [2, 128, 96], F32, dbg)
    g.gT = dram(nc, "gT", [512, T], F32, dbg)
    g.bonT = dram(nc, "bonT", [512, T], F32, dbg)
    g.Vtm = dram(nc, "Vtm", [T, 512], BF16, dbg)
    g.gamA = dram(nc, "gamA", [2, 512, 68], F32, dbg)
    g.gam = [g.gamA[0], g.gamA[1]]
    g.FMA = dram(nc, "FMA", [2, 4, 512, T], BF16, dbg)
    g.FM = [[g.FMA[d, k] for k in range(4)] for d in range(2)]
    g.TMA = dram(nc, "TMA", [2, T, 2, 512], BF16, dbg)
    g.TM = [g.TMA[0], g.TMA[1]]
    g.mixT = dram(nc, "mixT", [1024, T], BF16, dbg)
    g.xbcT = dram(nc, "xbcT", [4096, T], F32, False)
    g.zTM = dram(nc, "zTM", [T, 2048], BF16, False)
    g.dtTM = dram(nc, "dtTM", [T, 32], F32, dbg)
    g.xsTM = dram(nc, "xsTM", [T, 2048], F32, dbg)
    g.BT = dram(nc, "BT", [1024, T], BF16, dbg)
    g.CT = dram(nc, "CT", [1024, T], BF16, dbg)
    g.BTM = dram(nc, "BTM", [T, 1024], BF16, False)
    g.YsA = dram(nc, "YsA", [2, T, 2048], F32, dbg)
    g.Yssd = [g.YsA[0], g.YsA[1]]
    g.mixT1 = dram(nc, "mixT1", [2048, T], BF16, dbg)
    g.YA = dram(nc, "YA", [2, T, 512], F32, dbg)
    g.Y = [g.YA[0], g.YA[1]]
    with ExitStack() as es:
        kb = KB(nc, es)
        kb.es_t = None
        g.ident = kb.sb("ident", [128, 128])
        g.ones_bf = kb.sb("ones_bf", [128, 128], BF16)
        g.eps_t = kb.sb("eps_t", [128, 8])
        g.mod = [kb.sb(f"mod{li}", [128, 48, 2]) for li in range(2)]
        g.sc1 = [kb.sb(f"sc1_{li}", [128, 8, 2]) for li in range(2)]
        g.sc2 = [kb.sb(f"sc2_{li}", [128, 8, 2]) for li in range(2)]
        kb.load(g.ident[:], g.ident_h[:, :], W=[g.ident])
        kb.memset(g.ones_bf[:], 1.0, W=[g.ones_bf])
        g.ident_bf = kb.sb("ident_bf", [128, 128], BF16)
        kb.copy(g.ident_bf[:], g.ident[:], R=[g.ident], W=[g.ident_bf])
        kb.memset(g.eps_t[:, 0:1], EPS, W=[g.eps_t])
        kb.memset(g.eps_t[:, 1:2], 1e-12, W=[g.eps_t])
        kb.memset(g.eps_t[:, 2:3], 64e-5, W=[g.eps_t])
        kb.memset(g.eps_t[:, 3:4], 0.0, W=[g.eps_t])
        kb.memset(g.eps_t[:, 4:5], 1.0, W=[g.eps_t])
        phases = [("mods", phase_mods), ("xT", phase_xT), ("A0", phase_A0), ("B0", phase_B0), ("C0", phase_C0), ("C2", phase_C2), ("D0", phase_D0),
                  ("E0", lambda kb, g: phase_E(kb, g, 0, g.hy_w_out, 8, g.mixT, BLK512)),
                  ("F0", lambda kb, g: phase_F(kb, g, 0, BLK256)),
                  ("A1", phase_A1), ("B1", phase_B1), ("C1", phase_C1), ("C3", phase_C3),
                  ("E1", lambda kb, g: phase_E(kb, g, 1, g.ssd_w_out, 16, g.mixT1, BLK512L)),
                  ("F1", lambda kb, g: phase_F(kb, g, 1, BLK256L)), ("G", phase_G)]
        for name, fn in phases:
            if only is not None and name not in only:
                continue
            fn(kb, g)
            if stop_after == name:
                break
        if debug:
            for li in range(2):
                kb.store(g.modD[li], g.mod[li][:].rearrange("p a b -> p (a b)"), R=[g.mod[li]])
        kb.finish()
        print("instructions:", kb.nins)
    return nc


def rope_tables():
    inv = 10000.0 ** (-np.arange(0, 32, 2, dtype=np.float32) / 32.0)
    t = np.arange(4096)
    rows = (t // 64).astype(np.float32)
    cols = (t % 64).astype(np.float32)
    ar = rows[:, None] * inv[None, :]
    ac = cols[:, None] * inv[None, :]
    cosT = np.ones((128, T), np.float32)
    sinT = np.zeros((128, T), np.float32)
    for p in range(128):
        d = p % 64
        ang = ar if d < 32 else ac
        i = d % 16
        first = (d % 32) < 16
        cosT[p, 256:] = np.cos(ang[:, i])
        sinT[p, 256:] = (-np.sin(ang[:, i])) if first else np.sin(ang[:, i])
    return cosT, sinT


def swap_cols(w):
    idx = np.arange(512)
    d = idx % 32
    partner = np.where(d < 16, idx + 16, idx - 16)
    return w[:, partner]


def host_consts(inp):
    C = {}
    f = np.float32
    C["ada_w"] = np.ascontiguousarray(inp["ada_w"], dtype=f)
    C["ada_bT"] = np.ascontiguousarray(inp["ada_b"].reshape(2, 48, 128).transpose(0, 2, 1), dtype=f)
    C["g1T"] = np.ascontiguousarray(inp["norm1_g"].reshape(2, 8, 128).transpose(0, 2, 1), dtype=f)
    C["g2T"] = np.ascontiguousarray(inp["norm2_g"].reshape(2, 8, 128).transpose(0, 2, 1), dtype=f)
    C["gfT"] = np.ascontiguousarray(inp["norm_f_g"].reshape(8, 128).T, dtype=f)
    w = inp["hy_w_in"][0]
    q = w[:, 1792:2304]
    k = w[:, 2304:2816]
    v = w[:, 2816:3328]
    C["w_in0"] = np.ascontiguousarray(np.concatenate([w[:, :1792], q, swap_cols(q), k, swap_cols(k), v], axis=1), dtype=f)
    C["cosT"], C["sinT"] = rope_tables()
    C["ident_h"] = np.eye(128, dtype=f)
    C["hy_w_out"] = np.ascontiguousarray(inp["hy_w_out"][0], dtype=f)
    C["mlp_w1"] = np.ascontiguousarray(inp["mlp_w1"], dtype=f)
    C["mlp_w2"] = np.ascontiguousarray(inp["mlp_w2"], dtype=f)
    col = lambda a: np.ascontiguousarray(np.asarray(a, dtype=f).reshape(-1, 128).T)
    rw = np.zeros((128, 58), f)
    rw[:, 0:14] = col(inp["rwkv_mu"][0])
    rw[:, 14:18] = col(inp["rwkv_k_k"][0])
    rw[:, 18:22] = col(inp["rwkv_k_a"][0])
    rw[:, 22:26] = col(inp["rwkv_r_k"][0].reshape(-1))
    rw[:, 26:30] = col(inp["rwkv_ln_w"][0])
    rw[:, 30:34] = col(inp["rwkv_ln_b"][0])
    rw[:, 34:38] = col(inp["rwkv_w0"][0, 0])
    rw[:, 38:42] = col(inp["rwkv_w0"][0, 1])
    rw[:, 42:46] = col(inp["rwkv_a0"][0, 0])
    rw[:, 46:50] = col(inp["rwkv_a0"][0, 1])
    C["rw_cols"] = rw
    lora = np.zeros((128, 2, 512), f)
    lora[0:64] = inp["rwkv_w_up"][0].transpose(1, 0, 2)
    lora[64:128] = inp["rwkv_a_up"][0].transpose(1, 0, 2)
    C["rw_lora"] = lora
    C["rw_gup"] = np.ascontiguousarray(inp["rwkv_g_up"][0], dtype=f)
    blk = np.zeros((128, 128), f)
    blk[:64, :64] = 1
    blk[64:, 64:] = 1
    C["blk_h"] = blk
    cm = np.ones((128, 512), f)
    cm[:, ::64] = 0
    C["cmask_h"] = cm
    s = np.arange(64)[:, None]
    t = np.arange(64)[None, :]
    m = np.zeros((64, 2, 3, 64), f)
    m[:, 0, 0] = (s < t)
    m[:, 0, 1] = (s <= t)
    m[:, 0, 2] = (s > t)
    m[:, 1, 0] = (s > t)
    m[:, 1, 1] = (s >= t)
    m[:, 1, 2] = (s < t)
    C["masks_h"] = m
    C["diff_cols"] = np.stack([inp["diff_lq1"][0], inp["diff_lk1"][0], inp["diff_lq2"][0], inp["diff_lk2"][0]], axis=1).astype(f)
    C["subln_g"] = np.ascontiguousarray(inp["diff_subln_g"][0].reshape(128, 1), dtype=f)
    C["ssd_w_in"] = np.ascontiguousarray(inp["ssd_w_in"][0], dtype=f)
    C["ssd_w_out"] = np.ascontiguousarray(inp["ssd_w_out"][0], dtype=f)
    C["conv_wT"] = np.ascontiguousarray(inp["ssd_conv_w"][0].reshape(5, 32, 128).transpose(2, 1, 0), dtype=f)
    C["conv_bT"] = col(inp["ssd_conv_b"][0])
    prm = np.zeros((128, 5, 32), f)
    prm[:, 0] = inp["ssd_dt_bias"][0, 0][None, :]
    prm[:, 1] = inp["ssd_dt_bias"][0, 1][None, :]
    prm[:, 2] = inp["ssd_a_log"][0, 0][None, :]
    prm[:, 3] = inp["ssd_a_log"][0, 1][None, :]
    prm[:, 4] = inp["ssd_d"][0][None, :]
    C["ssd_prm"] = prm
    C["ssd_ngT"] = col(inp["ssd_norm_g"][0])
    jj = np.arange(128)[:, None]
    ll = np.arange(128)[None, :]
    C["ut1_h"] = (jj <= ll).astype(f)
    C["lt1_h"] = (jj >= ll).astype(f)
    sel = np.zeros((32, 32, 128), f)
    for h in range(32):
        sel[h, h, :] = 1.0
    C["sel_h"] = sel
    return C


def core_inputs(inp, C, b):
    m = dict(C)
    m["x"] = np.ascontiguousarray(inp["x"][b], dtype=np.float32)
    m["ctx"] = np.ascontiguousarray(inp["ctx"][b], dtype=np.float32)
    cv = np.stack([inp["c"][b], inp["c_ctx"]], axis=0).astype(np.float32)
    m["cT"] = np.ascontiguousarray(cv.reshape(2, 8, 128).transpose(2, 1, 0))
    return m


def kernel(**inputs):
    inp = {k: np.asarray(v) for k, v in inputs.items()}
    C = host_consts(inp)
    nc = build()
    in_maps = [core_inputs(inp, C, b) for b in range(8)]
    res = run_bass_kernel_spmd(nc, in_maps, core_ids=list(range(8)))
    return np.stack([np.asarray(r["out"]) for r in res.results], axis=0).astype(np.float32)
```

```python
import concourse.bass as bass
import concourse.mybir as mybir

F32 = mybir.dt.float32
BF16 = mybir.dt.bfloat16
AF = mybir.ActivationFunctionType
ALU = mybir.AluOpType
AX = mybir.AxisListType


class Src:
    def __init__(s, kb, name, inc, limit):
        s.kb, s.name, s.inc, s.limit = kb, name, inc, limit
        s.sems = []
        s.n = 0

    def sem_for(s, n):
        e = (n - 1) // s.limit
        while len(s.sems) <= e:
            s.sems.append(s.kb.es.enter_context(s.kb.nc.semaphore(f"{s.name}_{len(s.sems)}")))
        return s.sems[e], ((n - 1) % s.limit + 1) * s.inc


class Tk:
    __slots__ = ("w", "r")

    def __init__(s):
        s.w = {}
        s.r = {}


class TT:
    def __init__(s, t, k=None, ps=False):
        s.t = t
        s.k = k if k is not None else Tk()
        s.ps = ps

    def __getitem__(s, idx):
        return s.t[idx]


class KB:
    NSLOT = 20

    def __init__(s, nc, es):
        s.nc, s.es = nc, es
        s.eng = {"pe": nc.tensor, "dve": nc.vector, "act": nc.scalar, "pool": nc.gpsimd, "sp": nc.sync}
        s.src = {k: Src(s, "c" + k, 1, 30000) for k in s.eng}
        s.waited = {k: {} for k in s.eng}
        s.slots = {q: [Src(s, f"d{q}{i}", 16, 1800) for i in range(s.NSLOT)] for q in ("sp", "pool", "act")}
        s.rr = {q: 0 for q in s.slots}
        s.nins = 0
        s.same_engine_sync = True

    def sb(s, name, shape, dt=F32):
        return TT(s.es.enter_context(s.nc.sbuf_tensor("g_" + name, list(shape), dt)))

    def ps(s, name, shape, dt=F32):
        return TT(s.es.enter_context(s.nc.psum_tensor("gp_" + name, list(shape), dt)))

    def _deps(s, R, W, me=None):
        d = {}
        for r in R:
            k = r.k if isinstance(r, TT) else r
            for src, n in k.w.items():
                if d.get(src, 0) < n:
                    d[src] = n
            if isinstance(r, TT) and r.ps:
                for src, n in k.r.items():
                    if src is not me and d.get(src, 0) < n:
                        d[src] = n
        for w in W:
            k = w.k if isinstance(w, TT) else w
            for dd in (k.w, k.r):
                for src, n in dd.items():
                    if d.get(src, 0) < n:
                        d[src] = n
        return d

    def _wait(s, eng, d):
        wd = s.waited[eng]
        for src, n in d.items():
            if src is s.src[eng] and (eng == "pe" or not s.same_engine_sync):
                continue
            if wd.get(src, 0) >= n:
                continue
            sem, val = src.sem_for(n)
            s.eng[eng].wait_ge(sem, val)
            wd[src] = n

    def _mark(s, src, n, R, W):
        for w in W:
            k = w.k if isinstance(w, TT) else w
            k.w = {src: n}
            k.r = {}
        for r in R:
            k = r.k if isinstance(r, TT) else r
            if k.r.get(src, 0) < n:
                k.r[src] = n

    def op(s, eng, fn, R=(), W=()):
        d = s._deps(R, W, s.src[eng])
        s._wait(eng, d)
        src = s.src[eng]
        src.n += 1
        sem, _ = src.sem_for(src.n)
        ins = fn(s.eng[eng])
        ins.then_inc(sem, 1)
        s._mark(src, src.n, R, W)
        s.nins += 1

    def dma(s, q, out, in_, R=(), W=(), **kw):
        i = s.rr[q]
        s.rr[q] = (i + 1) % s.NSLOT
        slot = s.slots[q][i]
        d = s._deps(R, W)
        if slot.n > 0 and d.get(slot, 0) < slot.n:
            d[slot] = slot.n
        s._wait(q, d)
        slot.n += 1
        sem, _ = slot.sem_for(slot.n)
        s.eng[q].dma_start(out=out, in_=in_, **kw).then_inc(sem, 16)
        s._mark(slot, slot.n, R, W)
        s.nins += 1

    def load(s, out, in_, R=(), W=(), **kw):
        s.dma("sp", out, in_, R, W, **kw)

    def store(s, out, in_, R=(), W=(), **kw):
        s.dma("pool", out, in_, R, W, **kw)

    def finish(s):
        d = {}
        for q in s.slots:
            for sl in s.slots[q]:
                if sl.n:
                    d[sl] = sl.n
        for k, src in s.src.items():
            if src.n and k != "sp":
                d[src] = src.n
        s._wait("sp", d)

    def mm(s, out, lhsT, rhs, start=True, stop=True, R=(), W=()):
        s.op("pe", lambda e: e.matmul(out, lhsT=lhsT, rhs=rhs, start=start, stop=stop), R, W)

    def tr(s, out, in_, ident, R=(), W=()):
        s.op("pe", lambda e: e.transpose(out, in_, ident), R, W)

    def act(s, out, in_, func, bias=None, scale=None, R=(), W=(), accum_out=None):
        kw = {}
        if bias is not None:
            kw["bias"] = bias
        if scale is not None:
            kw["scale"] = scale
        if accum_out is not None:
            kw["accum_out"] = accum_out
        s.op("act", lambda e: e.activation(out=out, in_=in_, func=func, **kw), R, W)

    def tt(s, out, in0, in1, op, R=(), W=(), eng="dve"):
        s.op(eng, lambda e: e.tensor_tensor(out=out, in0=in0, in1=in1, op=op), R, W)

    def ts(s, out, in0, s1, s2, op0, op1=None, R=(), W=(), eng="dve"):
        if op1 is None:
            s.op(eng, lambda e: e.tensor_scalar(out=out, in0=in0, scalar1=s1, scalar2=None, op0=op0), R, W)
        else:
            s.op(eng, lambda e: e.tensor_scalar(out=out, in0=in0, scalar1=s1, scalar2=s2, op0=op0, op1=op1), R, W)

    def stt(s, out, in0, scalar, in1, op0, op1, R=(), W=()):
        s.op("dve", lambda e: e.scalar_tensor_tensor(out=out, in0=in0, scalar=scalar, in1=in1, op0=op0, op1=op1), R, W)

    def copy(s, out, in_, R=(), W=(), eng="dve"):
        if eng == "act":
            s.op("act", lambda e: e.copy(out=out, in_=in_), R, W)
        else:
            s.op(eng, lambda e: e.tensor_copy(out=out, in_=in_), R, W)

    def memset(s, ap, val, W=(), eng="dve"):
        s.op(eng, lambda e: e.memset(ap, val), (), W)
import math
import numpy as np
from contextlib import ExitStack, contextmanager
from concourse.bass_utils import run_bass_kernel_spmd

T = 4352
NCTX = 256
TBS = [(0, 256)] + [(256 + 512 * i, 512) for i in range(8)]
KAPPA = math.exp(-0.5)
EPS = 1e-6


class G:
    pass


@contextmanager
def phase(kb):
    old = kb.es
    barrier(kb)
    with ExitStack() as es:
        kb.es_t = es
        yield
        barrier(kb)
    kb.es_t = None


def barrier(kb):
    d = {}
    for q in kb.slots:
        for sl in kb.slots[q]:
            if sl.n:
                d[sl] = sl.n
    for k, src in kb.src.items():
        if src.n:
            d[src] = src.n
    for e in ("pe", "dve", "act", "pool", "sp"):
        dd = {s_: n for s_, n in d.items() if s_ is not kb.src[e]}
        kb._wait(e, dd)


_uid = [0]


def sbt(kb, name, shape, dt=F32):
    _uid[0] += 1
    return TT(kb.es_t.enter_context(kb.nc.sbuf_tensor(f"s{_uid[0]}_{name}", list(shape), dt)))


def pst(kb, name, shape=None, dt=F32):
    _uid[0] += 1
    full = [128, 512] if dt == F32 else [128, 1024]
    return TT(kb.es_t.enter_context(kb.nc.psum_tensor(f"p{_uid[0]}_{name}", full, dt)), ps=True)


def run_interleaved(gens):
    gens = list(gens)
    while gens:
        for g_ in list(gens):
            try:
                next(g_)
            except StopIteration:
                gens.remove(g_)


class PPool:
    def __init__(s, banks):
        s.b = banks
        s.live = [False] * len(banks)
        s.i = 0

    def get(s):
        n = len(s.b)
        for k in range(n):
            j = (s.i + k) % n
            if not s.live[j]:
                s.live[j] = True
                s.i = (j + 1) % n
                return s.b[j]
        raise RuntimeError("PSUM pool exhausted")

    def put(s, bank):
        s.live[s.b.index(bank)] = False


class Rot:
    def __init__(s, items):
        s.items = items
        s.i = 0

    def next(s):
        x = s.items[s.i]
        s.i = (s.i + 1) % len(s.items)
        return x


_AS_INPUT = set()


def dram(nc, name, shape, dt, debug):
    kind = "ExternalInput" if name in _AS_INPUT else ("ExternalOutput" if debug else "Internal")
    return nc.dram_tensor(name, list(shape), dt, kind=kind).ap()


def phase_mods(kb, g):
    with phase(kb):
        run_interleaved([_mods_gen(kb, g), _xT_gen(kb, g)])


def _mods_gen(kb, g):
    if True:
        cT = sbt(kb, "cT", [128, 8, 2])
        scT = sbt(kb, "scT", [128, 8, 2])
        sg_ = sbt(kb, "sgc", [128, 8, 2])
        kb.load(cT[:], g.cT[:, :, :], W=[cT])
        kb.act(sg_[:], cT[:], AF.Sigmoid, R=[cT], W=[sg_])
        kb.tt(scT[:], cT[:], sg_[:], ALU.mult, R=[cT, sg_], W=[scT])
        wb = Rot([sbt(kb, f"adaw{i}", [128, 8, 1024]) for i in range(2)])
        pmb = pst(kb, "pmod")
        pm = TT(pmb.t[:, 0:96].rearrange("p (a b) -> p a b", b=2), pmb.k, ps=True)
        adab = sbt(kb, "adab", [128, 48])
        g1 = sbt(kb, "g1", [128, 8])
        g2 = sbt(kb, "g2", [128, 8])
        for li in range(2):
            kb.load(adab[:], g.ada_bT[li], W=[adab])
            kb.load(g1[:], g.g1T[li], W=[g1])
            kb.load(g2[:], g.g2T[li], W=[g2])
            src = g.ada_w[li].rearrange("(kc p) n -> p kc n", p=128)
            for pc in range(6):
                w = wb.next()
                kb.load(w[:], src[:, :, pc * 1024:(pc + 1) * 1024], W=[w])
                yield
                for cc in range(8):
                    col = pc * 8 + cc
                    for kc in range(8):
                        kb.mm(pm[:, col, :], w[:, kc, cc * 128:(cc + 1) * 128], scT[:, kc, :],
                              start=(kc == 0), stop=(kc == 7), R=[w, scT], W=[pm])
            mod = g.mod[li]
            kb.tt(mod[:], pm[:], adab[:, :, None].to_broadcast([128, 48, 2]), ALU.add, R=[pm, adab], W=[mod])
            for (sc, gi, m) in ((g.sc1[li], g1, 1), (g.sc2[li], g2, 4)):
                kb.ts(sc[:], mod[:, m * 8:(m + 1) * 8, :], 1.0, None, ALU.add, R=[mod], W=[sc])
                kb.tt(sc[:], sc[:], gi[:, :, None].to_broadcast([128, 8, 2]), ALU.mult, R=[sc, gi], W=[sc])


def phase_xT(kb, g):
    return


def _xT_gen(kb, g):
    if True:
        xin = Rot([sbt(kb, f"xin{i}", [128, 1024]) for i in range(2)])
        xo = Rot([sbt(kb, f"xo{i}", [128, 8, 128]) for i in range(2)])
        pt = Rot([pst(kb, f"pT{i}") for i in range(4)])
        dst = g.xT.rearrange("(kc p) t -> p kc t", p=128)
        for i in range(34):
            xi = xin.next()
            src = g.ctx[i * 128:(i + 1) * 128, :] if i < 2 else g.x[(i - 2) * 128:(i - 1) * 128, :]
            kb.load(xi[:], src, W=[xi])
            o = xo.next()
            for hf in range(2):
                p = pt.next()
                for j in range(4):
                    kc = hf * 4 + j
                    kb.tr(p[:, j * 128:(j + 1) * 128], xi[:, kc * 128:(kc + 1) * 128], g.ident[:], R=[xi, g.ident], W=[p])
                kb.copy(o[:, hf * 4:(hf + 1) * 4, :], p[:, :].rearrange("p (a b) -> p a b", b=128), R=[p], W=[o], eng=("act" if hf else "dve"))
            kb.store(dst[:, :, i * 128:(i + 1) * 128], o[:], R=[o])
            if i % 3 == 2:
                yield


class NormBufs:
    def __init__(s, kb, tag):
        s.sq = sbt(kb, f"nsq{tag}", [128, 8, 512], BF16)
        s.tmp = sbt(kb, f"ntmp{tag}", [128, 8, 512])
        s.rt = sbt(kb, f"nrt{tag}", [128, 512])
        s.rstd = sbt(kb, f"nrstd{tag}", [128, 512])
        s.ss = pst(kb, f"nss{tag}", [128, 512])


def norm_mod(kb, g, nb, xTb, n, sc, sh, j, hT, sh_off=0):
    kb.act(nb.sq[:, :, :n], xTb[:, :, :n], AF.Square, R=[xTb], W=[nb.sq])
    for kc in range(8):
        kb.mm(nb.ss[:, :n], g.ones_bf[:], nb.sq[:, kc, :n], start=(kc == 0), stop=(kc == 7), R=[nb.sq, g.ones_bf], W=[nb.ss])
    kb.act(nb.rt[:, :n], nb.ss[:, :n], AF.Sqrt, bias=g.eps_t[:, 0:1], scale=1.0 / 1024.0, R=[nb.ss, g.eps_t], W=[nb.rt])
    kb.op("dve", lambda e: e.reciprocal(out=nb.rstd[:, :n], in_=nb.rt[:, :n]), R=[nb.rt], W=[nb.rstd])
    kb.tt(nb.tmp[:, :, :n], xTb[:, :, :n], nb.rstd[:, None, :n].to_broadcast([128, 8, n]), ALU.mult, R=[xTb, nb.rstd], W=[nb.tmp])
    for kc in range(8):
        kb.act(hT[:, kc, :n], nb.tmp[:, kc, :n], AF.Identity, bias=sh[:, sh_off + kc, j:j + 1], scale=sc[:, kc, j:j + 1],
               R=[nb.tmp, sc, sh], W=[hT])


def load_weight_bf16(kb, W, src_ap, ncols, stg, piece=256):
    kcn = src_ap.shape[0] // 128
    i = 0
    for kc in range(kcn):
        c0 = 0
        while c0 < ncols:
            st = stg.next()
            w = min(ncols - c0, st.t.shape[1])
            kb.load(st.t[:, 0:w], src_ap[kc * 128:(kc + 1) * 128, c0:c0 + w], W=[st])
            kb.copy(W[:, kc, c0:c0 + w], st.t[:, 0:w], R=[st], W=[W], eng=("act" if i % 2 else "dve"))
            i += 1
            c0 += w


def phase_A0(kb, g):
    with phase(kb):
        W = sbt(kb, "wA", [128, 8, 4352], BF16)
        stg = Rot([sbt(kb, f"wstg{i}", [128, 2048]) for i in range(2)])
        import os
        STG = int(os.environ.get("A0_STAGE", "9"))
        load_weight_bf16(kb, W, g.w_in0, 4352, stg)
        nb = NormBufs(kb, "A")
        xb = Rot([sbt(kb, f"xTb{i}", [128, 8, 512]) for i in range(2)])
        hTs = Rot([sbt(kb, f"hT{i}", [128, 8, 512], BF16) for i in range(2)])
        cosb = Rot([sbt(kb, f"cos{i}", [128, 512]) for i in range(2)])
        sinb = Rot([sbt(kb, f"sin{i}", [128, 512]) for i in range(2)])
        pmm = Rot([pst(kb, f"pmm{i}", [128, 512]) for i in range(6)])
        st32 = Rot([sbt(kb, f"st32_{i}", [128, 512]) for i in range(4)])
        st16 = Rot([sbt(kb, f"st16_{i}", [128, 512], BF16) for i in range(6)])
        t1s = Rot([sbt(kb, f"t1_{i}", [128, 512]) for i in range(2)])
        t2s = Rot([sbt(kb, f"t2_{i}", [128, 512]) for i in range(2)])
        xsrc = g.xT.rearrange("(kc p) t -> p kc t", p=128)
        for bi, (s0, n) in enumerate(TBS):
            if STG < 2 or (STG < 9 and bi > 0):
                break
            j = 1 if bi == 0 else 0
            x = xb.next()
            kb.load(x[:, :, :n], xsrc[:, :, s0:s0 + n], W=[x])
            cs, sn = cosb.next(), sinb.next()
            kb.load(cs[:, :n], g.cosT[:, s0:s0 + n], W=[cs])
            kb.load(sn[:, :n], g.sinT[:, s0:s0 + n], W=[sn])
            hT = hTs.next()
            norm_mod(kb, g, nb, x, n, g.sc1[0], g.mod[0], j, hT, sh_off=0)

            def proj(ct):
                p = pmm.next()
                for kc in range(8):
                    kb.mm(p[:, :n], W[:, kc, ct * 128:(ct + 1) * 128], hT[:, kc, :n], start=(kc == 0), stop=(kc == 7), R=[W, hT], W=[p])
                return p
            if STG < 3:
                continue
            for ct in range(14):
                p = proj(ct)
                st = st32.next()
                kb.copy(st[:, :n], p[:, :n], R=[p], W=[st], eng=("act" if ct % 2 else "dve"))
                kb.store(g.PrT[ct * 128:(ct + 1) * 128, s0:s0 + n], st[:, :n], R=[st])
            if STG < 4:
                continue
            for h in range(4):
                for (base, dst_rot, dst_pl) in ((14, g.QrotT, g.QplT), (22, g.KrotT, None)):
                    pq = proj(base + h)
                    pw = proj(base + 4 + h)
                    t1, t2 = t1s.next(), t2s.next()
                    kb.tt(t1[:, :n], pq[:, :n], cs[:, :n], ALU.mult, R=[pq, cs], W=[t1])
                    kb.tt(t2[:, :n], pw[:, :n], sn[:, :n], ALU.mult, R=[pw, sn], W=[t2])
                    so = st16.next()
                    kb.tt(so[:, :n], t1[:, :n], t2[:, :n], ALU.add, R=[t1, t2], W=[so])
                    if not os.environ.get("NOSTORE4"):
                        kb.store(dst_rot[h * 128:(h + 1) * 128, s0:s0 + n], so[:, :n], R=[so])
                    if dst_pl is not None:
                        sp_ = st16.next()
                        kb.copy(sp_[:, :n], pq[:, :n], R=[pq], W=[sp_], eng="act")
                        if not os.environ.get("NOSTORE4"):
                            kb.store(dst_pl[h * 128:(h + 1) * 128, s0:s0 + n], sp_[:, :n], R=[sp_])
            if STG < 5:
                continue
            for tt_ in range(n // 128):
                p = pmm.next()
                for kc in range(8):
                    kb.mm(p[:, :], hT[:, kc, tt_ * 128:(tt_ + 1) * 128], W[:, kc, 3840:4352], start=(kc == 0), stop=(kc == 7), R=[W, hT], W=[p])
                so = st16.next()
                kb.copy(so[:], p[:], R=[p], W=[so], eng=("act" if tt_ % 2 else "dve"))
                kb.store(g.Vd[s0 + tt_ * 128:s0 + (tt_ + 1) * 128, :], so[:], R=[so])

def phase_B0(kb, g):
    with phase(kb):
        rwc = sbt(kb, "rwc", [128, 58])
        hmu = sbt(kb, "hmu", [128, 14])
        omm = sbt(kb, "omm", [128, 14])
        omka = sbt(kb, "omka", [128, 4])
        hrk = sbt(kb, "hrk", [128, 4])
        lora = sbt(kb, "lora", [128, 2, 512])
        gup = sbt(kb, "gup", [128, 512])
        blk = sbt(kb, "blk", [128, 128])
        cmask = sbt(kb, "cmask", [128, 512])
        kb.load(rwc[:], g.rw_cols[:, :], W=[rwc])
        kb.load(lora[:], g.rw_lora[:, :, :], W=[lora])
        kb.load(gup[:], g.rw_gup[:, :], W=[gup])
        kb.load(blk[:], g.blk_h[:, :], W=[blk])
        kb.load(cmask[:], g.cmask_h[:, :], W=[cmask])
        kb.ts(hmu[:], rwc[:, 0:14], 0.5, None, ALU.mult, R=[rwc], W=[hmu])
        kb.ts(omm[:], rwc[:, 0:14], -1.0, 1.0, ALU.mult, ALU.add, R=[rwc], W=[omm])
        kb.ts(omka[:], rwc[:, 18:22], -1.0, 1.0, ALU.mult, ALU.add, R=[rwc], W=[omka])
        kb.ts(hrk[:], rwc[:, 22:26], 0.5, None, ALU.mult, R=[rwc], W=[hrk])

        pin = sbt(kb, "pin", [128, 14, 514])
        psx = sbt(kb, "psx", [128, 14, 512])
        lwin = sbt(kb, "lwin", [128, 512])
        sgd = sbt(kb, "sgd", [128, 512])
        F = lambda nm, dt=F32: sbt(kb, nm, [128, 512], dt)
        R2 = lambda nm: Rot([F(f"{nm}{i}") for i in range(2)])
        hp_rots = [R2(nm) for nm in ("kku", "sq", "rt", "rs", "kk", "rk", "bon", "kbs")]
        d_rots = [R2(nm) for nm in ("sg", "a_", "tmp", "kmod", "b_", "ci", "cr", "ce", "e1", "e2", "e3")]
        fm16 = [Rot([F(f"fm{k}_{i}", BF16) for i in range(2)]) for k in range(4)]
        vb = F("vb", BF16)
        st32 = Rot([F(f"bst{i}") for i in range(2)])
        gst = Rot([sbt(kb, f"gst{i}", [128, 8]) for i in range(2)])
        tms = Rot([sbt(kb, f"tms{i}", [128, 1024], BF16) for i in range(3)])
        pmm = Rot([pst(kb, f"bp{i}") for i in range(4)])
        ptr = Rot([pst(kb, f"bt{i}", dt=BF16) for i in range(3)])
        psrc = g.PrT.rearrange("(ti p) t -> p ti t", p=128)
        for bi, (s0, n) in enumerate(TBS):
            nt = n // 128
            nch = n // 64
            c0 = s0 // 64
            lo = s0 if s0 in (0, NCTX) else s0 - 1
            hi = s0 + n if (s0 + n) in (NCTX, T) else s0 + n + 1
            kb.memset(pin[:, :, 0:1], 0.0, W=[pin])
            kb.memset(pin[:, :, n + 1:n + 2], 0.0, W=[pin])
            kb.load(pin[:, :, lo - s0 + 1:hi - s0 + 1], psrc[:, :, lo:hi], W=[pin])
            kb.tt(psx[:, :, :n], pin[:, :, 0:n], pin[:, :, 2:n + 2], ALU.add, R=[pin], W=[psx])
            for ti in range(14):
                kb.act(psx[:, ti, :n], psx[:, ti, :n], AF.Identity, scale=hmu[:, ti:ti + 1], R=[psx, hmu], W=[psx])
            for ti in range(14):
                kb.stt(psx[:, ti, :n], pin[:, ti, 1:n + 1], omm[:, ti:ti + 1], psx[:, ti, :n], ALU.mult, ALU.add, R=[pin, psx, omm], W=[psx])
            kb.act(lwin[0:64, :n], psx[0:64, 12, :n], AF.Tanh, R=[psx], W=[lwin])
            kb.copy(lwin[64:128, :n], psx[64:128, 12, :n], R=[psx], W=[lwin])
            kb.act(sgd[:, :n], psx[:, 13, :n], AF.Sigmoid, R=[psx], W=[sgd])
            for hp in range(4):
                kku, sq, rt, rs, kk, rk, bon, kbs = [R_.next() for R_ in hp_rots]
                r = psx[:, hp, :n]
                k = psx[:, 4 + hp, :n]
                v = psx[:, 8 + hp, :n]
                hs = slice(hp * 128, (hp + 1) * 128)
                kb.ts(kku[:, :n], k, rwc[:, 14 + hp:15 + hp], None, ALU.mult, R=[psx, rwc], W=[kku])
                kb.tt(sq[:, :n], kku[:, :n], kku[:, :n], ALU.mult, R=[kku], W=[sq])
                p = pmm.next()
                kb.mm(p[:, :n], blk[:], sq[:, :n], R=[blk, sq], W=[p])
                kb.act(rt[:, :n], p[:, :n], AF.Sqrt, bias=g.eps_t[:, 1:2], scale=1.0, R=[p, g.eps_t], W=[rt])
                kb.op("dve", lambda e: e.reciprocal(out=rs[:, :n], in_=rt[:, :n]), R=[rt], W=[rs])
                kb.tt(kk[:, :n], kku[:, :n], rs[:, :n], ALU.mult, R=[kku, rs], W=[kk])
                p = pmm.next()
                kb.mm(p[:, :n], gup[:, hs], sgd[:, :n], R=[gup, sgd], W=[p])
                st = st32.next()
                kb.copy(st[:, :n], p[:, :n], R=[p], W=[st], eng="act")
                kb.dma("sp", g.gT[hs, s0:s0 + n], st[:, :n], R=[st])
                kb.copy(vb[:, :n], v, R=[psx], W=[vb], eng="act")
                pt_ = ptr.next()
                for tt_ in range(nt):
                    kb.tr(pt_[:, tt_ * 128:(tt_ + 1) * 128], vb[:, tt_ * 128:(tt_ + 1) * 128], g.ident_bf[:], R=[vb, g.ident_bf], W=[pt_])
                tm = tms.next()
                kb.copy(tm[:, :nt * 128], pt_[:, :nt * 128], R=[pt_], W=[tm], eng="act")
                for tt_ in range(nt):
                    kb.dma("sp", g.Vtm[s0 + tt_ * 128:s0 + (tt_ + 1) * 128, hs], tm[:, tt_ * 128:(tt_ + 1) * 128], R=[tm])
                for d in range(2):
                    sg, a_, tmp, kmod, b_, ci, cr, ce, e1, e2, e3 = [R_.next() for R_ in d_rots]
                    p = pmm.next()
                    kb.mm(p[:, :n], lora[0:64, d, hs], lwin[0:64, :n], R=[lora, lwin], W=[p])
                    kb.act(sg[:, :n], p[:, :n], AF.Sigmoid, bias=rwc[:, 34 + 4 * d + hp:35 + 4 * d + hp], scale=1.0, R=[p, rwc], W=[sg])
                    p = pmm.next()
                    kb.mm(p[:, :n], lora[64:128, d, hs], lwin[64:128, :n], R=[lora, lwin], W=[p])
                    kb.act(a_[:, :n], p[:, :n], AF.Sigmoid, bias=rwc[:, 42 + 4 * d + hp:43 + 4 * d + hp], scale=1.0, R=[p, rwc], W=[a_])
                    kb.ts(tmp[:, :n], a_[:, :n], rwc[:, 18 + hp:19 + hp], omka[:, hp:hp + 1], ALU.mult, ALU.add, R=[a_, rwc, omka], W=[tmp])
                    kb.tt(kmod[:, :n], tmp[:, :n], k, ALU.mult, R=[tmp, psx], W=[kmod])
                    kb.tt(b_[:, :n], kk[:, :n], a_[:, :n], ALU.mult, R=[kk, a_], W=[b_])
                    if d == 0:
                        kb.copy(kbs[:, :n], kmod[:, :n], R=[kmod], W=[kbs], eng="act")
                    else:
                        kb.tt(kbs[:, :n], kbs[:, :n], kmod[:, :n], ALU.add, R=[kbs, kmod], W=[kbs])
                    kb.op("dve", lambda e: e.tensor_tensor_scan(out=ci[:, :n], data0=cmask[:, :n], data1=sg[:, :n], initial=0.0,
                                                                op0=ALU.mult, op1=ALU.add), R=[cmask, sg], W=[ci])
                    cc = ci
                    if d == 1:
                        kb.tt(tmp[:, :n], sg[:, :n], ci[:, :n], ALU.subtract, R=[sg, ci], W=[tmp])
                        civ = ci[:, :n].rearrange("p (c t) -> p c t", t=64)
                        kb.tt(cr[:, :n].rearrange("p (c t) -> p c t", t=64), tmp[:, :n].rearrange("p (c t) -> p c t", t=64),
                              civ[:, :, 63:64].to_broadcast([128, nch, 64]), ALU.add, R=[tmp, ci], W=[cr])
                        cc = cr
                    kb.tt(ce[:, :n], cc[:, :n], sg[:, :n], ALU.subtract, R=[cc, sg], W=[ce])
                    kb.act(e1[:, :n], cc[:, :n], AF.Exp, scale=KAPPA, R=[cc], W=[e1])
                    kb.act(e2[:, :n], cc[:, :n], AF.Exp, scale=-KAPPA, R=[cc], W=[e2])
                    kb.act(e3[:, :n], ce[:, :n], AF.Exp, scale=-KAPPA, R=[ce], W=[e3])
                    fa, fr, fb, fk = [fm16[i].next() for i in range(4)]
                    kb.stt(fa[:, :n], kk[:, :n], -1.0, e3[:, :n], ALU.mult, ALU.mult, R=[kk, e3], W=[fa])
                    kb.tt(fr[:, :n], r, e2[:, :n], ALU.mult, R=[psx, e2], W=[fr])
                    kb.tt(fb[:, :n], b_[:, :n], e1[:, :n], ALU.mult, R=[b_, e1], W=[fb])
                    kb.tt(fk[:, :n], kmod[:, :n], e1[:, :n], ALU.mult, R=[kmod, e1], W=[fk])
                    gs = gst.next()
                    e2v = e2[:, :n].rearrange("p (c t) -> p c t", t=64)
                    col = 63 if d == 0 else 0
                    kb.copy(gs[:, :nch], e2v[:, :, col], R=[e2], W=[gs], eng="pool")
                    kb.dma("sp", g.gam[d][hs, c0:c0 + nch], gs[:, :nch], R=[gs])
                    for kind, f in enumerate((fa, fr, fb, fk)):
                        kb.dma("sp", g.FM[d][kind][hs, s0:s0 + n], f[:, :n], R=[f])
                    for kind, f in ((0, fb), (1, fk)):
                        pt_ = ptr.next()
                        for tt_ in range(nt):
                            kb.tr(pt_[:, tt_ * 128:(tt_ + 1) * 128], f[:, tt_ * 128:(tt_ + 1) * 128], g.ident_bf[:], R=[f, g.ident_bf], W=[pt_])
                        tm = tms.next()
                        kb.copy(tm[:, :nt * 128], pt_[:, :nt * 128], R=[pt_], W=[tm], eng=("act" if kind else "dve"))
                        for tt_ in range(nt):
                            kb.dma("sp", g.TM[d][s0 + tt_ * 128:s0 + (tt_ + 1) * 128, kind, hs], tm[:, tt_ * 128:(tt_ + 1) * 128], R=[tm])
                kb.tt(rk[:, :n], r, kbs[:, :n], ALU.mult, R=[psx, kbs], W=[rk])
                kb.ts(rk[:, :n], rk[:, :n], hrk[:, hp:hp + 1], None, ALU.mult, R=[rk, hrk], W=[rk])
                p = pmm.next()
                kb.mm(p[:, :n], blk[:], rk[:, :n], R=[blk, rk], W=[p])
                kb.tt(bon[:, :n], p[:, :n], v, ALU.mult, R=[p, psx], W=[bon])
                kb.dma("sp", g.bonT[hs, s0:s0 + n], bon[:, :n], R=[bon])


def phase_C0(kb, g):
    NL = 5
    with phase(kb):
        mk = sbt(kb, "mk", [64, 2, 3, 64])
        kb.load(mk[:], g.masks_h[:, :, :, :], W=[mk])
        poolA = PPool([pst(kb, f"cpa{i}") for i in range(5)])
        poolB = PPool([pst(kb, f"cpb{i}") for i in range(3)])
        st = []
        for d in range(2):
            s = G()
            s.gam = sbt(kb, f"gam{d}", [64, 8, 68])
            kb.load(s.gam[:], g.gam[d].rearrange("(h k) c -> k h c", k=64), W=[s.gam])
            s.fm = Rot([sbt(kb, f"fm{d}_{i}", [64, 4, 8, 64], BF16) for i in range(2)])
            s.tm = Rot([sbt(kb, f"tm{d}_{i}", [64, 2, 512], BF16) for i in range(2)])
            s.vt = Rot([sbt(kb, f"vt{d}_{i}", [64, 512], BF16) for i in range(2)])
            s.Sf = sbt(kb, f"Sf{d}", [64, 8, 64])
            s.Sb = Rot([sbt(kb, f"Sb{d}_{i}", [64, 8, 64], BF16) for i in range(2)])
            s.Nm = Rot([sbt(kb, f"Nm{d}_{i}", [64, 8, 128], BF16) for i in range(2)])
            s.Nkm = Rot([sbt(kb, f"Nkm{d}_{i}", [64, 8, 128], BF16) for i in range(2)])
            s.Inv = Rot([sbt(kb, f"Inv{d}_{i}", [64, 8, 64], BF16) for i in range(2)])
            s.NT = sbt(kb, f"NT{d}", [64, 8, 64], BF16)
            s.X = sbt(kb, f"X{d}", [64, 8, 64])
            s.Xb = Rot([sbt(kb, f"Xb{d}_{i}", [64, 8, 64], BF16) for i in range(2)])
            s.P = Rot([sbt(kb, f"P{d}_{i}", [64, 8, 64], BF16) for i in range(2)])
            s.PT = Rot([sbt(kb, f"PT{d}_{i}", [64, 8, 64], BF16) for i in range(2)])
            s.W1 = sbt(kb, f"W1{d}", [64, 8, 64], BF16)
            s.UT = sbt(kb, f"UT{d}", [64, 8, 64], BF16)
            s.Yst = Rot([sbt(kb, f"Yst{d}_{i}", [64, 512]) for i in range(2)])
            kb.memset(s.Sf[:], 0.0, W=[s.Sf])
            s.sb = s.Sb.next()
            kb.memset(s.sb[:], 0.0, W=[s.sb])
            s.mAR = mk[:, d, 0:2, :].rearrange("p a t -> p (a t)")[:, None, :].to_broadcast([64, 4, 128])
            s.mT = mk[:, d, 2, :][:, None, :].to_broadcast([64, 8, 64])
            st.append(s)
        Ibc = g.ident[0:64, 0:64][:, None, :].to_broadcast([64, 8, 64])
        order = [list(range(68)), [3, 2, 1, 0] + list(range(67, 3, -1))]

        def v3(p):
            return p[0:64, :].rearrange("p (h t) -> p h t", t=64)

        def partA(d, c, rec):
            s = st[d]
            t0 = c * 64
            yield
            fm, tm, vt = s.fm.next(), s.tm.next(), s.vt.next()
            Nm, Nkm = s.Nm.next(), s.Nkm.next()
            for kind in range(4):
                kb.load(fm[:, kind, :, :], g.FM[d][kind].rearrange("(h k) t -> k h t", k=64)[:, :, t0:t0 + 64], W=[fm])
            kb.load(tm[:], g.TM[d][t0:t0 + 64, :, :], W=[tm])
            kb.load(vt[:], g.Vtm[t0:t0 + 64, :], W=[vt])
            for (lk, dst) in ((2, Nm), (3, Nkm)):
                for hh in range(2):
                    p = poolA.get()
                    for h4 in range(4):
                        h = hh * 4 + h4
                        kb.mm(p[0:64, h4 * 128:(h4 + 1) * 128], fm[:, lk, h, :], fm[:, 0:2, h, :], R=[fm], W=[p])
                    kb.tt(dst[:, hh * 4:(hh + 1) * 4, :], p[0:64, :].rearrange("p (h t) -> p h t", t=128), s.mAR, ALU.mult, R=[p, mk], W=[dst])
                    poolA.put(p)
                    yield
            p = poolA.get()
            for h in range(8):
                kb.mm(p[0:64, h * 64:(h + 1) * 64], fm[:, 0, h, :], fm[:, 2, h, :], R=[fm], W=[p])
            kb.tt(s.NT[:], v3(p), s.mT, ALU.mult, R=[p, mk], W=[s.NT])
            poolA.put(p)
            yield
            kb.tt(s.X[:], Nm[:, :, 0:64], Ibc, ALU.add, R=[Nm, g.ident], W=[s.X])
            xb = s.Xb.next()
            kb.copy(xb[:], s.X[:], R=[s.X], W=[xb], eng="act")
            P_ap = lambda h: Nm[:, h, 0:64]
            PT_ap = lambda h: s.NT[:, h, :]
            Pt, PTt = Nm, s.NT
            for lv in range(NL):
                last = lv == NL - 1
                p1 = None
                if not last:
                    p1 = poolA.get()
                    for h in range(8):
                        kb.mm(p1[0:64, h * 64:(h + 1) * 64], PT_ap(h), P_ap(h), R=[Pt, PTt], W=[p1])
                p2 = poolA.get()
                for h in range(8):
                    kb.mm(p2[0:64, h * 64:(h + 1) * 64], P_ap(h), PT_ap(h), R=[Pt, PTt], W=[p2])
                yield
                nPT = s.PT.next()
                kb.copy(nPT[:], v3(p2), R=[p2], W=[nPT], eng="act")
                poolA.put(p2)
                if not last:
                    nP = s.P.next()
                    kb.copy(nP[:], v3(p1), R=[p1], W=[nP], eng="act")
                    poolA.put(p1)
                    Pt = nP
                    P_ap = (lambda t_: (lambda h: t_[:, h, :]))(nP)
                PTt = nPT
                PT_ap = (lambda t_: (lambda h: t_[:, h, :]))(nPT)
                yield
                p3 = poolA.get()
                for h in range(8):
                    kb.mm(p3[0:64, h * 64:(h + 1) * 64], PT_ap(h), xb[:, h, :], R=[PTt, xb], W=[p3])
                yield
                kb.tt(s.X[:], s.X[:], v3(p3), ALU.add, R=[s.X, p3], W=[s.X])
                poolA.put(p3)
                xb = s.Inv.next() if last else s.Xb.next()
                kb.copy(xb[:], s.X[:], R=[s.X], W=[xb], eng="act")
                yield
            rec.update(fm=fm, tm=tm, vt=vt, Nm=Nm, Nkm=Nkm, inv=xb, c=c)

        def partB(d, rec):
            s = st[d]
            fm, tm, vt, Nm, Nkm, inv, c = (rec[k] for k in ("fm", "tm", "vt", "Nm", "Nkm", "inv", "c"))
            t0 = c * 64
            sb = s.sb
            yield
            pw = poolB.get()
            for h in range(8):
                hs = slice(h * 64, (h + 1) * 64)
                kb.mm(pw[0:64, hs], Nkm[:, h, 0:64], vt[:, hs], start=True, stop=False, R=[Nkm, vt], W=[pw])
                kb.mm(pw[0:64, hs], fm[:, 0, h, :], sb[:, h, :], start=False, stop=True, R=[fm, sb], W=[pw])
            yield
            kb.copy(s.W1[:], v3(pw), R=[pw], W=[s.W1], eng="act")
            poolB.put(pw)
            pu = poolB.get()
            for h in range(8):
                kb.mm(pu[0:64, h * 64:(h + 1) * 64], inv[:, h, :], s.W1[:, h, :], R=[inv, s.W1], W=[pu])
            yield
            kb.copy(s.UT[:], v3(pu), R=[pu], W=[s.UT], eng="act")
            poolB.put(pu)
            pn = poolB.get()
            for h in range(8):
                hs = slice(h * 64, (h + 1) * 64)
                kb.mm(pn[0:64, hs], tm[:, 0, hs], s.UT[:, h, :], start=True, stop=False, R=[tm, s.UT], W=[pn])
                kb.mm(pn[0:64, hs], tm[:, 1, hs], vt[:, hs], start=False, stop=True, R=[tm, vt], W=[pn])
            yield
            kb.tt(s.Sf[:], s.Sf[:], v3(pn), ALU.add, R=[s.Sf, pn], W=[s.Sf])
            poolB.put(pn)
            kb.tt(s.Sf[:], s.Sf[:], s.gam[:, :, c:c + 1].to_broadcast([64, 8, 64]), ALU.mult, R=[s.Sf, s.gam], W=[s.Sf])
            nsb = s.Sb.next()
            kb.copy(nsb[:], s.Sf[:], R=[s.Sf], W=[nsb], eng="act")
            s.sb = nsb
            yield
            py = poolB.get()
            for h in range(8):
                hs = slice(h * 64, (h + 1) * 64)
                kb.mm(py[0:64, hs], fm[:, 1, h, :], sb[:, h, :], start=True, stop=False, R=[fm, sb], W=[py])
                kb.mm(py[0:64, hs], Nm[:, h, 64:128], s.UT[:, h, :], start=False, stop=False, R=[Nm, s.UT], W=[py])
                kb.mm(py[0:64, hs], Nkm[:, h, 64:128], vt[:, hs], start=False, stop=True, R=[Nkm, vt], W=[py])
            ys = s.Yst.next()
            kb.copy(ys[:], py[0:64, :], R=[py], W=[ys])
            poolB.put(py)
            kb.store(g.Y[d][t0:t0 + 64, :], ys[:], R=[ys])

        recs = [{}, {}]
        run_interleaved([partA(0, order[0][0], recs[0]), partA(1, order[1][0], recs[1])])
        for i in range(68):
            cur = recs
            gens = [partB(0, cur[0]), partB(1, cur[1])]
            recs = [{}, {}]
            if i + 1 < 68:
                gens += [partA(0, order[0][i + 1], recs[0]), partA(1, order[1][i + 1], recs[1])]
            run_interleaved(gens)

def phase_C2(kb, g):
    with phase(kb):
        rwc = sbt(kb, "rwc2", [128, 58])
        kb.load(rwc[:], g.rw_cols[:, :], W=[rwc])
        yf = Rot([sbt(kb, f"yf{i}", [128, 512]) for i in range(2)])
        yb = Rot([sbt(kb, f"yb{i}", [128, 512]) for i in range(2)])
        y = sbt(kb, "y", [128, 512])
        sq = sbt(kb, "ysq", [128, 512])
        sm = sbt(kb, "ysm", [128, 8])
        vr = sbt(kb, "yvr", [128, 8])
        rt = sbt(kb, "yrt", [128, 8])
        rs = sbt(kb, "yrs", [128, 8])
        yn = Rot([sbt(kb, f"yn{i}", [128, 512]) for i in range(2)])
        pb = [pst(kb, f"c2p{i}") for i in range(4)]
        bon = Rot([sbt(kb, f"bon{i}", [128, 4, 512]) for i in range(2)])
        gt = Rot([sbt(kb, f"gt{i}", [128, 4, 512]) for i in range(2)])
        a1 = Rot([sbt(kb, f"a1_{i}", [128, 512]) for i in range(2)])
        mo = Rot([sbt(kb, f"mo{i}", [128, 512], BF16) for i in range(3)])
        for bi, (s0, n) in enumerate(TBS):
            nt = n // 128
            bo, gg = bon.next(), gt.next()
            kb.load(bo[:, :, :n], g.bonT.rearrange("(c p) t -> p c t", p=128)[:, :, s0:s0 + n], W=[bo])
            kb.load(gg[:, :, :n], g.gT.rearrange("(c p) t -> p c t", p=128)[:, :, s0:s0 + n], W=[gg])
            for tt_ in range(nt):
                t0 = s0 + tt_ * 128
                a, b = yf.next(), yb.next()
                kb.load(a[:], g.Y[0][t0:t0 + 128, :], W=[a])
                kb.load(b[:], g.Y[1][t0:t0 + 128, :], W=[b])
                kb.tt(y[:], a[:], b[:], ALU.add, R=[a, b], W=[y])
                y3 = y[:].rearrange("p (h v) -> p h v", v=64)
                kb.op("dve", lambda e: e.tensor_reduce(out=sm[:], in_=y3, axis=AX.X, op=ALU.add), R=[y], W=[sm])
                kb.ts(sm[:], sm[:], -1.0 / 64.0, None, ALU.mult, R=[sm], W=[sm])
                kb.tt(y3, y3, sm[:, :, None].to_broadcast([128, 8, 64]), ALU.add, R=[y, sm], W=[y])
                kb.tt(sq[:], y[:], y[:], ALU.mult, R=[y], W=[sq])
                kb.op("dve", lambda e: e.tensor_reduce(out=vr[:], in_=sq[:].rearrange("p (h v) -> p h v", v=64), axis=AX.X, op=ALU.add), R=[sq], W=[vr])
                kb.act(rt[:], vr[:], AF.Sqrt, bias=g.eps_t[:, 2:3], scale=1.0 / 64.0, R=[vr, g.eps_t], W=[rt])
                kb.op("dve", lambda e: e.reciprocal(out=rs[:], in_=rt[:]), R=[rt], W=[rs])
                yo = yn.next()
                kb.tt(yo[:].rearrange("p (h v) -> p h v", v=64), y3, rs[:, :, None].to_broadcast([128, 8, 64]), ALU.mult, R=[y, rs], W=[yo])
                for ct in range(4):
                    kb.tr(pb[ct][:, tt_ * 128:(tt_ + 1) * 128], yo[:, ct * 128:(ct + 1) * 128], g.ident[:], R=[yo, g.ident], W=[pb[ct]])
            for ct in range(4):
                t1 = a1.next()
                kb.act(t1[:, :n], pb[ct][:, :n], AF.Identity, bias=rwc[:, 30 + ct:31 + ct], scale=rwc[:, 26 + ct:27 + ct], R=[pb[ct], rwc], W=[t1])
                kb.tt(t1[:, :n], t1[:, :n], bo[:, ct, :n], ALU.add, R=[t1, bo], W=[t1])
                o = mo.next()
                kb.tt(o[:, :n], t1[:, :n], gg[:, ct, :n], ALU.mult, R=[t1, gg], W=[o])
                kb.store(g.mixT[ct * 128:(ct + 1) * 128, s0:s0 + n], o[:, :n], R=[o])


def phase_D0(kb, g):
    LAM_INIT = 0.2
    with phase(kb):
        dc = sbt(kb, "dc", [64, 4])
        pr = sbt(kb, "dpr", [64, 2])
        onesf = sbt(kb, "onesf", [64, 128])
        lam = sbt(kb, "lam", [128, 2])
        nlam = sbt(kb, "nlam", [128, 1])
        slg = sbt(kb, "slg", [128, 1])
        kb.load(dc[:], g.diff_cols[:, :], W=[dc])
        kb.load(slg[:], g.subln_g[:, :], W=[slg])
        kb.memset(onesf[:], 1.0, W=[onesf])
        kb.tt(pr[:, 0:1], dc[:, 0:1], dc[:, 1:2], ALU.mult, R=[dc], W=[pr])
        kb.tt(pr[:, 1:2], dc[:, 2:3], dc[:, 3:4], ALU.mult, R=[dc], W=[pr])
        ps = Rot([pst(kb, f"dps{i}") for i in range(3)])
        pfin = pst(kb, "dpfin")
        acc = [[pst(kb, f"dacc{m}{q}") for q in range(2)] for m in range(2)]
        pl = ps.next()
        kb.mm(pl[:, 0:2], onesf[:], pr[:], R=[onesf, pr], W=[pl])
        kb.act(lam[:], pl[:, 0:2], AF.Exp, R=[pl], W=[lam])
        kb.tt(nlam[:], lam[:, 1:2], lam[:, 0:1], ALU.subtract, R=[lam], W=[nlam])
        kb.ts(nlam[:], nlam[:], -LAM_INIT, None, ALU.add, R=[nlam], W=[nlam])
        KT = sbt(kb, "KT", [128, T], BF16)
        QR = [sbt(kb, f"QR{m}", [128, T], BF16) for m in range(2)]
        QP = [sbt(kb, f"QP{m}", [128, T], BF16) for m in range(2)]
        for m in range(2):
            zs = slice(64, 128) if m == 0 else slice(0, 64)
            kb.memset(QR[m][zs, :], 0.0, W=[QR[m]])
            kb.memset(QP[m][zs, :], 0.0, W=[QP[m]])
        VA = sbt(kb, "VA", [128, 34, 130], BF16)
        pT = Rot([sbt(kb, f"pT{i}", [128, 512], BF16) for i in range(4)])
        rz = sbt(kb, "rz", [128, 4])
        o0 = sbt(kb, "o0", [128, 128])
        o = sbt(kb, "o", [128, 128])
        osq = sbt(kb, "osq", [128, 128])
        ssq = sbt(kb, "ssq", [128, 1])
        rt = sbt(kb, "drt", [128, 1])
        rs = sbt(kb, "drs", [128, 1])
        on = Rot([sbt(kb, f"on{i}", [128, 128]) for i in range(2)])
        so = Rot([sbt(kb, f"dso{i}", [128, 256], BF16) for i in range(2)])
        accS = Rot([sbt(kb, f"accS{i}", [128, 4, 129]) for i in range(2)])

        def finalize(aS, h, q0):
            s_ = so.next()
            for qt in range(2):
                a0_, a1_ = aS[:, qt, :], aS[:, 2 + qt, :]
                kb.op("dve", lambda e_: e_.reciprocal(out=rz[:, 0:1], in_=a0_[:, 128:129]), R=[aS], W=[rz])
                kb.op("dve", lambda e_: e_.reciprocal(out=rz[:, 1:2], in_=a1_[:, 128:129]), R=[aS], W=[rz])
                kb.tt(rz[:, 2:3], rz[:, 1:2], nlam[:], ALU.mult, R=[rz, nlam], W=[rz])
                kb.ts(o0[:], a0_[:, 0:128], rz[:, 0:1], None, ALU.mult, R=[aS, rz], W=[o0])
                kb.stt(o[:], a1_[:, 0:128], rz[:, 2:3], o0[:], ALU.mult, ALU.add, R=[aS, rz, o0], W=[o])
                yield
                kb.act(osq[:], o[:], AF.Square, R=[o], W=[osq, ssq], accum_out=ssq[:])
                yield
                kb.act(rt[:], ssq[:], AF.Sqrt, bias=g.eps_t[:, 0:1], scale=1.0 / 128.0, R=[ssq, g.eps_t], W=[rt])
                yield
                kb.op("dve", lambda e_: e_.reciprocal(out=rs[:], in_=rt[:]), R=[rt], W=[rs])
                on_ = on.next()
                kb.ts(on_[:], o[:], rs[:, 0:1], 1.0 - LAM_INIT, ALU.mult, ALU.mult, R=[o, rs], W=[on_])
                yield
                kb.tr(pfin[:, 0:128], on_[:], g.ident[:], R=[on_, g.ident], W=[pfin])
                yield
                kb.act(s_[:, qt * 128:(qt + 1) * 128], pfin[:, 0:128], AF.Copy, scale=slg[:, 0:1], R=[pfin, slg], W=[s_])
                yield
            kb.store(g.mixT[512 + h * 128:512 + (h + 1) * 128, q0:q0 + 256], s_[:], R=[s_])

        fin = iter(())
        chunks = [(0, True)] + [(256 + 256 * i, False) for i in range(16)]
        import os
        DS = int(os.environ.get("D0_STAGE", "9"))
        if DS < 9:
            chunks = chunks[:2]
        for h in range(4 if DS == 9 else 1):
            hs = slice(h * 128, (h + 1) * 128)
            kb.load(KT[:], g.KrotT[hs, :], W=[KT])
            for m in range(2):
                ms = slice(m * 64, (m + 1) * 64)
                kb.load(QR[m][ms, :], g.QrotT[h * 128 + m * 64:h * 128 + (m + 1) * 64, :], W=[QR[m]])
                kb.load(QP[m][ms, :], g.QplT[h * 128 + m * 64:h * 128 + (m + 1) * 64, :], W=[QP[m]])
            kb.load(VA[:, :, 0:128], g.Vd.rearrange("(kt p) c -> p kt c", p=128)[:, :, hs], W=[VA])
            kb.memset(VA[:, :, 128:129], 1.0, W=[VA])
            for (q0, isctx) in chunks:
                if DS < 2:
                    break
                kts = [0, 1] if isctx else list(range(34))
                def score(kt):
                    Q = QP if (not isctx and kt < 2) else QR
                    p = ps.next()
                    ks = slice(kt * 128, (kt + 1) * 128)
                    kb.mm(p[:, 0:256], KT[:, ks], Q[0][:, q0:q0 + 256], R=[KT, Q[0]], W=[p])
                    kb.mm(p[:, 256:512], KT[:, ks], Q[1][:, q0:q0 + 256], R=[KT, Q[1]], W=[p])
                    return p
                LA = 2
                pq = [score(kts[k]) for k in range(min(LA, len(kts)))]
                for i, kt in enumerate(kts):
                    p = pq.pop(0)
                    if i + LA < len(kts):
                        pq.append(score(kts[i + LA]))
                    e = pT.next()
                    kb.act(e[:], p[:], AF.Exp, scale=0.125, R=[p], W=[e])
                    next(fin, None)
                    if DS < 3:
                        continue
                    for m in range(2):
                        for qt in range(2):
                            kb.mm(acc[m][qt][:, 0:129], e[:, m * 256 + qt * 128:m * 256 + (qt + 1) * 128], VA[:, kt, 0:129],
                                  start=(i == 0), stop=(i == len(kts) - 1), R=[e, VA], W=[acc[m][qt]])
                if DS < 4:
                    continue
                for _ in fin:
                    pass
                aS = accS.next()
                for m in range(2):
                    for qt in range(2):
                        kb.copy(aS[:, m * 2 + qt, :], acc[m][qt][:, 0:129], R=[acc[m][qt]], W=[aS])
                fin = finalize(aS, h, q0)
        for _ in fin:
            pass


def phase_E(kb, g, li, w_ap, kcn, mixT, blocks):
    with phase(kb):
        Wo = sbt(kb, "Wo", [128, kcn, 1024], BF16)
        stg = Rot([sbt(kb, f"estg{i}", [128, kcn * 128]) for i in range(2)])
        load_weight_bf16(kb, Wo, w_ap, 1024, stg, piece=128)
        xb = Rot([sbt(kb, f"exb{i}", [128, 8, 512]) for i in range(2)])
        mb = Rot([sbt(kb, f"emb{i}", [128, kcn, 512], BF16) for i in range(2)])
        pmm = Rot([pst(kb, f"ep{i}") for i in range(4)])
        xsrc = g.xT.rearrange("(kc p) t -> p kc t", p=128)
        msrc = mixT.rearrange("(kc p) t -> p kc t", p=128)
        for (s0, n, j) in blocks:
            x, m = xb.next(), mb.next()
            kb.load(x[:, :, :n], xsrc[:, :, s0:s0 + n], W=[x])
            kb.load(m[:, :, :n], msrc[:, :, s0:s0 + n], W=[m])
            for ct in range(8):
                p = pmm.next()
                for kc in range(kcn):
                    kb.mm(p[:, :n], Wo[:, kc, ct * 128:(ct + 1) * 128], m[:, kc, :n], start=(kc == 0), stop=(kc == kcn - 1), R=[Wo, m], W=[p])
                kb.stt(x[:, ct, :n], p[:, :n], g.mod[li][:, 16 + ct, j:j + 1], x[:, ct, :n], ALU.mult, ALU.add, R=[p, g.mod[li], x], W=[x])
            kb.store(xsrc[:, :, s0:s0 + n], x[:, :, :n], R=[x])


def phase_F(kb, g, li, blocks):
    with phase(kb):
        W1 = sbt(kb, "W1", [128, 8, 4096], BF16)
        W2 = sbt(kb, "W2", [128, 32, 1024], BF16)
        stg0 = sbt(kb, "fstg0", [128, 1024])
        nb = G()
        nb.sq = sbt(kb, "fsq", [128, 8, 256], BF16)
        nb.tmp = sbt(kb, "ftmp", [128, 8, 256])
        nb.rt = sbt(kb, "frt", [128, 256])
        nb.rstd = sbt(kb, "frstd", [128, 256])
        nb.ss = pst(kb, "fss")
        stg = Rot([stg0,
                   TT(nb.tmp.t[:, 0:4, :].rearrange("p a b -> p (a b)")),
                   TT(nb.tmp.t[:, 4:8, :].rearrange("p a b -> p (a b)"))])
        i_ = 0
        for (Wt, src, kcn, ncols) in ((W1, g.mlp_w1[li], 8, 4096), (W2, g.mlp_w2[li], 32, 1024)):
            for kc in range(kcn):
                for c0 in range(0, ncols, 1024):
                    st = stg.next()
                    kb.load(st[:, 0:1024], src[kc * 128:(kc + 1) * 128, c0:c0 + 1024], W=[st])
                    kb.copy(Wt[:, kc, c0:c0 + 1024], st[:, 0:1024], R=[st], W=[Wt], eng=("act" if i_ % 2 else "dve"))
                    i_ += 1
        barrier(kb)
        xb = Rot([sbt(kb, f"fxb{i}", [128, 8, 256]) for i in range(2)])
        hTs = Rot([sbt(kb, f"fhT{i}", [128, 8, 256], BF16) for i in range(2)])
        hid = sbt(kb, "fhid", [128, 32, 256], BF16)
        rl = Rot([sbt(kb, f"frl{i}", [128, 256]) for i in range(3)])
        pmm = Rot([pst(kb, f"fp{i}") for i in range(6)])
        xsrc = g.xT.rearrange("(kc p) t -> p kc t", p=128)

        def prep(blk):
            s0, n, j = blk
            x = xb.next()
            kb.load(x[:, :, :n], xsrc[:, :, s0:s0 + n], W=[x])
            hT = hTs.next()
            norm_mod(kb, g, nb, x, n, g.sc2[li], g.mod[li], j, hT, sh_off=24)
            return x, hT

        nxt = prep(blocks[0])
        for bi, (s0, n, j) in enumerate(blocks):
            x, hT = nxt
            for hc in range(32):
                p = pmm.next()
                for kc in range(8):
                    kb.mm(p[:, :n], W1[:, kc, hc * 128:(hc + 1) * 128], hT[:, kc, :n], start=(kc == 0), stop=(kc == 7), R=[W1, hT], W=[p])
                r = rl.next()
                kb.act(r[:, :n], p[:, :n], AF.Relu, R=[p], W=[r])
                kb.tt(hid[:, hc, :n], r[:, :n], p[:, :n], ALU.mult, R=[r, p], W=[hid])
            if bi + 1 < len(blocks):
                nxt = prep(blocks[bi + 1])
            for ct in range(8):
                p = pmm.next()
                for hc in range(32):
                    kb.mm(p[:, :n], W2[:, hc, ct * 128:(ct + 1) * 128], hid[:, hc, :n], start=(hc == 0), stop=(hc == 31), R=[W2, hid], W=[p])
                kb.stt(x[:, ct, :n], p[:, :n], g.mod[li][:, 40 + ct, j:j + 1], x[:, ct, :n], ALU.mult, ALU.add, R=[p, g.mod[li], x], W=[x])
            kb.store(xsrc[:, :, s0:s0 + n], x[:, :, :n], R=[x])


BLK512 = [(s0, n, 1 if s0 == 0 else 0) for (s0, n) in TBS]
BLK256 = [(s0, 256, 1 if s0 == 0 else 0) for s0 in range(0, T, 256)]

def phase_A1(kb, g):
    with phase(kb):
        W = sbt(kb, "wA1", [128, 8, 6176], BF16)
        stg = Rot([sbt(kb, f"w1stg{i}", [128, 2048]) for i in range(2)])
        load_weight_bf16(kb, W, g.ssd_w_in, 6176, stg)
        nb = NormBufs(kb, "A1")
        xb = Rot([sbt(kb, f"a1x{i}", [128, 8, 512]) for i in range(2)])
        hTs = Rot([sbt(kb, f"a1h{i}", [128, 8, 512], BF16) for i in range(2)])
        pmm = Rot([pst(kb, f"a1p{i}") for i in range(6)])
        st32 = Rot([sbt(kb, f"a1s{i}", [128, 512]) for i in range(4)])
        st16 = Rot([sbt(kb, f"a1z{i}", [128, 512], BF16) for i in range(4)])
        xsrc = g.xT.rearrange("(kc p) t -> p kc t", p=128)
        for bi, (s0, n) in enumerate(TBS):
            j = 1 if bi == 0 else 0
            x = xb.next()
            kb.load(x[:, :, :n], xsrc[:, :, s0:s0 + n], W=[x])
            hT = hTs.next()
            norm_mod(kb, g, nb, x, n, g.sc1[1], g.mod[1], j, hT, sh_off=0)
            for ct in range(32):
                p = pmm.next()
                c0 = 2048 + ct * 128
                for kc in range(8):
                    kb.mm(p[:, :n], W[:, kc, c0:c0 + 128], hT[:, kc, :n], start=(kc == 0), stop=(kc == 7), R=[W, hT], W=[p])
                st = st32.next()
                kb.copy(st[:, :n], p[:, :n], R=[p], W=[st], eng=("act" if ct % 2 else "dve"))
                kb.store(g.xbcT[ct * 128:(ct + 1) * 128, s0:s0 + n], st[:, :n], R=[st])
            for tt_ in range(n // 128):
                ts_ = slice(tt_ * 128, (tt_ + 1) * 128)
                t0 = s0 + tt_ * 128
                for zc in range(4):
                    p = pmm.next()
                    for kc in range(8):
                        kb.mm(p[:, :], hT[:, kc, ts_], W[:, kc, zc * 512:(zc + 1) * 512], start=(kc == 0), stop=(kc == 7), R=[W, hT], W=[p])
                    so = st16.next()
                    kb.copy(so[:], p[:], R=[p], W=[so], eng=("act" if zc % 2 else "dve"))
                    kb.store(g.zTM[t0:t0 + 128, zc * 512:(zc + 1) * 512], so[:], R=[so])
                p = pmm.next()
                for kc in range(8):
                    kb.mm(p[:, 0:32], hT[:, kc, ts_], W[:, kc, 6144:6176], start=(kc == 0), stop=(kc == 7), R=[W, hT], W=[p])
                st = st32.next()
                kb.copy(st[:, 0:32], p[:, 0:32], R=[p], W=[st])
                kb.store(g.dtTM[t0:t0 + 128, :], st[:, 0:32], R=[st])


def phase_B1(kb, g):
    with phase(kb):
        cw = sbt(kb, "cw", [128, 32, 5])
        cb = sbt(kb, "cb", [128, 32])
        kb.load(cw[:], g.conv_wT[:, :, :], W=[cw])
        kb.load(cb[:], g.conv_bT[:, :], W=[cb])
        xin = Rot([sbt(kb, f"b1x{i}", [128, 8, 516]) for i in range(2)])
        acc = Rot([sbt(kb, f"b1a{i}", [128, 512]) for i in range(2)])
        u32 = Rot([sbt(kb, f"b1u{i}", [128, 512]) for i in range(2)])
        u16 = Rot([sbt(kb, f"b1v{i}", [128, 512], BF16) for i in range(3)])
        p32 = Rot([pst(kb, f"b1p{i}") for i in range(3)])
        p16 = Rot([pst(kb, f"b1q{i}", dt=BF16) for i in range(2)])
        t32 = Rot([sbt(kb, f"b1t{i}", [128, 512]) for i in range(3)])
        t16 = Rot([sbt(kb, f"b1s{i}", [128, 512], BF16) for i in range(3)])
        src = g.xbcT.rearrange("(ti p) t -> p ti t", p=128)
        for bi, (s0, n) in enumerate(TBS):
            nt = n // 128
            seq0, seq1 = (0, NCTX) if s0 < NCTX else (NCTX, T)
            lo = max(seq0, s0 - 2)
            hi = min(seq1, s0 + n + 2)
            for grp in range(4):
                xi = xin.next()
                kb.memset(xi[:, :, 0:2], 0.0, W=[xi])
                kb.memset(xi[:, :, n + 2:n + 4], 0.0, W=[xi])
                kb.load(xi[:, :, lo - s0 + 2:hi - s0 + 2], src[:, grp * 8:(grp + 1) * 8, lo:hi], W=[xi])
                for t8 in range(8):
                    ti = grp * 8 + t8
                    a = acc.next()
                    kb.ts(a[:, :n], xi[:, t8, 0:n], cw[:, ti, 0:1], cb[:, ti:ti + 1], ALU.mult, ALU.add, R=[xi, cw, cb], W=[a])
                    for k in range(1, 5):
                        kb.stt(a[:, :n], xi[:, t8, k:k + n], cw[:, ti, k:k + 1], a[:, :n], ALU.mult, ALU.add, R=[xi, cw, a], W=[a])
                    if ti < 16:
                        u = u32.next()
                        kb.act(u[:, :n], a[:, :n], AF.Silu, R=[a], W=[u])
                        p = p32.next()
                        for tt_ in range(nt):
                            kb.tr(p[:, tt_ * 128:(tt_ + 1) * 128], u[:, tt_ * 128:(tt_ + 1) * 128], g.ident[:], R=[u, g.ident], W=[p])
                        t = t32.next()
                        kb.copy(t[:, :n], p[:, :n], R=[p], W=[t], eng=("act" if ti % 2 else "dve"))
                        for tt_ in range(nt):
                            kb.store(g.xsTM[s0 + tt_ * 128:s0 + (tt_ + 1) * 128, ti * 128:(ti + 1) * 128], t[:, tt_ * 128:(tt_ + 1) * 128], R=[t])
                    else:
                        u = u16.next()
                        kb.act(u[:, :n], a[:, :n], AF.Silu, R=[a], W=[u])
                        if ti < 24:
                            gi = ti - 16
                            kb.store(g.BT[gi * 128:(gi + 1) * 128, s0:s0 + n], u[:, :n], R=[u])
                            p = p16.next()
                            for tt_ in range(nt):
                                kb.tr(p[:, tt_ * 128:(tt_ + 1) * 128], u[:, tt_ * 128:(tt_ + 1) * 128], g.ident_bf[:], R=[u, g.ident_bf], W=[p])
                            t = t16.next()
                            kb.copy(t[:, :n], p[:, :n], R=[p], W=[t], eng="act")
                            for tt_ in range(nt):
                                kb.store(g.BTM[s0 + tt_ * 128:s0 + (tt_ + 1) * 128, gi * 128:(gi + 1) * 128], t[:, tt_ * 128:(tt_ + 1) * 128], R=[t])
                        else:
                            gi = ti - 24
                            kb.store(g.CT[gi * 128:(gi + 1) * 128, s0:s0 + n], u[:, :n], R=[u])


def phase_C1(kb, g):
    with phase(kb):
        UT1 = sbt(kb, "UT1", [128, 128])
        LT1 = sbt(kb, "LT1", [128, 128])
        onesf = sbt(kb, "c1ones", [128, 128])
        prm = sbt(kb, "prm", [128, 5, 32])
        aneg = sbt(kb, "aneg", [128, 2, 32])
        kb.load(UT1[:], g.ut1_h[:, :], W=[UT1])
        kb.load(LT1[:], g.lt1_h[:, :], W=[LT1])
        SLT = [sbt(kb, "sLT", [128, 128]), sbt(kb, "sUT", [128, 128])]
        kb.tt(SLT[0][:], LT1[:], g.ident[:], ALU.subtract, R=[LT1, g.ident], W=[SLT[0]])
        kb.tt(SLT[1][:], UT1[:], g.ident[:], ALU.subtract, R=[UT1, g.ident], W=[SLT[1]])
        kb.load(prm[:], g.ssd_prm[:, :, :], W=[prm])
        kb.memset(onesf[:], 1.0, W=[onesf])
        kb.act(aneg[:], prm[:, 2:4, :], AF.Exp, R=[prm], W=[aneg])
        kb.ts(aneg[:], aneg[:], -1.0, None, ALU.mult, R=[aneg], W=[aneg])
        tri = [UT1, LT1]
        triB = [sbt(kb, "UT1b", [128, 128], BF16), sbt(kb, "LT1b", [128, 128], BF16)]
        kb.copy(triB[0][:], UT1[:], R=[UT1], W=[triB[0]])
        kb.copy(triB[1][:], LT1[:], R=[LT1], W=[triB[1]])
        pydd = [[pst(kb, f"c1y{d}{i}") for i in range(2)] for d in range(2)]
        pbig = Rot([pst(kb, f"c1b{i}") for i in range(2)])
        psm = Rot([pst(kb, f"c1s{i}") for i in range(2)])
        st = []
        for d in range(2):
            s = G()
            s.xs = Rot([sbt(kb, f"xs{d}_{i}", [128, 32, 64]) for i in range(1)])
            s.D = sbt(kb, f"Dcs{d}", [128, 32, 128], BF16)
            s.bt = Rot([sbt(kb, f"bt{d}_{i}", [128, 8, 128], BF16) for i in range(2)])
            s.ct = Rot([sbt(kb, f"ct{d}_{i}", [128, 8, 128], BF16) for i in range(2)])
            s.btm = Rot([sbt(kb, f"btm{d}_{i}", [128, 1024], BF16) for i in range(2)])
            s.dt = Rot([sbt(kb, f"dt{d}_{i}", [128, 32]) for i in range(2)])
            s.hf = sbt(kb, f"hf{d}", [128, 32, 64])
            s.hb = Rot([sbt(kb, f"hb{d}_{i}", [128, 32, 64], BF16) for i in range(2)])
            s.xdt = sbt(kb, f"xdt{d}", [128, 32, 64], BF16)
            s.xdw = sbt(kb, f"xdw{d}", [128, 32, 64], BF16)
            s.yo = sbt(kb, f"yo{d}", [128, 8, 64])
            s.y = Rot([sbt(kb, f"y{d}_{i}", [128, 32, 64]) for i in range(1)])
            s.cbm = sbt(kb, f"cbm{d}", [128, 8, 128], BF16)
            kb.memset(s.hf[:], 0.0, W=[s.hf])
            s.h = s.hb.next()
            kb.memset(s.h[:], 0.0, W=[s.h])
            st.append(s)
        sm = lambda nm, w=32: sbt(kb, nm, [128, w])
        ex, dtd, dta, cs, ncs, ecs, wts, etot, csT = [[sm(f"{nm}{d}") for d in range(2)] for nm in
                                                       ("ex", "dtd", "dta", "cs", "ncs", "ecs", "wts", "etot", "csTx")]
        csTs = [sbt(kb, f"csT{d}", [32, 128]) for d in range(2)]
        E4 = Rot([sbt(kb, f"E4{i}", [128, 512], BF16) for i in range(3)])
        G4 = Rot([sbt(kb, f"G4{i}", [128, 4, 128], BF16) for i in range(3)])
        order = [list(range(34)), [1, 0] + list(range(33, 1, -1))]

        def chunk(d, c):
            s = st[d]
            t0 = c * 128
            yield
            xs, bt, ct, btm, dt = s.xs.next(), s.bt.next(), s.ct.next(), s.btm.next(), s.dt.next()
            kb.load(xs[:].rearrange("p h q -> p (h q)"), g.xsTM[t0:t0 + 128, :], W=[xs])
            kb.load(bt[:], g.BT.rearrange("(g n) t -> n g t", n=128)[:, :, t0:t0 + 128], W=[bt])
            kb.load(ct[:], g.CT.rearrange("(g n) t -> n g t", n=128)[:, :, t0:t0 + 128], W=[ct])
            kb.load(btm[:], g.BTM[t0:t0 + 128, :], W=[btm])
            kb.load(dt[:], g.dtTM[t0:t0 + 128, :], W=[dt])
            kb.tt(ex[d][:], dt[:], prm[:, d, :], ALU.add, R=[dt, prm], W=[ex[d]])
            kb.act(ex[d][:], ex[d][:], AF.Exp, R=[ex[d]], W=[ex[d]])
            kb.act(dtd[d][:], ex[d][:], AF.Ln, bias=g.eps_t[:, 4:5], scale=1.0, R=[ex[d], g.eps_t], W=[dtd[d]])
            kb.tt(dta[d][:], dtd[d][:], aneg[:, d, :], ALU.mult, R=[dtd[d], aneg], W=[dta[d]])
            p = psm.next()
            kb.mm(p[:, 0:32], tri[d][:], dta[d][:], R=[tri[d], dta[d]], W=[p])
            kb.mm(p[:, 32:64], onesf[:], dta[d][:], R=[onesf, dta[d]], W=[p])
            kb.copy(cs[d][:], p[:, 0:32], R=[p], W=[cs[d]])
            kb.act(ecs[d][:], p[:, 0:32], AF.Exp, R=[p], W=[ecs[d]])
            kb.act(etot[d][:], p[:, 32:64], AF.Exp, R=[p], W=[etot[d]])
            kb.tt(wts[d][:], p[:, 32:64], cs[d][:], ALU.subtract, R=[p, cs[d]], W=[wts[d]])
            kb.act(wts[d][:], wts[d][:], AF.Exp, R=[wts[d]], W=[wts[d]])
            kb.tt(s.D[:], SLT[d][:, None, :].to_broadcast([128, 32, 128]), dta[d][:, :, None].to_broadcast([128, 32, 128]), ALU.mult,
                  R=[SLT[d], dta[d]], W=[s.D])
            yield
            kb.tt(s.xdt[:], xs[:], dtd[d][:, :, None].to_broadcast([128, 32, 64]), ALU.mult, R=[xs, dtd[d]], W=[s.xdt])
            kb.tt(s.xdw[:], s.xdt[:], wts[d][:, :, None].to_broadcast([128, 32, 64]), ALU.mult, R=[s.xdt, wts[d]], W=[s.xdw])
            for hf in range(2):
                p = pbig.next()
                for g4 in range(4):
                    gi = hf * 4 + g4
                    kb.mm(p[:, g4 * 128:(g4 + 1) * 128], bt[:, gi, :], ct[:, gi, :], R=[bt, ct], W=[p])
                kb.tt(s.cbm[:, hf * 4:(hf + 1) * 4, :], p[:, :].rearrange("p (g l) -> p g l", l=128),
                      tri[d][:, None, :].to_broadcast([128, 4, 128]), ALU.mult, R=[p, tri[d]], W=[s.cbm])
            h_old = s.h
            yt = s.y.next()
            for hf in range(2):
                pyd = pydd[d]
                for g4 in range(4):
                    gi = hf * 4 + g4
                    pc = psm.next()
                    for h4 in range(4):
                        kb.mm(pc[:, h4 * 128:(h4 + 1) * 128], s.D[:, gi * 4 + h4, :], triB[d][:], R=[s.D, triB[d]], W=[pc])
                    e4 = E4.next()
                    kb.act(e4[:], pc[:, :], AF.Exp, R=[pc], W=[e4])
                    g4t = G4.next()
                    kb.tt(g4t[:], e4[:].rearrange("p (h l) -> p h l", l=128), s.cbm[:, gi, None, :].to_broadcast([128, 4, 128]),
                          ALU.mult, R=[e4, s.cbm], W=[g4t])
                    for h4 in range(4):
                        h = gi * 4 + h4
                        h16 = h - hf * 16
                        pb = pyd[h16 // 8]
                        kb.mm(pb[:, (h16 % 8) * 64:(h16 % 8 + 1) * 64], g4t[:, h4, :], s.xdt[:, h, :], R=[g4t, s.xdt], W=[pb])
                    yield
                pyo = [pbig.next(), pbig.next()]
                for g4 in range(4):
                    gi = hf * 4 + g4
                    pb = pyo[g4 // 2]
                    kb.mm(pb[:, (g4 % 2) * 256:(g4 % 2 + 1) * 256], ct[:, gi, :], h_old[:, gi * 4:(gi + 1) * 4, :], R=[ct, h_old], W=[pb])
                for q in range(2):
                    hs = slice(hf * 16 + q * 8, hf * 16 + (q + 1) * 8)
                    kb.tt(s.yo[:], pyo[q][:, :].rearrange("p (h q) -> p h q", q=64), ecs[d][:, hs, None].to_broadcast([128, 8, 64]),
                          ALU.mult, R=[pyo[q], ecs[d]], W=[s.yo])
                    kb.tt(yt[:, hs, :], pyd[q][:, :].rearrange("p (h q) -> p h q", q=64), s.yo[:], ALU.add, R=[pyd[q], s.yo], W=[yt])
            kb.store(g.Yssd[d][t0:t0 + 128, :], yt[:].rearrange("p h q -> p (h q)"), R=[yt])
            yield
            nh = s.hb.next()
            for q in range(4):
                p = pbig.next()
                for g2 in range(2):
                    gi = q * 2 + g2
                    kb.mm(p[:, g2 * 256:(g2 + 1) * 256], btm[:, gi * 128:(gi + 1) * 128], s.xdw[:, gi * 4:(gi + 1) * 4, :], R=[btm, s.xdw], W=[p])
                hs = slice(q * 8, (q + 1) * 8)
                kb.tt(s.hf[:, hs, :], s.hf[:, hs, :], etot[d][:, hs, None].to_broadcast([128, 8, 64]), ALU.mult, R=[s.hf, etot[d]], W=[s.hf])
                kb.tt(s.hf[:, hs, :], s.hf[:, hs, :], p[:, :].rearrange("p (h q) -> p h q", q=64), ALU.add, R=[s.hf, p], W=[s.hf])
            kb.copy(nh[:], s.hf[:], R=[s.hf], W=[nh], eng="act")
            s.h = nh

        for i in range(34):
            run_interleaved([chunk(0, order[0][i]), chunk(1, order[1][i])])


def phase_C3(kb, g):
    with phase(kb):
        prm = sbt(kb, "c3prm", [128, 5, 32])
        ng = sbt(kb, "c3ng", [128, 16])
        kb.load(prm[:], g.ssd_prm[:, :, :], W=[prm])
        kb.load(ng[:], g.ssd_ngT[:, :], W=[ng])
        y0 = Rot([sbt(kb, f"c3a{i}", [128, 32, 64]) for i in range(2)])
        y1 = Rot([sbt(kb, f"c3b{i}", [128, 32, 64]) for i in range(2)])
        xs = Rot([sbt(kb, f"c3x{i}", [128, 32, 64]) for i in range(2)])
        z = Rot([sbt(kb, f"c3z{i}", [128, 2048], BF16) for i in range(2)])
        sz = sbt(kb, "c3sz", [128, 2048])
        sq = sbt(kb, "c3sq", [128, 2048])
        ss = sbt(kb, "c3ss", [128, 8])
        rt = sbt(kb, "c3rt", [128, 8])
        rs = sbt(kb, "c3rs", [128, 8])
        yn = sbt(kb, "c3yn", [128, 4, 2048])
        pb = Rot([pst(kb, f"c3p{i}") for i in range(4)])
        so = Rot([sbt(kb, f"c3o{i}", [128, 512], BF16) for i in range(3)])
        for bi, (s0, n) in enumerate(TBS):
            nt = n // 128
            for tt_ in range(nt):
                t0 = s0 + tt_ * 128
                a, b, x, zz = y0.next(), y1.next(), xs.next(), z.next()
                kb.load(a[:].rearrange("p h q -> p (h q)"), g.Yssd[0][t0:t0 + 128, :], W=[a])
                kb.load(b[:].rearrange("p h q -> p (h q)"), g.Yssd[1][t0:t0 + 128, :], W=[b])
                kb.load(x[:].rearrange("p h q -> p (h q)"), g.xsTM[t0:t0 + 128, :], W=[x])
                kb.load(zz[:], g.zTM[t0:t0 + 128, :], W=[zz])
                kb.tt(a[:], a[:], b[:], ALU.add, R=[a, b], W=[a])
                kb.tt(x[:], x[:], prm[:, 4, :, None].to_broadcast([128, 32, 64]), ALU.mult, R=[x, prm], W=[x])
                kb.tt(a[:], a[:], x[:], ALU.add, R=[a, x], W=[a])
                kb.act(sz[:], zz[:], AF.Silu, R=[zz], W=[sz])
                af = a[:].rearrange("p h q -> p (h q)")
                kb.tt(af, af, sz[:], ALU.mult, R=[a, sz], W=[a])
                kb.tt(sq[:], af, af, ALU.mult, R=[a], W=[sq])
                kb.op("dve", lambda e: e.tensor_reduce(out=ss[:], in_=sq[:].rearrange("p (g c) -> p g c", c=256), axis=AX.X, op=ALU.add), R=[sq], W=[ss])
                kb.act(rt[:], ss[:], AF.Sqrt, bias=g.eps_t[:, 0:1], scale=1.0 / 256.0, R=[ss, g.eps_t], W=[rt])
                kb.op("dve", lambda e: e.reciprocal(out=rs[:], in_=rt[:]), R=[rt], W=[rs])
                kb.tt(yn[:, tt_, :].rearrange("p (g c) -> p g c", c=256), af.rearrange("p (g c) -> p g c", c=256),
                      rs[:, :, None].to_broadcast([128, 8, 256]), ALU.mult, R=[a, rs], W=[yn])
            for ct in range(16):
                p = pb.next()
                for tt_ in range(nt):
                    kb.tr(p[:, tt_ * 128:(tt_ + 1) * 128], yn[:, tt_, ct * 128:(ct + 1) * 128], g.ident[:], R=[yn, g.ident], W=[p])
                o = so.next()
                kb.act(o[:, :n], p[:, :n], AF.Copy, scale=ng[:, ct:ct + 1], R=[p, ng], W=[o])
                kb.store(g.mixT1[ct * 128:(ct + 1) * 128, s0:s0 + n], o[:, :n], R=[o])


def phase_G(kb, g):
    with phase(kb):
        gf = sbt(kb, "gf", [128, 8])
        kb.load(gf[:], g.gfT[:, :], W=[gf])
        xb = Rot([sbt(kb, f"gx{i}", [128, 8, 512]) for i in range(2)])
        sq = sbt(kb, "gsq", [128, 8, 512], BF16)
        ssp = pst(kb, "gss")
        rt = sbt(kb, "grt", [128, 512])
        rs = sbt(kb, "grs", [128, 512])
        xn = sbt(kb, "gxn", [128, 8, 512])
        pt = Rot([pst(kb, f"gp{i}") for i in range(4)])
        o = Rot([sbt(kb, f"go{i}", [128, 1024]) for i in range(2)])
        xsrc = g.xT.rearrange("(kc p) t -> p kc t", p=128)
        for (s0, n) in TBS[1:]:
            x = xb.next()
            kb.load(x[:, :, :n], xsrc[:, :, s0:s0 + n], W=[x])
            kb.act(sq[:, :, :n], x[:, :, :n], AF.Square, R=[x], W=[sq])
            for kc in range(8):
                kb.mm(ssp[:, :n], g.ones_bf[:], sq[:, kc, :n], start=(kc == 0), stop=(kc == 7), R=[sq, g.ones_bf], W=[ssp])
            kb.act(rt[:, :n], ssp[:, :n], AF.Sqrt, bias=g.eps_t[:, 0:1], scale=1.0 / 1024.0, R=[ssp, g.eps_t], W=[rt])
            kb.op("dve", lambda e: e.reciprocal(out=rs[:, :n], in_=rt[:, :n]), R=[rt], W=[rs])
            kb.tt(xn[:, :, :n], x[:, :, :n], rs[:, None, :n].to_broadcast([128, 8, n]), ALU.mult, R=[x, rs], W=[xn])
            for kc in range(8):
                if kc % 2:
                    kb.act(xn[:, kc, :n], xn[:, kc, :n], AF.Identity, scale=gf[:, kc:kc + 1], R=[xn, gf], W=[xn])
                else:
                    kb.ts(xn[:, kc, :n], xn[:, kc, :n], gf[:, kc:kc + 1], None, ALU.mult, R=[xn, gf], W=[xn])
            for tt_ in range(n // 128):
                oo = o.next()
                for hf in range(2):
                    p = pt.next()
                    for j in range(4):
                        kc = hf * 4 + j
                        kb.tr(p[:, j * 128:(j + 1) * 128], xn[:, kc, tt_ * 128:(tt_ + 1) * 128], g.ident[:], R=[xn, g.ident], W=[p])
                    kb.copy(oo[:, hf * 512:(hf + 1) * 512], p[:, :], R=[p], W=[oo], eng=("act" if hf else "dve"))
                t0 = s0 - NCTX + tt_ * 128
                kb.store(g.out[t0:t0 + 128, :], oo[:], R=[oo])


BLK512L = BLK512[1:]
BLK256L = BLK256[1:]

def declare_inputs(nc, g, shapes):
    for name, (shape, dt) in shapes.items():
        setattr(g, name, nc.dram_tensor(name, list(shape), dt, kind="ExternalInput").ap())


def input_shapes():
    S = {}
    S["x"] = ([4096, 1024], F32)
    S["ctx"] = ([256, 1024], F32)
    S["cT"] = ([128, 8, 2], F32)
    S["ada_w"] = ([2, 1024, 6144], F32)
    S["ada_bT"] = ([2, 128, 48], F32)
    S["g1T"] = ([2, 128, 8], F32)
    S["g2T"] = ([2, 128, 8], F32)
    S["gfT"] = ([128, 8], F32)
    S["w_in0"] = ([1024, 4352], F32)
    S["cosT"] = ([128, T], F32)
    S["sinT"] = ([128, T], F32)
    S["ident_h"] = ([128, 128], F32)
    S["hy_w_out"] = ([1024, 1024], F32)
    S["mlp_w1"] = ([2, 1024, 4096], F32)
    S["mlp_w2"] = ([2, 4096, 1024], F32)
    S["rw_cols"] = ([128, 14 + 4 * 7 + 16], F32)
    S["rw_lora"] = ([128, 2, 512], F32)
    S["rw_gup"] = ([128, 512], F32)
    S["blk_h"] = ([128, 128], F32)
    S["cmask_h"] = ([128, 512], F32)
    S["masks_h"] = ([64, 2, 3, 64], F32)
    S["diff_cols"] = ([64, 4], F32)
    S["subln_g"] = ([128, 1], F32)
    S["ssd_w_in"] = ([1024, 6176], F32)
    S["ssd_w_out"] = ([2048, 1024], F32)
    S["conv_wT"] = ([128, 32, 5], F32)
    S["conv_bT"] = ([128, 32], F32)
    S["ssd_prm"] = ([128, 5, 32], F32)
    S["ssd_ngT"] = ([128, 16], F32)
    S["ut1_h"] = ([128, 128], F32)
    S["lt1_h"] = ([128, 128], F32)
    S["sel_h"] = ([32, 32, 128], F32)
    return S


def build(debug=False, stop_after=None, only=None, as_input=()):
    nc = bass.Bass("TRN2", target_bir_lowering=False)
    _AS_INPUT.clear()
    _AS_INPUT.update(as_input)
    g = G()
    declare_inputs(nc, g, input_shapes())
    g.out = nc.dram_tensor("out", [4096, 1024], F32, kind="ExternalOutput").ap()
    dbg = debug
    g.xT = dram(nc, "xT", [1024, T], F32, dbg)
    g.PrT = dram(nc, "PrT", [1792, T], F32, dbg)
    g.QrotT = dram(nc, "QrotT", [512, T], BF16, dbg)
    g.QplT = dram(nc, "QplT", [512, T], BF16, dbg)
    g.KrotT = dram(nc, "KrotT", [512, T], BF16, dbg)
    g.Vd = dram(nc, "Vd", [T, 512], BF16, dbg)
    g.modD = dram(nc, "modD", [2, 128, 96], F32, dbg)
    g.gT = dram(nc, "gT", [512, T], F32, dbg)
    g.bonT = dram(nc, "bonT", [512, T], F32, dbg)
    g.Vtm = dram(nc, "Vtm", [T, 512], BF16, dbg)
    g.gamA = dram(nc, "gamA", [2, 512, 68], F32, dbg)
    g.gam = [g.gamA[0], g.gamA[1]]
    g.FMA = dram(nc, "FMA", [2, 4, 512, T], BF16, dbg)
    g.FM = [[g.FMA[d, k] for k in range(4)] for d in range(2)]
    g.TMA = dram(nc, "TMA", [2, T, 2, 512], BF16, dbg)
    g.TM = [g.TMA[0], g.TMA[1]]
    g.mixT = dram(nc, "mixT", [1024, T], BF16, dbg)
    g.xbcT = dram(nc, "xbcT", [4096, T], F32, False)
    g.zTM = dram(nc, "zTM", [T, 2048], BF16, False)
    g.dtTM = dram(nc, "dtTM", [T, 32], F32, dbg)
    g.xsTM = dram(nc, "xsTM", [T, 2048], F32, dbg)
    g.BT = dram(nc, "BT", [1024, T], BF16, dbg)
    g.CT = dram(nc, "CT", [1024, T], BF16, dbg)
    g.BTM = dram(nc, "BTM", [T, 1024], BF16, False)
    g.YsA = dram(nc, "YsA", [2, T, 2048], F32, dbg)
    g.Yssd = [g.YsA[0], g.YsA[1]]
    g.mixT1 = dram(nc, "mixT1", [2048, T], BF16, dbg)
    g.YA = dram(nc, "YA", [2, T, 512], F32, dbg)
    g.Y = [g.YA[0], g.YA[1]]
    with ExitStack() as es:
        kb = KB(nc, es)
        kb.es_t = None
        g.ident = kb.sb("ident", [128, 128])
        g.ones_bf = kb.sb("ones_bf", [128, 128], BF16)
        g.eps_t = kb.sb("eps_t", [128, 8])
        g.mod = [kb.sb(f"mod{li}", [128, 48, 2]) for li in range(2)]
        g.sc1 = [kb.sb(f"sc1_{li}", [128, 8, 2]) for li in range(2)]
        g.sc2 = [kb.sb(f"sc2_{li}", [128, 8, 2]) for li in range(2)]
        kb.load(g.ident[:], g.ident_h[:, :], W=[g.ident])
        kb.memset(g.ones_bf[:], 1.0, W=[g.ones_bf])
        g.ident_bf = kb.sb("ident_bf", [128, 128], BF16)
        kb.copy(g.ident_bf[:], g.ident[:], R=[g.ident], W=[g.ident_bf])
        kb.memset(g.eps_t[:, 0:1], EPS, W=[g.eps_t])
        kb.memset(g.eps_t[:, 1:2], 1e-12, W=[g.eps_t])
        kb.memset(g.eps_t[:, 2:3], 64e-5, W=[g.eps_t])
        kb.memset(g.eps_t[:, 3:4], 0.0, W=[g.eps_t])
        kb.memset(g.eps_t[:, 4:5], 1.0, W=[g.eps_t])
        phases = [("mods", phase_mods), ("xT", phase_xT), ("A0", phase_A0), ("B0", phase_B0), ("C0", phase_C0), ("C2", phase_C2), ("D0", phase_D0),
                  ("E0", lambda kb, g: phase_E(kb, g, 0, g.hy_w_out, 8, g.mixT, BLK512)),
                  ("F0", lambda kb, g: phase_F(kb, g, 0, BLK256)),
                  ("A1", phase_A1), ("B1", phase_B1), ("C1", phase_C1), ("C3", phase_C3),
                  ("E1", lambda kb, g: phase_E(kb, g, 1, g.ssd_w_out, 16, g.mixT1, BLK512L)),
                  ("F1", lambda kb, g: phase_F(kb, g, 1, BLK256L)), ("G", phase_G)]
        for name, fn in phases:
            if only is not None and name not in only:
                continue
            fn(kb, g)
            if stop_after == name:
                break
        if debug:
            for li in range(2):
                kb.store(g.modD[li], g.mod[li][:].rearrange("p a b -> p (a b)"), R=[g.mod[li]])
        kb.finish()
        print("instructions:", kb.nins)
    return nc


def rope_tables():
    inv = 10000.0 ** (-np.arange(0, 32, 2, dtype=np.float32) / 32.0)
    t = np.arange(4096)
    rows = (t // 64).astype(np.float32)
    cols = (t % 64).astype(np.float32)
    ar = rows[:, None] * inv[None, :]
    ac = cols[:, None] * inv[None, :]
    cosT = np.ones((128, T), np.float32)
    sinT = np.zeros((128, T), np.float32)
    for p in range(128):
        d = p % 64
        ang = ar if d < 32 else ac
        i = d % 16
        first = (d % 32) < 16
        cosT[p, 256:] = np.cos(ang[:, i])
        sinT[p, 256:] = (-np.sin(ang[:, i])) if first else np.sin(ang[:, i])
    return cosT, sinT


def swap_cols(w):
    idx = np.arange(512)
    d = idx % 32
    partner = np.where(d < 16, idx + 16, idx - 16)
    return w[:, partner]


def host_consts(inp):
    C = {}
    f = np.float32
    C["ada_w"] = np.ascontiguousarray(inp["ada_w"], dtype=f)
    C["ada_bT"] = np.ascontiguousarray(inp["ada_b"].reshape(2, 48, 128).transpose(0, 2, 1), dtype=f)
    C["g1T"] = np.ascontiguousarray(inp["norm1_g"].reshape(2, 8, 128).transpose(0, 2, 1), dtype=f)
    C["g2T"] = np.ascontiguousarray(inp["norm2_g"].reshape(2, 8, 128).transpose(0, 2, 1), dtype=f)
    C["gfT"] = np.ascontiguousarray(inp["norm_f_g"].reshape(8, 128).T, dtype=f)
    w = inp["hy_w_in"][0]
    q = w[:, 1792:2304]
    k = w[:, 2304:2816]
    v = w[:, 2816:3328]
    C["w_in0"] = np.ascontiguousarray(np.concatenate([w[:, :1792], q, swap_cols(q), k, swap_cols(k), v], axis=1), dtype=f)
    C["cosT"], C["sinT"] = rope_tables()
    C["ident_h"] = np.eye(128, dtype=f)
    C["hy_w_out"] = np.ascontiguousarray(inp["hy_w_out"][0], dtype=f)
    C["mlp_w1"] = np.ascontiguousarray(inp["mlp_w1"], dtype=f)
    C["mlp_w2"] = np.ascontiguousarray(inp["mlp_w2"], dtype=f)
    col = lambda a: np.ascontiguousarray(np.asarray(a, dtype=f).reshape(-1, 128).T)
    rw = np.zeros((128, 58), f)
    rw[:, 0:14] = col(inp["rwkv_mu"][0])
    rw[:, 14:18] = col(inp["rwkv_k_k"][0])
    rw[:, 18:22] = col(inp["rwkv_k_a"][0])
    rw[:, 22:26] = col(inp["rwkv_r_k"][0].reshape(-1))
    rw[:, 26:30] = col(inp["rwkv_ln_w"][0])
    rw[:, 30:34] = col(inp["rwkv_ln_b"][0])
    rw[:, 34:38] = col(inp["rwkv_w0"][0, 0])
    rw[:, 38:42] = col(inp["rwkv_w0"][0, 1])
    rw[:, 42:46] = col(inp["rwkv_a0"][0, 0])
    rw[:, 46:50] = col(inp["rwkv_a0"][0, 1])
    C["rw_cols"] = rw
    lora = np.zeros((128, 2, 512), f)
    lora[0:64] = inp["rwkv_w_up"][0].transpose(1, 0, 2)
    lora[64:128] = inp["rwkv_a_up"][0].transpose(1, 0, 2)
    C["rw_lora"] = lora
    C["rw_gup"] = np.ascontiguousarray(inp["rwkv_g_up"][0], dtype=f)
    blk = np.zeros((128, 128), f)
    blk[:64, :64] = 1
    blk[64:, 64:] = 1
    C["blk_h"] = blk
    cm = np.ones((128, 512), f)
    cm[:, ::64] = 0
    C["cmask_h"] = cm
    s = np.arange(64)[:, None]
    t = np.arange(64)[None, :]
    m = np.zeros((64, 2, 3, 64), f)
    m[:, 0, 0] = (s < t)
    m[:, 0, 1] = (s <= t)
    m[:, 0, 2] = (s > t)
    m[:, 1, 0] = (s > t)
    m[:, 1, 1] = (s >= t)
    m[:, 1, 2] = (s < t)
    C["masks_h"] = m
    C["diff_cols"] = np.stack([inp["diff_lq1"][0], inp["diff_lk1"][0], inp["diff_lq2"][0], inp["diff_lk2"][0]], axis=1).astype(f)
    C["subln_g"] = np.ascontiguousarray(inp["diff_subln_g"][0].reshape(128, 1), dtype=f)
    C["ssd_w_in"] = np.ascontiguousarray(inp["ssd_w_in"][0], dtype=f)
    C["ssd_w_out"] = np.ascontiguousarray(inp["ssd_w_out"][0], dtype=f)
    C["conv_wT"] = np.ascontiguousarray(inp["ssd_conv_w"][0].reshape(5, 32, 128).transpose(2, 1, 0), dtype=f)
    C["conv_bT"] = col(inp["ssd_conv_b"][0])
    prm = np.zeros((128, 5, 32), f)
    prm[:, 0] = inp["ssd_dt_bias"][0, 0][None, :]
    prm[:, 1] = inp["ssd_dt_bias"][0, 1][None, :]
    prm[:, 2] = inp["ssd_a_log"][0, 0][None, :]
    prm[:, 3] = inp["ssd_a_log"][0, 1][None, :]
    prm[:, 4] = inp["ssd_d"][0][None, :]
    C["ssd_prm"] = prm
    C["ssd_ngT"] = col(inp["ssd_norm_g"][0])
    jj = np.arange(128)[:, None]
    ll = np.arange(128)[None, :]
    C["ut1_h"] = (jj <= ll).astype(f)
    C["lt1_h"] = (jj >= ll).astype(f)
    sel = np.zeros((32, 32, 128), f)
    for h in range(32):
        sel[h, h, :] = 1.0
    C["sel_h"] = sel
    return C


def core_inputs(inp, C, b):
    m = dict(C)
    m["x"] = np.ascontiguousarray(inp["x"][b], dtype=np.float32)
    m["ctx"] = np.ascontiguousarray(inp["ctx"][b], dtype=np.float32)
    cv = np.stack([inp["c"][b], inp["c_ctx"]], axis=0).astype(np.float32)
    m["cT"] = np.ascontiguousarray(cv.reshape(2, 8, 128).transpose(2, 1, 0))
    return m


def kernel(**inputs):
    inp = {k: np.asarray(v) for k, v in inputs.items()}
    C = host_consts(inp)
    nc = build()
    in_maps = [core_inputs(inp, C, b) for b in range(8)]
    res = run_bass_kernel_spmd(nc, in_maps, core_ids=list(range(8)))
    return np.stack([np.asarray(r["out"]) for r in res.results], axis=0).astype(np.float32)
```

```python
import concourse.bass as bass
import concourse.mybir as mybir

F32 = mybir.dt.float32
BF16 = mybir.dt.bfloat16
AF = mybir.ActivationFunctionType
ALU = mybir.AluOpType
AX = mybir.AxisListType


class Src:
    def __init__(s, kb, name, inc, limit):
        s.kb, s.name, s.inc, s.limit = kb, name, inc, limit
        s.sems = []
        s.n = 0

    def sem_for(s, n):
        e = (n - 1) // s.limit
        while len(s.sems) <= e:
            s.sems.append(s.kb.es.enter_context(s.kb.nc.semaphore(f"{s.name}_{len(s.sems)}")))
        return s.sems[e], ((n - 1) % s.limit + 1) * s.inc


class Tk:
    __slots__ = ("w", "r")

    def __init__(s):
        s.w = {}
        s.r = {}


class TT:
    def __init__(s, t, k=None, ps=False):
        s.t = t
        s.k = k if k is not None else Tk()
        s.ps = ps

    def __getitem__(s, idx):
        return s.t[idx]


class KB:
    NSLOT = 20

    def __init__(s, nc, es):
        s.nc, s.es = nc, es
        s.eng = {"pe": nc.tensor, "dve": nc.vector, "act": nc.scalar, "pool": nc.gpsimd, "sp": nc.sync}
        s.src = {k: Src(s, "c" + k, 1, 30000) for k in s.eng}
        s.waited = {k: {} for k in s.eng}
        s.slots = {q: [Src(s, f"d{q}{i}", 16, 1800) for i in range(s.NSLOT)] for q in ("sp", "pool", "act")}
        s.rr = {q: 0 for q in s.slots}
        s.nins = 0
        s.same_engine_sync = True

    def sb(s, name, shape, dt=F32):
        return TT(s.es.enter_context(s.nc.sbuf_tensor("g_" + name, list(shape), dt)))

    def ps(s, name, shape, dt=F32):
        return TT(s.es.enter_context(s.nc.psum_tensor("gp_" + name, list(shape), dt)))

    def _deps(s, R, W, me=None):
        d = {}
        for r in R:
            k = r.k if isinstance(r, TT) else r
            for src, n in k.w.items():
                if d.get(src, 0) < n:
                    d[src] = n
            if isinstance(r, TT) and r.ps:
                for src, n in k.r.items():
                    if src is not me and d.get(src, 0) < n:
                        d[src] = n
        for w in W:
            k = w.k if isinstance(w, TT) else w
            for dd in (k.w, k.r):
                for src, n in dd.items():
                    if d.get(src, 0) < n:
                        d[src] = n
        return d

    def _wait(s, eng, d):
        wd = s.waited[eng]
        for src, n in d.items():
            if src is s.src[eng] and (eng == "pe" or not s.same_engine_sync):
                continue
            if wd.get(src, 0) >= n:
                continue
            sem, val = src.sem_for(n)
            s.eng[eng].wait_ge(sem, val)
            wd[src] = n

    def _mark(s, src, n, R, W):
        for w in W:
            k = w.k if isinstance(w, TT) else w
            k.w = {src: n}
            k.r = {}
        for r in R:
            k = r.k if isinstance(r, TT) else r
            if k.r.get(src, 0) < n:
                k.r[src] = n

    def op(s, eng, fn, R=(), W=()):
        d = s._deps(R, W, s.src[eng])
        s._wait(eng, d)
        src = s.src[eng]
        src.n += 1
        sem, _ = src.sem_for(src.n)
        ins = fn(s.eng[eng])
        ins.then_inc(sem, 1)
        s._mark(src, src.n, R, W)
        s.nins += 1

    def dma(s, q, out, in_, R=(), W=(), **kw):
        i = s.rr[q]
        s.rr[q] = (i + 1) % s.NSLOT
        slot = s.slots[q][i]
        d = s._deps(R, W)
        if slot.n > 0 and d.get(slot, 0) < slot.n:
            d[slot] = slot.n
        s._wait(q, d)
        slot.n += 1
        sem, _ = slot.sem_for(slot.n)
        s.eng[q].dma_start(out=out, in_=in_, **kw).then_inc(sem, 16)
        s._mark(slot, slot.n, R, W)
        s.nins += 1

    def load(s, out, in_, R=(), W=(), **kw):
        s.dma("sp", out, in_, R, W, **kw)

    def store(s, out, in_, R=(), W=(), **kw):
        s.dma("pool", out, in_, R, W, **kw)

    def finish(s):
        d = {}
        for q in s.slots:
            for sl in s.slots[q]:
                if sl.n:
                    d[sl] = sl.n
        for k, src in s.src.items():
            if src.n and k != "sp":
                d[src] = src.n
        s._wait("sp", d)

    def mm(s, out, lhsT, rhs, start=True, stop=True, R=(), W=()):
        s.op("pe", lambda e: e.matmul(out, lhsT=lhsT, rhs=rhs, start=start, stop=stop), R, W)

    def tr(s, out, in_, ident, R=(), W=()):
        s.op("pe", lambda e: e.transpose(out, in_, ident), R, W)

    def act(s, out, in_, func, bias=None, scale=None, R=(), W=(), accum_out=None):
        kw = {}
        if bias is not None:
            kw["bias"] = bias
        if scale is not None:
            kw["scale"] = scale
        if accum_out is not None:
            kw["accum_out"] = accum_out
        s.op("act", lambda e: e.activation(out=out, in_=in_, func=func, **kw), R, W)

    def tt(s, out, in0, in1, op, R=(), W=(), eng="dve"):
        s.op(eng, lambda e: e.tensor_tensor(out=out, in0=in0, in1=in1, op=op), R, W)

    def ts(s, out, in0, s1, s2, op0, op1=None, R=(), W=(), eng="dve"):
        if op1 is None:
            s.op(eng, lambda e: e.tensor_scalar(out=out, in0=in0, scalar1=s1, scalar2=None, op0=op0), R, W)
        else:
            s.op(eng, lambda e: e.tensor_scalar(out=out, in0=in0, scalar1=s1, scalar2=s2, op0=op0, op1=op1), R, W)

    def stt(s, out, in0, scalar, in1, op0, op1, R=(), W=()):
        s.op("dve", lambda e: e.scalar_tensor_tensor(out=out, in0=in0, scalar=scalar, in1=in1, op0=op0, op1=op1), R, W)

    def copy(s, out, in_, R=(), W=(), eng="dve"):
        if eng == "act":
            s.op("act", lambda e: e.copy(out=out, in_=in_), R, W)
        else:
            s.op(eng, lambda e: e.tensor_copy(out=out, in_=in_), R, W)

    def memset(s, ap, val, W=(), eng="dve"):
        s.op(eng, lambda e: e.memset(ap, val), (), W)
import math
import numpy as np
from contextlib import ExitStack, contextmanager
from concourse.bass_utils import run_bass_kernel_spmd

T = 4352
NCTX = 256
TBS = [(0, 256)] + [(256 + 512 * i, 512) for i in range(8)]
KAPPA = math.exp(-0.5)
EPS = 1e-6


class G:
    pass


@contextmanager
def phase(kb):
    old = kb.es
    barrier(kb)
    with ExitStack() as es:
        kb.es_t = es
        yield
        barrier(kb)
    kb.es_t = None


def barrier(kb):
    d = {}
    for q in kb.slots:
        for sl in kb.slots[q]:
            if sl.n:
                d[sl] = sl.n
    for k, src in kb.src.items():
        if src.n:
            d[src] = src.n
    for e in ("pe", "dve", "act", "pool", "sp"):
        dd = {s_: n for s_, n in d.items() if s_ is not kb.src[e]}
        kb._wait(e, dd)


_uid = [0]


def sbt(kb, name, shape, dt=F32):
    _uid[0] += 1
    return TT(kb.es_t.enter_context(kb.nc.sbuf_tensor(f"s{_uid[0]}_{name}", list(shape), dt)))


def pst(kb, name, shape=None, dt=F32):
    _uid[0] += 1
    full = [128, 512] if dt == F32 else [128, 1024]
    return TT(kb.es_t.enter_context(kb.nc.psum_tensor(f"p{_uid[0]}_{name}", full, dt)), ps=True)


def run_interleaved(gens):
    gens = list(gens)
    while gens:
        for g_ in list(gens):
            try:
                next(g_)
            except StopIteration:
                gens.remove(g_)


class PPool:
    def __init__(s, banks):
        s.b = banks
        s.live = [False] * len(banks)
        s.i = 0

    def get(s):
        n = len(s.b)
        for k in range(n):
            j = (s.i + k) % n
            if not s.live[j]:
                s.live[j] = True
                s.i = (j + 1) % n
                return s.b[j]
        raise RuntimeError("PSUM pool exhausted")

    def put(s, bank):
        s.live[s.b.index(bank)] = False


class Rot:
    def __init__(s, items):
        s.items = items
        s.i = 0

    def next(s):
        x = s.items[s.i]
        s.i = (s.i + 1) % len(s.items)
        return x


_AS_INPUT = set()


def dram(nc, name, shape, dt, debug):
    kind = "ExternalInput" if name in _AS_INPUT else ("ExternalOutput" if debug else "Internal")
    return nc.dram_tensor(name, list(shape), dt, kind=kind).ap()


def phase_mods(kb, g):
    with phase(kb):
        run_interleaved([_mods_gen(kb, g), _xT_gen(kb, g)])


def _mods_gen(kb, g):
    if True:
        cT = sbt(kb, "cT", [128, 8, 2])
        scT = sbt(kb, "scT", [128, 8, 2])
        sg_ = sbt(kb, "sgc", [128, 8, 2])
        kb.load(cT[:], g.cT[:, :, :], W=[cT])
        kb.act(sg_[:], cT[:], AF.Sigmoid, R=[cT], W=[sg_])
        kb.tt(scT[:], cT[:], sg_[:], ALU.mult, R=[cT, sg_], W=[scT])
        wb = Rot([sbt(kb, f"adaw{i}", [128, 8, 1024]) for i in range(2)])
        pmb = pst(kb, "pmod")
        pm = TT(pmb.t[:, 0:96].rearrange("p (a b) -> p a b", b=2), pmb.k, ps=True)
        adab = sbt(kb, "adab", [128, 48])
        g1 = sbt(kb, "g1", [128, 8])
        g2 = sbt(kb, "g2", [128, 8])
        for li in range(2):
            kb.load(adab[:], g.ada_bT[li], W=[adab])
            kb.load(g1[:], g.g1T[li], W=[g1])
            kb.load(g2[:], g.g2T[li], W=[g2])
            src = g.ada_w[li].rearrange("(kc p) n -> p kc n", p=128)
            for pc in range(6):
                w = wb.next()
                kb.load(w[:], src[:, :, pc * 1024:(pc + 1) * 1024], W=[w])
                yield
                for cc in range(8):
                    col = pc * 8 + cc
                    for kc in range(8):
                        kb.mm(pm[:, col, :], w[:, kc, cc * 128:(cc + 1) * 128], scT[:, kc, :],
                              start=(kc == 0), stop=(kc == 7), R=[w, scT], W=[pm])
            mod = g.mod[li]
            kb.tt(mod[:], pm[:], adab[:, :, None].to_broadcast([128, 48, 2]), ALU.add, R=[pm, adab], W=[mod])
            for (sc, gi, m) in ((g.sc1[li], g1, 1), (g.sc2[li], g2, 4)):
                kb.ts(sc[:], mod[:, m * 8:(m + 1) * 8, :], 1.0, None, ALU.add, R=[mod], W=[sc])
                kb.tt(sc[:], sc[:], gi[:, :, None].to_broadcast([128, 8, 2]), ALU.mult, R=[sc, gi], W=[sc])


def phase_xT(kb, g):
    return


def _xT_gen(kb, g):
    if True:
        xin = Rot([sbt(kb, f"xin{i}", [128, 1024]) for i in range(2)])
        xo = Rot([sbt(kb, f"xo{i}", [128, 8, 128]) for i in range(2)])
        pt = Rot([pst(kb, f"pT{i}") for i in range(4)])
        dst = g.xT.rearrange("(kc p) t -> p kc t", p=128)
        for i in range(34):
            xi = xin.next()
            src = g.ctx[i * 128:(i + 1) * 128, :] if i < 2 else g.x[(i - 2) * 128:(i - 1) * 128, :]
            kb.load(xi[:], src, W=[xi])
            o = xo.next()
            for hf in range(2):
                p = pt.next()
                for j in range(4):
                    kc = hf * 4 + j
                    kb.tr(p[:, j * 128:(j + 1) * 128], xi[:, kc * 128:(kc + 1) * 128], g.ident[:], R=[xi, g.ident], W=[p])
                kb.copy(o[:, hf * 4:(hf + 1) * 4, :], p[:, :].rearrange("p (a b) -> p a b", b=128), R=[p], W=[o], eng=("act" if hf else "dve"))
            kb.store(dst[:, :, i * 128:(i + 1) * 128], o[:], R=[o])
            if i % 3 == 2:
                yield


class NormBufs:
    def __init__(s, kb, tag):
        s.sq = sbt(kb, f"nsq{tag}", [128, 8, 512], BF16)
        s.tmp = sbt(kb, f"ntmp{tag}", [128, 8, 512])
        s.rt = sbt(kb, f"nrt{tag}", [128, 512])
        s.rstd = sbt(kb, f"nrstd{tag}", [128, 512])
        s.ss = pst(kb, f"nss{tag}", [128, 512])


def norm_mod(kb, g, nb, xTb, n, sc, sh, j, hT, sh_off=0):
    kb.act(nb.sq[:, :, :n], xTb[:, :, :n], AF.Square, R=[xTb], W=[nb.sq])
    for kc in range(8):
        kb.mm(nb.ss[:, :n], g.ones_bf[:], nb.sq[:, kc, :n], start=(kc == 0), stop=(kc == 7), R=[nb.sq, g.ones_bf], W=[nb.ss])
    kb.act(nb.rt[:, :n], nb.ss[:, :n], AF.Sqrt, bias=g.eps_t[:, 0:1], scale=1.0 / 1024.0, R=[nb.ss, g.eps_t], W=[nb.rt])
    kb.op("dve", lambda e: e.reciprocal(out=nb.rstd[:, :n], in_=nb.rt[:, :n]), R=[nb.rt], W=[nb.rstd])
    kb.tt(nb.tmp[:, :, :n], xTb[:, :, :n], nb.rstd[:, None, :n].to_broadcast([128, 8, n]), ALU.mult, R=[xTb, nb.rstd], W=[nb.tmp])
    for kc in range(8):
        kb.act(hT[:, kc, :n], nb.tmp[:, kc, :n], AF.Identity, bias=sh[:, sh_off + kc, j:j + 1], scale=sc[:, kc, j:j + 1],
               R=[nb.tmp, sc, sh], W=[hT])


def load_weight_bf16(kb, W, src_ap, ncols, stg, piece=256):
    kcn = src_ap.shape[0] // 128
    i = 0
    for kc in range(kcn):
        c0 = 0
        while c0 < ncols:
            st = stg.next()
            w = min(ncols - c0, st.t.shape[1])
            kb.load(st.t[:, 0:w], src_ap[kc * 128:(kc + 1) * 128, c0:c0 + w], W=[st])
            kb.copy(W[:, kc, c0:c0 + w], st.t[:, 0:w], R=[st], W=[W], eng=("act" if i % 2 else "dve"))
            i += 1
            c0 += w


def phase_A0(kb, g):
    with phase(kb):
        W = sbt(kb, "wA", [128, 8, 4352], BF16)
        stg = Rot([sbt(kb, f"wstg{i}", [128, 2048]) for i in range(2)])
        import os
        STG = int(os.environ.get("A0_STAGE", "9"))
        load_weight_bf16(kb, W, g.w_in0, 4352, stg)
        nb = NormBufs(kb, "A")
        xb = Rot([sbt(kb, f"xTb{i}", [128, 8, 512]) for i in range(2)])
        hTs = Rot([sbt(kb, f"hT{i}", [128, 8, 512], BF16) for i in range(2)])
        cosb = Rot([sbt(kb, f"cos{i}", [128, 512]) for i in range(2)])
        sinb = Rot([sbt(kb, f"sin{i}", [128, 512]) for i in range(2)])
        pmm = Rot([pst(kb, f"pmm{i}", [128, 512]) for i in range(6)])
        st32 = Rot([sbt(kb, f"st32_{i}", [128, 512]) for i in range(4)])
        st16 = Rot([sbt(kb, f"st16_{i}", [128, 512], BF16) for i in range(6)])
        t1s = Rot([sbt(kb, f"t1_{i}", [128, 512]) for i in range(2)])
        t2s = Rot([sbt(kb, f"t2_{i}", [128, 512]) for i in range(2)])
        xsrc = g.xT.rearrange("(kc p) t -> p kc t", p=128)
        for bi, (s0, n) in enumerate(TBS):
            if STG < 2 or (STG < 9 and bi > 0):
                break
            j = 1 if bi == 0 else 0
            x = xb.next()
            kb.load(x[:, :, :n], xsrc[:, :, s0:s0 + n], W=[x])
            cs, sn = cosb.next(), sinb.next()
            kb.load(cs[:, :n], g.cosT[:, s0:s0 + n], W=[cs])
            kb.load(sn[:, :n], g.sinT[:, s0:s0 + n], W=[sn])
            hT = hTs.next()
            norm_mod(kb, g, nb, x, n, g.sc1[0], g.mod[0], j, hT, sh_off=0)

            def proj(ct):
                p = pmm.next()
                for kc in range(8):
                    kb.mm(p[:, :n], W[:, kc, ct * 128:(ct + 1) * 128], hT[:, kc, :n], start=(kc == 0), stop=(kc == 7), R=[W, hT], W=[p])
                return p
            if STG < 3:
                continue
            for ct in range(14):
                p = proj(ct)
                st = st32.next()
                kb.copy(st[:, :n], p[:, :n], R=[p], W=[st], eng=("act" if ct % 2 else "dve"))
                kb.store(g.PrT[ct * 128:(ct + 1) * 128, s0:s0 + n], st[:, :n], R=[st])
            if STG < 4:
                continue
            for h in range(4):
                for (base, dst_rot, dst_pl) in ((14, g.QrotT, g.QplT), (22, g.KrotT, None)):
                    pq = proj(base + h)
                    pw = proj(base + 4 + h)
                    t1, t2 = t1s.next(), t2s.next()
                    kb.tt(t1[:, :n], pq[:, :n], cs[:, :n], ALU.mult, R=[pq, cs], W=[t1])
                    kb.tt(t2[:, :n], pw[:, :n], sn[:, :n], ALU.mult, R=[pw, sn], W=[t2])
                    so = st16.next()
                    kb.tt(so[:, :n], t1[:, :n], t2[:, :n], ALU.add, R=[t1, t2], W=[so])
                    if not os.environ.get("NOSTORE4"):
                        kb.store(dst_rot[h * 128:(h + 1) * 128, s0:s0 + n], so[:, :n], R=[so])
                    if dst_pl is not None:
                        sp_ = st16.next()
                        kb.copy(sp_[:, :n], pq[:, :n], R=[pq], W=[sp_], eng="act")
                        if not os.environ.get("NOSTORE4"):
                            kb.store(dst_pl[h * 128:(h + 1) * 128, s0:s0 + n], sp_[:, :n], R=[sp_])
            if STG < 5:
                continue
            for tt_ in range(n // 128):
                p = pmm.next()
                for kc in range(8):
                    kb.mm(p[:, :], hT[:, kc, tt_ * 128:(tt_ + 1) * 128], W[:, kc, 3840:4352], start=(kc == 0), stop=(kc == 7), R=[W, hT], W=[p])
                so = st16.next()
                kb.copy(so[:], p[:], R=[p], W=[so], eng=("act" if tt_ % 2 else "dve"))
                kb.store(g.Vd[s0 + tt_ * 128:s0 + (tt_ + 1) * 128, :], so[:], R=[so])

def phase_B0(kb, g):
    with phase(kb):
        rwc = sbt(kb, "rwc", [128, 58])
        hmu = sbt(kb, "hmu", [128, 14])
        omm = sbt(kb, "omm", [128, 14])
        omka = sbt(kb, "omka", [128, 4])
        hrk = sbt(kb, "hrk", [128, 4])
        lora = sbt(kb, "lora", [128, 2, 512])
        gup = sbt(kb, "gup", [128, 512])
        blk = sbt(kb, "blk", [128, 128])
        cmask = sbt(kb, "cmask", [128, 512])
        kb.load(rwc[:], g.rw_cols[:, :], W=[rwc])
        kb.load(lora[:], g.rw_lora[:, :, :], W=[lora])
        kb.load(gup[:], g.rw_gup[:, :], W=[gup])
        kb.load(blk[:], g.blk_h[:, :], W=[blk])
        kb.load(cmask[:], g.cmask_h[:, :], W=[cmask])
        kb.ts(hmu[:], rwc[:, 0:14], 0.5, None, ALU.mult, R=[rwc], W=[hmu])
        kb.ts(omm[:], rwc[:, 0:14], -1.0, 1.0, ALU.mult, ALU.add, R=[rwc], W=[omm])
        kb.ts(omka[:], rwc[:, 18:22], -1.0, 1.0, ALU.mult, ALU.add, R=[rwc], W=[omka])
        kb.ts(hrk[:], rwc[:, 22:26], 0.5, None, ALU.mult, R=[rwc], W=[hrk])

        pin = sbt(kb, "pin", [128, 14, 514])
        psx = sbt(kb, "psx", [128, 14, 512])
        lwin = sbt(kb, "lwin", [128, 512])
        sgd = sbt(kb, "sgd", [128, 512])
        F = lambda nm, dt=F32: sbt(kb, nm, [128, 512], dt)
        R2 = lambda nm: Rot([F(f"{nm}{i}") for i in range(2)])
        hp_rots = [R2(nm) for nm in ("kku", "sq", "rt", "rs", "kk", "rk", "bon", "kbs")]
        d_rots = [R2(nm) for nm in ("sg", "a_", "tmp", "kmod", "b_", "ci", "cr", "ce", "e1", "e2", "e3")]
        fm16 = [Rot([F(f"fm{k}_{i}", BF16) for i in range(2)]) for k in range(4)]
        vb = F("vb", BF16)
        st32 = Rot([F(f"bst{i}") for i in range(2)])
        gst = Rot([sbt(kb, f"gst{i}", [128, 8]) for i in range(2)])
        tms = Rot([sbt(kb, f"tms{i}", [128, 1024], BF16) for i in range(3)])
        pmm = Rot([pst(kb, f"bp{i}") for i in range(4)])
        ptr = Rot([pst(kb, f"bt{i}", dt=BF16) for i in range(3)])
        psrc = g.PrT.rearrange("(ti p) t -> p ti t", p=128)
        for bi, (s0, n) in enumerate(TBS):
            nt = n // 128
            nch = n // 64
            c0 = s0 // 64
            lo = s0 if s0 in (0, NCTX) else s0 - 1
            hi = s0 + n if (s0 + n) in (NCTX, T) else s0 + n + 1
            kb.memset(pin[:, :, 0:1], 0.0, W=[pin])
            kb.memset(pin[:, :, n + 1:n + 2], 0.0, W=[pin])
            kb.load(pin[:, :, lo - s0 + 1:hi - s0 + 1], psrc[:, :, lo:hi], W=[pin])
            kb.tt(psx[:, :, :n], pin[:, :, 0:n], pin[:, :, 2:n + 2], ALU.add, R=[pin], W=[psx])
            for ti in range(14):
                kb.act(psx[:, ti, :n], psx[:, ti, :n], AF.Identity, scale=hmu[:, ti:ti + 1], R=[psx, hmu], W=[psx])
            for ti in range(14):
                kb.stt(psx[:, ti, :n], pin[:, ti, 1:n + 1], omm[:, ti:ti + 1], psx[:, ti, :n], ALU.mult, ALU.add, R=[pin, psx, omm], W=[psx])
            kb.act(lwin[0:64, :n], psx[0:64, 12, :n], AF.Tanh, R=[psx], W=[lwin])
            kb.copy(lwin[64:128, :n], psx[64:128, 12, :n], R=[psx], W=[lwin])
            kb.act(sgd[:, :n], psx[:, 13, :n], AF.Sigmoid, R=[psx], W=[sgd])
            for hp in range(4):
                kku, sq, rt, rs, kk, rk, bon, kbs = [R_.next() for R_ in hp_rots]
                r = psx[:, hp, :n]
                k = psx[:, 4 + hp, :n]
                v = psx[:, 8 + hp, :n]
                hs = slice(hp * 128, (hp + 1) * 128)
                kb.ts(kku[:, :n], k, rwc[:, 14 + hp:15 + hp], None, ALU.mult, R=[psx, rwc], W=[kku])
                kb.tt(sq[:, :n], kku[:, :n], kku[:, :n], ALU.mult, R=[kku], W=[sq])
                p = pmm.next()
                kb.mm(p[:, :n], blk[:], sq[:, :n], R=[blk, sq], W=[p])
                kb.act(rt[:, :n], p[:, :n], AF.Sqrt, bias=g.eps_t[:, 1:2], scale=1.0, R=[p, g.eps_t], W=[rt])
                kb.op("dve", lambda e: e.reciprocal(out=rs[:, :n], in_=rt[:, :n]), R=[rt], W=[rs])
                kb.tt(kk[:, :n], kku[:, :n], rs[:, :n], ALU.mult, R=[kku, rs], W=[kk])
                p = pmm.next()
                kb.mm(p[:, :n], gup[:, hs], sgd[:, :n], R=[gup, sgd], W=[p])
                st = st32.next()
                kb.copy(st[:, :n], p[:, :n], R=[p], W=[st], eng="act")
                kb.dma("sp", g.gT[hs, s0:s0 + n], st[:, :n], R=[st])
                kb.copy(vb[:, :n], v, R=[psx], W=[vb], eng="act")
                pt_ = ptr.next()
                for tt_ in range(nt):
                    kb.tr(pt_[:, tt_ * 128:(tt_ + 1) * 128], vb[:, tt_ * 128:(tt_ + 1) * 128], g.ident_bf[:], R=[vb, g.ident_bf], W=[pt_])
                tm = tms.next()
                kb.copy(tm[:, :nt * 128], pt_[:, :nt * 128], R=[pt_], W=[tm], eng="act")
                for tt_ in range(nt):
                    kb.dma("sp", g.Vtm[s0 + tt_ * 128:s0 + (tt_ + 1) * 128, hs], tm[:, tt_ * 128:(tt_ + 1) * 128], R=[tm])
                for d in range(2):
                    sg, a_, tmp, kmod, b_, ci, cr, ce, e1, e2, e3 = [R_.next() for R_ in d_rots]
                    p = pmm.next()
                    kb.mm(p[:, :n], lora[0:64, d, hs], lwin[0:64, :n], R=[lora, lwin], W=[p])
                    kb.act(sg[:, :n], p[:, :n], AF.Sigmoid, bias=rwc[:, 34 + 4 * d + hp:35 + 4 * d + hp], scale=1.0, R=[p, rwc], W=[sg])
                    p = pmm.next()
                    kb.mm(p[:, :n], lora[64:128, d, hs], lwin[64:128, :n], R=[lora, lwin], W=[p])
                    kb.act(a_[:, :n], p[:, :n], AF.Sigmoid, bias=rwc[:, 42 + 4 * d + hp:43 + 4 * d + hp], scale=1.0, R=[p, rwc], W=[a_])
                    kb.ts(tmp[:, :n], a_[:, :n], rwc[:, 18 + hp:19 + hp], omka[:, hp:hp + 1], ALU.mult, ALU.add, R=[a_, rwc, omka], W=[tmp])
                    kb.tt(kmod[:, :n], tmp[:, :n], k, ALU.mult, R=[tmp, psx], W=[kmod])
                    kb.tt(b_[:, :n], kk[:, :n], a_[:, :n], ALU.mult, R=[kk, a_], W=[b_])
                    if d == 0:
                        kb.copy(kbs[:, :n], kmod[:, :n], R=[kmod], W=[kbs], eng="act")
                    else:
                        kb.tt(kbs[:, :n], kbs[:, :n], kmod[:, :n], ALU.add, R=[kbs, kmod], W=[kbs])
                    kb.op("dve", lambda e: e.tensor_tensor_scan(out=ci[:, :n], data0=cmask[:, :n], data1=sg[:, :n], initial=0.0,
                                                                op0=ALU.mult, op1=ALU.add), R=[cmask, sg], W=[ci])
                    cc = ci
                    if d == 1:
                        kb.tt(tmp[:, :n], sg[:, :n], ci[:, :n], ALU.subtract, R=[sg, ci], W=[tmp])
                        civ = ci[:, :n].rearrange("p (c t) -> p c t", t=64)
                        kb.tt(cr[:, :n].rearrange("p (c t) -> p c t", t=64), tmp[:, :n].rearrange("p (c t) -> p c t", t=64),
                              civ[:, :, 63:64].to_broadcast([128, nch, 64]), ALU.add, R=[tmp, ci], W=[cr])
                        cc = cr
                    kb.tt(ce[:, :n], cc[:, :n], sg[:, :n], ALU.subtract, R=[cc, sg], W=[ce])
                    kb.act(e1[:, :n], cc[:, :n], AF.Exp, scale=KAPPA, R=[cc], W=[e1])
                    kb.act(e2[:, :n], cc[:, :n], AF.Exp, scale=-KAPPA, R=[cc], W=[e2])
                    kb.act(e3[:, :n], ce[:, :n], AF.Exp, scale=-KAPPA, R=[ce], W=[e3])
                    fa, fr, fb, fk = [fm16[i].next() for i in range(4)]
                    kb.stt(fa[:, :n], kk[:, :n], -1.0, e3[:, :n], ALU.mult, ALU.mult, R=[kk, e3], W=[fa])
                    kb.tt(fr[:, :n], r, e2[:, :n], ALU.mult, R=[psx, e2], W=[fr])
                    kb.tt(fb[:, :n], b_[:, :n], e1[:, :n], ALU.mult, R=[b_, e1], W=[fb])
                    kb.tt(fk[:, :n], kmod[:, :n], e1[:, :n], ALU.mult, R=[kmod, e1], W=[fk])
                    gs = gst.next()
                    e2v = e2[:, :n].rearrange("p (c t) -> p c t", t=64)
                    col = 63 if d == 0 else 0
                    kb.copy(gs[:, :nch], e2v[:, :, col], R=[e2], W=[gs], eng="pool")
                    kb.dma("sp", g.gam[d][hs, c0:c0 + nch], gs[:, :nch], R=[gs])
                    for kind, f in enumerate((fa, fr, fb, fk)):
                        kb.dma("sp", g.FM[d][kind][hs, s0:s0 + n], f[:, :n], R=[f])
                    for kind, f in ((0, fb), (1, fk)):
                        pt_ = ptr.next()
                        for tt_ in range(nt):
                            kb.tr(pt_[:, tt_ * 128:(tt_ + 1) * 128], f[:, tt_ * 128:(tt_ + 1) * 128], g.ident_bf[:], R=[f, g.ident_bf], W=[pt_])
                        tm = tms.next()
                        kb.copy(tm[:, :nt * 128], pt_[:, :nt * 128], R=[pt_], W=[tm], eng=("act" if kind else "dve"))
                        for tt_ in range(nt):
                            kb.dma("sp", g.TM[d][s0 + tt_ * 128:s0 + (tt_ + 1) * 128, kind, hs], tm[:, tt_ * 128:(tt_ + 1) * 128], R=[tm])
                kb.tt(rk[:, :n], r, kbs[:, :n], ALU.mult, R=[psx, kbs], W=[rk])
                kb.ts(rk[:, :n], rk[:, :n], hrk[:, hp:hp + 1], None, ALU.mult, R=[rk, hrk], W=[rk])
                p = pmm.next()
                kb.mm(p[:, :n], blk[:], rk[:, :n], R=[blk, rk], W=[p])
                kb.tt(bon[:, :n], p[:, :n], v, ALU.mult, R=[p, psx], W=[bon])
                kb.dma("sp", g.bonT[hs, s0:s0 + n], bon[:, :n], R=[bon])


def phase_C0(kb, g):
    NL = 5
    with phase(kb):
        mk = sbt(kb, "mk", [64, 2, 3, 64])
        kb.load(mk[:], g.masks_h[:, :, :, :], W=[mk])
        poolA = PPool([pst(kb, f"cpa{i}") for i in range(5)])
        poolB = PPool([pst(kb, f"cpb{i}") for i in range(3)])
        st = []
        for d in range(2):
            s = G()
            s.gam = sbt(kb, f"gam{d}", [64, 8, 68])
            kb.load(s.gam[:], g.gam[d].rearrange("(h k) c -> k h c", k=64), W=[s.gam])
            s.fm = Rot([sbt(kb, f"fm{d}_{i}", [64, 4, 8, 64], BF16) for i in range(2)])
            s.tm = Rot([sbt(kb, f"tm{d}_{i}", [64, 2, 512], BF16) for i in range(2)])
            s.vt = Rot([sbt(kb, f"vt{d}_{i}", [64, 512], BF16) for i in range(2)])
            s.Sf = sbt(kb, f"Sf{d}", [64, 8, 64])
            s.Sb = Rot([sbt(kb, f"Sb{d}_{i}", [64, 8, 64], BF16) for i in range(2)])
            s.Nm = Rot([sbt(kb, f"Nm{d}_{i}", [64, 8, 128], BF16) for i in range(2)])
            s.Nkm = Rot([sbt(kb, f"Nkm{d}_{i}", [64, 8, 128], BF16) for i in range(2)])
            s.Inv = Rot([sbt(kb, f"Inv{d}_{i}", [64, 8, 64], BF16) for i in range(2)])
            s.NT = sbt(kb, f"NT{d}", [64, 8, 64], BF16)
            s.X = sbt(kb, f"X{d}", [64, 8, 64])
            s.Xb = Rot([sbt(kb, f"Xb{d}_{i}", [64, 8, 64], BF16) for i in range(2)])
            s.P = Rot([sbt(kb, f"P{d}_{i}", [64, 8, 64], BF16) for i in range(2)])
            s.PT = Rot([sbt(kb, f"PT{d}_{i}", [64, 8, 64], BF16) for i in range(2)])
            s.W1 = sbt(kb, f"W1{d}", [64, 8, 64], BF16)
            s.UT = sbt(kb, f"UT{d}", [64, 8, 64], BF16)
            s.Yst = Rot([sbt(kb, f"Yst{d}_{i}", [64, 512]) for i in range(2)])
            kb.memset(s.Sf[:], 0.0, W=[s.Sf])
            s.sb = s.Sb.next()
            kb.memset(s.sb[:], 0.0, W=[s.sb])
            s.mAR = mk[:, d, 0:2, :].rearrange("p a t -> p (a t)")[:, None, :].to_broadcast([64, 4, 128])
            s.mT = mk[:, d, 2, :][:, None, :].to_broadcast([64, 8, 64])
            st.append(s)
        Ibc = g.ident[0:64, 0:64][:, None, :].to_broadcast([64, 8, 64])
        order = [list(range(68)), [3, 2, 1, 0] + list(range(67, 3, -1))]

        def v3(p):
            return p[0:64, :].rearrange("p (h t) -> p h t", t=64)

        def partA(d, c, rec):
            s = st[d]
            t0 = c * 64
            yield
            fm, tm, vt = s.fm.next(), s.tm.next(), s.vt.next()
            Nm, Nkm = s.Nm.next(), s.Nkm.next()
            for kind in range(4):
                kb.load(fm[:, kind, :, :], g.FM[d][kind].rearrange("(h k) t -> k h t", k=64)[:, :, t0:t0 + 64], W=[fm])
            kb.load(tm[:], g.TM[d][t0:t0 + 64, :, :], W=[tm])
            kb.load(vt[:], g.Vtm[t0:t0 + 64, :], W=[vt])
            for (lk, dst) in ((2, Nm), (3, Nkm)):
                for hh in range(2):
                    p = poolA.get()
                    for h4 in range(4):
                        h = hh * 4 + h4
                        kb.mm(p[0:64, h4 * 128:(h4 + 1) * 128], fm[:, lk, h, :], fm[:, 0:2, h, :], R=[fm], W=[p])
                    kb.tt(dst[:, hh * 4:(hh + 1) * 4, :], p[0:64, :].rearrange("p (h t) -> p h t", t=128), s.mAR, ALU.mult, R=[p, mk], W=[dst])
                    poolA.put(p)
                    yield
            p = poolA.get()
            for h in range(8):
                kb.mm(p[0:64, h * 64:(h + 1) * 64], fm[:, 0, h, :], fm[:, 2, h, :], R=[fm], W=[p])
            kb.tt(s.NT[:], v3(p), s.mT, ALU.mult, R=[p, mk], W=[s.NT])
            poolA.put(p)
            yield
            kb.tt(s.X[:], Nm[:, :, 0:64], Ibc, ALU.add, R=[Nm, g.ident], W=[s.X])
            xb = s.Xb.next()
            kb.copy(xb[:], s.X[:], R=[s.X], W=[xb], eng="act")
            P_ap = lambda h: Nm[:, h, 0:64]
            PT_ap = lambda h: s.NT[:, h, :]
            Pt, PTt = Nm, s.NT
            for lv in range(NL):
                last = lv == NL - 1
                p1 = None
                if not last:
                    p1 = poolA.get()
                    for h in range(8):
                        kb.mm(p1[0:64, h * 64:(h + 1) * 64], PT_ap(h), P_ap(h), R=[Pt, PTt], W=[p1])
                p2 = poolA.get()
                for h in range(8):
                    kb.mm(p2[0:64, h * 64:(h + 1) * 64], P_ap(h), PT_ap(h), R=[Pt, PTt], W=[p2])
                yield
                nPT = s.PT.next()
                kb.copy(nPT[:], v3(p2), R=[p2], W=[nPT], eng="act")
                poolA.put(p2)
                if not last:
                    nP = s.P.next()
                    kb.copy(nP[:], v3(p1), R=[p1], W=[nP], eng="act")
                    poolA.put(p1)
                    Pt = nP
                    P_ap = (lambda t_: (lambda h: t_[:, h, :]))(nP)
                PTt = nPT
                PT_ap = (lambda t_: (lambda h: t_[:, h, :]))(nPT)
                yield
                p3 = poolA.get()
                for h in range(8):
                    kb.mm(p3[0:64, h * 64:(h + 1) * 64], PT_ap(h), xb[:, h, :], R=[PTt, xb], W=[p3])
                yield
                kb.tt(s.X[:], s.X[:], v3(p3), ALU.add, R=[s.X, p3], W=[s.X])
                poolA.put(p3)
                xb = s.Inv.next() if last else s.Xb.next()
                kb.copy(xb[:], s.X[:], R=[s.X], W=[xb], eng="act")
                yield
            rec.update(fm=fm, tm=tm, vt=vt, Nm=Nm, Nkm=Nkm, inv=xb, c=c)

        def partB(d, rec):
            s = st[d]
            fm, tm, vt, Nm, Nkm, inv, c = (rec[k] for k in ("fm", "tm", "vt", "Nm", "Nkm", "inv", "c"))
            t0 = c * 64
            sb = s.sb
            yield
            pw = poolB.get()
            for h in range(8):
                hs = slice(h * 64, (h + 1) * 64)
                kb.mm(pw[0:64, hs], Nkm[:, h, 0:64], vt[:, hs], start=True, stop=False, R=[Nkm, vt], W=[pw])
                kb.mm(pw[0:64, hs], fm[:, 0, h, :], sb[:, h, :], start=False, stop=True, R=[fm, sb], W=[pw])
            yield
            kb.copy(s.W1[:], v3(pw), R=[pw], W=[s.W1], eng="act")
            poolB.put(pw)
            pu = poolB.get()
            for h in range(8):
                kb.mm(pu[0:64, h * 64:(h + 1) * 64], inv[:, h, :], s.W1[:, h, :], R=[inv, s.W1], W=[pu])
            yield
            kb.copy(s.UT[:], v3(pu), R=[pu], W=[s.UT], eng="act")
            poolB.put(pu)
            pn = poolB.get()
            for h in range(8):
                hs = slice(h * 64, (h + 1) * 64)
                kb.mm(pn[0:64, hs], tm[:, 0, hs], s.UT[:, h, :], start=True, stop=False, R=[tm, s.UT], W=[pn])
                kb.mm(pn[0:64, hs], tm[:, 1, hs], vt[:, hs], start=False, stop=True, R=[tm, vt], W=[pn])
            yield
            kb.tt(s.Sf[:], s.Sf[:], v3(pn), ALU.add, R=[s.Sf, pn], W=[s.Sf])
            poolB.put(pn)
            kb.tt(s.Sf[:], s.Sf[:], s.gam[:, :, c:c + 1].to_broadcast([64, 8, 64]), ALU.mult, R=[s.Sf, s.gam], W=[s.Sf])
            nsb = s.Sb.next()
            kb.copy(nsb[:], s.Sf[:], R=[s.Sf], W=[nsb], eng="act")
            s.sb = nsb
            yield
            py = poolB.get()
            for h in range(8):
                hs = slice(h * 64, (h + 1) * 64)
                kb.mm(py[0:64, hs], fm[:, 1, h, :], sb[:, h, :], start=True, stop=False, R=[fm, sb], W=[py])
                kb.mm(py[0:64, hs], Nm[:, h, 64:128], s.UT[:, h, :], start=False, stop=False, R=[Nm, s.UT], W=[py])
                kb.mm(py[0:64, hs], Nkm[:, h, 64:128], vt[:, hs], start=False, stop=True, R=[Nkm, vt], W=[py])
            ys = s.Yst.next()
            kb.copy(ys[:], py[0:64, :], R=[py], W=[ys])
            poolB.put(py)
            kb.store(g.Y[d][t0:t0 + 64, :], ys[:], R=[ys])

        recs = [{}, {}]
        run_interleaved([partA(0, order[0][0], recs[0]), partA(1, order[1][0], recs[1])])
        for i in range(68):
            cur = recs
            gens = [partB(0, cur[0]), partB(1, cur[1])]
            recs = [{}, {}]
            if i + 1 < 68:
                gens += [partA(0, order[0][i + 1], recs[0]), partA(1, order[1][i + 1], recs[1])]
            run_interleaved(gens)

def phase_C2(kb, g):
    with phase(kb):
        rwc = sbt(kb, "rwc2", [128, 58])
        kb.load(rwc[:], g.rw_cols[:, :], W=[rwc])
        yf = Rot([sbt(kb, f"yf{i}", [128, 512]) for i in range(2)])
        yb = Rot([sbt(kb, f"yb{i}", [128, 512]) for i in range(2)])
        y = sbt(kb, "y", [128, 512])
        sq = sbt(kb, "ysq", [128, 512])
        sm = sbt(kb, "ysm", [128, 8])
        vr = sbt(kb, "yvr", [128, 8])
        rt = sbt(kb, "yrt", [128, 8])
        rs = sbt(kb, "yrs", [128, 8])
        yn = Rot([sbt(kb, f"yn{i}", [128, 512]) for i in range(2)])
        pb = [pst(kb, f"c2p{i}") for i in range(4)]
        bon = Rot([sbt(kb, f"bon{i}", [128, 4, 512]) for i in range(2)])
        gt = Rot([sbt(kb, f"gt{i}", [128, 4, 512]) for i in range(2)])
        a1 = Rot([sbt(kb, f"a1_{i}", [128, 512]) for i in range(2)])
        mo = Rot([sbt(kb, f"mo{i}", [128, 512], BF16) for i in range(3)])
        for bi, (s0, n) in enumerate(TBS):
            nt = n // 128
            bo, gg = bon.next(), gt.next()
            kb.load(bo[:, :, :n], g.bonT.rearrange("(c p) t -> p c t", p=128)[:, :, s0:s0 + n], W=[bo])
            kb.load(gg[:, :, :n], g.gT.rearrange("(c p) t -> p c t", p=128)[:, :, s0:s0 + n], W=[gg])
            for tt_ in range(nt):
                t0 = s0 + tt_ * 128
                a, b = yf.next(), yb.next()
                kb.load(a[:], g.Y[0][t0:t0 + 128, :], W=[a])
                kb.load(b[:], g.Y[1][t0:t0 + 128, :], W=[b])
                kb.tt(y[:], a[:], b[:], ALU.add, R=[a, b], W=[y])
                y3 = y[:].rearrange("p (h v) -> p h v", v=64)
                kb.op("dve", lambda e: e.tensor_reduce(out=sm[:], in_=y3, axis=AX.X, op=ALU.add), R=[y], W=[sm])
                kb.ts(sm[:], sm[:], -1.0 / 64.0, None, ALU.mult, R=[sm], W=[sm])
                kb.tt(y3, y3, sm[:, :, None].to_broadcast([128, 8, 64]), ALU.add, R=[y, sm], W=[y])
                kb.tt(sq[:], y[:], y[:], ALU.mult, R=[y], W=[sq])
                kb.op("dve", lambda e: e.tensor_reduce(out=vr[:], in_=sq[:].rearrange("p (h v) -> p h v", v=64), axis=AX.X, op=ALU.add), R=[sq], W=[vr])
                kb.act(rt[:], vr[:], AF.Sqrt, bias=g.eps_t[:, 2:3], scale=1.0 / 64.0, R=[vr, g.eps_t], W=[rt])
                kb.op("dve", lambda e: e.reciprocal(out=rs[:], in_=rt[:]), R=[rt], W=[rs])
                yo = yn.next()
                kb.tt(yo[:].rearrange("p (h v) -> p h v", v=64), y3, rs[:, :, None].to_broadcast([128, 8, 64]), ALU.mult, R=[y, rs], W=[yo])
                for ct in range(4):
                    kb.tr(pb[ct][:, tt_ * 128:(tt_ + 1) * 128], yo[:, ct * 128:(ct + 1) * 128], g.ident[:], R=[yo, g.ident], W=[pb[ct]])
            for ct in range(4):
                t1 = a1.next()
                kb.act(t1[:, :n], pb[ct][:, :n], AF.Identity, bias=rwc[:, 30 + ct:31 + ct], scale=rwc[:, 26 + ct:27 + ct], R=[pb[ct], rwc], W=[t1])
                kb.tt(t1[:, :n], t1[:, :n], bo[:, ct, :n], ALU.add, R=[t1, bo], W=[t1])
                o = mo.next()
                kb.tt(o[:, :n], t1[:, :n], gg[:, ct, :n], ALU.mult, R=[t1, gg], W=[o])
                kb.store(g.mixT[ct * 128:(ct + 1) * 128, s0:s0 + n], o[:, :n], R=[o])


def phase_D0(kb, g):
    LAM_INIT = 0.2
    with phase(kb):
        dc = sbt(kb, "dc", [64, 4])
        pr = sbt(kb, "dpr", [64, 2])
        onesf = sbt(kb, "onesf", [64, 128])
        lam = sbt(kb, "lam", [128, 2])
        nlam = sbt(kb, "nlam", [128, 1])
        slg = sbt(kb, "slg", [128, 1])
        kb.load(dc[:], g.diff_cols[:, :], W=[dc])
        kb.load(slg[:], g.subln_g[:, :], W=[slg])
        kb.memset(onesf[:], 1.0, W=[onesf])
        kb.tt(pr[:, 0:1], dc[:, 0:1], dc[:, 1:2], ALU.mult, R=[dc], W=[pr])
        kb.tt(pr[:, 1:2], dc[:, 2:3], dc[:, 3:4], ALU.mult, R=[dc], W=[pr])
        ps = Rot([pst(kb, f"dps{i}") for i in range(3)])
        pfin = pst(kb, "dpfin")
        acc = [[pst(kb, f"dacc{m}{q}") for q in range(2)] for m in range(2)]
        pl = ps.next()
        kb.mm(pl[:, 0:2], onesf[:], pr[:], R=[onesf, pr], W=[pl])
        kb.act(lam[:], pl[:, 0:2], AF.Exp, R=[pl], W=[lam])
        kb.tt(nlam[:], lam[:, 1:2], lam[:, 0:1], ALU.subtract, R=[lam], W=[nlam])
        kb.ts(nlam[:], nlam[:], -LAM_INIT, None, ALU.add, R=[nlam], W=[nlam])
        KT = sbt(kb, "KT", [128, T], BF16)
        QR = [sbt(kb, f"QR{m}", [128, T], BF16) for m in range(2)]
        QP = [sbt(kb, f"QP{m}", [128, T], BF16) for m in range(2)]
        for m in range(2):
            zs = slice(64, 128) if m == 0 else slice(0, 64)
            kb.memset(QR[m][zs, :], 0.0, W=[QR[m]])
            kb.memset(QP[m][zs, :], 0.0, W=[QP[m]])
        VA = sbt(kb, "VA", [128, 34, 130], BF16)
        pT = Rot([sbt(kb, f"pT{i}", [128, 512], BF16) for i in range(4)])
        rz = sbt(kb, "rz", [128, 4])
        o0 = sbt(kb, "o0", [128, 128])
        o = sbt(kb, "o", [128, 128])
        osq = sbt(kb, "osq", [128, 128])
        ssq = sbt(kb, "ssq", [128, 1])
        rt = sbt(kb, "drt", [128, 1])
        rs = sbt(kb, "drs", [128, 1])
        on = Rot([sbt(kb, f"on{i}", [128, 128]) for i in range(2)])
        so = Rot([sbt(kb, f"dso{i}", [128, 256], BF16) for i in range(2)])
        accS = Rot([sbt(kb, f"accS{i}", [128, 4, 129]) for i in range(2)])

        def finalize(aS, h, q0):
            s_ = so.next()
            for qt in range(2):
                a0_, a1_ = aS[:, qt, :], aS[:, 2 + qt, :]
                kb.op("dve", lambda e_: e_.reciprocal(out=rz[:, 0:1], in_=a0_[:, 128:129]), R=[aS], W=[rz])
                kb.op("dve", lambda e_: e_.reciprocal(out=rz[:, 1:2], in_=a1_[:, 128:129]), R=[aS], W=[rz])
                kb.tt(rz[:, 2:3], rz[:, 1:2], nlam[:], ALU.mult, R=[rz, nlam], W=[rz])
                kb.ts(o0[:], a0_[:, 0:128], rz[:, 0:1], None, ALU.mult, R=[aS, rz], W=[o0])
                kb.stt(o[:], a1_[:, 0:128], rz[:, 2:3], o0[:], ALU.mult, ALU.add, R=[aS, rz, o0], W=[o])
                yield
                kb.act(osq[:], o[:], AF.Square, R=[o], W=[osq, ssq], accum_out=ssq[:])
                yield
                kb.act(rt[:], ssq[:], AF.Ln, bias=g.eps_t[:, 0:1], scale=1.0 / 128.0, R=[ssq, g.eps_t], W=[rt])
                yield
                kb.act(rs[:], rt[:], AF.Exp, scale=-0.5, R=[rt], W=[rs])
                on_ = on.next()
                kb.ts(on_[:], o[:], rs[:, 0:1], 1.0 - LAM_INIT, ALU.mult, ALU.mult, R=[o, rs], W=[on_])
                yield
                kb.tr(pfin[:, 0:128], on_[:], g.ident[:], R=[on_, g.ident], W=[pfin])
                yield
                kb.act(s_[:, qt * 128:(qt + 1) * 128], pfin[:, 0:128], AF.Copy, scale=slg[:, 0:1], R=[pfin, slg], W=[s_])
                yield
            kb.store(g.mixT[512 + h * 128:512 + (h + 1) * 128, q0:q0 + 256], s_[:], R=[s_])

        fin = iter(())
        chunks = [(0, True)] + [(256 + 256 * i, False) for i in range(16)]
        import os
        DS = int(os.environ.get("D0_STAGE", "9"))
        if DS < 9:
            chunks = chunks[:2]
        for h in range(4 if DS == 9 else 1):
            hs = slice(h * 128, (h + 1) * 128)
            kb.load(KT[:], g.KrotT[hs, :], W=[KT])
            for m in range(2):
                ms = slice(m * 64, (m + 1) * 64)
                kb.load(QR[m][ms, :], g.QrotT[h * 128 + m * 64:h * 128 + (m + 1) * 64, :], W=[QR[m]])
                kb.load(QP[m][ms, :], g.QplT[h * 128 + m * 64:h * 128 + (m + 1) * 64, :], W=[QP[m]])
            kb.load(VA[:, :, 0:128], g.Vd.rearrange("(kt p) c -> p kt c", p=128)[:, :, hs], W=[VA])
            kb.memset(VA[:, :, 128:129], 1.0, W=[VA])
            for (q0, isctx) in chunks:
                if DS < 2:
                    break
                kts = [0, 1] if isctx else list(range(34))
                def score(kt):
                    Q = QP if (not isctx and kt < 2) else QR
                    p = ps.next()
                    ks = slice(kt * 128, (kt + 1) * 128)
                    kb.mm(p[:, 0:256], KT[:, ks], Q[0][:, q0:q0 + 256], R=[KT, Q[0]], W=[p])
                    kb.mm(p[:, 256:512], KT[:, ks], Q[1][:, q0:q0 + 256], R=[KT, Q[1]], W=[p])
                    return p
                LA = 2
                pq = [score(kts[k]) for k in range(min(LA, len(kts)))]
                for i, kt in enumerate(kts):
                    p = pq.pop(0)
                    if i + LA < len(kts):
                        pq.append(score(kts[i + LA]))
                    e = pT.next()
                    kb.act(e[:], p[:], AF.Exp, scale=0.125, R=[p], W=[e])
                    next(fin, None)
                    if DS < 3:
                        continue
                    for m in range(2):
                        for qt in range(2):
                            kb.mm(acc[m][qt][:, 0:129], e[:, m * 256 + qt * 128:m * 256 + (qt + 1) * 128], VA[:, kt, 0:129],
                                  start=(i == 0), stop=(i == len(kts) - 1), R=[e, VA], W=[acc[m][qt]])
                if DS < 4:
                    continue
                for _ in fin:
                    pass
                aS = accS.next()
                for m in range(2):
                    for qt in range(2):
                        kb.copy(aS[:, m * 2 + qt, :], acc[m][qt][:, 0:129], R=[acc[m][qt]], W=[aS])
                fin = finalize(aS, h, q0)
        for _ in fin:
            pass


def phase_E(kb, g, li, w_ap, kcn, mixT, blocks):
    with phase(kb):
        Wo = sbt(kb, "Wo", [128, kcn, 1024], BF16)
        stg = Rot([sbt(kb, f"estg{i}", [128, kcn * 128]) for i in range(2)])
        load_weight_bf16(kb, Wo, w_ap, 1024, stg, piece=128)
        xb = Rot([sbt(kb, f"exb{i}", [128, 8, 512]) for i in range(2)])
        mb = Rot([sbt(kb, f"emb{i}", [128, kcn, 512], BF16) for i in range(2)])
        pmm = Rot([pst(kb, f"ep{i}") for i in range(4)])
        xsrc = g.xT.rearrange("(kc p) t -> p kc t", p=128)
        msrc = mixT.rearrange("(kc p) t -> p kc t", p=128)
        for (s0, n, j) in blocks:
            x, m = xb.next(), mb.next()
            kb.load(x[:, :, :n], xsrc[:, :, s0:s0 + n], W=[x])
            kb.load(m[:, :, :n], msrc[:, :, s0:s0 + n], W=[m])
            for ct in range(8):
                p = pmm.next()
                for kc in range(kcn):
                    kb.mm(p[:, :n], Wo[:, kc, ct * 128:(ct + 1) * 128], m[:, kc, :n], start=(kc == 0), stop=(kc == kcn - 1), R=[Wo, m], W=[p])
                kb.stt(x[:, ct, :n], p[:, :n], g.mod[li][:, 16 + ct, j:j + 1], x[:, ct, :n], ALU.mult, ALU.add, R=[p, g.mod[li], x], W=[x])
            kb.store(xsrc[:, :, s0:s0 + n], x[:, :, :n], R=[x])


def phase_F(kb, g, li, blocks):
    with phase(kb):
        W1 = sbt(kb, "W1", [128, 8, 4096], BF16)
        W2 = sbt(kb, "W2", [128, 32, 1024], BF16)
        stg0 = sbt(kb, "fstg0", [128, 1024])
        nb = G()
        nb.sq = sbt(kb, "fsq", [128, 8, 256], BF16)
        nb.tmp = sbt(kb, "ftmp", [128, 8, 256])
        nb.rt = sbt(kb, "frt", [128, 256])
        nb.rstd = sbt(kb, "frstd", [128, 256])
        nb.ss = pst(kb, "fss")
        stg = Rot([stg0,
                   TT(nb.tmp.t[:, 0:4, :].rearrange("p a b -> p (a b)")),
                   TT(nb.tmp.t[:, 4:8, :].rearrange("p a b -> p (a b)"))])
        i_ = 0
        for (Wt, src, kcn, ncols) in ((W1, g.mlp_w1[li], 8, 4096), (W2, g.mlp_w2[li], 32, 1024)):
            for kc in range(kcn):
                for c0 in range(0, ncols, 1024):
                    st = stg.next()
                    kb.load(st[:, 0:1024], src[kc * 128:(kc + 1) * 128, c0:c0 + 1024], W=[st])
                    kb.copy(Wt[:, kc, c0:c0 + 1024], st[:, 0:1024], R=[st], W=[Wt], eng=("act" if i_ % 2 else "dve"))
                    i_ += 1
        barrier(kb)
        xb = Rot([sbt(kb, f"fxb{i}", [128, 8, 256]) for i in range(2)])
        hTs = Rot([sbt(kb, f"fhT{i}", [128, 8, 256], BF16) for i in range(2)])
        hid = sbt(kb, "fhid", [128, 32, 256], BF16)
        rl = Rot([sbt(kb, f"frl{i}", [128, 256]) for i in range(3)])
        pmm = Rot([pst(kb, f"fp{i}") for i in range(6)])
        xsrc = g.xT.rearrange("(kc p) t -> p kc t", p=128)

        def prep(blk):
            s0, n, j = blk
            x = xb.next()
            kb.load(x[:, :, :n], xsrc[:, :, s0:s0 + n], W=[x])
            hT = hTs.next()
            norm_mod(kb, g, nb, x, n, g.sc2[li], g.mod[li], j, hT, sh_off=24)
            return x, hT

        nxt = prep(blocks[0])
        for bi, (s0, n, j) in enumerate(blocks):
            x, hT = nxt
            for hc in range(32):
                p = pmm.next()
                for kc in range(8):
                    kb.mm(p[:, :n], W1[:, kc, hc * 128:(hc + 1) * 128], hT[:, kc, :n], start=(kc == 0), stop=(kc == 7), R=[W1, hT], W=[p])
                r = rl.next()
                kb.act(r[:, :n], p[:, :n], AF.Relu, R=[p], W=[r])
                kb.tt(hid[:, hc, :n], r[:, :n], p[:, :n], ALU.mult, R=[r, p], W=[hid])
            if bi + 1 < len(blocks):
                nxt = prep(blocks[bi + 1])
            for ct in range(8):
                p = pmm.next()
                for hc in range(32):
                    kb.mm(p[:, :n], W2[:, hc, ct * 128:(ct + 1) * 128], hid[:, hc, :n], start=(hc == 0), stop=(hc == 31), R=[W2, hid], W=[p])
                kb.stt(x[:, ct, :n], p[:, :n], g.mod[li][:, 40 + ct, j:j + 1], x[:, ct, :n], ALU.mult, ALU.add, R=[p, g.mod[li], x], W=[x])
            kb.store(xsrc[:, :, s0:s0 + n], x[:, :, :n], R=[x])


BLK512 = [(s0, n, 1 if s0 == 0 else 0) for (s0, n) in TBS]
BLK256 = [(s0, 256, 1 if s0 == 0 else 0) for s0 in range(0, T, 256)]

def phase_A1(kb, g):
    with phase(kb):
        W = sbt(kb, "wA1", [128, 8, 6176], BF16)
        stg = Rot([sbt(kb, f"w1stg{i}", [128, 2048]) for i in range(2)])
        load_weight_bf16(kb, W, g.ssd_w_in, 6176, stg)
        nb = NormBufs(kb, "A1")
        xb = Rot([sbt(kb, f"a1x{i}", [128, 8, 512]) for i in range(2)])
        hTs = Rot([sbt(kb, f"a1h{i}", [128, 8, 512], BF16) for i in range(2)])
        pmm = Rot([pst(kb, f"a1p{i}") for i in range(6)])
        st32 = Rot([sbt(kb, f"a1s{i}", [128, 512]) for i in range(4)])
        st16 = Rot([sbt(kb, f"a1z{i}", [128, 512], BF16) for i in range(4)])
        xsrc = g.xT.rearrange("(kc p) t -> p kc t", p=128)
        for bi, (s0, n) in enumerate(TBS):
            j = 1 if bi == 0 else 0
            x = xb.next()
            kb.load(x[:, :, :n], xsrc[:, :, s0:s0 + n], W=[x])
            hT = hTs.next()
            norm_mod(kb, g, nb, x, n, g.sc1[1], g.mod[1], j, hT, sh_off=0)
            for ct in range(32):
                p = pmm.next()
                c0 = 2048 + ct * 128
                for kc in range(8):
                    kb.mm(p[:, :n], W[:, kc, c0:c0 + 128], hT[:, kc, :n], start=(kc == 0), stop=(kc == 7), R=[W, hT], W=[p])
                st = st32.next()
                kb.copy(st[:, :n], p[:, :n], R=[p], W=[st], eng=("act" if ct % 2 else "dve"))
                kb.store(g.xbcT[ct * 128:(ct + 1) * 128, s0:s0 + n], st[:, :n], R=[st])
            for tt_ in range(n // 128):
                ts_ = slice(tt_ * 128, (tt_ + 1) * 128)
                t0 = s0 + tt_ * 128
                for zc in range(4):
                    p = pmm.next()
                    for kc in range(8):
                        kb.mm(p[:, :], hT[:, kc, ts_], W[:, kc, zc * 512:(zc + 1) * 512], start=(kc == 0), stop=(kc == 7), R=[W, hT], W=[p])
                    so = st16.next()
                    kb.copy(so[:], p[:], R=[p], W=[so], eng=("act" if zc % 2 else "dve"))
                    kb.store(g.zTM[t0:t0 + 128, zc * 512:(zc + 1) * 512], so[:], R=[so])
                p = pmm.next()
                for kc in range(8):
                    kb.mm(p[:, 0:32], hT[:, kc, ts_], W[:, kc, 6144:6176], start=(kc == 0), stop=(kc == 7), R=[W, hT], W=[p])
                st = st32.next()
                kb.copy(st[:, 0:32], p[:, 0:32], R=[p], W=[st])
                kb.store(g.dtTM[t0:t0 + 128, :], st[:, 0:32], R=[st])


def phase_B1(kb, g):
    with phase(kb):
        cw = sbt(kb, "cw", [128, 32, 5])
        cb = sbt(kb, "cb", [128, 32])
        kb.load(cw[:], g.conv_wT[:, :, :], W=[cw])
        kb.load(cb[:], g.conv_bT[:, :], W=[cb])
        xin = Rot([sbt(kb, f"b1x{i}", [128, 8, 516]) for i in range(2)])
        acc = Rot([sbt(kb, f"b1a{i}", [128, 512]) for i in range(2)])
        u32 = Rot([sbt(kb, f"b1u{i}", [128, 512]) for i in range(2)])
        u16 = Rot([sbt(kb, f"b1v{i}", [128, 512], BF16) for i in range(3)])
        p32 = Rot([pst(kb, f"b1p{i}") for i in range(3)])
        p16 = Rot([pst(kb, f"b1q{i}", dt=BF16) for i in range(2)])
        t32 = Rot([sbt(kb, f"b1t{i}", [128, 512]) for i in range(3)])
        t16 = Rot([sbt(kb, f"b1s{i}", [128, 512], BF16) for i in range(3)])
        src = g.xbcT.rearrange("(ti p) t -> p ti t", p=128)
        for bi, (s0, n) in enumerate(TBS):
            nt = n // 128
            seq0, seq1 = (0, NCTX) if s0 < NCTX else (NCTX, T)
            lo = max(seq0, s0 - 2)
            hi = min(seq1, s0 + n + 2)
            for grp in range(4):
                xi = xin.next()
                kb.memset(xi[:, :, 0:2], 0.0, W=[xi])
                kb.memset(xi[:, :, n + 2:n + 4], 0.0, W=[xi])
                kb.load(xi[:, :, lo - s0 + 2:hi - s0 + 2], src[:, grp * 8:(grp + 1) * 8, lo:hi], W=[xi])
                for t8 in range(8):
                    ti = grp * 8 + t8
                    a = acc.next()
                    kb.ts(a[:, :n], xi[:, t8, 0:n], cw[:, ti, 0:1], cb[:, ti:ti + 1], ALU.mult, ALU.add, R=[xi, cw, cb], W=[a])
                    for k in range(1, 5):
                        kb.stt(a[:, :n], xi[:, t8, k:k + n], cw[:, ti, k:k + 1], a[:, :n], ALU.mult, ALU.add, R=[xi, cw, a], W=[a])
                    if ti < 16:
                        u = u32.next()
                        kb.act(u[:, :n], a[:, :n], AF.Silu, R=[a], W=[u])
                        p = p32.next()
                        for tt_ in range(nt):
                            kb.tr(p[:, tt_ * 128:(tt_ + 1) * 128], u[:, tt_ * 128:(tt_ + 1) * 128], g.ident[:], R=[u, g.ident], W=[p])
                        t = t32.next()
                        kb.copy(t[:, :n], p[:, :n], R=[p], W=[t], eng=("act" if ti % 2 else "dve"))
                        for tt_ in range(nt):
                            kb.store(g.xsTM[s0 + tt_ * 128:s0 + (tt_ + 1) * 128, ti * 128:(ti + 1) * 128], t[:, tt_ * 128:(tt_ + 1) * 128], R=[t])
                    else:
                        u = u16.next()
                        kb.act(u[:, :n], a[:, :n], AF.Silu, R=[a], W=[u])
                        if ti < 24:
                            gi = ti - 16
                            kb.store(g.BT[gi * 128:(gi + 1) * 128, s0:s0 + n], u[:, :n], R=[u])
                            p = p16.next()
                            for tt_ in range(nt):
                                kb.tr(p[:, tt_ * 128:(tt_ + 1) * 128], u[:, tt_ * 128:(tt_ + 1) * 128], g.ident_bf[:], R=[u, g.ident_bf], W=[p])
                            t = t16.next()
                            kb.copy(t[:, :n], p[:, :n], R=[p], W=[t], eng="act")
                            for tt_ in range(nt):
                                kb.store(g.BTM[s0 + tt_ * 128:s0 + (tt_ + 1) * 128, gi * 128:(gi + 1) * 128], t[:, tt_ * 128:(tt_ + 1) * 128], R=[t])
                        else:
                            gi = ti - 24
                            kb.store(g.CT[gi * 128:(gi + 1) * 128, s0:s0 + n], u[:, :n], R=[u])


def phase_C1(kb, g):
    with phase(kb):
        UT1 = sbt(kb, "UT1", [128, 128])
        LT1 = sbt(kb, "LT1", [128, 128])
        onesf = sbt(kb, "c1ones", [128, 128])
        prm = sbt(kb, "prm", [128, 5, 32])
        aneg = sbt(kb, "aneg", [128, 2, 32])
        kb.load(UT1[:], g.ut1_h[:, :], W=[UT1])
        kb.load(LT1[:], g.lt1_h[:, :], W=[LT1])
        SLT = [sbt(kb, "sLT", [128, 128]), sbt(kb, "sUT", [128, 128])]
        kb.tt(SLT[0][:], LT1[:], g.ident[:], ALU.subtract, R=[LT1, g.ident], W=[SLT[0]])
        kb.tt(SLT[1][:], UT1[:], g.ident[:], ALU.subtract, R=[UT1, g.ident], W=[SLT[1]])
        kb.load(prm[:], g.ssd_prm[:, :, :], W=[prm])
        kb.memset(onesf[:], 1.0, W=[onesf])
        kb.act(aneg[:], prm[:, 2:4, :], AF.Exp, R=[prm], W=[aneg])
        kb.ts(aneg[:], aneg[:], -1.0, None, ALU.mult, R=[aneg], W=[aneg])
        tri = [UT1, LT1]
        triB = [sbt(kb, "UT1b", [128, 128], BF16), sbt(kb, "LT1b", [128, 128], BF16)]
        kb.copy(triB[0][:], UT1[:], R=[UT1], W=[triB[0]])
        kb.copy(triB[1][:], LT1[:], R=[LT1], W=[triB[1]])
        pydd = [[pst(kb, f"c1y{d}{i}") for i in range(2)] for d in range(2)]
        pbig = Rot([pst(kb, f"c1b{i}") for i in range(2)])
        psm = Rot([pst(kb, f"c1s{i}") for i in range(2)])
        st = []
        for d in range(2):
            s = G()
            s.xs = Rot([sbt(kb, f"xs{d}_{i}", [128, 32, 64]) for i in range(1)])
            s.D = sbt(kb, f"Dcs{d}", [128, 32, 128], BF16)
            s.bt = Rot([sbt(kb, f"bt{d}_{i}", [128, 8, 128], BF16) for i in range(2)])
            s.ct = Rot([sbt(kb, f"ct{d}_{i}", [128, 8, 128], BF16) for i in range(2)])
            s.btm = Rot([sbt(kb, f"btm{d}_{i}", [128, 1024], BF16) for i in range(2)])
            s.dt = Rot([sbt(kb, f"dt{d}_{i}", [128, 32]) for i in range(2)])
            s.hf = sbt(kb, f"hf{d}", [128, 32, 64])
            s.hb = Rot([sbt(kb, f"hb{d}_{i}", [128, 32, 64], BF16) for i in range(2)])
            s.xdt = sbt(kb, f"xdt{d}", [128, 32, 64], BF16)
            s.xdw = sbt(kb, f"xdw{d}", [128, 32, 64], BF16)
            s.yo = sbt(kb, f"yo{d}", [128, 8, 64])
            s.y = Rot([sbt(kb, f"y{d}_{i}", [128, 32, 64]) for i in range(1)])
            s.cbm = sbt(kb, f"cbm{d}", [128, 8, 128], BF16)
            kb.memset(s.hf[:], 0.0, W=[s.hf])
            s.h = s.hb.next()
            kb.memset(s.h[:], 0.0, W=[s.h])
            st.append(s)
        sm = lambda nm, w=32: sbt(kb, nm, [128, w])
        ex, dtd, dta, cs, ncs, ecs, wts, etot, csT = [[sm(f"{nm}{d}") for d in range(2)] for nm in
                                                       ("ex", "dtd", "dta", "cs", "ncs", "ecs", "wts", "etot", "csTx")]
        csTs = [sbt(kb, f"csT{d}", [32, 128]) for d in range(2)]
        E4 = Rot([sbt(kb, f"E4{i}", [128, 512], BF16) for i in range(3)])
        G4 = Rot([sbt(kb, f"G4{i}", [128, 4, 128], BF16) for i in range(3)])
        order = [list(range(34)), [1, 0] + list(range(33, 1, -1))]

        def chunk(d, c):
            s = st[d]
            t0 = c * 128
            yield
            xs, bt, ct, btm, dt = s.xs.next(), s.bt.next(), s.ct.next(), s.btm.next(), s.dt.next()
            kb.load(xs[:].rearrange("p h q -> p (h q)"), g.xsTM[t0:t0 + 128, :], W=[xs])
            kb.load(bt[:], g.BT.rearrange("(g n) t -> n g t", n=128)[:, :, t0:t0 + 128], W=[bt])
            kb.load(ct[:], g.CT.rearrange("(g n) t -> n g t", n=128)[:, :, t0:t0 + 128], W=[ct])
            kb.load(btm[:], g.BTM[t0:t0 + 128, :], W=[btm])
            kb.load(dt[:], g.dtTM[t0:t0 + 128, :], W=[dt])
            kb.tt(ex[d][:], dt[:], prm[:, d, :], ALU.add, R=[dt, prm], W=[ex[d]])
            kb.act(ex[d][:], ex[d][:], AF.Exp, R=[ex[d]], W=[ex[d]])
            kb.act(dtd[d][:], ex[d][:], AF.Ln, bias=g.eps_t[:, 4:5], scale=1.0, R=[ex[d], g.eps_t], W=[dtd[d]])
            kb.tt(dta[d][:], dtd[d][:], aneg[:, d, :], ALU.mult, R=[dtd[d], aneg], W=[dta[d]])
            p = psm.next()
            kb.mm(p[:, 0:32], tri[d][:], dta[d][:], R=[tri[d], dta[d]], W=[p])
            kb.mm(p[:, 32:64], onesf[:], dta[d][:], R=[onesf, dta[d]], W=[p])
            kb.copy(cs[d][:], p[:, 0:32], R=[p], W=[cs[d]])
            kb.act(ecs[d][:], p[:, 0:32], AF.Exp, R=[p], W=[ecs[d]])
            kb.act(etot[d][:], p[:, 32:64], AF.Exp, R=[p], W=[etot[d]])
            kb.tt(wts[d][:], p[:, 32:64], cs[d][:], ALU.subtract, R=[p, cs[d]], W=[wts[d]])
            kb.act(wts[d][:], wts[d][:], AF.Exp, R=[wts[d]], W=[wts[d]])
            kb.tt(s.D[:], SLT[d][:, None, :].to_broadcast([128, 32, 128]), dta[d][:, :, None].to_broadcast([128, 32, 128]), ALU.mult,
                  R=[SLT[d], dta[d]], W=[s.D])
            yield
            kb.tt(s.xdt[:], xs[:], dtd[d][:, :, None].to_broadcast([128, 32, 64]), ALU.mult, R=[xs, dtd[d]], W=[s.xdt])
            kb.tt(s.xdw[:], s.xdt[:], wts[d][:, :, None].to_broadcast([128, 32, 64]), ALU.mult, R=[s.xdt, wts[d]], W=[s.xdw])
            for hf in range(2):
                p = pbig.next()
                for g4 in range(4):
                    gi = hf * 4 + g4
                    kb.mm(p[:, g4 * 128:(g4 + 1) * 128], bt[:, gi, :], ct[:, gi, :], R=[bt, ct], W=[p])
                kb.tt(s.cbm[:, hf * 4:(hf + 1) * 4, :], p[:, :].rearrange("p (g l) -> p g l", l=128),
                      tri[d][:, None, :].to_broadcast([128, 4, 128]), ALU.mult, R=[p, tri[d]], W=[s.cbm])
            h_old = s.h
            yt = s.y.next()
            for hf in range(2):
                pyd = pydd[d]
                for g4 in range(4):
                    gi = hf * 4 + g4
                    pc = psm.next()
                    for h4 in range(4):
                        kb.mm(pc[:, h4 * 128:(h4 + 1) * 128], s.D[:, gi * 4 + h4, :], triB[d][:], R=[s.D, triB[d]], W=[pc])
                    e4 = E4.next()
                    kb.act(e4[:], pc[:, :], AF.Exp, R=[pc], W=[e4])
                    g4t = G4.next()
                    kb.tt(g4t[:], e4[:].rearrange("p (h l) -> p h l", l=128), s.cbm[:, gi, None, :].to_broadcast([128, 4, 128]),
                          ALU.mult, R=[e4, s.cbm], W=[g4t])
                    for h4 in range(4):
                        h = gi * 4 + h4
                        h16 = h - hf * 16
                        pb = pyd[h16 // 8]
                        kb.mm(pb[:, (h16 % 8) * 64:(h16 % 8 + 1) * 64], g4t[:, h4, :], s.xdt[:, h, :], R=[g4t, s.xdt], W=[pb])
                    yield
                pyo = [pbig.next(), pbig.next()]
                for g4 in range(4):
                    gi = hf * 4 + g4
                    pb = pyo[g4 // 2]
                    kb.mm(pb[:, (g4 % 2) * 256:(g4 % 2 + 1) * 256], ct[:, gi, :], h_old[:, gi * 4:(gi + 1) * 4, :], R=[ct, h_old], W=[pb])
                for q in range(2):
                    hs = slice(hf * 16 + q * 8, hf * 16 + (q + 1) * 8)
                    kb.tt(s.yo[:], pyo[q][:, :].rearrange("p (h q) -> p h q", q=64), ecs[d][:, hs, None].to_broadcast([128, 8, 64]),
                          ALU.mult, R=[pyo[q], ecs[d]], W=[s.yo])
                    kb.tt(yt[:, hs, :], pyd[q][:, :].rearrange("p (h q) -> p h q", q=64), s.yo[:], ALU.add, R=[pyd[q], s.yo], W=[yt])
            kb.store(g.Yssd[d][t0:t0 + 128, :], yt[:].rearrange("p h q -> p (h q)"), R=[yt])
            yield
            nh = s.hb.next()
            for q in range(4):
                p = pbig.next()
                for g2 in range(2):
                    gi = q * 2 + g2
                    kb.mm(p[:, g2 * 256:(g2 + 1) * 256], btm[:, gi * 128:(gi + 1) * 128], s.xdw[:, gi * 4:(gi + 1) * 4, :], R=[btm, s.xdw], W=[p])
                hs = slice(q * 8, (q + 1) * 8)
                kb.tt(s.hf[:, hs, :], s.hf[:, hs, :], etot[d][:, hs, None].to_broadcast([128, 8, 64]), ALU.mult, R=[s.hf, etot[d]], W=[s.hf])
                kb.tt(s.hf[:, hs, :], s.hf[:, hs, :], p[:, :].rearrange("p (h q) -> p h q", q=64), ALU.add, R=[s.hf, p], W=[s.hf])
            kb.copy(nh[:], s.hf[:], R=[s.hf], W=[nh], eng="act")
            s.h = nh

        for i in range(34):
            run_interleaved([chunk(0, order[0][i]), chunk(1, order[1][i])])


def phase_C3(kb, g):
    with phase(kb):
        prm = sbt(kb, "c3prm", [128, 5, 32])
        ng = sbt(kb, "c3ng", [128, 16])
        kb.load(prm[:], g.ssd_prm[:, :, :], W=[prm])
        kb.load(ng[:], g.ssd_ngT[:, :], W=[ng])
        y0 = Rot([sbt(kb, f"c3a{i}", [128, 32, 64]) for i in range(2)])
        y1 = Rot([sbt(kb, f"c3b{i}", [128, 32, 64]) for i in range(2)])
        xs = Rot([sbt(kb, f"c3x{i}", [128, 32, 64]) for i in range(2)])
        z = Rot([sbt(kb, f"c3z{i}", [128, 2048], BF16) for i in range(2)])
        sz = sbt(kb, "c3sz", [128, 2048])
        sq = sbt(kb, "c3sq", [128, 2048])
        ss = sbt(kb, "c3ss", [128, 8])
        rt = sbt(kb, "c3rt", [128, 8])
        rs = sbt(kb, "c3rs", [128, 8])
        yn = sbt(kb, "c3yn", [128, 4, 2048])
        pb = Rot([pst(kb, f"c3p{i}") for i in range(4)])
        so = Rot([sbt(kb, f"c3o{i}", [128, 512], BF16) for i in range(3)])
        for bi, (s0, n) in enumerate(TBS):
            nt = n // 128
            for tt_ in range(nt):
                t0 = s0 + tt_ * 128
                a, b, x, zz = y0.next(), y1.next(), xs.next(), z.next()
                kb.load(a[:].rearrange("p h q -> p (h q)"), g.Yssd[0][t0:t0 + 128, :], W=[a])
                kb.load(b[:].rearrange("p h q -> p (h q)"), g.Yssd[1][t0:t0 + 128, :], W=[b])
                kb.load(x[:].rearrange("p h q -> p (h q)"), g.xsTM[t0:t0 + 128, :], W=[x])
                kb.load(zz[:], g.zTM[t0:t0 + 128, :], W=[zz])
                kb.tt(a[:], a[:], b[:], ALU.add, R=[a, b], W=[a])
                kb.tt(x[:], x[:], prm[:, 4, :, None].to_broadcast([128, 32, 64]), ALU.mult, R=[x, prm], W=[x])
                kb.tt(a[:], a[:], x[:], ALU.add, R=[a, x], W=[a])
                kb.act(sz[:], zz[:], AF.Silu, R=[zz], W=[sz])
                af = a[:].rearrange("p h q -> p (h q)")
                kb.tt(af, af, sz[:], ALU.mult, R=[a, sz], W=[a])
                kb.tt(sq[:], af, af, ALU.mult, R=[a], W=[sq])
                kb.op("dve", lambda e: e.tensor_reduce(out=ss[:], in_=sq[:].rearrange("p (g c) -> p g c", c=256), axis=AX.X, op=ALU.add), R=[sq], W=[ss])
                kb.act(rt[:], ss[:], AF.Sqrt, bias=g.eps_t[:, 0:1], scale=1.0 / 256.0, R=[ss, g.eps_t], W=[rt])
                kb.op("dve", lambda e: e.reciprocal(out=rs[:], in_=rt[:]), R=[rt], W=[rs])
                kb.tt(yn[:, tt_, :].rearrange("p (g c) -> p g c", c=256), af.rearrange("p (g c) -> p g c", c=256),
                      rs[:, :, None].to_broadcast([128, 8, 256]), ALU.mult, R=[a, rs], W=[yn])
            for ct in range(16):
                p = pb.next()
                for tt_ in range(nt):
                    kb.tr(p[:, tt_ * 128:(tt_ + 1) * 128], yn[:, tt_, ct * 128:(ct + 1) * 128], g.ident[:], R=[yn, g.ident], W=[p])
                o = so.next()
                kb.act(o[:, :n], p[:, :n], AF.Copy, scale=ng[:, ct:ct + 1], R=[p, ng], W=[o])
                kb.store(g.mixT1[ct * 128:(ct + 1) * 128, s0:s0 + n], o[:, :n], R=[o])


def phase_G(kb, g):
    with phase(kb):
        gf = sbt(kb, "gf", [128, 8])
        kb.load(gf[:], g.gfT[:, :], W=[gf])
        xb = Rot([sbt(kb, f"gx{i}", [128, 8, 512]) for i in range(2)])
        sq = sbt(kb, "gsq", [128, 8, 512], BF16)
        ssp = pst(kb, "gss")
        rt = sbt(kb, "grt", [128, 512])
        rs = sbt(kb, "grs", [128, 512])
        xn = sbt(kb, "gxn", [128, 8, 512])
        pt = Rot([pst(kb, f"gp{i}") for i in range(4)])
        o = Rot([sbt(kb, f"go{i}", [128, 1024]) for i in range(2)])
        xsrc = g.xT.rearrange("(kc p) t -> p kc t", p=128)
        for (s0, n) in TBS[1:]:
            x = xb.next()
            kb.load(x[:, :, :n], xsrc[:, :, s0:s0 + n], W=[x])
            kb.act(sq[:, :, :n], x[:, :, :n], AF.Square, R=[x], W=[sq])
            for kc in range(8):
                kb.mm(ssp[:, :n], g.ones_bf[:], sq[:, kc, :n], start=(kc == 0), stop=(kc == 7), R=[sq, g.ones_bf], W=[ssp])
            kb.act(rt[:, :n], ssp[:, :n], AF.Sqrt, bias=g.eps_t[:, 0:1], scale=1.0 / 1024.0, R=[ssp, g.eps_t], W=[rt])
            kb.op("dve", lambda e: e.reciprocal(out=rs[:, :n], in_=rt[:, :n]), R=[rt], W=[rs])
            kb.tt(xn[:, :, :n], x[:, :, :n], rs[:, None, :n].to_broadcast([128, 8, n]), ALU.mult, R=[x, rs], W=[xn])
            for kc in range(8):
                if kc % 2:
                    kb.act(xn[:, kc, :n], xn[:, kc, :n], AF.Identity, scale=gf[:, kc:kc + 1], R=[xn, gf], W=[xn])
                else:
                    kb.ts(xn[:, kc, :n], xn[:, kc, :n], gf[:, kc:kc + 1], None, ALU.mult, R=[xn, gf], W=[xn])
            for tt_ in range(n // 128):
                oo = o.next()
                for hf in range(2):
                    p = pt.next()
                    for j in range(4):
                        kc = hf * 4 + j
                        kb.tr(p[:, j * 128:(j + 1) * 128], xn[:, kc, tt_ * 128:(tt_ + 1) * 128], g.ident[:], R=[xn, g.ident], W=[p])
                    kb.copy(oo[:, hf * 512:(hf + 1) * 512], p[:, :], R=[p], W=[oo], eng=("act" if hf else "dve"))
                t0 = s0 - NCTX + tt_ * 128
                kb.store(g.out[t0:t0 + 128, :], oo[:], R=[oo])


BLK512L = BLK512[1:]
BLK256L = BLK256[1:]

def declare_inputs(nc, g, shapes):
    for name, (shape, dt) in shapes.items():
        setattr(g, name, nc.dram_tensor(name, list(shape), dt, kind="ExternalInput").ap())


def input_shapes():
    S = {}
    S["x"] = ([4096, 1024], F32)
    S["ctx"] = ([256, 1024], F32)
    S["cT"] = ([128, 8, 2], F32)
    S["ada_w"] = ([2, 1024, 6144], F32)
    S["ada_bT"] = ([2, 128, 48], F32)
    S["g1T"] = ([2, 128, 8], F32)
    S["g2T"] = ([2, 128, 8], F32)
    S["gfT"] = ([128, 8], F32)
    S["w_in0"] = ([1024, 4352], F32)
    S["cosT"] = ([128, T], F32)
    S["sinT"] = ([128, T], F32)
    S["ident_h"] = ([128, 128], F32)
    S["hy_w_out"] = ([1024, 1024], F32)
    S["mlp_w1"] = ([2, 1024, 4096], F32)
    S["mlp_w2"] = ([2, 4096, 1024], F32)
    S["rw_cols"] = ([128, 14 + 4 * 7 + 16], F32)
    S["rw_lora"] = ([128, 2, 512], F32)
    S["rw_gup"] = ([128, 512], F32)
    S["blk_h"] = ([128, 128], F32)
    S["cmask_h"] = ([128, 512], F32)
    S["masks_h"] = ([64, 2, 3, 64], F32)
    S["diff_cols"] = ([64, 4], F32)
    S["subln_g"] = ([128, 1], F32)
    S["ssd_w_in"] = ([1024, 6176], F32)
    S["ssd_w_out"] = ([2048, 1024], F32)
    S["conv_wT"] = ([128, 32, 5], F32)
    S["conv_bT"] = ([128, 32], F32)
    S["ssd_prm"] = ([128, 5, 32], F32)
    S["ssd_ngT"] = ([128, 16], F32)
    S["ut1_h"] = ([128, 128], F32)
    S["lt1_h"] = ([128, 128], F32)
    S["sel_h"] = ([32, 32, 128], F32)
    return S


def build(debug=False, stop_after=None, only=None, as_input=()):
    nc = bass.Bass("TRN2", target_bir_lowering=False)
    _AS_INPUT.clear()
    _AS_INPUT.update(as_input)
    g = G()
    declare_inputs(nc, g, input_shapes())
    g.out = nc.dram_tensor("out", [4096, 1024], F32, kind="ExternalOutput").ap()
    dbg = debug
    g.xT = dram(nc, "xT", [1024, T], F32, dbg)
    g.PrT = dram(nc, "PrT", [1792, T], F32, dbg)
    g.QrotT = dram(nc, "QrotT", [512, T], BF16, dbg)
    g.QplT = dram(nc, "QplT", [512, T], BF16, dbg)
    g.KrotT = dram(nc, "KrotT", [512, T], BF16, dbg)
    g.Vd = dram(nc, "Vd", [T, 512], BF16, dbg)
    g.modD = dram(nc, "modD", [2, 128, 96], F32, dbg)
    g.gT = dram(nc, "gT", [512, T], F32, dbg)
    g.bonT = dram(nc, "bonT", [512, T], F32, dbg)
    g.Vtm = dram(nc, "Vtm", [T, 512], BF16, dbg)
    g.gamA = dram(nc, "gamA", [2, 512, 68], F32, dbg)
    g.gam = [g.gamA[0], g.gamA[1]]
    g.FMA = dram(nc, "FMA", [2, 4, 512, T], BF16, dbg)
    g.FM = [[g.FMA[d, k] for k in range(4)] for d in range(2)]
    g.TMA = dram(nc, "TMA", [2, T, 2, 512], BF16, dbg)
    g.TM = [g.TMA[0], g.TMA[1]]
    g.mixT = dram(nc, "mixT", [1024, T], BF16, dbg)
    g.xbcT = dram(nc, "xbcT", [4096, T], F32, False)
    g.zTM = dram(nc, "zTM", [T, 2048], BF16, False)
    g.dtTM = dram(nc, "dtTM", [T, 32], F32, dbg)
    g.xsTM = dram(nc, "xsTM", [T, 2048], F32, dbg)
    g.BT = dram(nc, "BT", [1024, T], BF16, dbg)
    g.CT = dram(nc, "CT", [1024, T], BF16, dbg)
    g.BTM = dram(nc, "BTM", [T, 1024], BF16, False)
    g.YsA = dram(nc, "YsA", [2, T, 2048], F32, dbg)
    g.Yssd = [g.YsA[0], g.YsA[1]]
    g.mixT1 = dram(nc, "mixT1", [2048, T], BF16, dbg)
    g.YA = dram(nc, "YA", [2, T, 512], F32, dbg)
    g.Y = [g.YA[0], g.YA[1]]
    with ExitStack() as es:
        kb = KB(nc, es)
        kb.es_t = None
        g.ident = kb.sb("ident", [128, 128])
        g.ones_bf = kb.sb("ones_bf", [128, 128], BF16)
        g.eps_t = kb.sb("eps_t", [128, 8])
        g.mod = [kb.sb(f"mod{li}", [128, 48, 2]) for li in range(2)]
        g.sc1 = [kb.sb(f"sc1_{li}", [128, 8, 2]) for li in range(2)]
        g.sc2 = [kb.sb(f"sc2_{li}", [128, 8, 2]) for li in range(2)]
        kb.load(g.ident[:], g.ident_h[:, :], W=[g.ident])
        kb.memset(g.ones_bf[:], 1.0, W=[g.ones_bf])
        g.ident_bf = kb.sb("ident_bf", [128, 128], BF16)
        kb.copy(g.ident_bf[:], g.ident[:], R=[g.ident], W=[g.ident_bf])
        kb.memset(g.eps_t[:, 0:1], EPS, W=[g.eps_t])
        kb.memset(g.eps_t[:, 1:2], 1e-12, W=[g.eps_t])
        kb.memset(g.eps_t[:, 2:3], 64e-5, W=[g.eps_t])
        kb.memset(g.eps_t[:, 3:4], 0.0, W=[g.eps_t])
        kb.memset(g.eps_t[:, 4:5], 1.0, W=[g.eps_t])
        phases = [("mods", phase_mods), ("xT", phase_xT), ("A0", phase_A0), ("B0", phase_B0), ("C0", phase_C0), ("C2", phase_C2), ("D0", phase_D0),
                  ("E0", lambda kb, g: phase_E(kb, g, 0, g.hy_w_out, 8, g.mixT, BLK512)),
                  ("F0", lambda kb, g: phase_F(kb, g, 0, BLK256)),
                  ("A1", phase_A1), ("B1", phase_B1), ("C1", phase_C1), ("C3", phase_C3),
                  ("E1", lambda kb, g: phase_E(kb, g, 1, g.ssd_w_out, 16, g.mixT1, BLK512L)),
                  ("F1", lambda kb, g: phase_F(kb, g, 1, BLK256L)), ("G", phase_G)]
        for name, fn in phases:
            if only is not None and name not in only:
                continue
            fn(kb, g)
            if stop_after == name:
                break
        if debug:
            for li in range(2):
                kb.store(g.modD[li], g.mod[li][:].rearrange("p a b -> p (a b)"), R=[g.mod[li]])
        kb.finish()
        print("instructions:", kb.nins)
    return nc


def rope_tables():
    inv = 10000.0 ** (-np.arange(0, 32, 2, dtype=np.float32) / 32.0)
    t = np.arange(4096)
    rows = (t // 64).astype(np.float32)
    cols = (t % 64).astype(np.float32)
    ar = rows[:, None] * inv[None, :]
    ac = cols[:, None] * inv[None, :]
    cosT = np.ones((128, T), np.float32)
    sinT = np.zeros((128, T), np.float32)
    for p in range(128):
        d = p % 64
        ang = ar if d < 32 else ac
        i = d % 16
        first = (d % 32) < 16
        cosT[p, 256:] = np.cos(ang[:, i])
        sinT[p, 256:] = (-np.sin(ang[:, i])) if first else np.sin(ang[:, i])
    return cosT, sinT


def swap_cols(w):
    idx = np.arange(512)
    d = idx % 32
    partner = np.where(d < 16, idx + 16, idx - 16)
    return w[:, partner]


def host_consts(inp):
    C = {}
    f = np.float32
    C["ada_w"] = np.ascontiguousarray(inp["ada_w"], dtype=f)
    C["ada_bT"] = np.ascontiguousarray(inp["ada_b"].reshape(2, 48, 128).transpose(0, 2, 1), dtype=f)
    C["g1T"] = np.ascontiguousarray(inp["norm1_g"].reshape(2, 8, 128).transpose(0, 2, 1), dtype=f)
    C["g2T"] = np.ascontiguousarray(inp["norm2_g"].reshape(2, 8, 128).transpose(0, 2, 1), dtype=f)
    C["gfT"] = np.ascontiguousarray(inp["norm_f_g"].reshape(8, 128).T, dtype=f)
    w = inp["hy_w_in"][0]
    q = w[:, 1792:2304]
    k = w[:, 2304:2816]
    v = w[:, 2816:3328]
    C["w_in0"] = np.ascontiguousarray(np.concatenate([w[:, :1792], q, swap_cols(q), k, swap_cols(k), v], axis=1), dtype=f)
    C["cosT"], C["sinT"] = rope_tables()
    C["ident_h"] = np.eye(128, dtype=f)
    C["hy_w_out"] = np.ascontiguousarray(inp["hy_w_out"][0], dtype=f)
    C["mlp_w1"] = np.ascontiguousarray(inp["mlp_w1"], dtype=f)
    C["mlp_w2"] = np.ascontiguousarray(inp["mlp_w2"], dtype=f)
    col = lambda a: np.ascontiguousarray(np.asarray(a, dtype=f).reshape(-1, 128).T)
    rw = np.zeros((128, 58), f)
    rw[:, 0:14] = col(inp["rwkv_mu"][0])
    rw[:, 14:18] = col(inp["rwkv_k_k"][0])
    rw[:, 18:22] = col(inp["rwkv_k_a"][0])
    rw[:, 22:26] = col(inp["rwkv_r_k"][0].reshape(-1))
    rw[:, 26:30] = col(inp["rwkv_ln_w"][0])
    rw[:, 30:34] = col(inp["rwkv_ln_b"][0])
    rw[:, 34:38] = col(inp["rwkv_w0"][0, 0])
    rw[:, 38:42] = col(inp["rwkv_w0"][0, 1])
    rw[:, 42:46] = col(inp["rwkv_a0"][0, 0])
    rw[:, 46:50] = col(inp["rwkv_a0"][0, 1])
    C["rw_cols"] = rw
    lora = np.zeros((128, 2, 512), f)
    lora[0:64] = inp["rwkv_w_up"][0].transpose(1, 0, 2)
    lora[64:128] = inp["rwkv_a_up"][0].transpose(1, 0, 2)
    C["rw_lora"] = lora
    C["rw_gup"] = np.ascontiguousarray(inp["rwkv_g_up"][0], dtype=f)
    blk = np.zeros((128, 128), f)
    blk[:64, :64] = 1
    blk[64:, 64:] = 1
    C["blk_h"] = blk
    cm = np.ones((128, 512), f)
    cm[:, ::64] = 0
    C["cmask_h"] = cm
    s = np.arange(64)[:, None]
    t = np.arange(64)[None, :]
    m = np.zeros((64, 2, 3, 64), f)
    m[:, 0, 0] = (s < t)
    m[:, 0, 1] = (s <= t)
    m[:, 0, 2] = (s > t)
    m[:, 1, 0] = (s > t)
    m[:, 1, 1] = (s >= t)
    m[:, 1, 2] = (s < t)
    C["masks_h"] = m
    C["diff_cols"] = np.stack([inp["diff_lq1"][0], inp["diff_lk1"][0], inp["diff_lq2"][0], inp["diff_lk2"][0]], axis=1).astype(f)
    C["subln_g"] = np.ascontiguousarray(inp["diff_subln_g"][0].reshape(128, 1), dtype=f)
    C["ssd_w_in"] = np.ascontiguousarray(inp["ssd_w_in"][0], dtype=f)
    C["ssd_w_out"] = np.ascontiguousarray(inp["ssd_w_out"][0], dtype=f)
    C["conv_wT"] = np.ascontiguousarray(inp["ssd_conv_w"][0].reshape(5, 32, 128).transpose(2, 1, 0), dtype=f)
    C["conv_bT"] = col(inp["ssd_conv_b"][0])
    prm = np.zeros((128, 5, 32), f)
    prm[:, 0] = inp["ssd_dt_bias"][0, 0][None, :]
    prm[:, 1] = inp["ssd_dt_bias"][0, 1][None, :]
    prm[:, 2] = inp["ssd_a_log"][0, 0][None, :]
    prm[:, 3] = inp["ssd_a_log"][0, 1][None, :]
    prm[:, 4] = inp["ssd_d"][0][None, :]
    C["ssd_prm"] = prm
    C["ssd_ngT"] = col(inp["ssd_norm_g"][0])
    jj = np.arange(128)[:, None]
    ll = np.arange(128)[None, :]
    C["ut1_h"] = (jj <= ll).astype(f)
    C["lt1_h"] = (jj >= ll).astype(f)
    sel = np.zeros((32, 32, 128), f)
    for h in range(32):
        sel[h, h, :] = 1.0
    C["sel_h"] = sel
    return C


def core_inputs(inp, C, b):
    m = dict(C)
    m["x"] = np.ascontiguousarray(inp["x"][b], dtype=np.float32)
    m["ctx"] = np.ascontiguousarray(inp["ctx"][b], dtype=np.float32)
    cv = np.stack([inp["c"][b], inp["c_ctx"]], axis=0).astype(np.float32)
    m["cT"] = np.ascontiguousarray(cv.reshape(2, 8, 128).transpose(2, 1, 0))
    return m


def kernel(**inputs):
    inp = {k: np.asarray(v) for k, v in inputs.items()}
    C = host_consts(inp)
    nc = build()
    in_maps = [core_inputs(inp, C, b) for b in range(8)]
    res = run_bass_kernel_spmd(nc, in_maps, core_ids=list(range(8)))
    return np.stack([np.asarray(r["out"]) for r in res.results], axis=0).astype(np.float32)
```
